# Optimizing a Trainium2 kernel written in Bass

```python
import math
import jax, jax.numpy as jnp
from jax import lax
import numpy as np

D_MODEL = 1024
BATCH = 8
SEQ = 2048
DEPTH = 2

N_META = 16
CHUNK = 64
PAD_FRONT = CHUNK - N_META
N_BRANCH = 3
BRANCH_WIDTH = D_MODEL
NORM_EPS = 1e-6
DT_MIN = 0.001
DT_MAX = 0.1
SSD_HEAD_DIM = 64
SSD_HEADS = BRANCH_WIDTH // SSD_HEAD_DIM
SSD_WIDTH = SSD_HEADS * SSD_HEAD_DIM
SSD_GROUPS = 2
SSD_STATE = 128
SSD_CONV = 4
SSD_XBC = SSD_WIDTH + 2 * SSD_GROUPS * SSD_STATE
SC_WIDTH = BRANCH_WIDTH
SC_CONV = 3
DN_HEAD_DIM = 128
DN_HEADS = BRANCH_WIDTH // DN_HEAD_DIM
DN_WIDTH = DN_HEADS * DN_HEAD_DIM
DN_CONV = 4
IN_SIZES = (
    SSD_WIDTH,
    SSD_XBC,
    SSD_HEADS,
    SC_WIDTH,
    SC_WIDTH,
    SC_WIDTH,
    SC_WIDTH,
    3 * DN_WIDTH,
    DN_WIDTH,
    DN_HEADS,
    DN_HEADS,
    N_BRANCH * D_MODEL,
)
IN_WIDTH = sum(IN_SIZES)

kernel_name = "hybrid_ssd_shortconv_gdn_block"


def rms_norm(x, gain):
    xf = x.astype(jnp.float32)
    y = xf * lax.rsqrt(jnp.mean(xf * xf, axis=-1, keepdims=True) + NORM_EPS)
    return (y * gain.astype(jnp.float32)).astype(x.dtype)


def l2_normalize(x):
    xf = x.astype(jnp.float32)
    return (xf * lax.rsqrt(jnp.sum(xf * xf, axis=-1, keepdims=True) + NORM_EPS)).astype(x.dtype)


def causal_depthwise_conv(x, w):
    k, ch = w.shape
    return lax.conv_general_dilated(
        x, w[:, None, :].astype(x.dtype), window_strides=(1,), padding=[(k - 1, 0)],
        dimension_numbers=("NWC", "WIO", "NWC"), feature_group_count=ch)


def pad_front(a):
    return jnp.pad(a, [(0, 0), (PAD_FRONT, 0)] + [(0, 0)] * (a.ndim - 2))


def ssd_chunked(x, dt, a, b_mat, c_mat):
    f32 = jnp.float32
    bsz, t, g, r, p = x.shape
    n = b_mat.shape[-1]
    nc = t // CHUNK
    dt = dt.astype(f32)
    xc = (x.astype(f32) * dt[..., None]).reshape(bsz, nc, CHUNK, g, r, p)
    bc = b_mat.astype(f32).reshape(bsz, nc, CHUNK, g, n)
    cc = c_mat.astype(f32).reshape(bsz, nc, CHUNK, g, n)
    a_cum = jnp.cumsum((dt * a).reshape(bsz, nc, CHUNK, g, r), axis=2)
    causal = jnp.tril(jnp.ones((CHUNK, CHUNK), bool))[:, :, None, None]
    seg = a_cum[:, :, :, None] - a_cum[:, :, None, :]
    decay = jnp.exp(jnp.where(causal, seg, -jnp.inf))
    cb = jnp.einsum("bclgn,bcsgn->bclsg", cc, bc)
    y_diag = jnp.einsum("bclsgr,bcsgrp->bclgrp", cb[..., None] * decay, xc)
    a_last = a_cum[:, :, -1]
    states = jnp.einsum("bcsgn,bcsgrp->bcgrpn", bc,
                        xc * jnp.exp(a_last[:, :, None] - a_cum)[..., None])

    def step(h, inp):
        st, dec = inp
        return h * dec[..., None, None] + st, h

    h0 = jnp.zeros((bsz, g, r, p, n), f32)
    _, h_prev = lax.scan(step, h0, (jnp.moveaxis(states, 1, 0), jnp.moveaxis(jnp.exp(a_last), 1, 0)))
    h_prev = jnp.moveaxis(h_prev, 0, 1)
    y_off = jnp.einsum("bclgn,bcgrpn->bclgrp", cc, h_prev) * jnp.exp(a_cum)[..., None]
    return (y_diag + y_off).reshape(bsz, t, g, r, p)


def gated_delta_chunked(q, k, v, g, beta):
    f32 = jnp.float32
    bsz, t, h, dk = q.shape
    dv = v.shape[-1]
    nc = t // CHUNK

    def chunks(a):
        a = a.astype(f32).reshape((bsz, nc, CHUNK, h) + a.shape[3:])
        return jnp.swapaxes(a, 2, 3)

    qc, kc, vc, gc, bc = chunks(q), chunks(k), chunks(v), chunks(g), chunks(beta)
    g_cum = jnp.cumsum(gc, axis=-1)
    causal = jnp.tril(jnp.ones((CHUNK, CHUNK), bool))
    strict = jnp.tril(jnp.ones((CHUNK, CHUNK), bool), k=-1)
    decay = jnp.exp(jnp.where(causal, g_cum[..., :, None] - g_cum[..., None, :], -jnp.inf))
    kk = jnp.einsum("bchld,bchsd->bchls", kc, kc)
    lmat = jnp.where(strict, bc[..., None] * kk * decay, 0.0)
    rhs = jnp.concatenate([vc * bc[..., None], kc * (bc * jnp.exp(g_cum))[..., None]], axis=-1)
    sol = lax.linalg.triangular_solve(jnp.eye(CHUNK, dtype=f32) + lmat, rhs,
                                      left_side=True, lower=True)
    u, w = sol[..., :dv], sol[..., dv:]
    g_last = g_cum[..., -1]
    k_dec = kc * jnp.exp(g_last[..., None] - g_cum)[..., None]

    def step(s, inp):
        u_c, w_c, kd_c, gl_c = inp
        v_new = u_c - jnp.einsum("bhlk,bhkv->bhlv", w_c, s)
        s_next = s * jnp.exp(gl_c)[..., None, None] + jnp.einsum("bhlk,bhlv->bhkv", kd_c, v_new)
        return s_next, (s, v_new)

    s0 = jnp.zeros((bsz, h, dk, dv), f32)
    xs = (jnp.moveaxis(u, 1, 0), jnp.moveaxis(w, 1, 0), jnp.moveaxis(k_dec, 1, 0), jnp.moveaxis(g_last, 1, 0))
    _, (s_prev, v_new) = lax.scan(step, s0, xs)
    s_prev = jnp.moveaxis(s_prev, 0, 1)
    v_new = jnp.moveaxis(v_new, 0, 1)
    qk = jnp.einsum("bchld,bchsd->bchls", qc, kc) * decay
    o = (jnp.einsum("bchlk,bchkv->bchlv", qc * jnp.exp(g_cum)[..., None], s_prev)
         + jnp.einsum("bchls,bchsv->bchlv", qk, v_new))
    return jnp.swapaxes(o, 2, 3).reshape(bsz, t, h, dv)


def ssd_mixer(z, xbc, dt_raw, conv_w, conv_b, dt_bias, a_log, d_skip, norm_w):
    bsz, t, _ = z.shape
    r = SSD_HEADS // SSD_GROUPS
    xbc = jax.nn.silu(causal_depthwise_conv(xbc, conv_w) + conv_b.astype(xbc.dtype))
    xs, b_mat, c_mat = jnp.split(xbc, [SSD_WIDTH, SSD_WIDTH + SSD_GROUPS * SSD_STATE], axis=-1)
    xh = xs.reshape(bsz, t, SSD_GROUPS, r, SSD_HEAD_DIM)
    dt = jax.nn.softplus(dt_raw.astype(jnp.float32) + dt_bias.astype(jnp.float32))
    a = -jnp.exp(a_log.astype(jnp.float32)).reshape(SSD_GROUPS, r)
    y = ssd_chunked(pad_front(xh), pad_front(dt.reshape(bsz, t, SSD_GROUPS, r)), a,
                    pad_front(b_mat.reshape(bsz, t, SSD_GROUPS, SSD_STATE)),
                    pad_front(c_mat.reshape(bsz, t, SSD_GROUPS, SSD_STATE)))[:, PAD_FRONT:]
    y = y + d_skip.astype(jnp.float32).reshape(SSD_GROUPS, r)[..., None] * xh.astype(jnp.float32)
    y = y.astype(z.dtype).reshape(bsz, t, SSD_WIDTH) * jax.nn.silu(z)
    gs = SSD_WIDTH // SSD_GROUPS
    y = rms_norm(y.reshape(bsz, t, SSD_GROUPS, gs), norm_w.reshape(SSD_GROUPS, gs))
    return y.reshape(bsz, t, SSD_WIDTH)


def short_conv_mixer(b_gate, c_gate, h, gate, conv_w):
    y = b_gate * causal_depthwise_conv(c_gate * h, conv_w)
    return y * jax.nn.silu(gate)


def gated_deltanet_mixer(qkv, z, b_raw, a_raw, conv_w, dt_bias, a_log, norm_w):
    bsz, t, _ = qkv.shape
    qkv = jax.nn.silu(causal_depthwise_conv(qkv, conv_w))
    q, k, v = jnp.split(qkv, 3, axis=-1)
    q = l2_normalize(q.reshape(bsz, t, DN_HEADS, DN_HEAD_DIM)) * (DN_HEAD_DIM ** -0.5)
    k = l2_normalize(k.reshape(bsz, t, DN_HEADS, DN_HEAD_DIM))
    v = v.reshape(bsz, t, DN_HEADS, DN_HEAD_DIM)
    beta = jax.nn.sigmoid(b_raw.astype(jnp.float32))
    g = -jnp.exp(a_log.astype(jnp.float32)) * jax.nn.softplus(a_raw.astype(jnp.float32) + dt_bias.astype(jnp.float32))
    o = gated_delta_chunked(pad_front(q), pad_front(k), pad_front(v), pad_front(g),
                            pad_front(beta))[:, PAD_FRONT:]
    o = rms_norm(o.astype(z.dtype), norm_w) * jax.nn.silu(z.reshape(bsz, t, DN_HEADS, DN_HEAD_DIM))
    return o.reshape(bsz, t, DN_WIDTH)


def hybrid_layer(x, norm_pre, norm_post, w_in, ssd_conv_w, ssd_conv_b, ssd_dt_bias, ssd_a_log,
                 ssd_d, ssd_norm, sc_conv_w, dn_conv_w, dn_dt_bias, dn_a_log, dn_norm,
                 w_branch, w_out):
    bsz, t, _ = x.shape
    xn = rms_norm(x, norm_pre)
    proj = xn @ w_in.astype(x.dtype)
    (ssd_z, ssd_xbc, ssd_dt, sc_b, sc_c, sc_h, sc_g,
     dn_qkv, dn_z, dn_b, dn_a, gate_logits) = jnp.split(
        proj, np.cumsum(IN_SIZES)[:-1].tolist(), axis=-1)
    y_ssd = ssd_mixer(ssd_z, ssd_xbc, ssd_dt, ssd_conv_w, ssd_conv_b, ssd_dt_bias, ssd_a_log, ssd_d, ssd_norm)
    y_sc = short_conv_mixer(sc_b, sc_c, sc_h, sc_g, sc_conv_w)
    y_dn = gated_deltanet_mixer(dn_qkv, dn_z, dn_b, dn_a, dn_conv_w, dn_dt_bias, dn_a_log, dn_norm)
    ys = jnp.stack([y_ssd, y_sc, y_dn], axis=2)
    branch = jnp.einsum("btnw,nwd->btnd", ys, w_branch.astype(x.dtype))
    gates = jax.nn.sigmoid(gate_logits.reshape(bsz, t, N_BRANCH, D_MODEL))
    merged = jnp.sum(gates * branch, axis=2)
    out = merged @ w_out.astype(x.dtype)
    return x + rms_norm(out, norm_post)


def setup_inputs(seed: int = 0) -> dict:
    key = jax.random.key(seed)
    ks = jax.random.split(key, 18)
    f32 = jnp.float32

    def normal(k, shape, scale):
        return jax.random.normal(k, shape, f32) * scale

    def gain(k, shape):
        return 1.0 + 0.02 * jax.random.normal(k, shape, f32)

    def dt_bias(k, shape):
        u = jax.random.uniform(k, shape, f32)
        dt = jnp.exp(u * (math.log(DT_MAX) - math.log(DT_MIN)) + math.log(DT_MIN))
        return dt + jnp.log(-jnp.expm1(-dt))

    def a_log(k, shape):
        return jnp.log(jax.random.uniform(k, shape, f32, 1.0, 16.0))

    return {
        "x": normal(ks[0], (BATCH, SEQ, D_MODEL), 1.0),
        "meta_tokens": normal(ks[1], (N_META, D_MODEL), 1.0),
        "norm_pre": gain(ks[2], (DEPTH, D_MODEL)),
        "norm_post": gain(ks[3], (DEPTH, D_MODEL)),
        "w_in": normal(ks[4], (DEPTH, D_MODEL, IN_WIDTH), D_MODEL ** -0.5),
        "ssd_conv_w": normal(ks[5], (DEPTH, SSD_CONV, SSD_XBC), SSD_CONV ** -0.5),
        "ssd_conv_b": normal(ks[6], (DEPTH, SSD_XBC), 0.02),
        "ssd_dt_bias": dt_bias(ks[7], (DEPTH, SSD_HEADS)),
        "ssd_a_log": a_log(ks[8], (DEPTH, SSD_HEADS)),
        "ssd_d": gain(ks[9], (DEPTH, SSD_HEADS)),
        "ssd_norm": gain(ks[10], (DEPTH, SSD_WIDTH)),
        "sc_conv_w": normal(ks[11], (DEPTH, SC_CONV, SC_WIDTH), SC_CONV ** -0.5),
        "dn_conv_w": normal(ks[12], (DEPTH, DN_CONV, 3 * DN_WIDTH), DN_CONV ** -0.5),
        "dn_dt_bias": dt_bias(ks[13], (DEPTH, DN_HEADS)),
        "dn_a_log": a_log(ks[14], (DEPTH, DN_HEADS)),
        "dn_norm": gain(ks[15], (DEPTH, DN_HEAD_DIM)),
        "w_branch": normal(ks[16], (DEPTH, N_BRANCH, BRANCH_WIDTH, D_MODEL), BRANCH_WIDTH ** -0.5),
        "w_out": normal(ks[17], (DEPTH, D_MODEL, D_MODEL), D_MODEL ** -0.5),
    }


def reference(x, meta_tokens, norm_pre, norm_post, w_in, ssd_conv_w, ssd_conv_b, ssd_dt_bias,
              ssd_a_log, ssd_d, ssd_norm, sc_conv_w, dn_conv_w, dn_dt_bias, dn_a_log, dn_norm,
              w_branch, w_out):
    bsz = x.shape[0]
    meta = jnp.broadcast_to(meta_tokens.astype(x.dtype)[None], (bsz, N_META, D_MODEL))
    h = jnp.concatenate([meta, x], axis=1)
    for i in range(DEPTH):
        h = hybrid_layer(h, norm_pre[i], norm_post[i], w_in[i], ssd_conv_w[i], ssd_conv_b[i],
                         ssd_dt_bias[i], ssd_a_log[i], ssd_d[i], ssd_norm[i], sc_conv_w[i],
                         dn_conv_w[i], dn_dt_bias[i], dn_a_log[i], dn_norm[i], w_branch[i], w_out[i])
    return h[:, N_META:]
```

```python
import contextlib
import numpy as np
import concourse.bass as bass
import concourse.mybir as mybir

F32 = mybir.dt.float32
F32R = mybir.dt.float32r
BF16 = mybir.dt.bfloat16
ALU = mybir.AluOpType
AF = mybir.ActivationFunctionType
AX = mybir.AxisListType


class Op:
    __slots__ = ("eng", "fn", "deps", "chan", "needs_inc", "event", "idx", "dur", "succ", "nd", "ready", "fin", "pos", "lat", "tag")

    def __init__(self, eng, fn, deps, chan, dur):
        self.eng = eng
        self.fn = fn
        self.deps = deps
        self.chan = chan
        self.needs_inc = chan is not None
        self.event = None
        self.dur = dur
        self.succ = []
        self.ready = 0.0
        self.fin = 0.0


class Prog:
    ENGS = ("pe", "act", "dve", "pool", "sp")

    def __init__(self, nc, same_eng_sync=True, schedule=True, window=4000):
        self.nc = nc
        self.ops = []
        self.last_w = {}
        self.readers = {}
        self.same_eng_sync = same_eng_sync
        self.schedule = schedule
        self.window = window

    def op(self, eng, fn, reads=(), writes=(), chan=None, dur=0.1):
        deps = []
        for k in reads:
            w = self.last_w.get(k)
            if w is not None:
                deps.append(w)
        for k in writes:
            w = self.last_w.get(k)
            if w is not None:
                deps.append(w)
            deps.extend(self.readers.get(k, ()))
        o = Op(eng, fn, deps, chan, dur)
        o.tag = getattr(self, "cur_tag", "")
        o.idx = len(self.ops)
        self.ops.append(o)
        for k in reads:
            self.readers.setdefault(k, []).append(o)
        for k in writes:
            self.last_w[k] = o
            self.readers[k] = []
        return o

    def _sem_edge(self, d, o):
        if d.chan is None and o.chan is None and d.eng == o.eng:
            if o.eng == "pe" or not self.same_eng_sync:
                return False
        return True

    def _list_schedule(self):
        import heapq
        ops = self.ops
        for o in ops:
            ds = []
            seen = set()
            for d in o.deps:
                if d is o or id(d) in seen:
                    continue
                seen.add(id(d))
                ds.append(d)
            o.deps = ds
            o.nd = len(ds)
            for d in ds:
                d.succ.append(o)
        SEM_LAT = 0.12
        cp = [0.0] * len(ops)
        for o in reversed(ops):
            m = 0.0
            for s_ in o.succ:
                if cp[s_.idx] > m:
                    m = cp[s_.idx]
            cp[o.idx] = m + (o.dur + (2.0 if o.chan is not None else 0.0)) + 0.1
        self.cp_len = max(cp) if cp else 0.0
        free = {e: 0.0 for e in self.ENGS}
        pend = {e: [] for e in self.ENGS}
        avail = {e: [] for e in self.ENGS}
        order = {e: [] for e in self.ENGS}
        dma_pipe = 0.0
        for o in ops:
            if o.nd == 0:
                heapq.heappush(pend[o.eng], (0.0, o.idx))
        nleft = len(ops)
        lo = 0
        done = [False] * (len(ops) + 1)
        W = self.window
        cand_l = {e: [] for e in self.ENGS}
        for e in self.ENGS:
            while pend[e]:
                cand_l[e].append(heapq.heappop(pend[e])[1])
        while nleft:
            while done[lo]:
                lo += 1
            best = None
            for e in self.ENGS:
                t = free[e]
                bc_ = None
                for idx in cand_l[e]:
                    if idx >= lo + W:
                        continue
                    stt_ = max(t, ops[idx].ready)
                    key_ = (stt_, -cp[idx], idx)
                    if bc_ is None or key_ < bc_:
                        bc_ = key_
                if bc_ is None:
                    continue
                if best is None or bc_ < best[0]:
                    best = (bc_, e)
            st, idx, e = best[0][0], best[0][2], best[1]
            cand_l[e].remove(idx)
            o = ops[idx]
            if o.chan is not None:
                dma_pipe = max(dma_pipe, st) + o.dur
                o.fin = dma_pipe + 2.0
                free[e] = st + 0.06
            else:
                o.fin = st + o.dur
                free[e] = o.fin
            o.pos = len(order[e])
            order[e].append(o)
            done[idx] = True
            nleft -= 1
            for s in o.succ:
                lat = SEM_LAT if self._sem_edge(o, s) else 0.0
                if o.fin + lat > s.ready:
                    s.ready = o.fin + lat
                s.nd -= 1
                if s.nd == 0:
                    cand_l[s.eng].append(s.idx)
        self.sim_time = max(o.fin for o in ops)
        return order

    def finalize(self, stack):
        nc = self.nc
        if self.schedule:
            order = self._list_schedule()
        else:
            order = {e: [] for e in self.ENGS}
            for o in self.ops:
                seen = set()
                ds = []
                for d in o.deps:
                    if d is o or id(d) in seen:
                        continue
                    seen.add(id(d))
                    ds.append(d)
                o.deps = ds
                o.pos = len(order[o.eng])
                order[o.eng].append(o)
        for o in self.ops:
            o.deps = [d for d in o.deps if self._sem_edge(d, o)]
        sems = {}
        cnt = {}

        def getsem(name):
            if name not in sems:
                sems[name] = stack.enter_context(nc.semaphore(name))
                cnt[name] = 0
            return sems[name]

        chan_pos = {}
        for e in self.ENGS:
            for o in order[e]:
                if o.chan is not None:
                    chan_pos[o.chan] = chan_pos.get(o.chan, 0) + 1
                    o.lat = chan_pos[o.chan]
        per = {}
        nw = 0
        for e in self.ENGS:
            wpos = {}
            lst = []
            for o in order[e]:
                best = {}
                for d in o.deps:
                    if d.chan is not None:
                        st_, ps_ = d.chan, d.lat
                    else:
                        st_, ps_ = "e_" + d.eng, d.pos
                    if wpos.get(st_, -1) >= ps_:
                        continue
                    if st_ not in best or best[st_][0] < ps_:
                        best[st_] = (ps_, d)
                need = []
                for st_, (ps_, d) in best.items():
                    wpos[st_] = ps_
                    d.needs_inc = True
                    need.append(d)
                nw += len(need)
                lst.append((o, need))
            per[e] = lst
        for e in self.ENGS:
            for o in order[e]:
                if o.chan is not None:
                    getsem(o.chan)
                    cnt[o.chan] += 16
                    o.event = (o.chan, cnt[o.chan])
                elif o.needs_inc:
                    nm = "e_" + o.eng
                    getsem(nm)
                    cnt[nm] += 1
                    o.event = (nm, cnt[nm])
        self.nwaits = nw
        self.sem_max = dict(cnt)
        assert max(cnt.values()) < 30000, cnt

        def emit(eng_obj, lst):
            for o, need in lst:
                for d in need:
                    eng_obj.wait_ge(sems[d.event[0]], d.event[1])
                ins = o.fn(eng_obj)
                if o.event is not None:
                    if o.chan is not None:
                        ins.then_inc(sems[o.chan], 16)
                    else:
                        ins.then_inc(sems[o.event[0]], 1)

        with nc.Block() as block:
            @block.tensor
            def _(e):
                emit(e, per["pe"])

            @block.scalar
            def _(e):
                emit(e, per["act"])

            @block.vector
            def _(e):
                emit(e, per["dve"])

            @block.gpsimd
            def _(e):
                emit(e, per["pool"])

            @block.sync
            def _(e):
                emit(e, per["sp"])
                for s, v in cnt.items():
                    if v > 0:
                        e.wait_ge(sems[s], v)


from concourse.bass_utils import run_bass_kernel_spmd

D = 1024
KC = 8
NMETA = 16
EPS = 1e-6
PF_NPRE, PF_NPOST, PF_SCW, PF_SCB, PF_SD, PF_SNORM, PF_CCW, PF_DCW, PF_DNORM, NPF = 0, 8, 16, 64, 76, 84, 92, 116, 212, 213
O_Z, O_XBC, O_DT, O_SCB, O_SCC, O_SCH, O_SCG, O_Q, O_K, O_V, O_DZ, O_DB, O_DA, O_GATE = (
    0, 1024, 2560, 2576, 3600, 4624, 5648, 6672, 7696, 8720, 9744, 10768, 10776, 10784)
NW1 = 13824


def build_program(S, L, NTM=256, dbg=None, same_eng_sync=True, schedule=True, window=100000):
    TOK = NMETA + S
    TP = ((TOK + 63) // 64) * 64
    tiles = []
    c = 0
    while c < TP:
        n = min(NTM, TP - c)
        tiles.append((c, n))
        c += n
    nc = bass.Bass("TRN2", target_bir_lowering=False)
    dt_in = lambda n, s: nc.dram_tensor(n, s, F32, kind="ExternalInput").ap()
    x_d = dt_in("x", [S, D])
    meta_d = dt_in("meta", [NMETA, D])
    w1_d = dt_in("w1", [L, D, NW1])
    wsm_d = dt_in("wsm", [L, D, 32])
    wb_d = dt_in("wb", [L, 3, D, D])
    wo_d = dt_in("wo", [L, D, D])
    pf_d = dt_in("pf", [128, L, NPF])
    pb_d = dt_in("pb", [128, L, 48])
    cst_d = dt_in("cst", [128, 384])
    out_d = nc.dram_tensor("out", [S, D], F32, kind="ExternalOutput").ap()
    NBLK = 35
    wbf_d = nc.dram_tensor("wbf", [L, NBLK, 128, KC * 512], BF16, kind="Internal").ap()
    dbg_d = {}
    if dbg:
        for name, shp in dbg.items():
            dbg_d[name] = nc.dram_tensor("dbg_" + name, list(shp), F32, kind="ExternalOutput").ap()

    with contextlib.ExitStack() as st:
        def sb(n, s):
            return st.enter_context(nc.sbuf_tensor("s_" + n, list(s), F32))
        hT = sb("hT", [128, KC, TP])
        NWS = 3
        sbb = lambda n, shp: st.enter_context(nc.sbuf_tensor("s_" + n, list(shp), BF16))
        wsl = [sbb("wsl%d" % i, [128, KC, 512]) for i in range(NWS)]
        xnT = sbb("xnT", [128, KC, NTM])
        yB = sbb("yB", [128, KC, NTM])
        mergedB = sbb("mergedB", [128, KC, NTM])
        bcT = sb("bcT", [128, 4, NTM])
        yT = sb("yT", [128, KC, NTM])
        merged = sb("merged", [128, KC, NTM])
        xstg = [merged[:].rearrange("p a b -> p (a b)")[:, 0:1024]]
        dnT = sb("dnT", [128, 4, 8, NTM])
        xsT = dnT[:, 0]
        siluz = dnT[:, 1]
        NTMB = 16
        tmb = [sb("tm%d" % i, [64, 512 if i < 9 else 256]) for i in range(NTMB)]
        f128 = [sb("f128_%d" % i, [128, 256]) for i in range(4)]
        sst = sb("sst", [128, 1024])
        dst = sb("dst", [128, 1024])
        pre = [sb("pre%d" % i, [128, NTM + 3]) for i in range(2)]
        cacc = [sb("cacc%d" % i, [128, NTM]) for i in range(1)]
        sqs = [sb("sqs%d" % i, [128, NTM]) for i in range(1)]
        rsb = sb("rsb", [128, 2, NTM])
        sig = [sb("sig%d" % i, [128, NTM]) for i in range(2)]
        mtmp = [sb("mtmp%d" % i, [128, NTM]) for i in range(1)]
        tails = sb("tails", [128, 44, 3])
        pf = sb("pf", [128, L, NPF])
        pbt = sb("pbt", [128, L, 48])
        negA = sb("negA", [128, L, 24])
        cst = sb("cst", [128, 384])
        wsm = sbb("wsm", [128, L, KC, 32])
        tsm = sb("tsm", [64, NTM // 64, 32])
        tk = sb("tk", [64, NTM // 64, 48])
        smE = sb("smE", [64, 32])
        elast = sb("elast", [128, 16])
        fac = sb("fac", [64, 32])
        ps = st.enter_context(nc.psum_tensor("ps", [128, 4096], F32))

        ident = cst[:, 0:128]
        ones = cst[:, 128:256]
        U = cst[0:64, 256:320]
        G = cst[0:64, 320:384]
        I64 = cst[0:64, 0:64]

        p = Prog(nc, same_eng_sync=same_eng_sync, schedule=schedule, window=window)
        state = {"rr": 0, "tm": 0, "ws": 0, "i2": {}}

        def bank(i):
            return ps[:, i * 512:(i + 1) * 512], "b%d" % i

        def nb():
            i = state["rr"]
            state["rr"] = (i + 1) % 6
            return bank(i)

        def rot(name, n):
            i = state["i2"].get(name, 0)
            state["i2"][name] = (i + 1) % n
            return i

        def TB(i):
            return tmb[i], "tm%d" % i

        def ntm():
            i = state["tm"]
            state["tm"] = (i + 1) % NTMB
            return tmb[i], "tm%d" % i

        def fsz(ap):
            n = 1
            for d in ap.shape[1:]:
                n *= int(d)
            return n

        PASSES = 4.0

        def mm(out, lhsT, rhs, r, w, start=True, stop=True, passes=PASSES):
            p.op("pe", lambda e: e.matmul(out, lhsT, rhs, start=start, stop=stop), reads=r, writes=w,
                 dur=max(fsz(rhs), 64) * passes / 2400.0 + 0.03)

        def tr(out, in_, r, w):
            n = in_.shape[0]
            p.op("pe", lambda e: e.transpose(out, in_, ident[0:n, 0:n]), reads=r + ["cst"], writes=w, dur=0.09)

        def act(out, in_, func, r, w, bias=None, scale=None):
            kw = {}
            if bias is not None:
                kw["bias"] = bias
            if scale is not None:
                kw["scale"] = scale
            p.op("act", lambda e: e.activation(out, in_, func, **kw), reads=r, writes=w, dur=0.2 + fsz(out) / 960.0)

        def acopy(out, in_, r, w):
            p.op("act", lambda e: e.copy(out, in_), reads=r, writes=w, dur=0.2 + fsz(out) / 960.0)

        def edur(eng, n):
            return (0.12 + n / 960.0) if eng == "dve" else (0.2 + n / 400.0)

        def tt(out, a, b, op, r, w, eng="dve"):
            p.op(eng, lambda e: e.tensor_tensor(out, a, b, op), reads=r, writes=w, dur=edur(eng, fsz(out)))

        def ts(out, a, s1, op0, r, w, s2=None, op1=None, eng="dve"):
            if op1 is None:
                p.op(eng, lambda e: e.tensor_scalar(out, a, s1, None, op0), reads=r, writes=w, dur=edur(eng, fsz(out)))
            else:
                p.op(eng, lambda e: e.tensor_scalar(out, a, s1, s2, op0, op1), reads=r, writes=w, dur=edur(eng, fsz(out)))

        def stt(out, in0, scalar, in1, op0, op1, r, w):
            p.op("dve", lambda e: e.scalar_tensor_tensor(out, in0, scalar, in1, op0, op1), reads=r, writes=w,
                 dur=edur("dve", fsz(out)))

        def recip(out, in_, r, w):
            p.op("dve", lambda e: e.reciprocal(out, in_), reads=r, writes=w, dur=edur("dve", fsz(out)))

        def vcopy(out, in_, r, w, eng="dve"):
            p.op(eng, lambda e: e.tensor_copy(out, in_), reads=r, writes=w, dur=edur(eng, fsz(out)))

        def memset(ap, val, w, eng="pool"):
            p.op(eng, lambda e: e.memset(ap, val), writes=w, dur=edur(eng, fsz(ap)))

        def dma(out, in_, r, w, chan, eng="sp"):
            nbytes = (2 if out.dtype == BF16 else 4) * fsz(out) * int(out.shape[0]) * (3 if eng == "pool" else 1)
            p.op(eng, lambda e: e.dma_start(out=out, in_=in_), reads=r, writes=w, chan=chan, dur=nbytes / 250e3)

        def dump(name, ap, keys):
            if name in dbg_d:
                dma(dbg_d[name], ap, keys, [], "dbg")

        def rsqrt_to(out, in_, scale, r, w):
            act(out, in_, AF.Sqrt, r, w, bias=EPS, scale=scale)
            recip(out, out, w, w)

        dma(cst[:], cst_d, [], ["cst"], "ld_cst")
        dma(pf[:], pf_d, [], ["pf"], "ld_pf")
        dma(pbt[:], pb_d, [], ["pbt"], "ld_pb")
        for l in range(L):
            dma(wsm[:, l], wsm_d[l].rearrange("(k p) c -> p k c", p=128), [], ["wsm%d" % l], "ld_wsm%d" % l, eng="pool")
        for l in range(L):
            act(negA[:, l, 0:16], pbt[:, l, 16:32], AF.Exp, ["pbt"], ["negA"])
            act(negA[:, l, 16:24], pbt[:, l, 40:48], AF.Exp, ["pbt"], ["negA"])
        ts(negA[:], negA[:], -1.0, ALU.mult, ["negA"], ["negA"])

        hkey = lambda t: "hT%d" % t

        def tile_of(col):
            for ti, (c0, n) in enumerate(tiles):
                if c0 <= col < c0 + n:
                    return ti
            raise ValueError

        def hkeys(c_lo, c_hi):
            return sorted({hkey(tile_of(c)) for c in (c_lo, c_hi - 1)} | {hkey(t) for t in range(tile_of(c_lo), tile_of(c_hi - 1) + 1)})

        if TP > TOK:
            memset(hT[:, :, TOK:TP], 0.0, hkeys(TOK, TP))
        row_blocks = [("meta", 0, NMETA, 0)] + [("x", r0, min(128, S - r0), NMETA + r0) for r0 in range(0, S, 128)]
        for (src, r0, nr, col0) in row_blocks:
            si = rot("xstg", 1)
            skey = "merged"
            stg2 = xstg[si]
            srcap = meta_d[0:nr, :] if src == "meta" else x_d[r0:r0 + nr, :]
            dma(stg2[0:nr, :], srcap, [], [skey], "xch%d" % si)
            for half in range(2):
                bk, bkey = nb()
                for q in range(4):
                    k = half * 4 + q
                    tr(bk[:, q * 128:q * 128 + nr], stg2[0:nr, k * 128:(k + 1) * 128], [skey], [bkey])
                acopy(hT[:, half * 4:half * 4 + 4, col0:col0 + nr],
                      bk.rearrange("p (a b) -> p a b", b=128)[:, :, 0:nr], [bkey], hkeys(col0, col0 + nr))

        def blk_src(l, blk):
            if blk < 27:
                return w1_d[l][:, blk * 512:(blk + 1) * 512]
            if blk < 33:
                n, half = divmod(blk - 27, 2)
                return wb_d[l, n][:, half * 512:(half + 1) * 512]
            return wo_d[l][:, (blk - 33) * 512:(blk - 32) * 512]

        use_order = [0, 1, 2, 3, 4, 21, 27, 22, 28] + list(range(5, 13)) + [23, 29, 24, 30] + list(range(13, 21)) + [25, 31, 26, 32, 33, 34]
        assert sorted(use_order) == list(range(NBLK))
        GRP = 9
        grp_of = {}
        for l in range(L):
            for gi in range(0, NBLK, GRP):
                grp = use_order[gi:gi + GRP]
                chn = "cv%d_%d" % (l, gi // GRP)
                for blk in grp:
                    dma(wbf_d[l, blk].rearrange("p (k c) -> p k c", k=KC), blk_src(l, blk).rearrange("(k p) c -> p k c", p=128),
                        [], ["wbf%d_%d" % (l, blk)], chn, eng="pool")
                for blk in grp:
                    grp_of[(l, blk)] = grp

        def wload(l, blk):
            si = state["ws"] % NWS
            state["ws"] += 1
            key = "wsl%d" % si
            dma(wsl[si][:], wbf_d[l, blk].rearrange("p (k c) -> p k c", k=KC), ["wbf%d_%d" % (l, b_) for b_ in grp_of[(l, blk)]], [key],
                "wch%d" % si)
            return wsl[si], key

        def proj(out_bank, okey, wslot, wkey, coff, NT, rhs=None, rkey="xnT"):
            rhs = xnT if rhs is None else rhs
            for k in range(KC):
                mm(out_bank[:, 0:NT], wslot[:, k, coff:coff + 128], rhs[:, k, 0:NT], [wkey, rkey], [okey],
                   start=(k == 0), stop=(k == KC - 1), passes=1.0)

        def conv(bk, bkey, ti, wcol0, K, l, NT, dest, dkey, bias_col=None, mul=None, mulkey=None):
            H = K - 1
            i = rot("pre", 2)
            pr, pk = pre[i], "pre%d" % i
            ca, ck = cacc[0], "cacc0"
            vcopy(pr[:, 0:H], tails[:, ti, 0:H], ["tails%d" % ti], [pk], eng="pool")
            if mul is None:
                acopy(pr[:, H:H + NT], bk[:, 0:NT], [bkey], [pk])
            else:
                tt(pr[:, H:H + NT], bk[:, 0:NT], mul, ALU.mult, [bkey, mulkey], [pk])
            vcopy(tails[:, ti, 0:H], pr[:, NT:NT + H], [pk], ["tails%d" % ti], eng="pool")
            ts(ca[:, 0:NT], pr[:, 0:NT], pf[:, l, wcol0:wcol0 + 1], ALU.mult, [pk, "pf"], [ck])
            for k in range(1, K):
                o = dest if (k == K - 1 and K == 3) else ca[:, 0:NT]
                ok = [dkey] if (k == K - 1 and K == 3) else [ck]
                stt(o, pr[:, k:k + NT], pf[:, l, wcol0 + k:wcol0 + k + 1], ca[:, 0:NT], ALU.mult, ALU.add,
                    [pk, "pf", ck], ok)
            if K == 4:
                if bias_col is not None:
                    act(dest, ca[:, 0:NT], AF.Silu, [ck, "pf"], [dkey], bias=pf[:, l, bias_col:bias_col + 1])
                else:
                    act(dest, ca[:, 0:NT], AF.Silu, [ck], [dkey])

        bc = lambda ap, shape, axis: ap.unsqueeze(axis).to_broadcast(list(shape))

        for l in range(L):
            memset(sst[:], 0.0, ["sst"])
            memset(dst[:], 0.0, ["dst"])
            memset(tails[:], 0.0, ["tails%d" % i for i in range(44)])
            w1 = w1_d[l]
            for ti, (c0, NT) in enumerate(tiles):
                nch = NT // 64
                hk = hkey(ti)
                cols = slice(c0, c0 + NT)
                p.cur_tag = "S0.%d.%d" % (l, ti)
                bN, bNk = bank(7)
                for k in range(KC):
                    i = rot("sqs", 1)
                    act(sqs[i][:, 0:NT], hT[:, k, cols], AF.Square, [hk], ["sqs%d" % i])
                    mm(bN[:, 0:NT], ones, sqs[i][:, 0:NT], ["cst", "sqs%d" % i], [bNk], start=(k == 0), stop=(k == KC - 1))
                rsqrt_to(rsb[:, 0, 0:NT], bN[:, 0:NT], 1.0 / D, [bNk], ["rsb"])
                for k in range(KC):
                    stt(xnT[:, k, 0:NT], hT[:, k, cols], pf[:, l, PF_NPRE + k:PF_NPRE + k + 1], rsb[:, 0, 0:NT],
                        ALU.mult, ALU.mult, [hk, "pf", "rsb"], ["xnT"])
                if l == 0 and ti == 0 and "xnT" in dbg_d:
                    acopy(merged[:, :, 0:NT], xnT[:, :, 0:NT], ["xnT"], ["merged"])
                    dump("xnT", merged[:, :, 0:NT], ["merged"])
                p.cur_tag = "S1.%d.%d" % (l, ti)
                bS, bSk = bank(6)
                for c in range(nch):
                    for k in range(KC):
                        mm(bS[0:64, c * 32:(c + 1) * 32], xnT[:, k, c * 64:(c + 1) * 64], wsm[:, l, k, :], ["xnT", "wsm%d" % l], [bSk],
                           start=(k == 0), stop=(k == KC - 1), passes=1.0)
                acopy(tsm[:, 0:nch, :], bS[0:64, 0:nch * 32].rearrange("p (c f) -> p c f", f=32), [bSk], ["tsm"])
                tt(tk[:, 0:nch, 0:16], tsm[:, 0:nch, 0:16], bc(pbt[0:64, l, 0:16], [64, nch, 16], 1), ALU.add, ["tsm", "pbt"], ["tk"])
                tt(tk[:, 0:nch, 40:48], tsm[:, 0:nch, 24:32], bc(pbt[0:64, l, 32:40], [64, nch, 8], 1), ALU.add, ["tsm", "pbt"], ["tk"])
                act(tk[:, 0:nch, 0:16], tk[:, 0:nch, 0:16], AF.Exp, ["tk"], ["tk"])
                act(tk[:, 0:nch, 40:48], tk[:, 0:nch, 40:48], AF.Exp, ["tk"], ["tk"])
                act(tk[:, 0:nch, 0:16], tk[:, 0:nch, 0:16], AF.Ln, ["tk"], ["tk"], bias=1.0)
                act(tk[:, 0:nch, 40:48], tk[:, 0:nch, 40:48], AF.Ln, ["tk"], ["tk"], bias=1.0)
                act(tk[:, 0:nch, 32:40], tsm[:, 0:nch, 16:24], AF.Sigmoid, ["tsm"], ["tk"])
                tt(tk[:, 0:nch, 16:32], tk[:, 0:nch, 0:16], bc(negA[0:64, l, 0:16], [64, nch, 16], 1), ALU.mult, ["tk", "negA"], ["tk"])
                tt(tk[:, 0:nch, 40:48], tk[:, 0:nch, 40:48], bc(negA[0:64, l, 16:24], [64, nch, 8], 1), ALU.mult, ["tk", "negA"], ["tk"])
                if l == 0 and ti == 0:
                    dump("tk", tk[:, 0:nch, :], ["tk"])
                p.cur_tag = "S2.%d.%d" % (l, ti)
                for blk in range(2):
                    ws, wk = wload(l, blk)
                    for q in range(4):
                        fc = blk * 4 + q
                        bk, bkey = nb()
                        proj(bk, bkey, ws, wk, q * 128, NT)
                        act(siluz[:, fc, 0:NT], bk[:, 0:NT], AF.Silu, [bkey], ["dnT1"])
                for blk in range(3):
                    ws, wk = wload(l, 2 + blk)
                    for q in range(4):
                        fc = blk * 4 + q
                        bk, bkey = nb()
                        proj(bk, bkey, ws, wk, q * 128, NT)
                        if fc < 8:
                            dest, dkey = xsT[:, fc, 0:NT], "dnT0"
                        else:
                            dest, dkey = bcT[:, fc - 8, 0:NT], "bcT"
                        conv(bk, bkey, fc, PF_SCW + fc * 4, 4, l, NT, dest, dkey, bias_col=PF_SCB + fc)
                if l == 0 and ti == 0:
                    dump("xsT", xsT[:, :, 0:NT], ["dnT0"])
                    dump("bcT", bcT[:, :, 0:NT], ["bcT"])
                p.cur_tag = "S3.%d.%d" % (l, ti)
                for c in range(nch):
                    cc = slice(c * 64, (c + 1) * 64)
                    dt_c = tk[:, c, 0:16]
                    dtA_c = tk[:, c, 16:32]
                    bM, bMk = bank(6)
                    mm(bM[0:64, 0:16], U, dtA_c, ["cst", "tk"], [bMk])
                    mm(bM[0:64, 16:32], G, dtA_c, ["cst", "tk"], [bMk])
                    mm(bM[0:128, 32:48], ones[0:64, :], dtA_c, ["cst", "tk"], [bMk])
                    act(smE[:, 0:32], bM[0:64, 0:32], AF.Exp, [bMk], ["smE"])
                    act(elast[:, 0:16], bM[:, 32:48], AF.Exp, [bMk], ["elast"])
                    for g in range(2):
                        hs = slice(g * 8, (g + 1) * 8)
                        state["tm"] = 0
                        bX, bXk = nb()
                        for q in range(4):
                            tr(bX[0:64, q * 128:(q + 1) * 128], xsT[:, g * 4 + q, cc], ["dnT0"], [bXk])
                        bB, bBk = nb()
                        tr(bB[0:64, 0:128], bcT[:, g, cc], ["bcT"], [bBk])
                        xc, xck = ntm()
                        tt(xc[:].rearrange("p (h d) -> p h d", d=64), bX[0:64, :].rearrange("p (h d) -> p h d", d=64),
                           bc(dt_c[:, hs], [64, 8, 64], 2), ALU.mult, [bXk, "tk"], [xck])
                        xd, xdk = ntm()
                        tt(xd[:].rearrange("p (h d) -> p h d", d=64), xc[:].rearrange("p (h d) -> p h d", d=64),
                           bc(smE[:, 16 + g * 8:16 + (g + 1) * 8], [64, 8, 64], 2), ALU.mult, [xck, "smE"], [xdk])
                        btok, btk = ntm()
                        acopy(btok[:, 0:128], bB[0:64, 0:128], [bBk], [btk])
                        gm, gmk = ntm()
                        tt(gm[:].rearrange("p (h d) -> p h d", d=64), bc(G, [64, 8, 64], 1), bc(dtA_c[:, hs], [64, 8, 64], 2),
                           ALU.mult, ["cst", "tk"], [gmk])
                        bE, bEk = nb()
                        for h in range(8):
                            mm(bE[0:64, h * 64:(h + 1) * 64], gm[:, h * 64:(h + 1) * 64], U, [gmk, "cst"], [bEk])
                        E, Ek = ntm()
                        act(E[:], bE[0:64, :], AF.Exp, [bEk], [Ek])
                        bC, bCk = nb()
                        mm(bC[0:64, 0:64], bcT[:, g, cc], bcT[:, 2 + g, cc], ["bcT"], [bCk])
                        cbm, cbk = ntm()
                        tt(cbm[:, 0:64], bC[0:64, 0:64], U, ALU.mult, [bCk, "cst"], [cbk])
                        MT, MTk = ntm()
                        tt(MT[:].rearrange("p (h d) -> p h d", d=64), E[:].rearrange("p (h d) -> p h d", d=64),
                           bc(cbm[:, 0:64], [64, 8, 64], 1), ALU.mult, [Ek, cbk], [MTk])
                        bY, bYk = nb()
                        for h in range(8):
                            mm(bY[0:64, h * 64:(h + 1) * 64], MT[:, h * 64:(h + 1) * 64], xc[:, h * 64:(h + 1) * 64], [MTk, xck], [bYk])
                        bT, bTk = nb()
                        mm(bT[0:64, :], bcT[:, 2 + g, cc], sst[:, g * 512:(g + 1) * 512], ["bcT", "sst"], [bTk])
                        tmp, tmpk = ntm()
                        tt(tmp[:].rearrange("p (h d) -> p h d", d=64), bT[0:64, :].rearrange("p (h d) -> p h d", d=64),
                           bc(smE[:, g * 8:(g + 1) * 8], [64, 8, 64], 2), ALU.mult, [bTk, "smE"], [tmpk])
                        ytok, ytk = ntm()
                        tt(ytok[:], bY[0:64, :], tmp[:], ALU.add, [bYk, tmpk], [ytk])
                        bR, bRk = nb()
                        for q in range(4):
                            tr(bR[:, q * 64:(q + 1) * 64], ytok[:, q * 128:(q + 1) * 128], [ytk], [bRk])
                        acopy(yT[:, g * 4:(g + 1) * 4, cc], bR[:, 0:256].rearrange("p (a b) -> p a b", b=64), [bRk], ["yT"])
                        bU, bUk = nb()
                        mm(bU[:, :], btok[:, 0:128], xd[:], [btk, xdk], [bUk])
                        sg_ = sst[:, g * 512:(g + 1) * 512]
                        tt(sg_.rearrange("p (h d) -> p h d", d=64), sg_.rearrange("p (h d) -> p h d", d=64),
                           bc(elast[:, hs], [128, 8, 64], 2), ALU.mult, ["sst", "elast"], ["sst"])
                        tt(sg_, sg_, bU[:, :], ALU.add, ["sst", bUk], ["sst"])
                p.cur_tag = "S3b.%d.%d" % (l, ti)
                for fc in range(KC):
                    stt(yT[:, fc, 0:NT], xsT[:, fc, 0:NT], pf[:, l, PF_SD + fc:PF_SD + fc + 1], yT[:, fc, 0:NT], ALU.mult, ALU.add,
                        ["dnT0", "pf", "yT"], ["yT"])
                tt(yT[:, :, 0:NT], yT[:, :, 0:NT], siluz[:, :, 0:NT], ALU.mult, ["yT", "dnT1"], ["yT"])
                for g in range(2):
                    bN, bNk = bank(7)
                    for q in range(4):
                        fc = g * 4 + q
                        i = rot("sqs", 1)
                        act(sqs[i][:, 0:NT], yT[:, fc, 0:NT], AF.Square, ["yT"], ["sqs%d" % i])
                        mm(bN[:, 0:NT], ones, sqs[i][:, 0:NT], ["cst", "sqs%d" % i], [bNk], start=(q == 0), stop=(q == 3))
                    rsqrt_to(rsb[:, g, 0:NT], bN[:, 0:NT], 1.0 / 512, [bNk], ["rsb"])
                for fc in range(KC):
                    stt(yB[:, fc, 0:NT], yT[:, fc, 0:NT], pf[:, l, PF_SNORM + fc:PF_SNORM + fc + 1], rsb[:, fc // 4, 0:NT],
                        ALU.mult, ALU.mult, ["yT", "pf", "rsb"], ["yB"])
                if l == 0 and ti == 0:
                    acopy(yT[:, :, 0:NT], yB[:, :, 0:NT], ["yB"], ["yT"])
                    dump("yssd", yT[:, :, 0:NT], ["yT"])

                def branch(n):
                    p.cur_tag = "BR%d.%d.%d" % (n, l, ti)
                    for half in range(2):
                        gs, gk = wload(l, 21 + n * 2 + half)
                        bs, bk_ = wload(l, 27 + n * 2 + half)
                        for q in range(4):
                            d = half * 4 + q
                            bG, bGk = nb()
                            proj(bG, bGk, gs, gk, q * 128, NT)
                            i = rot("sig", 2)
                            act(sig[i][:, 0:NT], bG[:, 0:NT], AF.Sigmoid, [bGk], ["sig%d" % i])
                            bB2, bB2k = nb()
                            proj(bB2, bB2k, bs, bk_, q * 128, NT, rhs=yB, rkey="yB")
                            if n == 0:
                                tt(merged[:, d, 0:NT], bB2[:, 0:NT], sig[i][:, 0:NT], ALU.mult, [bB2k, "sig%d" % i], ["merged"])
                            else:
                                j = rot("mtmp", 1)
                                tt(mtmp[j][:, 0:NT], bB2[:, 0:NT], sig[i][:, 0:NT], ALU.mult, [bB2k, "sig%d" % i], ["mtmp%d" % j])
                                if n == 2:
                                    tt(mergedB[:, d, 0:NT], merged[:, d, 0:NT], mtmp[j][:, 0:NT], ALU.add, ["merged", "mtmp%d" % j], ["mergedB"], eng="pool")
                                else:
                                    tt(merged[:, d, 0:NT], merged[:, d, 0:NT], mtmp[j][:, 0:NT], ALU.add, ["merged", "mtmp%d" % j], ["merged"], eng="pool")

                branch(0)
                p.cur_tag = "S5.%d.%d" % (l, ti)
                for fc in range(KC):
                    ws, wk = wload(l, 5 + fc)
                    bH, bHk = nb()
                    proj(bH, bHk, ws, wk, 256, NT)
                    i = rot("sig", 2)
                    acopy(sig[i][:, 0:NT], bH[:, 0:NT], [bHk], ["sig%d" % i])
                    bC, bCk = nb()
                    proj(bC, bCk, ws, wk, 128, NT)
                    j = rot("mtmp", 1)
                    conv(bC, bCk, 12 + fc, PF_CCW + fc * 3, 3, l, NT, mtmp[j][:, 0:NT], "mtmp%d" % j, mul=sig[i][:, 0:NT], mulkey="sig%d" % i)
                    bB, bBk = nb()
                    proj(bB, bBk, ws, wk, 0, NT)
                    tt(yT[:, fc, 0:NT], bB[:, 0:NT], mtmp[j][:, 0:NT], ALU.mult, [bBk, "mtmp%d" % j], ["yT"])
                    bG, bGk = nb()
                    proj(bG, bGk, ws, wk, 384, NT)
                    i = rot("sig", 2)
                    act(sig[i][:, 0:NT], bG[:, 0:NT], AF.Silu, [bGk], ["sig%d" % i])
                    tt(yB[:, fc, 0:NT], yT[:, fc, 0:NT], sig[i][:, 0:NT], ALU.mult, ["yT", "sig%d" % i], ["yB"])
                if l == 0 and ti == 0:
                    acopy(yT[:, :, 0:NT], yB[:, :, 0:NT], ["yB"], ["yT"])
                    dump("ysc", yT[:, :, 0:NT], ["yT"])
                branch(1)
                p.cur_tag = "S6.%d.%d" % (l, ti)
                for h in range(8):
                    ws, wk = wload(l, 13 + h)
                    for which in range(3):
                        bk, bkey = nb()
                        proj(bk, bkey, ws, wk, which * 128, NT)
                        fcq = which * 8 + h
                        conv(bk, bkey, 20 + fcq, PF_DCW + fcq * 4, 4, l, NT, dnT[:, which, h, 0:NT], "dnT%d" % which)
                    bk, bkey = nb()
                    proj(bk, bkey, ws, wk, 384, NT)
                    act(dnT[:, 3, h, 0:NT], bk[:, 0:NT], AF.Silu, [bkey], ["dnT3"])
                if l == 0 and ti == 0:
                    dump("dnT", dnT[:, :, :, 0:NT], ["dnT0", "dnT1", "dnT2", "dnT3"])
                p.cur_tag = "S7.%d.%d" % (l, ti)
                for c in range(nch):
                    cc = slice(c * 64, (c + 1) * 64)
                    beta_c = tk[:, c, 32:40]
                    g_c = tk[:, c, 40:48]
                    bM, bMk = bank(6)
                    mm(bM[0:64, 0:8], U, g_c, ["cst", "tk"], [bMk])
                    mm(bM[0:64, 8:16], G, g_c, ["cst", "tk"], [bMk])
                    mm(bM[0:128, 16:24], ones[0:64, :], g_c, ["cst", "tk"], [bMk])
                    act(smE[:, 0:16], bM[0:64, 0:16], AF.Exp, [bMk], ["smE"])
                    act(elast[:, 0:8], bM[:, 16:24], AF.Exp, [bMk], ["elast"])
                    for hh in range(2):
                        hs = slice(hh * 4, (hh + 1) * 4)
                        v3 = lambda ap: ap.rearrange("p (h d) -> p h d", d=128)
                        m3 = lambda ap: ap.rearrange("p (h d) -> p h d", d=64)
                        toks = []
                        for which in range(3):
                            bX, bXk = nb()
                            for q in range(4):
                                tr(bX[0:64, q * 128:(q + 1) * 128], dnT[:, which, hh * 4 + q, cc], ["dnT%d" % which], [bXk])
                            tkb, tkk = TB(which)
                            if which < 2:
                                acopy(tkb[:], bX[0:64, :], [bXk], [tkk])
                            else:
                                tt(v3(tkb[:]), v3(bX[0:64, :]), bc(beta_c[:, hs], [64, 4, 128], 2), ALU.mult, [bXk, "tk"], [tkk])
                            toks.append((tkb, tkk))
                        (qtok, qtk), (ktok, ktk), (vb, vbk) = toks
                        sq, sqk = TB(3)
                        tt(sq[:], qtok[:], qtok[:], ALU.mult, [qtk], [sqk])
                        p.op("dve", dur=0.65, fn=lambda e, sq=sq: e.tensor_reduce(fac[:, 0:4], v3(sq[:]), AX.X, ALU.add), reads=[sqk], writes=["fac"])
                        tt(sq[:], ktok[:], ktok[:], ALU.mult, [ktk], [sqk])
                        p.op("dve", dur=0.65, fn=lambda e, sq=sq: e.tensor_reduce(fac[:, 4:8], v3(sq[:]), AX.X, ALU.add), reads=[sqk], writes=["fac"])
                        rsqrt_to(fac[:, 0:8], fac[:, 0:8], 1.0, ["fac"], ["fac"])
                        ts(fac[:, 8:12], fac[:, 0:4], 128.0 ** -0.5, ALU.mult, ["fac"], ["fac"])
                        tt(fac[:, 12:16], fac[:, 8:12], smE[:, hh * 4:(hh + 1) * 4], ALU.mult, ["fac", "smE"], ["fac"])
                        tt(fac[:, 16:20], fac[:, 4:8], beta_c[:, hs], ALU.mult, ["fac", "tk"], ["fac"])
                        tt(fac[:, 16:20], fac[:, 16:20], smE[:, hh * 4:(hh + 1) * 4], ALU.mult, ["fac", "smE"], ["fac"])
                        tt(fac[:, 20:24], fac[:, 4:8], smE[:, 8 + hh * 4:8 + (hh + 1) * 4], ALU.mult, ["fac", "smE"], ["fac"])
                        scaled = []
                        for bi, (src, srck, f0) in enumerate(((qtok, qtk, 8), (qtok, qtk, 12), (ktok, ktk, 4), (ktok, ktk, 16), (ktok, ktk, 20))):
                            o, ok = TB(4 + bi)
                            tt(v3(o[:]), v3(src[:]), bc(fac[:, f0:f0 + 4], [64, 4, 128], 2), ALU.mult, [srck, "fac"], [ok])
                            scaled.append((o, ok))
                        (qn, qnk), (qg, qgk), (kn, knk), (kb, kbk), (kd, kdk) = scaled
                        fTs = []
                        for j, (src, srck) in enumerate(((kn, knk), (qn, qnk), (qg, qgk))):
                            bX, bXk = nb()
                            for q in range(4):
                                tr(bX[:, q * 64:(q + 1) * 64], src[:, q * 128:(q + 1) * 128], [srck], [bXk])
                            acopy(f128[j][:], bX[:, 0:256], [bXk], ["f128_%d" % j])
                            fTs.append((f128[j], "f128_%d" % j))
                        (knT, knTk), (qnT, qnTk), (qgT, qgTk) = fTs
                        bK, bKk = nb()
                        bQ, bQk = nb()
                        for q in range(4):
                            qs = slice(q * 64, (q + 1) * 64)
                            mm(bK[0:64, qs], knT[:, qs], knT[:, qs], [knTk], [bKk])
                            mm(bQ[0:64, qs], knT[:, qs], qnT[:, qs], [knTk, qnTk], [bQk])
                        Gl, Glk = TB(9)
                        Gs, Gsk = TB(10)
                        tt(m3(Gl[:, 0:256]), bc(U, [64, 4, 64], 1), bc(g_c[:, hs], [64, 4, 64], 2), ALU.mult, ["cst", "tk"], [Glk])
                        tt(m3(Gs[:, 0:256]), bc(G, [64, 4, 64], 1), bc(g_c[:, hs], [64, 4, 64], 2), ALU.mult, ["cst", "tk"], [Gsk])
                        bS1, bS1k = nb()
                        bS2, bS2k = nb()
                        for q in range(4):
                            qs = slice(q * 64, (q + 1) * 64)
                            mm(bS1[0:64, qs], Gl[:, qs], G, [Glk, "cst"], [bS1k])
                            mm(bS2[0:64, qs], Gs[:, qs], U, [Gsk, "cst"], [bS2k])
                        E1, E1k = TB(11)
                        E2, E2k = TB(12)
                        act(E1[:, 0:256], bS1[0:64, 0:256], AF.Exp, [bS1k], [E1k])
                        act(E2[:, 0:256], bS2[0:64, 0:256], AF.Exp, [bS2k], [E2k])
                        tt(m3(Gl[:, 0:256]), bc(G, [64, 4, 64], 1), bc(beta_c[:, hs], [64, 4, 64], 2), ALU.mult, ["cst", "tk"], [Glk])
                        tt(E1[:, 0:256], E1[:, 0:256], Gl[:, 0:256], ALU.mult, [E1k, Glk], [E1k])
                        Lm, Lmk = TB(13)
                        tt(Lm[:, 0:256], bK[0:64, 0:256], E1[:, 0:256], ALU.mult, [bKk, E1k], [Lmk])
                        tt(m3(E2[:, 0:256]), m3(E2[:, 0:256]), bc(U, [64, 4, 64], 1), ALU.mult, [E2k, "cst"], [E2k])
                        Xs, Xsk = TB(14)
                        tt(Xs[:, 0:256], bQ[0:64, 0:256], E2[:, 0:256], ALU.mult, [bQk, E2k], [Xsk])
                        bL, bLk = nb()
                        for q in range(4):
                            qs = slice(q * 64, (q + 1) * 64)
                            tr(bL[0:64, qs], Lm[:, qs], [Lmk], [bLk])
                        LT, LTk = TB(15)
                        acopy(LT[:, 0:256], bL[0:64, 0:256], [bLk], [LTk])
                        P, Pk = TB(4)
                        tt(m3(P[:, 0:256]), bc(I64, [64, 4, 64], 1), m3(LT[:, 0:256]), ALU.subtract, ["cst", LTk], [Pk])
                        cur, curk, curT, curTk = Lm, Lmk, LT, LTk
                        for j in range(1, 6):
                            bA, bAk = nb()
                            for q in range(4):
                                qs = slice(q * 64, (q + 1) * 64)
                                mm(bA[0:64, qs], curT[:, qs], cur[:, qs], [curTk, curk], [bAk])
                            Xj, Xjk = TB({1: 5, 2: 9, 3: 11, 4: 13, 5: 5}[j])
                            acopy(Xj[:, 0:256], bA[0:64, 0:256], [bAk], [Xjk])
                            if j < 5:
                                bB, bBk = nb()
                                for q in range(4):
                                    qs = slice(q * 64, (q + 1) * 64)
                                    mm(bB[0:64, qs], cur[:, qs], curT[:, qs], [curk, curTk], [bBk])
                                XjT, XjTk = TB({1: 6, 2: 10, 3: 12, 4: 15}[j])
                                vcopy(XjT[:, 0:256], bB[0:64, 0:256], [bBk], [XjTk])
                            bP, bPk = nb()
                            for q in range(4):
                                qs = slice(q * 64, (q + 1) * 64)
                                mm(bP[0:64, qs], Xj[:, qs], P[:, qs], [Xjk, Pk], [bPk])
                            tt(P[:, 0:256], P[:, 0:256], bP[0:64, 0:256], ALU.add, [Pk, bPk], [Pk])
                            cur, curk = Xj, Xjk
                            if j < 5:
                                curT, curTk = XjT, XjTk
                        bW, bWk = nb()
                        for q in range(4):
                            mm(bW[:, q * 64:(q + 1) * 64], kb[:, q * 128:(q + 1) * 128], P[:, q * 64:(q + 1) * 64], [kbk, Pk], [bWk])
                        wTn, wTnk = f128[3], "f128_3"
                        p.op("act", lambda e, wTn=wTn, bW=bW: e.mul(wTn[:], bW[:, 0:256], -1.0), reads=[bWk], writes=[wTnk], dur=0.47)
                        bV, bVk = nb()
                        for q in range(4):
                            h = hh * 4 + q
                            mm(bV[0:64, q * 128:(q + 1) * 128], P[:, q * 64:(q + 1) * 64], vb[:, q * 128:(q + 1) * 128], [Pk, vbk], [bVk],
                               start=True, stop=False)
                            mm(bV[0:64, q * 128:(q + 1) * 128], wTn[:, q * 64:(q + 1) * 64], dst[:, h * 128:(h + 1) * 128], [wTnk, "dst"], [bVk],
                               start=False, stop=True)
                        vnew, vnk = TB(0)
                        acopy(vnew[:], bV[0:64, :], [bVk], [vnk])
                        bO, bOk = nb()
                        for q in range(4):
                            h = hh * 4 + q
                            mm(bO[0:64, q * 128:(q + 1) * 128], qgT[:, q * 64:(q + 1) * 64], dst[:, h * 128:(h + 1) * 128], [qgTk, "dst"], [bOk],
                               start=True, stop=False)
                            mm(bO[0:64, q * 128:(q + 1) * 128], Xs[:, q * 64:(q + 1) * 64], vnew[:, q * 128:(q + 1) * 128], [Xsk, vnk], [bOk],
                               start=False, stop=True)
                        otok, otk = TB(1)
                        acopy(otok[:], bO[0:64, :], [bOk], [otk])
                        tt(sq[:], otok[:], otok[:], ALU.mult, [otk], [sqk])
                        p.op("dve", dur=0.65, fn=lambda e, sq=sq: e.tensor_reduce(fac[:, 24:28], v3(sq[:]), AX.X, ALU.add), reads=[sqk], writes=["fac"])
                        rsqrt_to(fac[:, 24:28], fac[:, 24:28], 1.0 / 128, ["fac"], ["fac"])
                        tt(v3(otok[:]), v3(otok[:]), bc(fac[:, 24:28], [64, 4, 128], 2), ALU.mult, [otk, "fac"], [otk])
                        bR, bRk = nb()
                        for q in range(4):
                            tr(bR[:, q * 64:(q + 1) * 64], otok[:, q * 128:(q + 1) * 128], [otk], [bRk])
                        stt(yB[:, hh * 4:(hh + 1) * 4, cc], bR[:, 0:256].rearrange("p (a b) -> p a b", b=64),
                            pf[:, l, PF_DNORM:PF_DNORM + 1], dnT[:, 3, hh * 4:(hh + 1) * 4, cc], ALU.mult, ALU.mult,
                            [bRk, "pf", "dnT3"], ["yB"])
                        bU, bUk = nb()
                        for q in range(4):
                            mm(bU[:, q * 128:(q + 1) * 128], kd[:, q * 128:(q + 1) * 128], vnew[:, q * 128:(q + 1) * 128], [kdk, vnk], [bUk])
                        dh = dst[:, hh * 512:(hh + 1) * 512]
                        tt(v3(dh), v3(dh), bc(elast[:, hs], [128, 4, 128], 2), ALU.mult, ["dst", "elast"], ["dst"])
                        tt(dh, dh, bU[:, :], ALU.add, ["dst", bUk], ["dst"])
                if l == 0 and ti == 0:
                    acopy(yT[:, :, 0:NT], yB[:, :, 0:NT], ["yB"], ["yT"])
                    dump("ydn", yT[:, :, 0:NT], ["yT"])
                branch(2)
                if l == 0 and ti == 0:
                    acopy(merged[:, :, 0:NT], mergedB[:, :, 0:NT], ["mergedB"], ["merged"])
                    dump("merged", merged[:, :, 0:NT], ["merged"])
                p.cur_tag = "S8.%d.%d" % (l, ti)
                bN, bNk = bank(7)
                for half in range(2):
                    ws, wk = wload(l, 33 + half)
                    for q in range(4):
                        d = half * 4 + q
                        bO, bOk = nb()
                        proj(bO, bOk, ws, wk, q * 128, NT, rhs=mergedB, rkey="mergedB")
                        acopy(yT[:, d, 0:NT], bO[:, 0:NT], [bOk], ["yT"])
                        i = rot("sqs", 1)
                        act(sqs[i][:, 0:NT], bO[:, 0:NT], AF.Square, [bOk], ["sqs%d" % i])
                        mm(bN[:, 0:NT], ones, sqs[i][:, 0:NT], ["cst", "sqs%d" % i], [bNk], start=(d == 0), stop=(d == KC - 1))
                rsqrt_to(rsb[:, 0, 0:NT], bN[:, 0:NT], 1.0 / D, [bNk], ["rsb"])
                for d in range(KC):
                    j = rot("mtmp", 1)
                    stt(mtmp[j][:, 0:NT], yT[:, d, 0:NT], pf[:, l, PF_NPOST + d:PF_NPOST + d + 1], rsb[:, 0, 0:NT], ALU.mult, ALU.mult,
                        ["yT", "pf", "rsb"], ["mtmp%d" % j])
                    tt(hT[:, d, cols], hT[:, d, cols], mtmp[j][:, 0:NT], ALU.add, [hk, "mtmp%d" % j], [hk], eng="pool")
                if l == 0 and ti == 0:
                    dump("h1", hT[:, :, cols], [hk])

        for r0 in range(0, S, 128):
            nr = min(128, S - r0)
            col0 = NMETA + r0
            i = rot("xstg", 1)
            skey = "merged"
            stg2 = xstg[i]
            for half in range(2):
                bk, bkey = nb()
                for q in range(4):
                    k = half * 4 + q
                    tr(bk[0:nr, q * 128:(q + 1) * 128], hT[:, k, col0:col0 + nr], hkeys(col0, col0 + nr), [bkey])
                acopy(stg2[0:nr, half * 512:(half + 1) * 512], bk[0:nr, :], [bkey], [skey])
            dma(out_d[r0:r0 + nr, :], stg2[0:nr, :], [skey], [], "och")
        p.finalize(st)
    return nc, p


def make_consts():
    c = np.zeros((128, 384), np.float32)
    c[:, 0:128] = np.eye(128, dtype=np.float32)
    c[:, 128:256] = 1.0
    a = np.arange(64)
    c[0:64, 256:320] = (a[:, None] <= a[None, :]).astype(np.float32)
    c[0:64, 320:384] = (a[:, None] > a[None, :]).astype(np.float32)
    return c


def pack_weights(inp):
    L = inp["w_in"].shape[0]
    w_in = np.asarray(inp["w_in"], np.float32)
    cols = []
    cols += list(range(O_Z, O_Z + 1024))
    cols += list(range(O_XBC, O_XBC + 1536))
    for fc in range(8):
        for base in (O_SCB, O_SCC, O_SCH, O_SCG):
            cols += list(range(base + fc * 128, base + (fc + 1) * 128))
    for h in range(8):
        for base in (O_Q, O_K, O_V, O_DZ):
            cols += list(range(base + h * 128, base + (h + 1) * 128))
    cols += list(range(O_GATE, O_GATE + 3072))
    cols = np.asarray(cols)
    assert cols.size == NW1
    w1 = np.ascontiguousarray(w_in[:, :, cols])
    wsm = np.ascontiguousarray(np.concatenate([w_in[:, :, O_DT:O_DT + 16], w_in[:, :, O_DB:O_DB + 16]], axis=-1))
    pf = np.zeros((128, L, NPF), np.float32)
    fm = lambda v, n: np.asarray(v, np.float32).reshape(n, 128).T
    for l in range(L):
        pf[:, l, PF_NPRE:PF_NPRE + 8] = fm(inp["norm_pre"][l], 8)
        pf[:, l, PF_NPOST:PF_NPOST + 8] = fm(inp["norm_post"][l], 8)
        scw = np.asarray(inp["ssd_conv_w"][l], np.float32)
        pf[:, l, PF_SCW:PF_SCW + 48] = scw.reshape(4, 12, 128).transpose(2, 1, 0).reshape(128, 48)
        pf[:, l, PF_SCB:PF_SCB + 12] = fm(inp["ssd_conv_b"][l], 12)
        pf[:, l, PF_SD:PF_SD + 8] = fm(np.repeat(np.asarray(inp["ssd_d"][l], np.float32), 64), 8)
        pf[:, l, PF_SNORM:PF_SNORM + 8] = fm(inp["ssd_norm"][l], 8)
        ccw = np.asarray(inp["sc_conv_w"][l], np.float32)
        pf[:, l, PF_CCW:PF_CCW + 24] = ccw.reshape(3, 8, 128).transpose(2, 1, 0).reshape(128, 24)
        dcw = np.asarray(inp["dn_conv_w"][l], np.float32)
        pf[:, l, PF_DCW:PF_DCW + 96] = dcw.reshape(4, 24, 128).transpose(2, 1, 0).reshape(128, 96)
        pf[:, l, PF_DNORM] = np.asarray(inp["dn_norm"][l], np.float32)
    pb = np.zeros((128, L, 48), np.float32)
    for l in range(L):
        row = np.concatenate([inp["ssd_dt_bias"][l], inp["ssd_a_log"][l], inp["dn_dt_bias"][l], inp["dn_a_log"][l]]).astype(np.float32)
        pb[:, l, :] = np.broadcast_to(row[None, :], (128, 48))
    return dict(w1=w1, wsm=wsm, wb=np.ascontiguousarray(inp["w_branch"], np.float32), wo=np.ascontiguousarray(inp["w_out"], np.float32),
                pf=pf, pb=pb, cst=make_consts(), meta=np.ascontiguousarray(inp["meta_tokens"], np.float32))


_CACHE = {}


def kernel(**inputs):
    x = np.asarray(inputs["x"], np.float32)
    B, S, _ = x.shape
    L = inputs["w_in"].shape[0]
    shared = pack_weights(inputs)
    key = (S, L)
    if key not in _CACHE:
        _CACHE[key] = build_program(S, L)[0]
    nc = _CACHE[key]
    in_maps = []
    for b in range(B):
        m = dict(shared)
        m["x"] = np.ascontiguousarray(x[b])
        in_maps.append(m)
    res = run_bass_kernel_spmd(nc, in_maps, core_ids=list(range(B)))
    return np.stack([np.asarray(r["out"], np.float32) for r in res.results], axis=0)
```

```python
import contextlib
import numpy as np
import concourse.bass as bass
import concourse.mybir as mybir

F32 = mybir.dt.float32
F32R = mybir.dt.float32r
BF16 = mybir.dt.bfloat16
ALU = mybir.AluOpType
AF = mybir.ActivationFunctionType
AX = mybir.AxisListType


class Op:
    __slots__ = ("eng", "fn", "deps", "chan", "needs_inc", "event", "idx", "dur", "succ", "nd", "ready", "fin", "pos", "lat", "tag", "aset")

    def __init__(self, eng, fn, deps, chan, dur):
        self.eng = eng
        self.fn = fn
        self.deps = deps
        self.chan = chan
        self.needs_inc = chan is not None
        self.event = None
        self.dur = dur
        self.succ = []
        self.ready = 0.0
        self.fin = 0.0


class Prog:
    ENGS = ("pe", "act", "dve", "pool", "sp")

    def __init__(self, nc, same_eng_sync=True, schedule=True, window=4000):
        self.nc = nc
        self.ops = []
        self.last_w = {}
        self.readers = {}
        self.same_eng_sync = same_eng_sync
        self.schedule = schedule
        self.window = window

    def op(self, eng, fn, reads=(), writes=(), chan=None, dur=0.1):
        deps = []
        for k in reads:
            w = self.last_w.get(k)
            if w is not None:
                deps.append(w)
        for k in writes:
            w = self.last_w.get(k)
            if w is not None:
                deps.append(w)
            deps.extend(self.readers.get(k, ()))
        o = Op(eng, fn, deps, chan, dur)
        o.tag = getattr(self, "cur_tag", "")
        o.aset = None
        o.idx = len(self.ops)
        self.ops.append(o)
        for k in reads:
            self.readers.setdefault(k, []).append(o)
        for k in writes:
            self.last_w[k] = o
            self.readers[k] = []
        return o

    def _sem_edge(self, d, o):
        if d.chan is None and o.chan is None and d.eng == o.eng:
            if o.eng == "pe" or not self.same_eng_sync:
                return False
        return True

    def _list_schedule(self):
        import heapq
        ops = self.ops
        for o in ops:
            ds = []
            seen = set()
            for d in o.deps:
                if d is o or id(d) in seen:
                    continue
                seen.add(id(d))
                ds.append(d)
            o.deps = ds
            o.nd = len(ds)
            for d in ds:
                d.succ.append(o)
        SEM_LAT = 0.12
        cp = [0.0] * len(ops)
        for o in reversed(ops):
            m = 0.0
            for s_ in o.succ:
                if cp[s_.idx] > m:
                    m = cp[s_.idx]
            cp[o.idx] = m + (o.dur + (2.0 if o.chan is not None else 0.0)) + 0.1
        self.cp_len = max(cp) if cp else 0.0
        free = {e: 0.0 for e in self.ENGS}
        pend = {e: [] for e in self.ENGS}
        avail = {e: [] for e in self.ENGS}
        order = {e: [] for e in self.ENGS}
        dma_pipe = 0.0
        for o in ops:
            if o.nd == 0:
                heapq.heappush(pend[o.eng], (0.0, o.idx))
        nleft = len(ops)
        lo = 0
        done = [False] * (len(ops) + 1)
        W = self.window
        TBL = 1.3
        cur_set = [None]
        cand_l = {e: [] for e in self.ENGS}
        for e in self.ENGS:
            while pend[e]:
                cand_l[e].append(heapq.heappop(pend[e])[1])
        while nleft:
            while done[lo]:
                lo += 1
            best = None
            for e in self.ENGS:
                t = free[e]
                bc_ = None
                for idx in cand_l[e]:
                    if idx >= lo + W:
                        continue
                    stt_ = max(t, ops[idx].ready)
                    if e == "act":
                        as_ = ops[idx].aset
                        if as_ is not None and as_ != cur_set[0]:
                            stt_ += TBL
                    key_ = (stt_, -cp[idx], idx)
                    if bc_ is None or key_ < bc_:
                        bc_ = key_
                if bc_ is None:
                    continue
                if best is None or bc_ < best[0]:
                    best = (bc_, e)
            st, idx, e = best[0][0], best[0][2], best[1]
            cand_l[e].remove(idx)
            o = ops[idx]
            if e == "act" and o.aset is not None:
                cur_set[0] = o.aset
            if o.chan is not None:
                dma_pipe = max(dma_pipe, st) + o.dur
                o.fin = dma_pipe + 2.0
                free[e] = st + 0.06
            else:
                o.fin = st + o.dur
                free[e] = o.fin
            o.pos = len(order[e])
            order[e].append(o)
            done[idx] = True
            nleft -= 1
            for s in o.succ:
                lat = SEM_LAT if self._sem_edge(o, s) else 0.0
                if o.fin + lat > s.ready:
                    s.ready = o.fin + lat
                s.nd -= 1
                if s.nd == 0:
                    cand_l[s.eng].append(s.idx)
        self.sim_time = max(o.fin for o in ops)
        return order

    def finalize(self, stack):
        nc = self.nc
        if self.schedule:
            order = self._list_schedule()
        else:
            order = {e: [] for e in self.ENGS}
            for o in self.ops:
                seen = set()
                ds = []
                for d in o.deps:
                    if d is o or id(d) in seen:
                        continue
                    seen.add(id(d))
                    ds.append(d)
                o.deps = ds
                o.pos = len(order[o.eng])
                order[o.eng].append(o)
        for o in self.ops:
            o.deps = [d for d in o.deps if self._sem_edge(d, o)]
        sems = {}
        cnt = {}

        def getsem(name):
            if name not in sems:
                sems[name] = stack.enter_context(nc.semaphore(name))
                cnt[name] = 0
            return sems[name]

        chan_pos = {}
        for e in self.ENGS:
            for o in order[e]:
                if o.chan is not None:
                    chan_pos[o.chan] = chan_pos.get(o.chan, 0) + 1
                    o.lat = chan_pos[o.chan]
        per = {}
        nw = 0
        for e in self.ENGS:
            wpos = {}
            lst = []
            for o in order[e]:
                best = {}
                for d in o.deps:
                    if d.chan is not None:
                        st_, ps_ = d.chan, d.lat
                    else:
                        st_, ps_ = "e_" + d.eng, d.pos
                    if wpos.get(st_, -1) >= ps_:
                        continue
                    if st_ not in best or best[st_][0] < ps_:
                        best[st_] = (ps_, d)
                need = []
                for st_, (ps_, d) in best.items():
                    wpos[st_] = ps_
                    d.needs_inc = True
                    need.append(d)
                nw += len(need)
                lst.append((o, need))
            per[e] = lst
        for e in self.ENGS:
            for o in order[e]:
                if o.chan is not None:
                    getsem(o.chan)
                    cnt[o.chan] += 16
                    o.event = (o.chan, cnt[o.chan])
                elif o.needs_inc:
                    nm = "e_" + o.eng
                    getsem(nm)
                    cnt[nm] += 1
                    o.event = (nm, cnt[nm])
        self.nwaits = nw
        self.sem_max = dict(cnt)
        assert max(cnt.values()) < 30000, cnt

        def emit(eng_obj, lst):
            for o, need in lst:
                for d in need:
                    eng_obj.wait_ge(sems[d.event[0]], d.event[1])
                ins = o.fn(eng_obj)
                if o.event is not None:
                    if o.chan is not None:
                        ins.then_inc(sems[o.chan], 16)
                    else:
                        ins.then_inc(sems[o.event[0]], 1)

        with nc.Block() as block:
            @block.tensor
            def _(e):
                emit(e, per["pe"])

            @block.scalar
            def _(e):
                emit(e, per["act"])

            @block.vector
            def _(e):
                emit(e, per["dve"])

            @block.gpsimd
            def _(e):
                emit(e, per["pool"])

            @block.sync
            def _(e):
                emit(e, per["sp"])
                for s, v in cnt.items():
                    if v > 0:
                        e.wait_ge(sems[s], v)


from concourse.bass_utils import run_bass_kernel_spmd

D = 1024
KC = 8
NMETA = 16
EPS = 1e-6
PF_NPRE, PF_NPOST, PF_SCW, PF_SCB, PF_SD, PF_SNORM, PF_CCW, PF_DCW, PF_DNORM, NPF = 0, 8, 16, 64, 76, 84, 92, 116, 212, 213
O_Z, O_XBC, O_DT, O_SCB, O_SCC, O_SCH, O_SCG, O_Q, O_K, O_V, O_DZ, O_DB, O_DA, O_GATE = (
    0, 1024, 2560, 2576, 3600, 4624, 5648, 6672, 7696, 8720, 9744, 10768, 10776, 10784)
NW1 = 13824


def build_program(S, L, NTM=256, dbg=None, same_eng_sync=True, schedule=True, window=100000):
    TOK = NMETA + S
    TP = ((TOK + 63) // 64) * 64
    tiles = []
    c = 0
    while c < TP:
        n = min(NTM, TP - c)
        tiles.append((c, n))
        c += n
    nc = bass.Bass("TRN2", target_bir_lowering=False)
    dt_in = lambda n, s: nc.dram_tensor(n, s, F32, kind="ExternalInput").ap()
    x_d = dt_in("x", [S, D])
    meta_d = dt_in("meta", [NMETA, D])
    w1_d = dt_in("w1", [L, D, NW1])
    wsm_d = dt_in("wsm", [L, D, 32])
    wb_d = dt_in("wb", [L, 3, D, D])
    wo_d = dt_in("wo", [L, D, D])
    pf_d = dt_in("pf", [128, L, NPF])
    pb_d = dt_in("pb", [128, L, 48])
    cst_d = dt_in("cst", [128, 384])
    out_d = nc.dram_tensor("out", [S, D], F32, kind="ExternalOutput").ap()
    NBLK = 35
    wbf_d = nc.dram_tensor("wbf", [L, NBLK, 128, KC * 512], BF16, kind="Internal").ap()
    dbg_d = {}
    if dbg:
        for name, shp in dbg.items():
            dbg_d[name] = nc.dram_tensor("dbg_" + name, list(shp), F32, kind="ExternalOutput").ap()

    with contextlib.ExitStack() as st:
        def sb(n, s):
            return st.enter_context(nc.sbuf_tensor("s_" + n, list(s), F32))
        hT = sb("hT", [128, KC, TP])
        NWS = 3
        sbb = lambda n, shp: st.enter_context(nc.sbuf_tensor("s_" + n, list(shp), BF16))
        wsl = [sbb("wsl%d" % i, [128, KC, 512]) for i in range(NWS)]
        xnT = sbb("xnT", [128, KC, NTM])
        yB = sbb("yB", [128, KC, NTM])
        mergedB = sbb("mergedB", [128, KC, NTM])
        bcT = sb("bcT", [128, 4, NTM])
        yT = sb("yT", [128, KC, NTM])
        merged = sb("merged", [128, KC, NTM])
        xstg = [merged[:].rearrange("p a b -> p (a b)")[:, 0:1024]]
        dq = sbb("dq", [128, 3, 8, NTM])
        dz = sb("dz", [128, 8, NTM])
        xsT = dq[:, 0]
        siluz = dz[:]
        cstB = sbb("cstB", [128, 384])
        NTMB = 16
        fpool = [sb("fp%d" % i, [64, 512]) for i in range(7)]
        BA = [[sbb("ba%d_%d" % (h_, i), [64, 512]) for i in range(4)] for h_ in range(2)]
        BD = [[sbb("bd%d_%d" % (h_, i), [64, 256]) for i in range(8)] for h_ in range(2)]
        BFm = [[sbb("bm%d_%d" % (h_, i), [128, 256]) for i in range(4)] for h_ in range(2)]
        tbb = [sbb("tbb%d" % i, [64, 512]) for i in range(4)]
        sst = sb("sst", [128, 1024])
        dst = sb("dst", [128, 1024])
        dstB = sbb("dstB", [128, 1024])
        pre = [sb("pre%d" % i, [128, NTM + 3]) for i in range(2)]
        cacc = [sb("cacc%d" % i, [128, NTM]) for i in range(1)]
        sqs = [sb("sqs%d" % i, [128, NTM]) for i in range(1)]
        rsb = sb("rsb", [128, 2, NTM])
        sig = [sb("sig%d" % i, [128, NTM]) for i in range(2)]
        mtmp = [sb("mtmp%d" % i, [128, NTM]) for i in range(1)]
        tails = sb("tails", [128, 44, 3])
        pf = sb("pf", [128, L, NPF])
        pbt = sb("pbt", [128, L, 48])
        negA = sb("negA", [128, L, 24])
        cst = sb("cst", [128, 384])
        wsm = sbb("wsm", [128, L, KC, 32])
        tsm = sb("tsm", [64, NTM // 64, 32])
        tk = sb("tk", [64, NTM // 64, 48])
        smE = sb("smE", [64, 32])
        elast = sb("elast", [128, 16])
        facs = [sb("fac%d" % i, [64, 32]) for i in range(2)]
        ps = st.enter_context(nc.psum_tensor("ps", [128, 4096], F32))

        ident = cst[:, 0:128]
        ones = cst[:, 128:256]
        U = cst[0:64, 256:320]
        G = cst[0:64, 320:384]
        I64 = cst[0:64, 0:64]

        p = Prog(nc, same_eng_sync=same_eng_sync, schedule=schedule, window=window)
        state = {"rr": 0, "tm": 0, "ws": 0, "i2": {}}

        def bank(i):
            return ps[:, i * 512:(i + 1) * 512], "b%d" % i

        def nb():
            bs_ = state.get("bankset")
            if bs_ is None:
                i = state["rr"]
                state["rr"] = (i + 1) % 6
                return bank(i)
            j = state["i2"].get(("bs", bs_), 0)
            state["i2"][("bs", bs_)] = (j + 1) % len(bs_)
            return bank(bs_[j])

        def rot(name, n):
            i = state["i2"].get(name, 0)
            state["i2"][name] = (i + 1) % n
            return i

        def Bv(t, n):
            return t[:].bitcast(BF16)[:, 0:n]

        def F(i):
            return fpool[i], "fp%d" % i

        def A_(i):
            h_ = state["hh"]
            return BA[h_][i][:], "ba%d_%d" % (h_, i)

        def D_(i):
            h_ = state["hh"]
            return BD[h_][i][:], "bd%d_%d" % (h_, i)

        def M_(i):
            h_ = state["hh"]
            return BFm[h_][i][:], "bm%d_%d" % (h_, i)

        def fsz(ap):
            n = 1
            for d in ap.shape[1:]:
                n *= int(d)
            return n

        PASSES = 4.0

        def mm(out, lhsT, rhs, r, w, start=True, stop=True, passes=PASSES):
            p.op("pe", lambda e: e.matmul(out, lhsT, rhs, start=start, stop=stop), reads=r, writes=w,
                 dur=max(fsz(rhs), 64) * passes / 2400.0 + 0.03)

        def tr(out, in_, r, w):
            n = in_.shape[0]
            p.op("pe", lambda e: e.transpose(out, in_, ident[0:n, 0:n]), reads=r + ["cst"], writes=w, dur=0.09)

        def trb(out, in_, r, w):
            n = in_.shape[0]
            p.op("pe", lambda e: e.matmul(out, in_, cstB[0:n, 0:n], start=True, stop=True), reads=r + ["cstB"], writes=w,
                 dur=max(n, 64) / 2400.0 + 0.03)

        def act(out, in_, func, r, w, bias=None, scale=None):
            kw = {}
            if bias is not None:
                kw["bias"] = bias
            if scale is not None:
                kw["scale"] = scale
            o_ = p.op("act", lambda e: e.activation(out, in_, func, **kw), reads=r, writes=w, dur=0.2 + fsz(out) / 960.0)
            o_.aset = ASET.get(func)

        ASET = {AF.Silu: "silu", AF.Sigmoid: "sig", AF.Exp: "el", AF.Ln: "el", AF.Sqrt: "sqrt"}

        def acopy(out, in_, r, w):
            p.op("act", lambda e: e.copy(out, in_), reads=r, writes=w, dur=0.2 + fsz(out) / 960.0)

        def edur(eng, n):
            return (0.12 + n / 960.0) if eng == "dve" else (0.2 + n / 400.0)

        def tt(out, a, b, op, r, w, eng="dve"):
            p.op(eng, lambda e: e.tensor_tensor(out, a, b, op), reads=r, writes=w, dur=edur(eng, fsz(out)))

        def ts(out, a, s1, op0, r, w, s2=None, op1=None, eng="dve"):
            if op1 is None:
                p.op(eng, lambda e: e.tensor_scalar(out, a, s1, None, op0), reads=r, writes=w, dur=edur(eng, fsz(out)))
            else:
                p.op(eng, lambda e: e.tensor_scalar(out, a, s1, s2, op0, op1), reads=r, writes=w, dur=edur(eng, fsz(out)))

        def stt(out, in0, scalar, in1, op0, op1, r, w):
            p.op("dve", lambda e: e.scalar_tensor_tensor(out, in0, scalar, in1, op0, op1), reads=r, writes=w,
                 dur=edur("dve", fsz(out)))

        def recip(out, in_, r, w):
            p.op("dve", lambda e: e.reciprocal(out, in_), reads=r, writes=w, dur=edur("dve", fsz(out)))

        def vcopy(out, in_, r, w, eng="dve"):
            p.op(eng, lambda e: e.tensor_copy(out, in_), reads=r, writes=w, dur=edur(eng, fsz(out)))

        def memset(ap, val, w, eng="pool"):
            p.op(eng, lambda e: e.memset(ap, val), writes=w, dur=edur(eng, fsz(ap)))

        def dma(out, in_, r, w, chan, eng="sp"):
            nbytes = (2 if out.dtype == BF16 else 4) * fsz(out) * int(out.shape[0]) * (3 if eng == "pool" else 1)
            p.op(eng, lambda e: e.dma_start(out=out, in_=in_), reads=r, writes=w, chan=chan, dur=nbytes / 250e3)

        def dump(name, ap, keys):
            if name in dbg_d:
                dma(dbg_d[name], ap, keys, [], "dbg")

        def rsqrt_to(out, in_, scale, r, w):
            act(out, in_, AF.Ln, r, w, bias=EPS, scale=scale)
            act(out, out, AF.Exp, w, w, scale=-0.5)

        dma(cst[:], cst_d, [], ["cst"], "ld_cst")
        dma(pf[:], pf_d, [], ["pf"], "ld_pf")
        dma(pbt[:], pb_d, [], ["pbt"], "ld_pb")
        for l in range(L):
            dma(wsm[:, l], wsm_d[l].rearrange("(k p) c -> p k c", p=128), [], ["wsm%d" % l], "ld_wsm%d" % l, eng="pool")
        acopy(cstB[:], cst[:], ["cst"], ["cstB"])
        for l in range(L):
            act(negA[:, l, 0:16], pbt[:, l, 16:32], AF.Exp, ["pbt"], ["negA"])
            act(negA[:, l, 16:24], pbt[:, l, 40:48], AF.Exp, ["pbt"], ["negA"])
        ts(negA[:], negA[:], -1.0, ALU.mult, ["negA"], ["negA"])

        hkey = lambda t: "hT%d" % t

        def tile_of(col):
            for ti, (c0, n) in enumerate(tiles):
                if c0 <= col < c0 + n:
                    return ti
            raise ValueError

        def hkeys(c_lo, c_hi):
            return sorted({hkey(tile_of(c)) for c in (c_lo, c_hi - 1)} | {hkey(t) for t in range(tile_of(c_lo), tile_of(c_hi - 1) + 1)})

        if TP > TOK:
            memset(hT[:, :, TOK:TP], 0.0, hkeys(TOK, TP))
        row_blocks = [("meta", 0, NMETA, 0)] + [("x", r0, min(128, S - r0), NMETA + r0) for r0 in range(0, S, 128)]
        for (src, r0, nr, col0) in row_blocks:
            si = rot("xstg", 1)
            skey = "merged"
            stg2 = xstg[si]
            srcap = meta_d[0:nr, :] if src == "meta" else x_d[r0:r0 + nr, :]
            dma(stg2[0:nr, :], srcap, [], [skey], "xch%d" % si)
            for half in range(2):
                bk, bkey = nb()
                for q in range(4):
                    k = half * 4 + q
                    tr(bk[:, q * 128:q * 128 + nr], stg2[0:nr, k * 128:(k + 1) * 128], [skey], [bkey])
                acopy(hT[:, half * 4:half * 4 + 4, col0:col0 + nr],
                      bk.rearrange("p (a b) -> p a b", b=128)[:, :, 0:nr], [bkey], hkeys(col0, col0 + nr))

        def blk_src(l, blk):
            if blk < 27:
                return w1_d[l][:, blk * 512:(blk + 1) * 512]
            if blk < 33:
                n, half = divmod(blk - 27, 2)
                return wb_d[l, n][:, half * 512:(half + 1) * 512]
            return wo_d[l][:, (blk - 33) * 512:(blk - 32) * 512]

        use_order = [0, 1, 2, 3, 4, 21, 27, 22, 28] + list(range(5, 13)) + [23, 29, 24, 30] + list(range(13, 21)) + [25, 31, 26, 32, 33, 34]
        assert sorted(use_order) == list(range(NBLK))
        GRP = 9
        grp_of = {}
        for l in range(L):
            for gi in range(0, NBLK, GRP):
                grp = use_order[gi:gi + GRP]
                chn = "cv%d_%d" % (l, gi // GRP)
                for blk in grp:
                    dma(wbf_d[l, blk].rearrange("p (k c) -> p k c", k=KC), blk_src(l, blk).rearrange("(k p) c -> p k c", p=128),
                        [], ["wbf%d_%d" % (l, blk)], chn, eng="pool")
                for blk in grp:
                    grp_of[(l, blk)] = grp

        def wload(l, blk):
            si = state["ws"] % NWS
            state["ws"] += 1
            key = "wsl%d" % si
            dma(wsl[si][:], wbf_d[l, blk].rearrange("p (k c) -> p k c", k=KC), ["wbf%d_%d" % (l, b_) for b_ in grp_of[(l, blk)]], [key],
                "wch%d" % si)
            return wsl[si], key

        def proj(out_bank, okey, wslot, wkey, coff, NT, rhs=None, rkey="xnT"):
            rhs = xnT if rhs is None else rhs
            for k in range(KC):
                mm(out_bank[:, 0:NT], wslot[:, k, coff:coff + 128], rhs[:, k, 0:NT], [wkey, rkey], [okey],
                   start=(k == 0), stop=(k == KC - 1), passes=1.0)

        def conv(bk, bkey, ti, wcol0, K, l, NT, dest, dkey, bias_col=None, mul=None, mulkey=None):
            H = K - 1
            i = rot("pre", 2)
            pr, pk = pre[i], "pre%d" % i
            ca, ck = cacc[0], "cacc0"
            vcopy(pr[:, 0:H], tails[:, ti, 0:H], ["tails%d" % ti], [pk], eng="pool")
            if mul is None:
                acopy(pr[:, H:H + NT], bk[:, 0:NT], [bkey], [pk])
            else:
                tt(pr[:, H:H + NT], bk[:, 0:NT], mul, ALU.mult, [bkey, mulkey], [pk])
            vcopy(tails[:, ti, 0:H], pr[:, NT:NT + H], [pk], ["tails%d" % ti], eng="pool")
            wl = pf[:, l, wcol0 + K - 1:wcol0 + K]
            if mul is None:
                act(ca[:, 0:NT], bk[:, 0:NT], AF.Identity, [bkey, "pf"], [ck], scale=wl)
            else:
                act(ca[:, 0:NT], pr[:, H:H + NT], AF.Identity, [pk, "pf"], [ck], scale=wl)
            for k in range(0, K - 1):
                last = (k == K - 2 and K == 3)
                o = dest if last else ca[:, 0:NT]
                ok = [dkey] if last else [ck]
                stt(o, pr[:, k:k + NT], pf[:, l, wcol0 + k:wcol0 + k + 1], ca[:, 0:NT], ALU.mult, ALU.add,
                    [pk, "pf", ck], ok)
            if K == 4:
                if bias_col is not None:
                    act(dest, ca[:, 0:NT], AF.Silu, [ck, "pf"], [dkey], bias=pf[:, l, bias_col:bias_col + 1])
                else:
                    act(dest, ca[:, 0:NT], AF.Silu, [ck], [dkey])

        bc = lambda ap, shape, axis: ap.unsqueeze(axis).to_broadcast(list(shape))

        for l in range(L):
            memset(sst[:], 0.0, ["sst"])
            memset(dst[:], 0.0, ["dst"])
            memset(dstB[:], 0.0, ["dstB"])
            memset(tails[:], 0.0, ["tails%d" % i for i in range(44)])
            w1 = w1_d[l]
            for ti, (c0, NT) in enumerate(tiles):
                nch = NT // 64
                hk = hkey(ti)
                cols = slice(c0, c0 + NT)
                p.cur_tag = "S0.%d.%d" % (l, ti)
                bN, bNk = bank(7)
                for k in range(KC):
                    i = rot("sqs", 1)
                    act(sqs[i][:, 0:NT], hT[:, k, cols], AF.Square, [hk], ["sqs%d" % i])
                    mm(bN[:, 0:NT], ones, sqs[i][:, 0:NT], ["cst", "sqs%d" % i], [bNk], start=(k == 0), stop=(k == KC - 1))
                rsqrt_to(rsb[:, 0, 0:NT], bN[:, 0:NT], 1.0 / D, [bNk], ["rsb"])
                for k in range(KC):
                    stt(xnT[:, k, 0:NT], hT[:, k, cols], pf[:, l, PF_NPRE + k:PF_NPRE + k + 1], rsb[:, 0, 0:NT],
                        ALU.mult, ALU.mult, [hk, "pf", "rsb"], ["xnT"])
                if l == 0 and ti == 0 and "xnT" in dbg_d:
                    acopy(merged[:, :, 0:NT], xnT[:, :, 0:NT], ["xnT"], ["merged"])
                    dump("xnT", merged[:, :, 0:NT], ["merged"])
                p.cur_tag = "S1.%d.%d" % (l, ti)
                bS, bSk = bank(6)
                for c in range(nch):
                    for k in range(KC):
                        mm(bS[0:64, c * 32:(c + 1) * 32], xnT[:, k, c * 64:(c + 1) * 64], wsm[:, l, k, :], ["xnT", "wsm%d" % l], [bSk],
                           start=(k == 0), stop=(k == KC - 1), passes=1.0)
                acopy(tsm[:, 0:nch, :], bS[0:64, 0:nch * 32].rearrange("p (c f) -> p c f", f=32), [bSk], ["tsm"])
                tt(tk[:, 0:nch, 0:16], tsm[:, 0:nch, 0:16], bc(pbt[0:64, l, 0:16], [64, nch, 16], 1), ALU.add, ["tsm", "pbt"], ["tk"])
                tt(tk[:, 0:nch, 40:48], tsm[:, 0:nch, 24:32], bc(pbt[0:64, l, 32:40], [64, nch, 8], 1), ALU.add, ["tsm", "pbt"], ["tk"])
                act(tk[:, 0:nch, 0:16], tk[:, 0:nch, 0:16], AF.Exp, ["tk"], ["tk"])
                act(tk[:, 0:nch, 40:48], tk[:, 0:nch, 40:48], AF.Exp, ["tk"], ["tk"])
                act(tk[:, 0:nch, 0:16], tk[:, 0:nch, 0:16], AF.Ln, ["tk"], ["tk"], bias=1.0)
                act(tk[:, 0:nch, 40:48], tk[:, 0:nch, 40:48], AF.Ln, ["tk"], ["tk"], bias=1.0)
                act(tk[:, 0:nch, 32:40], tsm[:, 0:nch, 16:24], AF.Sigmoid, ["tsm"], ["tk"])
                tt(tk[:, 0:nch, 16:32], tk[:, 0:nch, 0:16], bc(negA[0:64, l, 0:16], [64, nch, 16], 1), ALU.mult, ["tk", "negA"], ["tk"])
                tt(tk[:, 0:nch, 40:48], tk[:, 0:nch, 40:48], bc(negA[0:64, l, 16:24], [64, nch, 8], 1), ALU.mult, ["tk", "negA"], ["tk"])
                if l == 0 and ti == 0:
                    dump("tk", tk[:, 0:nch, :], ["tk"])
                p.cur_tag = "S2.%d.%d" % (l, ti)
                for blk in range(2):
                    ws, wk = wload(l, blk)
                    for q in range(4):
                        fc = blk * 4 + q
                        bk, bkey = nb()
                        proj(bk, bkey, ws, wk, q * 128, NT)
                        act(siluz[:, fc, 0:NT], bk[:, 0:NT], AF.Silu, [bkey], ["dnT3"])
                for blk in range(3):
                    ws, wk = wload(l, 2 + blk)
                    for q in range(4):
                        fc = blk * 4 + q
                        bk, bkey = nb()
                        proj(bk, bkey, ws, wk, q * 128, NT)
                        if fc < 8:
                            dest, dkey = xsT[:, fc, 0:NT], "dnT0"
                        else:
                            dest, dkey = bcT[:, fc - 8, 0:NT], "bcT"
                        conv(bk, bkey, fc, PF_SCW + fc * 4, 4, l, NT, dest, dkey, bias_col=PF_SCB + fc)
                if l == 0 and ti == 0:
                    pass
                    dump("bcT", bcT[:, :, 0:NT], ["bcT"])
                p.cur_tag = "S3.%d.%d" % (l, ti)
                for c in range(nch):
                    cc = slice(c * 64, (c + 1) * 64)
                    dt_c = tk[:, c, 0:16]
                    dtA_c = tk[:, c, 16:32]
                    bM, bMk = bank(6)
                    mm(bM[0:64, 0:16], U, dtA_c, ["cst", "tk"], [bMk])
                    mm(bM[0:64, 16:32], G, dtA_c, ["cst", "tk"], [bMk])
                    mm(bM[0:128, 32:48], ones[0:64, :], dtA_c, ["cst", "tk"], [bMk])
                    act(smE[:, 0:32], bM[0:64, 0:32], AF.Exp, [bMk], ["smE"])
                    act(elast[:, 0:16], bM[:, 32:48], AF.Exp, [bMk], ["elast"])
                    for g in range(2):
                        hs = slice(g * 8, (g + 1) * 8)
                        state["tm"] = 0
                        bX, bXk = nb()
                        for q in range(4):
                            trb(bX[0:64, q * 128:(q + 1) * 128], xsT[:, g * 4 + q, cc], ["dnT0"], [bXk])
                        bB, bBk = nb()
                        tr(bB[0:64, 0:128], bcT[:, g, cc], ["bcT"], [bBk])
                        xc, xck = F(0)
                        tt(xc[:].rearrange("p (h d) -> p h d", d=64), bX[0:64, :].rearrange("p (h d) -> p h d", d=64),
                           bc(dt_c[:, hs], [64, 8, 64], 2), ALU.mult, [bXk, "tk"], [xck])
                        xd, xdk = F(1)
                        tt(xd[:].rearrange("p (h d) -> p h d", d=64), xc[:].rearrange("p (h d) -> p h d", d=64),
                           bc(smE[:, 16 + g * 8:16 + (g + 1) * 8], [64, 8, 64], 2), ALU.mult, [xck, "smE"], [xdk], eng="pool")
                        btok, btk = F(2)
                        acopy(btok[:, 0:128], bB[0:64, 0:128], [bBk], [btk])
                        gm, gmk = F(3)
                        tt(gm[:].rearrange("p (h d) -> p h d", d=64), bc(G, [64, 8, 64], 1), bc(dtA_c[:, hs], [64, 8, 64], 2),
                           ALU.mult, ["cst", "tk"], [gmk], eng="pool")
                        bE, bEk = nb()
                        for h in range(8):
                            mm(bE[0:64, h * 64:(h + 1) * 64], gm[:, h * 64:(h + 1) * 64], U, [gmk, "cst"], [bEk])
                        E, Ek = F(4)
                        act(E[:], bE[0:64, :], AF.Exp, [bEk], [Ek])
                        bC, bCk = nb()
                        mm(bC[0:64, 0:64], bcT[:, g, cc], bcT[:, 2 + g, cc], ["bcT"], [bCk])
                        cbm, cbk = F(2)[0][:, 128:192], "fp2"
                        tt(cbm[:, 0:64], bC[0:64, 0:64], U, ALU.mult, [bCk, "cst"], [cbk])
                        MT, MTk = F(5)
                        tt(MT[:].rearrange("p (h d) -> p h d", d=64), E[:].rearrange("p (h d) -> p h d", d=64),
                           bc(cbm[:, 0:64], [64, 8, 64], 1), ALU.mult, [Ek, cbk], [MTk])
                        bY, bYk = nb()
                        for h in range(8):
                            mm(bY[0:64, h * 64:(h + 1) * 64], MT[:, h * 64:(h + 1) * 64], xc[:, h * 64:(h + 1) * 64], [MTk, xck], [bYk])
                        bT, bTk = nb()
                        mm(bT[0:64, :], bcT[:, 2 + g, cc], sst[:, g * 512:(g + 1) * 512], ["bcT", "sst"], [bTk])
                        tmp, tmpk = F(6)
                        tt(tmp[:].rearrange("p (h d) -> p h d", d=64), bT[0:64, :].rearrange("p (h d) -> p h d", d=64),
                           bc(smE[:, g * 8:(g + 1) * 8], [64, 8, 64], 2), ALU.mult, [bTk, "smE"], [tmpk])
                        ytok, ytk = tbb[3], "tbb3"
                        tt(ytok[:], bY[0:64, :], tmp[:], ALU.add, [bYk, tmpk], [ytk])
                        bR, bRk = nb()
                        for q in range(4):
                            trb(bR[:, q * 64:(q + 1) * 64], ytok[:, q * 128:(q + 1) * 128], [ytk], [bRk])
                        acopy(yT[:, g * 4:(g + 1) * 4, cc], bR[:, 0:256].rearrange("p (a b) -> p a b", b=64), [bRk], ["yT"])
                        bU, bUk = nb()
                        mm(bU[:, :], btok[:, 0:128], xd[:], [btk, xdk], [bUk])
                        sg_ = sst[:, g * 512:(g + 1) * 512]
                        tt(sg_.rearrange("p (h d) -> p h d", d=64), sg_.rearrange("p (h d) -> p h d", d=64),
                           bc(elast[:, hs], [128, 8, 64], 2), ALU.mult, ["sst", "elast"], ["sst"], eng="pool")
                        tt(sg_, sg_, bU[:, :], ALU.add, ["sst", bUk], ["sst"])
                p.cur_tag = "S3b.%d.%d" % (l, ti)
                for fc in range(KC):
                    stt(yT[:, fc, 0:NT], xsT[:, fc, 0:NT], pf[:, l, PF_SD + fc:PF_SD + fc + 1], yT[:, fc, 0:NT], ALU.mult, ALU.add,
                        ["dnT0", "pf", "yT"], ["yT"])
                tt(yT[:, :, 0:NT], yT[:, :, 0:NT], siluz[:, :, 0:NT], ALU.mult, ["yT", "dnT3"], ["yT"])
                for g in range(2):
                    bN, bNk = bank(7)
                    for q in range(4):
                        fc = g * 4 + q
                        i = rot("sqs", 1)
                        act(sqs[i][:, 0:NT], yT[:, fc, 0:NT], AF.Square, ["yT"], ["sqs%d" % i])
                        mm(bN[:, 0:NT], ones, sqs[i][:, 0:NT], ["cst", "sqs%d" % i], [bNk], start=(q == 0), stop=(q == 3))
                    rsqrt_to(rsb[:, g, 0:NT], bN[:, 0:NT], 1.0 / 512, [bNk], ["rsb"])
                for fc in range(KC):
                    stt(yB[:, fc, 0:NT], yT[:, fc, 0:NT], pf[:, l, PF_SNORM + fc:PF_SNORM + fc + 1], rsb[:, fc // 4, 0:NT],
                        ALU.mult, ALU.mult, ["yT", "pf", "rsb"], ["yB"])
                if l == 0 and ti == 0:
                    acopy(yT[:, :, 0:NT], yB[:, :, 0:NT], ["yB"], ["yT"])
                    dump("yssd", yT[:, :, 0:NT], ["yT"])

                def branch(n):
                    p.cur_tag = "BR%d.%d.%d" % (n, l, ti)
                    for half in range(2):
                        gs, gk = wload(l, 21 + n * 2 + half)
                        bs, bk_ = wload(l, 27 + n * 2 + half)
                        for q in range(4):
                            d = half * 4 + q
                            bG, bGk = nb()
                            proj(bG, bGk, gs, gk, q * 128, NT)
                            i = rot("sig", 2)
                            act(sig[i][:, 0:NT], bG[:, 0:NT], AF.Sigmoid, [bGk], ["sig%d" % i])
                            bB2, bB2k = nb()
                            proj(bB2, bB2k, bs, bk_, q * 128, NT, rhs=yB, rkey="yB")
                            if n == 0:
                                tt(merged[:, d, 0:NT], bB2[:, 0:NT], sig[i][:, 0:NT], ALU.mult, [bB2k, "sig%d" % i], ["merged"])
                            else:
                                j = rot("mtmp", 1)
                                tt(mtmp[j][:, 0:NT], bB2[:, 0:NT], sig[i][:, 0:NT], ALU.mult, [bB2k, "sig%d" % i], ["mtmp%d" % j])
                                if n == 2:
                                    tt(mergedB[:, d, 0:NT], merged[:, d, 0:NT], mtmp[j][:, 0:NT], ALU.add, ["merged", "mtmp%d" % j], ["mergedB"], eng="pool")
                                else:
                                    tt(merged[:, d, 0:NT], merged[:, d, 0:NT], mtmp[j][:, 0:NT], ALU.add, ["merged", "mtmp%d" % j], ["merged"], eng="pool")

                branch(0)
                p.cur_tag = "S5.%d.%d" % (l, ti)
                for fc in range(KC):
                    ws, wk = wload(l, 5 + fc)
                    bH, bHk = nb()
                    proj(bH, bHk, ws, wk, 256, NT)
                    i = rot("sig", 2)
                    acopy(sig[i][:, 0:NT], bH[:, 0:NT], [bHk], ["sig%d" % i])
                    bC, bCk = nb()
                    proj(bC, bCk, ws, wk, 128, NT)
                    j = rot("mtmp", 1)
                    conv(bC, bCk, 12 + fc, PF_CCW + fc * 3, 3, l, NT, mtmp[j][:, 0:NT], "mtmp%d" % j, mul=sig[i][:, 0:NT], mulkey="sig%d" % i)
                    bB, bBk = nb()
                    proj(bB, bBk, ws, wk, 0, NT)
                    tt(yT[:, fc, 0:NT], bB[:, 0:NT], mtmp[j][:, 0:NT], ALU.mult, [bBk, "mtmp%d" % j], ["yT"])
                    bG, bGk = nb()
                    proj(bG, bGk, ws, wk, 384, NT)
                    i = rot("sig", 2)
                    act(sig[i][:, 0:NT], bG[:, 0:NT], AF.Silu, [bGk], ["sig%d" % i])
                    tt(yB[:, fc, 0:NT], yT[:, fc, 0:NT], sig[i][:, 0:NT], ALU.mult, ["yT", "sig%d" % i], ["yB"])
                if l == 0 and ti == 0:
                    acopy(yT[:, :, 0:NT], yB[:, :, 0:NT], ["yB"], ["yT"])
                    dump("ysc", yT[:, :, 0:NT], ["yT"])
                branch(1)
                p.cur_tag = "S6.%d.%d" % (l, ti)
                for h in range(8):
                    ws, wk = wload(l, 13 + h)
                    for which in range(3):
                        bk, bkey = nb()
                        proj(bk, bkey, ws, wk, which * 128, NT)
                        fcq = which * 8 + h
                        conv(bk, bkey, 20 + fcq, PF_DCW + fcq * 4, 4, l, NT, dq[:, which, h, 0:NT], "dnT%d" % which)
                    bk, bkey = nb()
                    proj(bk, bkey, ws, wk, 384, NT)
                    act(dz[:, h, 0:NT], bk[:, 0:NT], AF.Silu, [bkey], ["dnT3"])
                if l == 0 and ti == 0:
                    pass
                p.cur_tag = "S7.%d.%d" % (l, ti)
                for c in range(nch):
                    cc = slice(c * 64, (c + 1) * 64)
                    beta_c = tk[:, c, 32:40]
                    g_c = tk[:, c, 40:48]
                    bM, bMk = bank(6)
                    mm(bM[0:64, 0:8], U, g_c, ["cst", "tk"], [bMk])
                    mm(bM[0:64, 8:16], G, g_c, ["cst", "tk"], [bMk])
                    mm(bM[0:128, 16:24], ones[0:64, :], g_c, ["cst", "tk"], [bMk])
                    act(smE[:, 0:16], bM[0:64, 0:16], AF.Exp, [bMk], ["smE"])
                    act(elast[:, 0:8], bM[:, 16:24], AF.Exp, [bMk], ["elast"])
                    for hh in range(2):
                        hs = slice(hh * 4, (hh + 1) * 4)
                        state["hh"] = hh
                        state["bankset"] = (0, 1, 2) if hh == 0 else (3, 4, 5)
                        fac = facs[hh]
                        fack = "fac%d" % hh
                        v3 = lambda ap: ap.rearrange("p (h d) -> p h d", d=128)
                        m3 = lambda ap: ap.rearrange("p (h d) -> p h d", d=64)
                        toks = []
                        for which in range(3):
                            bX, bXk = nb()
                            for q in range(4):
                                trb(bX[0:64, q * 128:(q + 1) * 128], dq[:, which, hh * 4 + q, cc], ["dnT%d" % which], [bXk])
                            if which < 2:
                                tkb, tkk = F(which)
                                acopy(tkb[:], bX[0:64, :], [bXk], [tkk])
                            else:
                                tkb, tkk = A_(0)
                                tt(v3(tkb), v3(bX[0:64, :]), bc(beta_c[:, hs], [64, 4, 128], 2), ALU.mult, [bXk, "tk"], [tkk])
                            toks.append((tkb, tkk))
                        (qtok, qtk), (ktok, ktk), (vb, vbk) = toks
                        for src_, srck_, c0_ in ((qtok, qtk, 0), (ktok, ktk, 4)):
                            sqb, sqk = nb()
                            act(sqb[0:64, :], src_[:], AF.Square, [srck_], [sqk])
                            p.op("dve", dur=0.65, fn=lambda e, sqb=sqb, c0_=c0_, fac=fac: e.tensor_reduce(fac[:, c0_:c0_ + 4], v3(sqb[0:64, :]), AX.X, ALU.add),
                                 reads=[sqk], writes=[fack])
                        rsqrt_to(fac[:, 0:8], fac[:, 0:8], 1.0, [fack], [fack])
                        ts(fac[:, 8:12], fac[:, 0:4], 128.0 ** -0.5, ALU.mult, [fack], [fack])
                        tt(fac[:, 12:16], fac[:, 8:12], smE[:, hh * 4:(hh + 1) * 4], ALU.mult, [fack, "smE"], [fack])
                        tt(fac[:, 16:20], fac[:, 4:8], beta_c[:, hs], ALU.mult, [fack, "tk"], [fack])
                        tt(fac[:, 16:20], fac[:, 16:20], smE[:, hh * 4:(hh + 1) * 4], ALU.mult, [fack, "smE"], [fack])
                        tt(fac[:, 20:24], fac[:, 4:8], smE[:, 8 + hh * 4:8 + (hh + 1) * 4], ALU.mult, [fack, "smE"], [fack])
                        scaled = []
                        for bi, (src, srck, f0) in enumerate(((qtok, qtk, 8), (qtok, qtk, 12), (ktok, ktk, 4), (ktok, ktk, 16), (ktok, ktk, 20))):
                            if bi < 3:
                                o, ok = tbb[bi][:], "tbb%d" % bi
                            else:
                                o, ok = A_(bi - 2)
                            tt(v3(o), v3(src[:]), bc(fac[:, f0:f0 + 4], [64, 4, 128], 2), ALU.mult, [srck, fack], [ok],
                               eng=("pool" if bi >= 3 else "dve"))
                            scaled.append((o, ok))
                        (qn, qnk), (qg, qgk), (kn, knk), (kb, kbk), (kd, kdk) = scaled
                        fTs = []
                        for j, (src, srck) in enumerate(((kn, knk), (qn, qnk), (qg, qgk))):
                            bX, bXk = nb()
                            for q in range(4):
                                trb(bX[:, q * 64:(q + 1) * 64], src[:, q * 128:(q + 1) * 128], [srck], [bXk])
                            fv, fvk = M_(j)
                            acopy(fv, bX[:, 0:256], [bXk], [fvk])
                            fTs.append((fv, fvk))
                        (knT, knTk), (qnT, qnTk), (qgT, qgTk) = fTs
                        Gl, Glk = F(2)
                        Gs, Gsk = F(3)
                        tt(m3(Gl[:, 0:256]), bc(U, [64, 4, 64], 1), bc(g_c[:, hs], [64, 4, 64], 2), ALU.mult, ["cst", "tk"], [Glk], eng="pool")
                        tt(m3(Gs[:, 0:256]), bc(G, [64, 4, 64], 1), bc(g_c[:, hs], [64, 4, 64], 2), ALU.mult, ["cst", "tk"], [Gsk], eng="pool")
                        bS1, bS1k = nb()
                        bS2, bS2k = nb()
                        for q in range(4):
                            qs = slice(q * 64, (q + 1) * 64)
                            mm(bS1[0:64, qs], Gl[:, qs], G, [Glk, "cst"], [bS1k])
                            mm(bS2[0:64, qs], Gs[:, qs], U, [Gsk, "cst"], [bS2k])
                        E1, E1k = F(4)
                        E2, E2k = F(5)
                        act(E1[:, 0:256], bS1[0:64, 0:256], AF.Exp, [bS1k], [E1k])
                        act(E2[:, 0:256], bS2[0:64, 0:256], AF.Exp, [bS2k], [E2k])
                        bK, bKk = nb()
                        bQ, bQk = nb()
                        for q in range(4):
                            qs = slice(q * 64, (q + 1) * 64)
                            mm(bK[0:64, qs], knT[:, qs], knT[:, qs], [knTk], [bKk], passes=1.0)
                            mm(bQ[0:64, qs], knT[:, qs], qnT[:, qs], [knTk, qnTk], [bQk], passes=1.0)
                        tt(m3(Gl[:, 0:256]), bc(G, [64, 4, 64], 1), bc(beta_c[:, hs], [64, 4, 64], 2), ALU.mult, ["cst", "tk"], [Glk])
                        tt(E1[:, 0:256], E1[:, 0:256], Gl[:, 0:256], ALU.mult, [E1k, Glk], [E1k])
                        Lm, Lmk = D_(0)
                        tt(Lm[:, 0:256], bK[0:64, 0:256], E1[:, 0:256], ALU.mult, [bKk, E1k], [Lmk])
                        tt(m3(E2[:, 0:256]), m3(E2[:, 0:256]), bc(U, [64, 4, 64], 1), ALU.mult, [E2k, "cst"], [E2k])
                        Xs, Xsk = D_(1)
                        tt(Xs[:, 0:256], bQ[0:64, 0:256], E2[:, 0:256], ALU.mult, [bQk, E2k], [Xsk])
                        bL, bLk = nb()
                        for q in range(4):
                            qs = slice(q * 64, (q + 1) * 64)
                            trb(bL[0:64, qs], Lm[:, qs], [Lmk], [bLk])
                        LT, LTk = D_(2)
                        acopy(LT[:, 0:256], bL[0:64, 0:256], [bLk], [LTk])
                        P, Pk = D_(3)
                        tt(m3(P[:, 0:256]), bc(I64, [64, 4, 64], 1), m3(LT[:, 0:256]), ALU.subtract, ["cst", LTk], [Pk])
                        cur, curk, curT, curTk = Lm, Lmk, LT, LTk
                        for j in range(1, 6):
                            bA, bAk = nb()
                            for q in range(4):
                                qs = slice(q * 64, (q + 1) * 64)
                                mm(bA[0:64, qs], curT[:, qs], cur[:, qs], [curTk, curk], [bAk], passes=1.0)
                            Xj, Xjk = D_({1: 4, 2: 6, 3: 4, 4: 6, 5: 4}[j])
                            acopy(Xj[:, 0:256], bA[0:64, 0:256], [bAk], [Xjk])
                            if j < 5:
                                bB, bBk = nb()
                                for q in range(4):
                                    qs = slice(q * 64, (q + 1) * 64)
                                    mm(bB[0:64, qs], cur[:, qs], curT[:, qs], [curk, curTk], [bBk], passes=1.0)
                                XjT, XjTk = D_({1: 5, 2: 7, 3: 5, 4: 7}[j])
                                acopy(XjT[:, 0:256], bB[0:64, 0:256], [bBk], [XjTk])
                            bP, bPk = nb()
                            for q in range(4):
                                qs = slice(q * 64, (q + 1) * 64)
                                mm(bP[0:64, qs], Xj[:, qs], P[:, qs], [Xjk, Pk], [bPk], passes=1.0)
                            tt(P[:, 0:256], P[:, 0:256], bP[0:64, 0:256], ALU.add, [Pk, bPk], [Pk])
                            cur, curk = Xj, Xjk
                            if j < 5:
                                curT, curTk = XjT, XjTk
                        bW, bWk = nb()
                        for q in range(4):
                            mm(bW[:, q * 64:(q + 1) * 64], kb[:, q * 128:(q + 1) * 128], P[:, q * 64:(q + 1) * 64], [kbk, Pk], [bWk], passes=1.0)
                        wTn, wTnk = M_(3)
                        p.op("act", lambda e, wTn=wTn, bW=bW: e.mul(wTn, bW[:, 0:256], -1.0), reads=[bWk], writes=[wTnk], dur=0.47)
                        bV, bVk = nb()
                        for q in range(4):
                            h = hh * 4 + q
                            mm(bV[0:64, q * 128:(q + 1) * 128], P[:, q * 64:(q + 1) * 64], vb[:, q * 128:(q + 1) * 128], [Pk, vbk], [bVk],
                               start=True, stop=False, passes=1.0)
                            mm(bV[0:64, q * 128:(q + 1) * 128], wTn[:, q * 64:(q + 1) * 64], dstB[:, h * 128:(h + 1) * 128], [wTnk, "dstB"], [bVk],
                               start=False, stop=True, passes=1.0)
                        vnew, vnk = A_(3)
                        acopy(vnew, bV[0:64, :], [bVk], [vnk])
                        bO, bOk = nb()
                        for q in range(4):
                            h = hh * 4 + q
                            mm(bO[0:64, q * 128:(q + 1) * 128], qgT[:, q * 64:(q + 1) * 64], dstB[:, h * 128:(h + 1) * 128], [qgTk, "dstB"], [bOk],
                               start=True, stop=False, passes=1.0)
                            mm(bO[0:64, q * 128:(q + 1) * 128], Xs[:, q * 64:(q + 1) * 64], vnew[:, q * 128:(q + 1) * 128], [Xsk, vnk], [bOk],
                               start=False, stop=True, passes=1.0)
                        sqb, sqk = nb()
                        act(sqb[0:64, :], bO[0:64, :], AF.Square, [bOk], [sqk])
                        p.op("dve", dur=0.65, fn=lambda e, sqb=sqb, fac=fac: e.tensor_reduce(fac[:, 24:28], v3(sqb[0:64, :]), AX.X, ALU.add),
                             reads=[sqk], writes=[fack])
                        rsqrt_to(fac[:, 24:28], fac[:, 24:28], 1.0 / 128, [fack], [fack])
                        otb, otbk = tbb[3], "tbb3"
                        tt(v3(otb[:]), v3(bO[0:64, :]), bc(fac[:, 24:28], [64, 4, 128], 2), ALU.mult, [bOk, fack], [otbk])
                        bR, bRk = nb()
                        for q in range(4):
                            trb(bR[:, q * 64:(q + 1) * 64], otb[:, q * 128:(q + 1) * 128], [otbk], [bRk])
                        stt(yB[:, hh * 4:(hh + 1) * 4, cc], bR[:, 0:256].rearrange("p (a b) -> p a b", b=64),
                            pf[:, l, PF_DNORM:PF_DNORM + 1], dz[:, hh * 4:(hh + 1) * 4, cc], ALU.mult, ALU.mult,
                            [bRk, "pf", "dnT3"], ["yB"])
                        bU, bUk = nb()
                        for q in range(4):
                            mm(bU[:, q * 128:(q + 1) * 128], kd[:, q * 128:(q + 1) * 128], vnew[:, q * 128:(q + 1) * 128], [kdk, vnk], [bUk], passes=1.0)
                        dh = dst[:, hh * 512:(hh + 1) * 512]
                        tt(v3(dh), v3(dh), bc(elast[:, hs], [128, 4, 128], 2), ALU.mult, ["dst", "elast"], ["dst"], eng="pool")
                        tt(dh, dh, bU[:, :], ALU.add, ["dst", bUk], ["dst"])
                        acopy(dstB[:, hh * 512:(hh + 1) * 512], dh, ["dst"], ["dstB"])
                state["hh"] = 0
                state["bankset"] = None
                if l == 0 and ti == 0:
                    acopy(yT[:, :, 0:NT], yB[:, :, 0:NT], ["yB"], ["yT"])
                    dump("ydn", yT[:, :, 0:NT], ["yT"])
                branch(2)
                if l == 0 and ti == 0:
                    acopy(merged[:, :, 0:NT], mergedB[:, :, 0:NT], ["mergedB"], ["merged"])
                    dump("merged", merged[:, :, 0:NT], ["merged"])
                p.cur_tag = "S8.%d.%d" % (l, ti)
                bN, bNk = bank(7)
                for half in range(2):
                    ws, wk = wload(l, 33 + half)
                    for q in range(4):
                        d = half * 4 + q
                        bO, bOk = nb()
                        proj(bO, bOk, ws, wk, q * 128, NT, rhs=mergedB, rkey="mergedB")
                        acopy(yT[:, d, 0:NT], bO[:, 0:NT], [bOk], ["yT"])
                        i = rot("sqs", 1)
                        act(sqs[i][:, 0:NT], bO[:, 0:NT], AF.Square, [bOk], ["sqs%d" % i])
                        mm(bN[:, 0:NT], ones, sqs[i][:, 0:NT], ["cst", "sqs%d" % i], [bNk], start=(d == 0), stop=(d == KC - 1))
                rsqrt_to(rsb[:, 0, 0:NT], bN[:, 0:NT], 1.0 / D, [bNk], ["rsb"])
                for d in range(KC):
                    j = rot("mtmp", 1)
                    stt(mtmp[j][:, 0:NT], yT[:, d, 0:NT], pf[:, l, PF_NPOST + d:PF_NPOST + d + 1], rsb[:, 0, 0:NT], ALU.mult, ALU.mult,
                        ["yT", "pf", "rsb"], ["mtmp%d" % j])
                    tt(hT[:, d, cols], hT[:, d, cols], mtmp[j][:, 0:NT], ALU.add, [hk, "mtmp%d" % j], [hk], eng="pool")
                if l == 0 and ti == 0:
                    dump("h1", hT[:, :, cols], [hk])

        for r0 in range(0, S, 128):
            nr = min(128, S - r0)
            col0 = NMETA + r0
            i = rot("xstg", 1)
            skey = "merged"
            stg2 = xstg[i]
            for half in range(2):
                bk, bkey = nb()
                for q in range(4):
                    k = half * 4 + q
                    tr(bk[0:nr, q * 128:(q + 1) * 128], hT[:, k, col0:col0 + nr], hkeys(col0, col0 + nr), [bkey])
                acopy(stg2[0:nr, half * 512:(half + 1) * 512], bk[0:nr, :], [bkey], [skey])
            dma(out_d[r0:r0 + nr, :], stg2[0:nr, :], [skey], [], "och")
        p.finalize(st)
    return nc, p


def make_consts():
    c = np.zeros((128, 384), np.float32)
    c[:, 0:128] = np.eye(128, dtype=np.float32)
    c[:, 128:256] = 1.0
    a = np.arange(64)
    c[0:64, 256:320] = (a[:, None] <= a[None, :]).astype(np.float32)
    c[0:64, 320:384] = (a[:, None] > a[None, :]).astype(np.float32)
    return c


def pack_weights(inp):
    L = inp["w_in"].shape[0]
    w_in = np.asarray(inp["w_in"], np.float32)
    cols = []
    cols += list(range(O_Z, O_Z + 1024))
    cols += list(range(O_XBC, O_XBC + 1536))
    for fc in range(8):
        for base in (O_SCB, O_SCC, O_SCH, O_SCG):
            cols += list(range(base + fc * 128, base + (fc + 1) * 128))
    for h in range(8):
        for base in (O_Q, O_K, O_V, O_DZ):
            cols += list(range(base + h * 128, base + (h + 1) * 128))
    cols += list(range(O_GATE, O_GATE + 3072))
    cols = np.asarray(cols)
    assert cols.size == NW1
    w1 = np.ascontiguousarray(w_in[:, :, cols])
    wsm = np.ascontiguousarray(np.concatenate([w_in[:, :, O_DT:O_DT + 16], w_in[:, :, O_DB:O_DB + 16]], axis=-1))
    pf = np.zeros((128, L, NPF), np.float32)
    fm = lambda v, n: np.asarray(v, np.float32).reshape(n, 128).T
    for l in range(L):
        pf[:, l, PF_NPRE:PF_NPRE + 8] = fm(inp["norm_pre"][l], 8)
        pf[:, l, PF_NPOST:PF_NPOST + 8] = fm(inp["norm_post"][l], 8)
        scw = np.asarray(inp["ssd_conv_w"][l], np.float32)
        pf[:, l, PF_SCW:PF_SCW + 48] = scw.reshape(4, 12, 128).transpose(2, 1, 0).reshape(128, 48)
        pf[:, l, PF_SCB:PF_SCB + 12] = fm(inp["ssd_conv_b"][l], 12)
        pf[:, l, PF_SD:PF_SD + 8] = fm(np.repeat(np.asarray(inp["ssd_d"][l], np.float32), 64), 8)
        pf[:, l, PF_SNORM:PF_SNORM + 8] = fm(inp["ssd_norm"][l], 8)
        ccw = np.asarray(inp["sc_conv_w"][l], np.float32)
        pf[:, l, PF_CCW:PF_CCW + 24] = ccw.reshape(3, 8, 128).transpose(2, 1, 0).reshape(128, 24)
        dcw = np.asarray(inp["dn_conv_w"][l], np.float32)
        pf[:, l, PF_DCW:PF_DCW + 96] = dcw.reshape(4, 24, 128).transpose(2, 1, 0).reshape(128, 96)
        pf[:, l, PF_DNORM] = np.asarray(inp["dn_norm"][l], np.float32)
    pb = np.zeros((128, L, 48), np.float32)
    for l in range(L):
        row = np.concatenate([inp["ssd_dt_bias"][l], inp["ssd_a_log"][l], inp["dn_dt_bias"][l], inp["dn_a_log"][l]]).astype(np.float32)
        pb[:, l, :] = np.broadcast_to(row[None, :], (128, 48))
    return dict(w1=w1, wsm=wsm, wb=np.ascontiguousarray(inp["w_branch"], np.float32), wo=np.ascontiguousarray(inp["w_out"], np.float32),
                pf=pf, pb=pb, cst=make_consts(), meta=np.ascontiguousarray(inp["meta_tokens"], np.float32))


_CACHE = {}


def kernel(**inputs):
    x = np.asarray(inputs["x"], np.float32)
    B, S, _ = x.shape
    L = inputs["w_in"].shape[0]
    shared = pack_weights(inputs)
    key = (S, L)
    if key not in _CACHE:
        _CACHE[key] = build_program(S, L)[0]
    nc = _CACHE[key]
    in_maps = []
    for b in range(B):
        m = dict(shared)
        m["x"] = np.ascontiguousarray(x[b])
        in_maps.append(m)
    res = run_bass_kernel_spmd(nc, in_maps, core_ids=list(range(B)))
    return np.stack([np.asarray(r["out"], np.float32) for r in res.results], axis=0)
```

```python
import contextlib
import numpy as np
import concourse.bass as bass
import concourse.mybir as mybir

F32 = mybir.dt.float32
F32R = mybir.dt.float32r
BF16 = mybir.dt.bfloat16
ALU = mybir.AluOpType
AF = mybir.ActivationFunctionType
AX = mybir.AxisListType


class Op:
    __slots__ = ("eng", "fn", "deps", "chan", "needs_inc", "event", "idx", "dur", "succ", "nd", "ready", "fin", "pos", "lat", "tag", "aset")

    def __init__(self, eng, fn, deps, chan, dur):
        self.eng = eng
        self.fn = fn
        self.deps = deps
        self.chan = chan
        self.needs_inc = chan is not None
        self.event = None
        self.dur = dur
        self.succ = []
        self.ready = 0.0
        self.fin = 0.0


class Prog:
    ENGS = ("pe", "act", "dve", "pool", "sp")

    def __init__(self, nc, same_eng_sync=True, schedule=True, window=4000):
        self.nc = nc
        self.ops = []
        self.last_w = {}
        self.readers = {}
        self.same_eng_sync = same_eng_sync
        self.schedule = schedule
        self.window = window

    def op(self, eng, fn, reads=(), writes=(), chan=None, dur=0.1):
        deps = []
        for k in reads:
            w = self.last_w.get(k)
            if w is not None:
                deps.append(w)
        for k in writes:
            w = self.last_w.get(k)
            if w is not None:
                deps.append(w)
            deps.extend(self.readers.get(k, ()))
        o = Op(eng, fn, deps, chan, dur)
        o.tag = getattr(self, "cur_tag", "")
        o.aset = None
        o.idx = len(self.ops)
        self.ops.append(o)
        for k in reads:
            self.readers.setdefault(k, []).append(o)
        for k in writes:
            self.last_w[k] = o
            self.readers[k] = []
        return o

    def _sem_edge(self, d, o):
        if d.chan is None and o.chan is None and d.eng == o.eng:
            if o.eng == "pe" or not self.same_eng_sync:
                return False
        return True

    def _list_schedule(self):
        import heapq
        ops = self.ops
        for o in ops:
            ds = []
            seen = set()
            for d in o.deps:
                if d is o or id(d) in seen:
                    continue
                seen.add(id(d))
                ds.append(d)
            o.deps = ds
            o.nd = len(ds)
            for d in ds:
                d.succ.append(o)
        SEM_LAT = 0.12
        cp = [0.0] * len(ops)
        for o in reversed(ops):
            m = 0.0
            for s_ in o.succ:
                if cp[s_.idx] > m:
                    m = cp[s_.idx]
            cp[o.idx] = m + (o.dur + (2.0 if o.chan is not None else 0.0)) + 0.1
        self.cp_len = max(cp) if cp else 0.0
        free = {e: 0.0 for e in self.ENGS}
        pend = {e: [] for e in self.ENGS}
        avail = {e: [] for e in self.ENGS}
        order = {e: [] for e in self.ENGS}
        dma_pipe = 0.0
        for o in ops:
            if o.nd == 0:
                heapq.heappush(pend[o.eng], (0.0, o.idx))
        nleft = len(ops)
        lo = 0
        done = [False] * (len(ops) + 1)
        W = self.window
        TBL = 1.3
        cur_set = [None]
        cand_l = {e: [] for e in self.ENGS}
        for e in self.ENGS:
            while pend[e]:
                cand_l[e].append(heapq.heappop(pend[e])[1])
        while nleft:
            while done[lo]:
                lo += 1
            best = None
            for e in self.ENGS:
                t = free[e]
                bc_ = None
                for idx in cand_l[e]:
                    if idx >= lo + W:
                        continue
                    stt_ = max(t, ops[idx].ready)
                    if e == "act":
                        as_ = ops[idx].aset
                        if as_ is not None and as_ != cur_set[0]:
                            stt_ += TBL
                    key_ = (stt_, -cp[idx], idx)
                    if bc_ is None or key_ < bc_:
                        bc_ = key_
                if bc_ is None:
                    continue
                if best is None or bc_ < best[0]:
                    best = (bc_, e)
            st, idx, e = best[0][0], best[0][2], best[1]
            cand_l[e].remove(idx)
            o = ops[idx]
            if e == "act" and o.aset is not None:
                cur_set[0] = o.aset
            if o.chan is not None:
                dma_pipe = max(dma_pipe, st) + o.dur
                o.fin = dma_pipe + 2.0
                free[e] = st + 0.06
            else:
                o.fin = st + o.dur
                free[e] = o.fin
            o.pos = len(order[e])
            order[e].append(o)
            done[idx] = True
            nleft -= 1
            for s in o.succ:
                lat = SEM_LAT if self._sem_edge(o, s) else 0.0
                if o.fin + lat > s.ready:
                    s.ready = o.fin + lat
                s.nd -= 1
                if s.nd == 0:
                    cand_l[s.eng].append(s.idx)
        self.sim_time = max(o.fin for o in ops)
        return order

    def finalize(self, stack):
        nc = self.nc
        if self.schedule:
            order = self._list_schedule()
        else:
            order = {e: [] for e in self.ENGS}
            for o in self.ops:
                seen = set()
                ds = []
                for d in o.deps:
                    if d is o or id(d) in seen:
                        continue
                    seen.add(id(d))
                    ds.append(d)
                o.deps = ds
                o.pos = len(order[o.eng])
                order[o.eng].append(o)
        for o in self.ops:
            o.deps = [d for d in o.deps if self._sem_edge(d, o)]
        sems = {}
        cnt = {}

        def getsem(name):
            if name not in sems:
                sems[name] = stack.enter_context(nc.semaphore(name))
                cnt[name] = 0
            return sems[name]

        chan_pos = {}
        for e in self.ENGS:
            for o in order[e]:
                if o.chan is not None:
                    chan_pos[o.chan] = chan_pos.get(o.chan, 0) + 1
                    o.lat = chan_pos[o.chan]
        per = {}
        nw = 0
        for e in self.ENGS:
            wpos = {}
            lst = []
            for o in order[e]:
                best = {}
                for d in o.deps:
                    if d.chan is not None:
                        st_, ps_ = d.chan, d.lat
                    else:
                        st_, ps_ = "e_" + d.eng, d.pos
                    if wpos.get(st_, -1) >= ps_:
                        continue
                    if st_ not in best or best[st_][0] < ps_:
                        best[st_] = (ps_, d)
                need = []
                for st_, (ps_, d) in best.items():
                    wpos[st_] = ps_
                    d.needs_inc = True
                    need.append(d)
                nw += len(need)
                lst.append((o, need))
            per[e] = lst
        for e in self.ENGS:
            for o in order[e]:
                if o.chan is not None:
                    getsem(o.chan)
                    cnt[o.chan] += 16
                    o.event = (o.chan, cnt[o.chan])
                elif o.needs_inc:
                    nm = "e_" + o.eng
                    getsem(nm)
                    cnt[nm] += 1
                    o.event = (nm, cnt[nm])
        self.nwaits = nw
        self.sem_max = dict(cnt)
        assert max(cnt.values()) < 30000, cnt

        def emit(eng_obj, lst):
            for o, need in lst:
                for d in need:
                    eng_obj.wait_ge(sems[d.event[0]], d.event[1])
                ins = o.fn(eng_obj)
                if o.event is not None:
                    if o.chan is not None:
                        ins.then_inc(sems[o.chan], 16)
                    else:
                        ins.then_inc(sems[o.event[0]], 1)

        with nc.Block() as block:
            @block.tensor
            def _(e):
                emit(e, per["pe"])

            @block.scalar
            def _(e):
                emit(e, per["act"])

            @block.vector
            def _(e):
                emit(e, per["dve"])

            @block.gpsimd
            def _(e):
                emit(e, per["pool"])

            @block.sync
            def _(e):
                emit(e, per["sp"])
                for s, v in cnt.items():
                    if v > 0:
                        e.wait_ge(sems[s], v)


from concourse.bass_utils import run_bass_kernel_spmd

D = 1024
KC = 8
NMETA = 16
EPS = 1e-6
PF_NPRE, PF_NPOST, PF_SCW, PF_SCB, PF_SD, PF_SNORM, PF_CCW, PF_DCW, PF_DNORM, NPF = 0, 8, 16, 64, 76, 84, 92, 116, 212, 213
O_Z, O_XBC, O_DT, O_SCB, O_SCC, O_SCH, O_SCG, O_Q, O_K, O_V, O_DZ, O_DB, O_DA, O_GATE = (
    0, 1024, 2560, 2576, 3600, 4624, 5648, 6672, 7696, 8720, 9744, 10768, 10776, 10784)
NW1 = 13824


def build_program(S, L, NTM=256, dbg=None, same_eng_sync=True, schedule=True, window=100000):
    TOK = NMETA + S
    TP = ((TOK + 63) // 64) * 64
    tiles = []
    c = 0
    while c < TP:
        n = min(NTM, TP - c)
        tiles.append((c, n))
        c += n
    nc = bass.Bass("TRN2", target_bir_lowering=False)
    dt_in = lambda n, s: nc.dram_tensor(n, s, F32, kind="ExternalInput").ap()
    x_d = dt_in("x", [S, D])
    meta_d = dt_in("meta", [NMETA, D])
    w1_d = dt_in("w1", [L, D, NW1])
    wsm_d = dt_in("wsm", [L, D, 32])
    wb_d = dt_in("wb", [L, 3, D, D])
    wo_d = dt_in("wo", [L, D, D])
    pf_d = dt_in("pf", [128, L, NPF])
    pb_d = dt_in("pb", [128, L, 48])
    cst_d = dt_in("cst", [128, 384])
    out_d = nc.dram_tensor("out", [S, D], F32, kind="ExternalOutput").ap()
    NBLK = 35
    wbf_d = nc.dram_tensor("wbf", [L, NBLK, 128, KC * 512], BF16, kind="Internal").ap()
    dbg_d = {}
    if dbg:
        for name, shp in dbg.items():
            dbg_d[name] = nc.dram_tensor("dbg_" + name, list(shp), F32, kind="ExternalOutput").ap()

    with contextlib.ExitStack() as st:
        def sb(n, s):
            return st.enter_context(nc.sbuf_tensor("s_" + n, list(s), F32))
        hT = sb("hT", [128, KC, TP])
        NWS = 3
        sbb = lambda n, shp: st.enter_context(nc.sbuf_tensor("s_" + n, list(shp), BF16))
        wsl = [sbb("wsl%d" % i, [128, KC, 512]) for i in range(NWS)]
        xnT = sbb("xnT", [128, KC, NTM])
        yB = sbb("yB", [128, KC, NTM])
        mergedB = sbb("mergedB", [128, KC, NTM])
        bcT = sb("bcT", [128, 4, NTM])
        yT = sb("yT", [128, KC, NTM])
        merged = sb("merged", [128, KC, NTM])
        xstg = [merged[:].rearrange("p a b -> p (a b)")[:, 0:1024]]
        dq = sbb("dq", [128, 3, 8, NTM])
        dz = sbb("dz", [128, 8, NTM])
        xsT = dq[:, 0]
        siluz = dz[:]
        cstB = sbb("cstB", [128, 384])
        NTMB = 16
        fpool = [sb("fp%d" % i, [64, 512]) for i in range(6)]
        BA = [[sbb("ba%d_%d" % (h_, i), [64, 512]) for i in range(4)] for h_ in range(2)]
        BD = [[sbb("bd%d_%d" % (h_, i), [64, 256]) for i in range(8)] for h_ in range(2)]
        BFm = [[sbb("bm%d_%d" % (h_, i), [128, 256]) for i in range(4)] for h_ in range(2)]
        tbb = [sbb("tbb%d" % i, [64, 512]) for i in range(4)]
        sst = sb("sst", [128, 1024])
        dst = sb("dst", [128, 1024])
        dstB = sbb("dstB", [128, 1024])
        pre = [sb("pre%d" % i, [128, NTM + 3]) for i in range(4)]
        cacc = [sb("cacc%d" % i, [128, NTM]) for i in range(3)]
        sqs = [sb("sqs%d" % i, [128, NTM]) for i in range(2)]
        rsb = sb("rsb", [128, 2, NTM])
        sig = [sb("sig%d" % i, [128, NTM]) for i in range(2)]
        mtmp = [sb("mtmp%d" % i, [128, NTM]) for i in range(2)]
        tails = sb("tails", [128, 44, 3])
        pf = sb("pf", [128, L, NPF])
        pbt = sb("pbt", [128, L, 48])
        negA = sb("negA", [128, L, 24])
        cst = sb("cst", [128, 384])
        wsm = sbb("wsm", [128, L, KC, 32])
        tsm = sb("tsm", [64, NTM // 64, 32])
        tk = sb("tk", [64, NTM // 64, 48])
        smE = sb("smE", [64, 32])
        elast = sb("elast", [128, 16])
        facs = [sb("fac%d" % i, [64, 32]) for i in range(2)]
        ps = st.enter_context(nc.psum_tensor("ps", [128, 4096], F32))

        ident = cst[:, 0:128]
        ones = cst[:, 128:256]
        U = cst[0:64, 256:320]
        G = cst[0:64, 320:384]
        I64 = cst[0:64, 0:64]

        p = Prog(nc, same_eng_sync=same_eng_sync, schedule=schedule, window=window)
        state = {"rr": 0, "tm": 0, "ws": 0, "i2": {}}

        def bank(i):
            return ps[:, i * 512:(i + 1) * 512], "b%d" % i

        def nb():
            bs_ = state.get("bankset")
            if bs_ is None:
                i = state["rr"]
                state["rr"] = (i + 1) % 6
                return bank(i)
            j = state["i2"].get(("bs", bs_), 0)
            state["i2"][("bs", bs_)] = (j + 1) % len(bs_)
            return bank(bs_[j])

        def rot(name, n):
            i = state["i2"].get(name, 0)
            state["i2"][name] = (i + 1) % n
            return i

        def Bv(t, n):
            return t[:].bitcast(BF16)[:, 0:n]

        def F(i):
            return fpool[i], "fp%d" % i

        def A_(i):
            h_ = state["hh"]
            return BA[h_][i][:], "ba%d_%d" % (h_, i)

        def D_(i):
            h_ = state["hh"]
            return BD[h_][i][:], "bd%d_%d" % (h_, i)

        def M_(i):
            h_ = state["hh"]
            return BFm[h_][i][:], "bm%d_%d" % (h_, i)

        def fsz(ap):
            n = 1
            for d in ap.shape[1:]:
                n *= int(d)
            return n

        PASSES = 4.0

        def mm(out, lhsT, rhs, r, w, start=True, stop=True, passes=PASSES):
            p.op("pe", lambda e: e.matmul(out, lhsT, rhs, start=start, stop=stop), reads=r, writes=w,
                 dur=max(fsz(rhs), 64) * passes / 2400.0 + 0.03)

        def tr(out, in_, r, w):
            n = in_.shape[0]
            p.op("pe", lambda e: e.transpose(out, in_, ident[0:n, 0:n]), reads=r + ["cst"], writes=w, dur=0.09)

        def trb(out, in_, r, w):
            n = in_.shape[0]
            p.op("pe", lambda e: e.matmul(out, in_, cstB[0:n, 0:n], start=True, stop=True), reads=r + ["cstB"], writes=w,
                 dur=max(n, 64) / 2400.0 + 0.03)

        def act(out, in_, func, r, w, bias=None, scale=None):
            kw = {}
            if bias is not None:
                kw["bias"] = bias
            if scale is not None:
                kw["scale"] = scale
            o_ = p.op("act", lambda e: e.activation(out, in_, func, **kw), reads=r, writes=w, dur=0.2 + fsz(out) / 960.0)
            o_.aset = ASET.get(func)

        ASET = {AF.Silu: "silu", AF.Sigmoid: "sig", AF.Exp: "el", AF.Ln: "el", AF.Sqrt: "sqrt"}

        def acopy(out, in_, r, w):
            p.op("act", lambda e: e.copy(out, in_), reads=r, writes=w, dur=0.2 + fsz(out) / 960.0)

        def edur(eng, n):
            return (0.12 + n / 960.0) if eng == "dve" else (0.2 + n / 400.0)

        def tt(out, a, b, op, r, w, eng="dve"):
            p.op(eng, lambda e: e.tensor_tensor(out, a, b, op), reads=r, writes=w, dur=edur(eng, fsz(out)))

        def ts(out, a, s1, op0, r, w, s2=None, op1=None, eng="dve"):
            if op1 is None:
                p.op(eng, lambda e: e.tensor_scalar(out, a, s1, None, op0), reads=r, writes=w, dur=edur(eng, fsz(out)))
            else:
                p.op(eng, lambda e: e.tensor_scalar(out, a, s1, s2, op0, op1), reads=r, writes=w, dur=edur(eng, fsz(out)))

        def stt(out, in0, scalar, in1, op0, op1, r, w):
            p.op("dve", lambda e: e.scalar_tensor_tensor(out, in0, scalar, in1, op0, op1), reads=r, writes=w,
                 dur=edur("dve", fsz(out)))

        def recip(out, in_, r, w):
            p.op("dve", lambda e: e.reciprocal(out, in_), reads=r, writes=w, dur=edur("dve", fsz(out)))

        def vcopy(out, in_, r, w, eng="dve"):
            p.op(eng, lambda e: e.tensor_copy(out, in_), reads=r, writes=w, dur=edur(eng, fsz(out)))

        def memset(ap, val, w, eng="pool"):
            p.op(eng, lambda e: e.memset(ap, val), writes=w, dur=edur(eng, fsz(ap)))

        def dma(out, in_, r, w, chan, eng="sp"):
            nbytes = (2 if out.dtype == BF16 else 4) * fsz(out) * int(out.shape[0]) * (3 if eng == "pool" else 1)
            p.op(eng, lambda e: e.dma_start(out=out, in_=in_), reads=r, writes=w, chan=chan, dur=nbytes / 250e3)

        def dump(name, ap, keys):
            if name in dbg_d:
                dma(dbg_d[name], ap, keys, [], "dbg")

        def rsqrt_to(out, in_, scale, r, w):
            act(out, in_, AF.Ln, r, w, bias=EPS, scale=scale)
            act(out, out, AF.Exp, w, w, scale=-0.5)

        dma(cst[:], cst_d, [], ["cst"], "ld_cst")
        dma(pf[:], pf_d, [], ["pf"], "ld_pf")
        dma(pbt[:], pb_d, [], ["pbt"], "ld_pb")
        for l in range(L):
            dma(wsm[:, l], wsm_d[l].rearrange("(k p) c -> p k c", p=128), [], ["wsm%d" % l], "ld_wsm%d" % l, eng="pool")
        acopy(cstB[:], cst[:], ["cst"], ["cstB"])
        for l in range(L):
            act(negA[:, l, 0:16], pbt[:, l, 16:32], AF.Exp, ["pbt"], ["negA"])
            act(negA[:, l, 16:24], pbt[:, l, 40:48], AF.Exp, ["pbt"], ["negA"])
        ts(negA[:], negA[:], -1.0, ALU.mult, ["negA"], ["negA"])

        hkey = lambda t: "hT%d" % t

        def tile_of(col):
            for ti, (c0, n) in enumerate(tiles):
                if c0 <= col < c0 + n:
                    return ti
            raise ValueError

        def hkeys(c_lo, c_hi):
            return sorted({hkey(tile_of(c)) for c in (c_lo, c_hi - 1)} | {hkey(t) for t in range(tile_of(c_lo), tile_of(c_hi - 1) + 1)})

        if TP > TOK:
            memset(hT[:, :, TOK:TP], 0.0, hkeys(TOK, TP))
        row_blocks = [("meta", 0, NMETA, 0)] + [("x", r0, min(128, S - r0), NMETA + r0) for r0 in range(0, S, 128)]
        for (src, r0, nr, col0) in row_blocks:
            si = rot("xstg", 1)
            skey = "merged"
            stg2 = xstg[si]
            srcap = meta_d[0:nr, :] if src == "meta" else x_d[r0:r0 + nr, :]
            dma(stg2[0:nr, :], srcap, [], [skey], "xch%d" % si)
            for half in range(2):
                bk, bkey = nb()
                for q in range(4):
                    k = half * 4 + q
                    tr(bk[:, q * 128:q * 128 + nr], stg2[0:nr, k * 128:(k + 1) * 128], [skey], [bkey])
                acopy(hT[:, half * 4:half * 4 + 4, col0:col0 + nr],
                      bk.rearrange("p (a b) -> p a b", b=128)[:, :, 0:nr], [bkey], hkeys(col0, col0 + nr))

        def blk_src(l, blk):
            if blk < 27:
                return w1_d[l][:, blk * 512:(blk + 1) * 512]
            if blk < 33:
                n, half = divmod(blk - 27, 2)
                return wb_d[l, n][:, half * 512:(half + 1) * 512]
            return wo_d[l][:, (blk - 33) * 512:(blk - 32) * 512]

        use_order = [0, 1, 2, 3, 4, 21, 27, 22, 28] + list(range(5, 13)) + [23, 29, 24, 30] + list(range(13, 21)) + [25, 31, 26, 32, 33, 34]
        assert sorted(use_order) == list(range(NBLK))
        GRP = 9
        grp_of = {}
        for l in range(L):
            for gi in range(0, NBLK, GRP):
                grp = use_order[gi:gi + GRP]
                chn = "cv%d_%d" % (l, gi // GRP)
                for blk in grp:
                    dma(wbf_d[l, blk].rearrange("p (k c) -> p k c", k=KC), blk_src(l, blk).rearrange("(k p) c -> p k c", p=128),
                        [], ["wbf%d_%d" % (l, blk)], chn, eng="pool")
                for blk in grp:
                    grp_of[(l, blk)] = grp

        def wload(l, blk):
            si = state["ws"] % NWS
            state["ws"] += 1
            key = "wsl%d" % si
            dma(wsl[si][:], wbf_d[l, blk].rearrange("p (k c) -> p k c", k=KC), ["wbf%d_%d" % (l, b_) for b_ in grp_of[(l, blk)]], [key],
                "wch%d" % si)
            return wsl[si], key

        def proj(out_bank, okey, wslot, wkey, coff, NT, rhs=None, rkey="xnT"):
            rhs = xnT if rhs is None else rhs
            for k in range(KC):
                mm(out_bank[:, 0:NT], wslot[:, k, coff:coff + 128], rhs[:, k, 0:NT], [wkey, rkey], [okey],
                   start=(k == 0), stop=(k == KC - 1), passes=1.0)

        def conv(bk, bkey, ti, wcol0, K, l, NT, dest, dkey, bias_col=None, mul=None, mulkey=None):
            H = K - 1
            i = rot("pre", 4)
            pr, pk = pre[i], "pre%d" % i
            ca, ck = cacc[i % 3], "cacc%d" % (i % 3)
            vcopy(pr[:, 0:H], tails[:, ti, 0:H], ["tails%d" % ti], [pk], eng="pool")
            if mul is None:
                acopy(pr[:, H:H + NT], bk[:, 0:NT], [bkey], [pk])
            else:
                tt(pr[:, H:H + NT], bk[:, 0:NT], mul, ALU.mult, [bkey, mulkey], [pk])
            vcopy(tails[:, ti, 0:H], pr[:, NT:NT + H], [pk], ["tails%d" % ti], eng="pool")
            wl = pf[:, l, wcol0 + K - 1:wcol0 + K]
            if mul is None:
                act(ca[:, 0:NT], bk[:, 0:NT], AF.Identity, [bkey, "pf"], [ck], scale=wl)
            else:
                act(ca[:, 0:NT], pr[:, H:H + NT], AF.Identity, [pk, "pf"], [ck], scale=wl)
            for k in range(0, K - 1):
                last = (k == K - 2 and K == 3)
                o = dest if last else ca[:, 0:NT]
                ok = [dkey] if last else [ck]
                stt(o, pr[:, k:k + NT], pf[:, l, wcol0 + k:wcol0 + k + 1], ca[:, 0:NT], ALU.mult, ALU.add,
                    [pk, "pf", ck], ok)
            if K == 4:
                if bias_col is not None:
                    act(dest, ca[:, 0:NT], AF.Silu, [ck, "pf"], [dkey], bias=pf[:, l, bias_col:bias_col + 1])
                else:
                    act(dest, ca[:, 0:NT], AF.Silu, [ck], [dkey])

        bc = lambda ap, shape, axis: ap.unsqueeze(axis).to_broadcast(list(shape))

        for l in range(L):
            memset(sst[:], 0.0, ["sst"])
            memset(dst[:], 0.0, ["dst"])
            memset(dstB[:], 0.0, ["dstB"])
            memset(tails[:], 0.0, ["tails%d" % i for i in range(44)])
            w1 = w1_d[l]
            for ti, (c0, NT) in enumerate(tiles):
                nch = NT // 64
                hk = hkey(ti)
                cols = slice(c0, c0 + NT)
                p.cur_tag = "S0.%d.%d" % (l, ti)
                bN, bNk = bank(7)
                for k in range(KC):
                    i = rot("sqs", 2)
                    act(sqs[i][:, 0:NT], hT[:, k, cols], AF.Square, [hk], ["sqs%d" % i])
                    mm(bN[:, 0:NT], ones, sqs[i][:, 0:NT], ["cst", "sqs%d" % i], [bNk], start=(k == 0), stop=(k == KC - 1))
                rsqrt_to(rsb[:, 0, 0:NT], bN[:, 0:NT], 1.0 / D, [bNk], ["rsb"])
                for k in range(KC):
                    stt(xnT[:, k, 0:NT], hT[:, k, cols], pf[:, l, PF_NPRE + k:PF_NPRE + k + 1], rsb[:, 0, 0:NT],
                        ALU.mult, ALU.mult, [hk, "pf", "rsb"], ["xnT"])
                if l == 0 and ti == 0 and "xnT" in dbg_d:
                    acopy(merged[:, :, 0:NT], xnT[:, :, 0:NT], ["xnT"], ["merged"])
                    dump("xnT", merged[:, :, 0:NT], ["merged"])
                p.cur_tag = "S1.%d.%d" % (l, ti)
                bS, bSk = bank(6)
                for c in range(nch):
                    for k in range(KC):
                        mm(bS[0:64, c * 32:(c + 1) * 32], xnT[:, k, c * 64:(c + 1) * 64], wsm[:, l, k, :], ["xnT", "wsm%d" % l], [bSk],
                           start=(k == 0), stop=(k == KC - 1), passes=1.0)
                acopy(tsm[:, 0:nch, :], bS[0:64, 0:nch * 32].rearrange("p (c f) -> p c f", f=32), [bSk], ["tsm"])
                tt(tk[:, 0:nch, 0:16], tsm[:, 0:nch, 0:16], bc(pbt[0:64, l, 0:16], [64, nch, 16], 1), ALU.add, ["tsm", "pbt"], ["tk"])
                tt(tk[:, 0:nch, 40:48], tsm[:, 0:nch, 24:32], bc(pbt[0:64, l, 32:40], [64, nch, 8], 1), ALU.add, ["tsm", "pbt"], ["tk"])
                act(tk[:, 0:nch, 0:16], tk[:, 0:nch, 0:16], AF.Exp, ["tk"], ["tk"])
                act(tk[:, 0:nch, 40:48], tk[:, 0:nch, 40:48], AF.Exp, ["tk"], ["tk"])
                act(tk[:, 0:nch, 0:16], tk[:, 0:nch, 0:16], AF.Ln, ["tk"], ["tk"], bias=1.0)
                act(tk[:, 0:nch, 40:48], tk[:, 0:nch, 40:48], AF.Ln, ["tk"], ["tk"], bias=1.0)
                act(tk[:, 0:nch, 32:40], tsm[:, 0:nch, 16:24], AF.Sigmoid, ["tsm"], ["tk"])
                tt(tk[:, 0:nch, 16:32], tk[:, 0:nch, 0:16], bc(negA[0:64, l, 0:16], [64, nch, 16], 1), ALU.mult, ["tk", "negA"], ["tk"])
                tt(tk[:, 0:nch, 40:48], tk[:, 0:nch, 40:48], bc(negA[0:64, l, 16:24], [64, nch, 8], 1), ALU.mult, ["tk", "negA"], ["tk"])
                if l == 0 and ti == 0:
                    dump("tk", tk[:, 0:nch, :], ["tk"])
                p.cur_tag = "S2.%d.%d" % (l, ti)
                for blk in range(2):
                    ws, wk = wload(l, blk)
                    for q in range(4):
                        fc = blk * 4 + q
                        bk, bkey = nb()
                        proj(bk, bkey, ws, wk, q * 128, NT)
                        act(siluz[:, fc, 0:NT], bk[:, 0:NT], AF.Silu, [bkey], ["dnT3"])
                for blk in range(3):
                    ws, wk = wload(l, 2 + blk)
                    for q in range(4):
                        fc = blk * 4 + q
                        bk, bkey = nb()
                        proj(bk, bkey, ws, wk, q * 128, NT)
                        if fc < 8:
                            dest, dkey = xsT[:, fc, 0:NT], "dnT0"
                        else:
                            dest, dkey = bcT[:, fc - 8, 0:NT], "bcT"
                        conv(bk, bkey, fc, PF_SCW + fc * 4, 4, l, NT, dest, dkey, bias_col=PF_SCB + fc)
                if l == 0 and ti == 0:
                    pass
                    dump("bcT", bcT[:, :, 0:NT], ["bcT"])
                p.cur_tag = "S3.%d.%d" % (l, ti)
                for c in range(nch):
                    cc = slice(c * 64, (c + 1) * 64)
                    dt_c = tk[:, c, 0:16]
                    dtA_c = tk[:, c, 16:32]
                    bM, bMk = bank(6)
                    mm(bM[0:64, 0:16], U, dtA_c, ["cst", "tk"], [bMk])
                    mm(bM[0:64, 16:32], G, dtA_c, ["cst", "tk"], [bMk])
                    mm(bM[0:128, 32:48], ones[0:64, :], dtA_c, ["cst", "tk"], [bMk])
                    act(smE[:, 0:32], bM[0:64, 0:32], AF.Exp, [bMk], ["smE"])
                    act(elast[:, 0:16], bM[:, 32:48], AF.Exp, [bMk], ["elast"])
                    for g in range(2):
                        hs = slice(g * 8, (g + 1) * 8)
                        state["tm"] = 0
                        bX, bXk = nb()
                        for q in range(4):
                            trb(bX[0:64, q * 128:(q + 1) * 128], xsT[:, g * 4 + q, cc], ["dnT0"], [bXk])
                        bB, bBk = nb()
                        tr(bB[0:64, 0:128], bcT[:, g, cc], ["bcT"], [bBk])
                        xc, xck = F(0)
                        tt(xc[:].rearrange("p (h d) -> p h d", d=64), bX[0:64, :].rearrange("p (h d) -> p h d", d=64),
                           bc(dt_c[:, hs], [64, 8, 64], 2), ALU.mult, [bXk, "tk"], [xck])
                        xd, xdk = F(1)
                        tt(xd[:].rearrange("p (h d) -> p h d", d=64), xc[:].rearrange("p (h d) -> p h d", d=64),
                           bc(smE[:, 16 + g * 8:16 + (g + 1) * 8], [64, 8, 64], 2), ALU.mult, [xck, "smE"], [xdk], eng="pool")
                        btok, btk = F(2)
                        acopy(btok[:, 0:128], bB[0:64, 0:128], [bBk], [btk])
                        gm, gmk = F(3)
                        tt(gm[:].rearrange("p (h d) -> p h d", d=64), bc(G, [64, 8, 64], 1), bc(dtA_c[:, hs], [64, 8, 64], 2),
                           ALU.mult, ["cst", "tk"], [gmk], eng="pool")
                        bE, bEk = nb()
                        for h in range(8):
                            mm(bE[0:64, h * 64:(h + 1) * 64], gm[:, h * 64:(h + 1) * 64], U, [gmk, "cst"], [bEk])
                        E, Ek = F(4)
                        act(E[:], bE[0:64, :], AF.Exp, [bEk], [Ek])
                        bC, bCk = nb()
                        mm(bC[0:64, 0:64], bcT[:, g, cc], bcT[:, 2 + g, cc], ["bcT"], [bCk])
                        cbm, cbk = F(2)[0][:, 128:192], "fp2"
                        tt(cbm[:, 0:64], bC[0:64, 0:64], U, ALU.mult, [bCk, "cst"], [cbk])
                        MT, MTk = F(5)
                        tt(MT[:].rearrange("p (h d) -> p h d", d=64), E[:].rearrange("p (h d) -> p h d", d=64),
                           bc(cbm[:, 0:64], [64, 8, 64], 1), ALU.mult, [Ek, cbk], [MTk])
                        bY, bYk = nb()
                        for h in range(8):
                            mm(bY[0:64, h * 64:(h + 1) * 64], MT[:, h * 64:(h + 1) * 64], xc[:, h * 64:(h + 1) * 64], [MTk, xck], [bYk])
                        bT, bTk = nb()
                        mm(bT[0:64, :], bcT[:, 2 + g, cc], sst[:, g * 512:(g + 1) * 512], ["bcT", "sst"], [bTk])
                        tmp, tmpk = F(4)
                        tt(tmp[:].rearrange("p (h d) -> p h d", d=64), bT[0:64, :].rearrange("p (h d) -> p h d", d=64),
                           bc(smE[:, g * 8:(g + 1) * 8], [64, 8, 64], 2), ALU.mult, [bTk, "smE"], [tmpk])
                        ytok, ytk = tbb[3], "tbb3"
                        tt(ytok[:], bY[0:64, :], tmp[:], ALU.add, [bYk, tmpk], [ytk])
                        bR, bRk = nb()
                        for q in range(4):
                            trb(bR[:, q * 64:(q + 1) * 64], ytok[:, q * 128:(q + 1) * 128], [ytk], [bRk])
                        acopy(yT[:, g * 4:(g + 1) * 4, cc], bR[:, 0:256].rearrange("p (a b) -> p a b", b=64), [bRk], ["yT"])
                        bU, bUk = nb()
                        mm(bU[:, :], btok[:, 0:128], xd[:], [btk, xdk], [bUk])
                        sg_ = sst[:, g * 512:(g + 1) * 512]
                        tt(sg_.rearrange("p (h d) -> p h d", d=64), sg_.rearrange("p (h d) -> p h d", d=64),
                           bc(elast[:, hs], [128, 8, 64], 2), ALU.mult, ["sst", "elast"], ["sst"], eng="pool")
                        tt(sg_, sg_, bU[:, :], ALU.add, ["sst", bUk], ["sst"])
                p.cur_tag = "S3b.%d.%d" % (l, ti)
                for fc in range(KC):
                    stt(yT[:, fc, 0:NT], xsT[:, fc, 0:NT], pf[:, l, PF_SD + fc:PF_SD + fc + 1], yT[:, fc, 0:NT], ALU.mult, ALU.add,
                        ["dnT0", "pf", "yT"], ["yT"])
                tt(yT[:, :, 0:NT], yT[:, :, 0:NT], siluz[:, :, 0:NT], ALU.mult, ["yT", "dnT3"], ["yT"])
                for g in range(2):
                    bN, bNk = bank(7)
                    for q in range(4):
                        fc = g * 4 + q
                        i = rot("sqs", 2)
                        act(sqs[i][:, 0:NT], yT[:, fc, 0:NT], AF.Square, ["yT"], ["sqs%d" % i])
                        mm(bN[:, 0:NT], ones, sqs[i][:, 0:NT], ["cst", "sqs%d" % i], [bNk], start=(q == 0), stop=(q == 3))
                    rsqrt_to(rsb[:, g, 0:NT], bN[:, 0:NT], 1.0 / 512, [bNk], ["rsb"])
                for fc in range(KC):
                    stt(yB[:, fc, 0:NT], yT[:, fc, 0:NT], pf[:, l, PF_SNORM + fc:PF_SNORM + fc + 1], rsb[:, fc // 4, 0:NT],
                        ALU.mult, ALU.mult, ["yT", "pf", "rsb"], ["yB"])
                if l == 0 and ti == 0:
                    acopy(yT[:, :, 0:NT], yB[:, :, 0:NT], ["yB"], ["yT"])
                    dump("yssd", yT[:, :, 0:NT], ["yT"])

                def branch(n):
                    p.cur_tag = "BR%d.%d.%d" % (n, l, ti)
                    for half in range(2):
                        gs, gk = wload(l, 21 + n * 2 + half)
                        bs, bk_ = wload(l, 27 + n * 2 + half)
                        for q in range(4):
                            d = half * 4 + q
                            bG, bGk = nb()
                            proj(bG, bGk, gs, gk, q * 128, NT)
                            i = rot("sig", 2)
                            act(sig[i][:, 0:NT], bG[:, 0:NT], AF.Sigmoid, [bGk], ["sig%d" % i])
                            bB2, bB2k = nb()
                            proj(bB2, bB2k, bs, bk_, q * 128, NT, rhs=yB, rkey="yB")
                            if n == 0:
                                tt(merged[:, d, 0:NT], bB2[:, 0:NT], sig[i][:, 0:NT], ALU.mult, [bB2k, "sig%d" % i], ["merged"])
                            else:
                                j = rot("mtmp", 2)
                                tt(mtmp[j][:, 0:NT], bB2[:, 0:NT], sig[i][:, 0:NT], ALU.mult, [bB2k, "sig%d" % i], ["mtmp%d" % j])
                                if n == 2:
                                    tt(mergedB[:, d, 0:NT], merged[:, d, 0:NT], mtmp[j][:, 0:NT], ALU.add, ["merged", "mtmp%d" % j], ["mergedB"], eng="pool")
                                else:
                                    tt(merged[:, d, 0:NT], merged[:, d, 0:NT], mtmp[j][:, 0:NT], ALU.add, ["merged", "mtmp%d" % j], ["merged"], eng="pool")

                branch(0)
                p.cur_tag = "S5.%d.%d" % (l, ti)
                for fc in range(KC):
                    ws, wk = wload(l, 5 + fc)
                    bH, bHk = nb()
                    proj(bH, bHk, ws, wk, 256, NT)
                    i = rot("sig", 2)
                    acopy(sig[i][:, 0:NT], bH[:, 0:NT], [bHk], ["sig%d" % i])
                    bC, bCk = nb()
                    proj(bC, bCk, ws, wk, 128, NT)
                    j = rot("mtmp", 2)
                    conv(bC, bCk, 12 + fc, PF_CCW + fc * 3, 3, l, NT, mtmp[j][:, 0:NT], "mtmp%d" % j, mul=sig[i][:, 0:NT], mulkey="sig%d" % i)
                    bB, bBk = nb()
                    proj(bB, bBk, ws, wk, 0, NT)
                    tt(yT[:, fc, 0:NT], bB[:, 0:NT], mtmp[j][:, 0:NT], ALU.mult, [bBk, "mtmp%d" % j], ["yT"])
                    bG, bGk = nb()
                    proj(bG, bGk, ws, wk, 384, NT)
                    i = rot("sig", 2)
                    act(sig[i][:, 0:NT], bG[:, 0:NT], AF.Silu, [bGk], ["sig%d" % i])
                    tt(yB[:, fc, 0:NT], yT[:, fc, 0:NT], sig[i][:, 0:NT], ALU.mult, ["yT", "sig%d" % i], ["yB"])
                if l == 0 and ti == 0:
                    acopy(yT[:, :, 0:NT], yB[:, :, 0:NT], ["yB"], ["yT"])
                    dump("ysc", yT[:, :, 0:NT], ["yT"])
                branch(1)
                p.cur_tag = "S6.%d.%d" % (l, ti)
                for h in range(8):
                    ws, wk = wload(l, 13 + h)
                    for which in range(3):
                        bk, bkey = nb()
                        proj(bk, bkey, ws, wk, which * 128, NT)
                        fcq = which * 8 + h
                        conv(bk, bkey, 20 + fcq, PF_DCW + fcq * 4, 4, l, NT, dq[:, which, h, 0:NT], "dnT%d" % which)
                    bk, bkey = nb()
                    proj(bk, bkey, ws, wk, 384, NT)
                    act(dz[:, h, 0:NT], bk[:, 0:NT], AF.Silu, [bkey], ["dnT3"])
                if l == 0 and ti == 0:
                    pass
                p.cur_tag = "S7.%d.%d" % (l, ti)
                for c in range(nch):
                    cc = slice(c * 64, (c + 1) * 64)
                    beta_c = tk[:, c, 32:40]
                    g_c = tk[:, c, 40:48]
                    bM, bMk = bank(6)
                    mm(bM[0:64, 0:8], U, g_c, ["cst", "tk"], [bMk])
                    mm(bM[0:64, 8:16], G, g_c, ["cst", "tk"], [bMk])
                    mm(bM[0:128, 16:24], ones[0:64, :], g_c, ["cst", "tk"], [bMk])
                    act(smE[:, 0:16], bM[0:64, 0:16], AF.Exp, [bMk], ["smE"])
                    act(elast[:, 0:8], bM[:, 16:24], AF.Exp, [bMk], ["elast"])
                    for hh in range(2):
                        hs = slice(hh * 4, (hh + 1) * 4)
                        state["hh"] = hh
                        state["bankset"] = (0, 1, 2) if hh == 0 else (3, 4, 5)
                        fac = facs[hh]
                        fack = "fac%d" % hh
                        v3 = lambda ap: ap.rearrange("p (h d) -> p h d", d=128)
                        m3 = lambda ap: ap.rearrange("p (h d) -> p h d", d=64)
                        toks = []
                        for which in range(3):
                            bX, bXk = nb()
                            for q in range(4):
                                trb(bX[0:64, q * 128:(q + 1) * 128], dq[:, which, hh * 4 + q, cc], ["dnT%d" % which], [bXk])
                            if which < 2:
                                tkb, tkk = F(which)
                                acopy(tkb[:], bX[0:64, :], [bXk], [tkk])
                            else:
                                tkb, tkk = A_(0)
                                tt(v3(tkb), v3(bX[0:64, :]), bc(beta_c[:, hs], [64, 4, 128], 2), ALU.mult, [bXk, "tk"], [tkk])
                            toks.append((tkb, tkk))
                        (qtok, qtk), (ktok, ktk), (vb, vbk) = toks
                        for src_, srck_, c0_ in ((qtok, qtk, 0), (ktok, ktk, 4)):
                            sqb, sqk = nb()
                            act(sqb[0:64, :], src_[:], AF.Square, [srck_], [sqk])
                            p.op("dve", dur=0.65, fn=lambda e, sqb=sqb, c0_=c0_, fac=fac: e.tensor_reduce(fac[:, c0_:c0_ + 4], v3(sqb[0:64, :]), AX.X, ALU.add),
                                 reads=[sqk], writes=[fack])
                        rsqrt_to(fac[:, 0:8], fac[:, 0:8], 1.0, [fack], [fack])
                        ts(fac[:, 8:12], fac[:, 0:4], 128.0 ** -0.5, ALU.mult, [fack], [fack])
                        tt(fac[:, 12:16], fac[:, 8:12], smE[:, hh * 4:(hh + 1) * 4], ALU.mult, [fack, "smE"], [fack])
                        tt(fac[:, 16:20], fac[:, 4:8], beta_c[:, hs], ALU.mult, [fack, "tk"], [fack])
                        tt(fac[:, 16:20], fac[:, 16:20], smE[:, hh * 4:(hh + 1) * 4], ALU.mult, [fack, "smE"], [fack])
                        tt(fac[:, 20:24], fac[:, 4:8], smE[:, 8 + hh * 4:8 + (hh + 1) * 4], ALU.mult, [fack, "smE"], [fack])
                        scaled = []
                        for bi, (src, srck, f0) in enumerate(((qtok, qtk, 8), (qtok, qtk, 12), (ktok, ktk, 4), (ktok, ktk, 16), (ktok, ktk, 20))):
                            if bi < 3:
                                o, ok = tbb[bi][:], "tbb%d" % bi
                            else:
                                o, ok = A_(bi - 2)
                            tt(v3(o), v3(src[:]), bc(fac[:, f0:f0 + 4], [64, 4, 128], 2), ALU.mult, [srck, fack], [ok],
                               eng=("pool" if bi >= 3 else "dve"))
                            scaled.append((o, ok))
                        (qn, qnk), (qg, qgk), (kn, knk), (kb, kbk), (kd, kdk) = scaled
                        fTs = []
                        for j, (src, srck) in enumerate(((kn, knk), (qn, qnk), (qg, qgk))):
                            bX, bXk = nb()
                            for q in range(4):
                                trb(bX[:, q * 64:(q + 1) * 64], src[:, q * 128:(q + 1) * 128], [srck], [bXk])
                            fv, fvk = M_(j)
                            acopy(fv, bX[:, 0:256], [bXk], [fvk])
                            fTs.append((fv, fvk))
                        (knT, knTk), (qnT, qnTk), (qgT, qgTk) = fTs
                        Gl, Glk = F(2)
                        Gs, Gsk = F(3)
                        tt(m3(Gl[:, 0:256]), bc(U, [64, 4, 64], 1), bc(g_c[:, hs], [64, 4, 64], 2), ALU.mult, ["cst", "tk"], [Glk], eng="pool")
                        tt(m3(Gs[:, 0:256]), bc(G, [64, 4, 64], 1), bc(g_c[:, hs], [64, 4, 64], 2), ALU.mult, ["cst", "tk"], [Gsk], eng="pool")
                        bS1, bS1k = nb()
                        bS2, bS2k = nb()
                        for q in range(4):
                            qs = slice(q * 64, (q + 1) * 64)
                            mm(bS1[0:64, qs], Gl[:, qs], G, [Glk, "cst"], [bS1k])
                            mm(bS2[0:64, qs], Gs[:, qs], U, [Gsk, "cst"], [bS2k])
                        E1, E1k = F(4)
                        E2, E2k = F(5)
                        act(E1[:, 0:256], bS1[0:64, 0:256], AF.Exp, [bS1k], [E1k])
                        act(E2[:, 0:256], bS2[0:64, 0:256], AF.Exp, [bS2k], [E2k])
                        bK, bKk = nb()
                        bQ, bQk = nb()
                        for q in range(4):
                            qs = slice(q * 64, (q + 1) * 64)
                            mm(bK[0:64, qs], knT[:, qs], knT[:, qs], [knTk], [bKk], passes=1.0)
                            mm(bQ[0:64, qs], knT[:, qs], qnT[:, qs], [knTk, qnTk], [bQk], passes=1.0)
                        tt(m3(Gl[:, 0:256]), bc(G, [64, 4, 64], 1), bc(beta_c[:, hs], [64, 4, 64], 2), ALU.mult, ["cst", "tk"], [Glk])
                        tt(E1[:, 0:256], E1[:, 0:256], Gl[:, 0:256], ALU.mult, [E1k, Glk], [E1k])
                        Lm, Lmk = D_(0)
                        tt(Lm[:, 0:256], bK[0:64, 0:256], E1[:, 0:256], ALU.mult, [bKk, E1k], [Lmk])
                        tt(m3(E2[:, 0:256]), m3(E2[:, 0:256]), bc(U, [64, 4, 64], 1), ALU.mult, [E2k, "cst"], [E2k])
                        Xs, Xsk = D_(1)
                        tt(Xs[:, 0:256], bQ[0:64, 0:256], E2[:, 0:256], ALU.mult, [bQk, E2k], [Xsk])
                        bL, bLk = nb()
                        for q in range(4):
                            qs = slice(q * 64, (q + 1) * 64)
                            trb(bL[0:64, qs], Lm[:, qs], [Lmk], [bLk])
                        LT, LTk = D_(2)
                        acopy(LT[:, 0:256], bL[0:64, 0:256], [bLk], [LTk])
                        P, Pk = D_(3)
                        tt(m3(P[:, 0:256]), bc(I64, [64, 4, 64], 1), m3(LT[:, 0:256]), ALU.subtract, ["cst", LTk], [Pk])
                        cur, curk, curT, curTk = Lm, Lmk, LT, LTk
                        for j in range(1, 6):
                            bA, bAk = nb()
                            for q in range(4):
                                qs = slice(q * 64, (q + 1) * 64)
                                mm(bA[0:64, qs], curT[:, qs], cur[:, qs], [curTk, curk], [bAk], passes=1.0)
                            Xj, Xjk = D_({1: 4, 2: 6, 3: 4, 4: 6, 5: 4}[j])
                            acopy(Xj[:, 0:256], bA[0:64, 0:256], [bAk], [Xjk])
                            if j < 5:
                                bB, bBk = nb()
                                for q in range(4):
                                    qs = slice(q * 64, (q + 1) * 64)
                                    mm(bB[0:64, qs], cur[:, qs], curT[:, qs], [curk, curTk], [bBk], passes=1.0)
                                XjT, XjTk = D_({1: 5, 2: 7, 3: 5, 4: 7}[j])
                                vcopy(XjT[:, 0:256], bB[0:64, 0:256], [bBk], [XjTk])
                            bP, bPk = nb()
                            for q in range(4):
                                qs = slice(q * 64, (q + 1) * 64)
                                mm(bP[0:64, qs], Xj[:, qs], P[:, qs], [Xjk, Pk], [bPk], passes=1.0)
                            tt(P[:, 0:256], P[:, 0:256], bP[0:64, 0:256], ALU.add, [Pk, bPk], [Pk])
                            cur, curk = Xj, Xjk
                            if j < 5:
                                curT, curTk = XjT, XjTk
                        bW, bWk = nb()
                        for q in range(4):
                            mm(bW[:, q * 64:(q + 1) * 64], kb[:, q * 128:(q + 1) * 128], P[:, q * 64:(q + 1) * 64], [kbk, Pk], [bWk], passes=1.0)
                        wTn, wTnk = M_(3)
                        p.op("act", lambda e, wTn=wTn, bW=bW: e.mul(wTn, bW[:, 0:256], -1.0), reads=[bWk], writes=[wTnk], dur=0.47)
                        bV, bVk = nb()
                        for q in range(4):
                            h = hh * 4 + q
                            mm(bV[0:64, q * 128:(q + 1) * 128], P[:, q * 64:(q + 1) * 64], vb[:, q * 128:(q + 1) * 128], [Pk, vbk], [bVk],
                               start=True, stop=False, passes=1.0)
                            mm(bV[0:64, q * 128:(q + 1) * 128], wTn[:, q * 64:(q + 1) * 64], dstB[:, h * 128:(h + 1) * 128], [wTnk, "dstB"], [bVk],
                               start=False, stop=True, passes=1.0)
                        vnew, vnk = A_(3)
                        acopy(vnew, bV[0:64, :], [bVk], [vnk])
                        bO, bOk = nb()
                        for q in range(4):
                            h = hh * 4 + q
                            mm(bO[0:64, q * 128:(q + 1) * 128], qgT[:, q * 64:(q + 1) * 64], dstB[:, h * 128:(h + 1) * 128], [qgTk, "dstB"], [bOk],
                               start=True, stop=False, passes=1.0)
                            mm(bO[0:64, q * 128:(q + 1) * 128], Xs[:, q * 64:(q + 1) * 64], vnew[:, q * 128:(q + 1) * 128], [Xsk, vnk], [bOk],
                               start=False, stop=True, passes=1.0)
                        sqb, sqk = nb()
                        act(sqb[0:64, :], bO[0:64, :], AF.Square, [bOk], [sqk])
                        p.op("dve", dur=0.65, fn=lambda e, sqb=sqb, fac=fac: e.tensor_reduce(fac[:, 24:28], v3(sqb[0:64, :]), AX.X, ALU.add),
                             reads=[sqk], writes=[fack])
                        rsqrt_to(fac[:, 24:28], fac[:, 24:28], 1.0 / 128, [fack], [fack])
                        otb, otbk = tbb[3], "tbb3"
                        tt(v3(otb[:]), v3(bO[0:64, :]), bc(fac[:, 24:28], [64, 4, 128], 2), ALU.mult, [bOk, fack], [otbk])
                        bR, bRk = nb()
                        for q in range(4):
                            trb(bR[:, q * 64:(q + 1) * 64], otb[:, q * 128:(q + 1) * 128], [otbk], [bRk])
                        stt(yB[:, hh * 4:(hh + 1) * 4, cc], bR[:, 0:256].rearrange("p (a b) -> p a b", b=64),
                            pf[:, l, PF_DNORM:PF_DNORM + 1], dz[:, hh * 4:(hh + 1) * 4, cc], ALU.mult, ALU.mult,
                            [bRk, "pf", "dnT3"], ["yB"])
                        bU, bUk = nb()
                        for q in range(4):
                            mm(bU[:, q * 128:(q + 1) * 128], kd[:, q * 128:(q + 1) * 128], vnew[:, q * 128:(q + 1) * 128], [kdk, vnk], [bUk], passes=1.0)
                        dh = dst[:, hh * 512:(hh + 1) * 512]
                        tt(v3(dh), v3(dh), bc(elast[:, hs], [128, 4, 128], 2), ALU.mult, ["dst", "elast"], ["dst"], eng="pool")
                        tt(dh, dh, bU[:, :], ALU.add, ["dst", bUk], ["dst"])
                        acopy(dstB[:, hh * 512:(hh + 1) * 512], dh, ["dst"], ["dstB"])
                state["hh"] = 0
                state["bankset"] = None
                if l == 0 and ti == 0:
                    acopy(yT[:, :, 0:NT], yB[:, :, 0:NT], ["yB"], ["yT"])
                    dump("ydn", yT[:, :, 0:NT], ["yT"])
                branch(2)
                if l == 0 and ti == 0:
                    acopy(merged[:, :, 0:NT], mergedB[:, :, 0:NT], ["mergedB"], ["merged"])
                    dump("merged", merged[:, :, 0:NT], ["merged"])
                p.cur_tag = "S8.%d.%d" % (l, ti)
                bN, bNk = bank(7)
                for half in range(2):
                    ws, wk = wload(l, 33 + half)
                    for q in range(4):
                        d = half * 4 + q
                        bO, bOk = nb()
                        proj(bO, bOk, ws, wk, q * 128, NT, rhs=mergedB, rkey="mergedB")
                        acopy(yT[:, d, 0:NT], bO[:, 0:NT], [bOk], ["yT"])
                        i = rot("sqs", 2)
                        act(sqs[i][:, 0:NT], bO[:, 0:NT], AF.Square, [bOk], ["sqs%d" % i])
                        mm(bN[:, 0:NT], ones, sqs[i][:, 0:NT], ["cst", "sqs%d" % i], [bNk], start=(d == 0), stop=(d == KC - 1))
                rsqrt_to(rsb[:, 0, 0:NT], bN[:, 0:NT], 1.0 / D, [bNk], ["rsb"])
                for d in range(KC):
                    j = rot("mtmp", 2)
                    stt(mtmp[j][:, 0:NT], yT[:, d, 0:NT], pf[:, l, PF_NPOST + d:PF_NPOST + d + 1], rsb[:, 0, 0:NT], ALU.mult, ALU.mult,
                        ["yT", "pf", "rsb"], ["mtmp%d" % j])
                    tt(hT[:, d, cols], hT[:, d, cols], mtmp[j][:, 0:NT], ALU.add, [hk, "mtmp%d" % j], [hk], eng="pool")
                if l == 0 and ti == 0:
                    dump("h1", hT[:, :, cols], [hk])

        for r0 in range(0, S, 128):
            nr = min(128, S - r0)
            col0 = NMETA + r0
            i = rot("xstg", 1)
            skey = "merged"
            stg2 = xstg[i]
            for half in range(2):
                bk, bkey = nb()
                for q in range(4):
                    k = half * 4 + q
                    tr(bk[0:nr, q * 128:(q + 1) * 128], hT[:, k, col0:col0 + nr], hkeys(col0, col0 + nr), [bkey])
                acopy(stg2[0:nr, half * 512:(half + 1) * 512], bk[0:nr, :], [bkey], [skey])
            dma(out_d[r0:r0 + nr, :], stg2[0:nr, :], [skey], [], "och")
        p.finalize(st)
    return nc, p


def make_consts():
    c = np.zeros((128, 384), np.float32)
    c[:, 0:128] = np.eye(128, dtype=np.float32)
    c[:, 128:256] = 1.0
    a = np.arange(64)
    c[0:64, 256:320] = (a[:, None] <= a[None, :]).astype(np.float32)
    c[0:64, 320:384] = (a[:, None] > a[None, :]).astype(np.float32)
    return c


def pack_weights(inp):
    L = inp["w_in"].shape[0]
    w_in = np.asarray(inp["w_in"], np.float32)
    cols = []
    cols += list(range(O_Z, O_Z + 1024))
    cols += list(range(O_XBC, O_XBC + 1536))
    for fc in range(8):
        for base in (O_SCB, O_SCC, O_SCH, O_SCG):
            cols += list(range(base + fc * 128, base + (fc + 1) * 128))
    for h in range(8):
        for base in (O_Q, O_K, O_V, O_DZ):
            cols += list(range(base + h * 128, base + (h + 1) * 128))
    cols += list(range(O_GATE, O_GATE + 3072))
    cols = np.asarray(cols)
    assert cols.size == NW1
    w1 = np.ascontiguousarray(w_in[:, :, cols])
    wsm = np.ascontiguousarray(np.concatenate([w_in[:, :, O_DT:O_DT + 16], w_in[:, :, O_DB:O_DB + 16]], axis=-1))
    pf = np.zeros((128, L, NPF), np.float32)
    fm = lambda v, n: np.asarray(v, np.float32).reshape(n, 128).T
    for l in range(L):
        pf[:, l, PF_NPRE:PF_NPRE + 8] = fm(inp["norm_pre"][l], 8)
        pf[:, l, PF_NPOST:PF_NPOST + 8] = fm(inp["norm_post"][l], 8)
        scw = np.asarray(inp["ssd_conv_w"][l], np.float32)
        pf[:, l, PF_SCW:PF_SCW + 48] = scw.reshape(4, 12, 128).transpose(2, 1, 0).reshape(128, 48)
        pf[:, l, PF_SCB:PF_SCB + 12] = fm(inp["ssd_conv_b"][l], 12)
        pf[:, l, PF_SD:PF_SD + 8] = fm(np.repeat(np.asarray(inp["ssd_d"][l], np.float32), 64), 8)
        pf[:, l, PF_SNORM:PF_SNORM + 8] = fm(inp["ssd_norm"][l], 8)
        ccw = np.asarray(inp["sc_conv_w"][l], np.float32)
        pf[:, l, PF_CCW:PF_CCW + 24] = ccw.reshape(3, 8, 128).transpose(2, 1, 0).reshape(128, 24)
        dcw = np.asarray(inp["dn_conv_w"][l], np.float32)
        pf[:, l, PF_DCW:PF_DCW + 96] = dcw.reshape(4, 24, 128).transpose(2, 1, 0).reshape(128, 96)
        pf[:, l, PF_DNORM] = np.asarray(inp["dn_norm"][l], np.float32)
    pb = np.zeros((128, L, 48), np.float32)
    for l in range(L):
        row = np.concatenate([inp["ssd_dt_bias"][l], inp["ssd_a_log"][l], inp["dn_dt_bias"][l], inp["dn_a_log"][l]]).astype(np.float32)
        pb[:, l, :] = np.broadcast_to(row[None, :], (128, 48))
    return dict(w1=w1, wsm=wsm, wb=np.ascontiguousarray(inp["w_branch"], np.float32), wo=np.ascontiguousarray(inp["w_out"], np.float32),
                pf=pf, pb=pb, cst=make_consts(), meta=np.ascontiguousarray(inp["meta_tokens"], np.float32))


_CACHE = {}


def kernel(**inputs):
    x = np.asarray(inputs["x"], np.float32)
    B, S, _ = x.shape
    L = inputs["w_in"].shape[0]
    shared = pack_weights(inputs)
    key = (S, L)
    if key not in _CACHE:
        _CACHE[key] = build_program(S, L)[0]
    nc = _CACHE[key]
    in_maps = []
    for b in range(B):
        m = dict(shared)
        m["x"] = np.ascontiguousarray(x[b])
        in_maps.append(m)
    res = run_bass_kernel_spmd(nc, in_maps, core_ids=list(range(B)))
    return np.stack([np.asarray(r["out"], np.float32) for r in res.results], axis=0)
```

```python
import contextlib
import numpy as np
import concourse.bass as bass
import concourse.mybir as mybir

F32 = mybir.dt.float32
F32R = mybir.dt.float32r
BF16 = mybir.dt.bfloat16
ALU = mybir.AluOpType
AF = mybir.ActivationFunctionType
AX = mybir.AxisListType


class Op:
    __slots__ = ("eng", "fn", "deps", "chan", "needs_inc", "event", "idx", "dur", "succ", "nd", "ready", "fin", "pos", "lat", "tag", "aset")

    def __init__(self, eng, fn, deps, chan, dur):
        self.eng = eng
        self.fn = fn
        self.deps = deps
        self.chan = chan
        self.needs_inc = chan is not None
        self.event = None
        self.dur = dur
        self.succ = []
        self.ready = 0.0
        self.fin = 0.0


class Prog:
    ENGS = ("pe", "act", "dve", "pool", "sp")

    def __init__(self, nc, same_eng_sync=True, schedule=True, window=4000):
        self.nc = nc
        self.ops = []
        self.last_w = {}
        self.readers = {}
        self.same_eng_sync = same_eng_sync
        self.schedule = schedule
        self.window = window

    def op(self, eng, fn, reads=(), writes=(), chan=None, dur=0.1):
        deps = []
        for k in reads:
            w = self.last_w.get(k)
            if w is not None:
                deps.append(w)
        for k in writes:
            w = self.last_w.get(k)
            if w is not None:
                deps.append(w)
            deps.extend(self.readers.get(k, ()))
        o = Op(eng, fn, deps, chan, dur)
        o.tag = getattr(self, "cur_tag", "")
        o.aset = None
        o.idx = len(self.ops)
        self.ops.append(o)
        for k in reads:
            self.readers.setdefault(k, []).append(o)
        for k in writes:
            self.last_w[k] = o
            self.readers[k] = []
        return o

    def _sem_edge(self, d, o):
        if d.chan is None and o.chan is None and d.eng == o.eng:
            if o.eng == "pe" or not self.same_eng_sync:
                return False
        return True

    def _list_schedule(self):
        import heapq
        ops = self.ops
        for o in ops:
            ds = []
            seen = set()
            for d in o.deps:
                if d is o or id(d) in seen:
                    continue
                seen.add(id(d))
                ds.append(d)
            o.deps = ds
            o.nd = len(ds)
            for d in ds:
                d.succ.append(o)
        SEM_LAT = 0.12
        cp = [0.0] * len(ops)
        for o in reversed(ops):
            m = 0.0
            for s_ in o.succ:
                if cp[s_.idx] > m:
                    m = cp[s_.idx]
            cp[o.idx] = m + (o.dur + (2.0 if o.chan is not None else 0.0)) + 0.1
        self.cp_len = max(cp) if cp else 0.0
        free = {e: 0.0 for e in self.ENGS}
        pend = {e: [] for e in self.ENGS}
        avail = {e: [] for e in self.ENGS}
        order = {e: [] for e in self.ENGS}
        dma_pipe = 0.0
        for o in ops:
            if o.nd == 0:
                heapq.heappush(pend[o.eng], (0.0, o.idx))
        nleft = len(ops)
        lo = 0
        done = [False] * (len(ops) + 1)
        W = self.window
        TBL = 1.3
        cur_set = [None]
        cand_l = {e: [] for e in self.ENGS}
        for e in self.ENGS:
            while pend[e]:
                cand_l[e].append(heapq.heappop(pend[e])[1])
        while nleft:
            while done[lo]:
                lo += 1
            best = None
            for e in self.ENGS:
                t = free[e]
                bc_ = None
                for idx in cand_l[e]:
                    if idx >= lo + W:
                        continue
                    stt_ = max(t, ops[idx].ready)
                    if e == "act":
                        as_ = ops[idx].aset
                        if as_ is not None and as_ != cur_set[0]:
                            stt_ += TBL
                    key_ = (stt_, -cp[idx], idx)
                    if bc_ is None or key_ < bc_:
                        bc_ = key_
                if bc_ is None:
                    continue
                if best is None or bc_ < best[0]:
                    best = (bc_, e)
            st, idx, e = best[0][0], best[0][2], best[1]
            cand_l[e].remove(idx)
            o = ops[idx]
            if e == "act" and o.aset is not None:
                cur_set[0] = o.aset
            if o.chan is not None:
                dma_pipe = max(dma_pipe, st) + o.dur
                o.fin = dma_pipe + 2.0
                free[e] = st + 0.06
            else:
                o.fin = st + o.dur
                free[e] = o.fin
            o.pos = len(order[e])
            order[e].append(o)
            done[idx] = True
            nleft -= 1
            for s in o.succ:
                lat = SEM_LAT if self._sem_edge(o, s) else 0.0
                if o.fin + lat > s.ready:
                    s.ready = o.fin + lat
                s.nd -= 1
                if s.nd == 0:
                    cand_l[s.eng].append(s.idx)
        self.sim_time = max(o.fin for o in ops)
        return order

    def finalize(self, stack):
        nc = self.nc
        if self.schedule:
            order = self._list_schedule()
        else:
            order = {e: [] for e in self.ENGS}
            for o in self.ops:
                seen = set()
                ds = []
                for d in o.deps:
                    if d is o or id(d) in seen:
                        continue
                    seen.add(id(d))
                    ds.append(d)
                o.deps = ds
                o.pos = len(order[o.eng])
                order[o.eng].append(o)
        for o in self.ops:
            o.deps = [d for d in o.deps if self._sem_edge(d, o)]
        sems = {}
        cnt = {}

        def getsem(name):
            if name not in sems:
                sems[name] = stack.enter_context(nc.semaphore(name))
                cnt[name] = 0
            return sems[name]

        chan_pos = {}
        for e in self.ENGS:
            for o in order[e]:
                if o.chan is not None:
                    chan_pos[o.chan] = chan_pos.get(o.chan, 0) + 1
                    o.lat = chan_pos[o.chan]
        per = {}
        nw = 0
        for e in self.ENGS:
            wpos = {}
            lst = []
            for o in order[e]:
                best = {}
                for d in o.deps:
                    if d.chan is not None:
                        st_, ps_ = d.chan, d.lat
                    else:
                        st_, ps_ = "e_" + d.eng, d.pos
                    if wpos.get(st_, -1) >= ps_:
                        continue
                    if st_ not in best or best[st_][0] < ps_:
                        best[st_] = (ps_, d)
                need = []
                for st_, (ps_, d) in best.items():
                    wpos[st_] = ps_
                    d.needs_inc = True
                    need.append(d)
                nw += len(need)
                lst.append((o, need))
            per[e] = lst
        for e in self.ENGS:
            for o in order[e]:
                if o.chan is not None:
                    getsem(o.chan)
                    cnt[o.chan] += 16
                    o.event = (o.chan, cnt[o.chan])
                elif o.needs_inc:
                    nm = "e_" + o.eng
                    getsem(nm)
                    cnt[nm] += 1
                    o.event = (nm, cnt[nm])
        self.nwaits = nw
        self.sem_max = dict(cnt)
        assert max(cnt.values()) < 30000, cnt

        def emit(eng_obj, lst):
            for o, need in lst:
                for d in need:
                    eng_obj.wait_ge(sems[d.event[0]], d.event[1])
                ins = o.fn(eng_obj)
                if o.event is not None:
                    if o.chan is not None:
                        ins.then_inc(sems[o.chan], 16)
                    else:
                        ins.then_inc(sems[o.event[0]], 1)

        with nc.Block() as block:
            @block.tensor
            def _(e):
                emit(e, per["pe"])

            @block.scalar
            def _(e):
                emit(e, per["act"])

            @block.vector
            def _(e):
                emit(e, per["dve"])

            @block.gpsimd
            def _(e):
                emit(e, per["pool"])

            @block.sync
            def _(e):
                emit(e, per["sp"])
                for s, v in cnt.items():
                    if v > 0:
                        e.wait_ge(sems[s], v)


from concourse.bass_utils import run_bass_kernel_spmd

D = 1024
KC = 8
NMETA = 16
EPS = 1e-6
PF_NPRE, PF_NPOST, PF_SCW, PF_SCB, PF_SD, PF_SNORM, PF_CCW, PF_DCW, PF_DNORM, NPF = 0, 8, 16, 64, 76, 84, 92, 116, 212, 213
O_Z, O_XBC, O_DT, O_SCB, O_SCC, O_SCH, O_SCG, O_Q, O_K, O_V, O_DZ, O_DB, O_DA, O_GATE = (
    0, 1024, 2560, 2576, 3600, 4624, 5648, 6672, 7696, 8720, 9744, 10768, 10776, 10784)
NW1 = 13824


def build_program(S, L, NTM=256, dbg=None, same_eng_sync=True, schedule=True, window=100000):
    TOK = NMETA + S
    TP = ((TOK + 63) // 64) * 64
    tiles = []
    c = 0
    while c < TP:
        n = min(NTM, TP - c)
        tiles.append((c, n))
        c += n
    nc = bass.Bass("TRN2", target_bir_lowering=False)
    dt_in = lambda n, s: nc.dram_tensor(n, s, F32, kind="ExternalInput").ap()
    x_d = dt_in("x", [S, D])
    meta_d = dt_in("meta", [NMETA, D])
    w1_d = dt_in("w1", [L, D, NW1])
    wsm_d = dt_in("wsm", [L, D, 32])
    wb_d = dt_in("wb", [L, 3, D, D])
    wo_d = dt_in("wo", [L, D, D])
    pf_d = dt_in("pf", [128, L, NPF])
    pb_d = dt_in("pb", [128, L, 48])
    cst_d = dt_in("cst", [128, 384])
    out_d = nc.dram_tensor("out", [S, D], F32, kind="ExternalOutput").ap()
    NBLK = 35
    wbf_d = nc.dram_tensor("wbf", [L, NBLK, 128, KC * 512], BF16, kind="Internal").ap()
    dbg_d = {}
    if dbg:
        for name, shp in dbg.items():
            dbg_d[name] = nc.dram_tensor("dbg_" + name, list(shp), F32, kind="ExternalOutput").ap()

    with contextlib.ExitStack() as st:
        def sb(n, s):
            return st.enter_context(nc.sbuf_tensor("s_" + n, list(s), F32))
        hT = sb("hT", [128, KC, TP])
        NWS = 3
        sbb = lambda n, shp: st.enter_context(nc.sbuf_tensor("s_" + n, list(shp), BF16))
        wsl = [sbb("wsl%d" % i, [128, KC, 512]) for i in range(NWS)]
        xnT = sbb("xnT", [128, KC, NTM])
        yB = sbb("yB", [128, KC, NTM])
        mergedB = sbb("mergedB", [128, KC, NTM])
        bcT = sb("bcT", [128, 4, NTM])
        yT = sb("yT", [128, KC, NTM])
        merged = sb("merged", [128, KC, NTM])
        xstg = [merged[:].rearrange("p a b -> p (a b)")[:, 0:1024]]
        dq = sbb("dq", [128, 3, 8, NTM])
        dz = sbb("dz", [128, 8, NTM])
        xsT = dq[:, 0]
        siluz = dz[:]
        cstB = sbb("cstB", [128, 384])
        NTMB = 16
        fpool = [sb("fp%d" % i, [64, 512]) for i in range(6)]
        BA = [[sbb("ba%d_%d" % (h_, i), [64, 512]) for i in range(4)] for h_ in range(2)]
        BD = [[sbb("bd%d_%d" % (h_, i), [64, 256]) for i in range(8)] for h_ in range(2)]
        BFm = [[sbb("bm%d_%d" % (h_, i), [128, 256]) for i in range(4)] for h_ in range(2)]
        tbb = [sbb("tbb%d" % i, [64, 512]) for i in range(4)]
        sst = sb("sst", [128, 1024])
        dst = sb("dst", [128, 1024])
        dstB = sbb("dstB", [128, 1024])
        pre = [sb("pre%d" % i, [128, NTM + 3]) for i in range(4)]
        cacc = [sb("cacc%d" % i, [128, NTM]) for i in range(3)]
        sqs = [sb("sqs%d" % i, [128, NTM]) for i in range(2)]
        rsb = sb("rsb", [128, 2, NTM])
        sig = [sb("sig%d" % i, [128, NTM]) for i in range(2)]
        mtmp = [sb("mtmp%d" % i, [128, NTM]) for i in range(2)]
        tails = sb("tails", [128, 44, 3])
        pf = sb("pf", [128, L, NPF])
        pbt = sb("pbt", [128, L, 48])
        negA = sb("negA", [128, L, 24])
        cst = sb("cst", [128, 384])
        wsm = sbb("wsm", [128, L, KC, 32])
        tsm = sb("tsm", [64, NTM // 64, 32])
        tk = sb("tk", [64, NTM // 64, 48])
        smE = sb("smE", [64, 32])
        elast = sb("elast", [128, 16])
        facs = [sb("fac%d" % i, [64, 32]) for i in range(2)]
        ps = st.enter_context(nc.psum_tensor("ps", [128, 4096], F32))

        ident = cst[:, 0:128]
        ones = cst[:, 128:256]
        U = cst[0:64, 256:320]
        G = cst[0:64, 320:384]
        I64 = cst[0:64, 0:64]

        p = Prog(nc, same_eng_sync=same_eng_sync, schedule=schedule, window=window)
        state = {"rr": 0, "tm": 0, "ws": 0, "i2": {}}

        def bank(i):
            return ps[:, i * 512:(i + 1) * 512], "b%d" % i

        def nb():
            bs_ = state.get("bankset")
            if bs_ is None:
                i = state["rr"]
                state["rr"] = (i + 1) % 6
                return bank(i)
            j = state["i2"].get(("bs", bs_), 0)
            state["i2"][("bs", bs_)] = (j + 1) % len(bs_)
            return bank(bs_[j])

        def rot(name, n):
            i = state["i2"].get(name, 0)
            state["i2"][name] = (i + 1) % n
            return i

        def Bv(t, n):
            return t[:].bitcast(BF16)[:, 0:n]

        def F(i):
            return fpool[i], "fp%d" % i

        def A_(i):
            h_ = state["hh"]
            return BA[h_][i][:], "ba%d_%d" % (h_, i)

        def D_(i):
            h_ = state["hh"]
            return BD[h_][i][:], "bd%d_%d" % (h_, i)

        def M_(i):
            h_ = state["hh"]
            return BFm[h_][i][:], "bm%d_%d" % (h_, i)

        def fsz(ap):
            n = 1
            for d in ap.shape[1:]:
                n *= int(d)
            return n

        PASSES = 4.0

        def mm(out, lhsT, rhs, r, w, start=True, stop=True, passes=PASSES):
            p.op("pe", lambda e: e.matmul(out, lhsT, rhs, start=start, stop=stop), reads=r, writes=w,
                 dur=max(fsz(rhs), 64) * passes / 2400.0 + 0.03)

        def tr(out, in_, r, w):
            n = in_.shape[0]
            p.op("pe", lambda e: e.transpose(out, in_, ident[0:n, 0:n]), reads=r + ["cst"], writes=w, dur=0.09)

        def trb(out, in_, r, w):
            n = in_.shape[0]
            p.op("pe", lambda e: e.matmul(out, in_, cstB[0:n, 0:n], start=True, stop=True), reads=r + ["cstB"], writes=w,
                 dur=max(n, 64) / 2400.0 + 0.03)

        def act(out, in_, func, r, w, bias=None, scale=None):
            kw = {}
            if bias is not None:
                kw["bias"] = bias
            if scale is not None:
                kw["scale"] = scale
            o_ = p.op("act", lambda e: e.activation(out, in_, func, **kw), reads=r, writes=w, dur=0.2 + fsz(out) / 960.0)
            o_.aset = ASET.get(func)

        ASET = {AF.Silu: "silu", AF.Sigmoid: "sig", AF.Exp: "el", AF.Ln: "el", AF.Sqrt: "sqrt"}

        def acopy(out, in_, r, w):
            p.op("act", lambda e: e.copy(out, in_), reads=r, writes=w, dur=0.2 + fsz(out) / 960.0)

        def edur(eng, n):
            return (0.12 + n / 960.0) if eng == "dve" else (0.2 + n / 400.0)

        def tt(out, a, b, op, r, w, eng="dve"):
            p.op(eng, lambda e: e.tensor_tensor(out, a, b, op), reads=r, writes=w, dur=edur(eng, fsz(out)))

        def ts(out, a, s1, op0, r, w, s2=None, op1=None, eng="dve"):
            if op1 is None:
                p.op(eng, lambda e: e.tensor_scalar(out, a, s1, None, op0), reads=r, writes=w, dur=edur(eng, fsz(out)))
            else:
                p.op(eng, lambda e: e.tensor_scalar(out, a, s1, s2, op0, op1), reads=r, writes=w, dur=edur(eng, fsz(out)))

        def stt(out, in0, scalar, in1, op0, op1, r, w):
            p.op("dve", lambda e: e.scalar_tensor_tensor(out, in0, scalar, in1, op0, op1), reads=r, writes=w,
                 dur=edur("dve", fsz(out)))

        def recip(out, in_, r, w):
            p.op("dve", lambda e: e.reciprocal(out, in_), reads=r, writes=w, dur=edur("dve", fsz(out)))

        def vcopy(out, in_, r, w, eng="dve"):
            p.op(eng, lambda e: e.tensor_copy(out, in_), reads=r, writes=w, dur=edur(eng, fsz(out)))

        def memset(ap, val, w, eng="pool"):
            p.op(eng, lambda e: e.memset(ap, val), writes=w, dur=edur(eng, fsz(ap)))

        def dma(out, in_, r, w, chan, eng="sp"):
            nbytes = (2 if out.dtype == BF16 else 4) * fsz(out) * int(out.shape[0]) * (3 if eng == "pool" else 1)
            p.op(eng, lambda e: e.dma_start(out=out, in_=in_), reads=r, writes=w, chan=chan, dur=nbytes / 250e3)

        def dump(name, ap, keys):
            if name in dbg_d:
                dma(dbg_d[name], ap, keys, [], "dbg")

        def rsqrt_to(out, in_, scale, r, w):
            act(out, in_, AF.Ln, r, w, bias=EPS, scale=scale)
            act(out, out, AF.Exp, w, w, scale=-0.5)

        dma(cst[:], cst_d, [], ["cst"], "ld_cst")
        dma(pf[:], pf_d, [], ["pf"], "ld_pf")
        dma(pbt[:], pb_d, [], ["pbt"], "ld_pb")
        for l in range(L):
            dma(wsm[:, l], wsm_d[l].rearrange("(k p) c -> p k c", p=128), [], ["wsm%d" % l], "ld_wsm%d" % l, eng="pool")
        acopy(cstB[:], cst[:], ["cst"], ["cstB"])
        for l in range(L):
            act(negA[:, l, 0:16], pbt[:, l, 16:32], AF.Exp, ["pbt"], ["negA"])
            act(negA[:, l, 16:24], pbt[:, l, 40:48], AF.Exp, ["pbt"], ["negA"])
        ts(negA[:], negA[:], -1.0, ALU.mult, ["negA"], ["negA"])

        hkey = lambda t: "hT%d" % t

        def tile_of(col):
            for ti, (c0, n) in enumerate(tiles):
                if c0 <= col < c0 + n:
                    return ti
            raise ValueError

        def hkeys(c_lo, c_hi):
            return sorted({hkey(tile_of(c)) for c in (c_lo, c_hi - 1)} | {hkey(t) for t in range(tile_of(c_lo), tile_of(c_hi - 1) + 1)})

        if TP > TOK:
            memset(hT[:, :, TOK:TP], 0.0, hkeys(TOK, TP))
        row_blocks = [("meta", 0, NMETA, 0)] + [("x", r0, min(128, S - r0), NMETA + r0) for r0 in range(0, S, 128)]
        for (src, r0, nr, col0) in row_blocks:
            si = rot("xstg", 1)
            skey = "merged"
            stg2 = xstg[si]
            srcap = meta_d[0:nr, :] if src == "meta" else x_d[r0:r0 + nr, :]
            dma(stg2[0:nr, :], srcap, [], [skey], "xch%d" % si)
            for half in range(2):
                bk, bkey = nb()
                for q in range(4):
                    k = half * 4 + q
                    tr(bk[:, q * 128:q * 128 + nr], stg2[0:nr, k * 128:(k + 1) * 128], [skey], [bkey])
                acopy(hT[:, half * 4:half * 4 + 4, col0:col0 + nr],
                      bk.rearrange("p (a b) -> p a b", b=128)[:, :, 0:nr], [bkey], hkeys(col0, col0 + nr))

        def blk_src(l, blk):
            if blk < 27:
                return w1_d[l][:, blk * 512:(blk + 1) * 512]
            if blk < 33:
                n, half = divmod(blk - 27, 2)
                return wb_d[l, n][:, half * 512:(half + 1) * 512]
            return wo_d[l][:, (blk - 33) * 512:(blk - 32) * 512]

        use_order = [0, 1, 2, 3, 4, 21, 27, 22, 28] + list(range(5, 13)) + [23, 29, 24, 30] + list(range(13, 21)) + [25, 31, 26, 32, 33, 34]
        assert sorted(use_order) == list(range(NBLK))
        GRP = 9
        grp_of = {}
        for l in range(L):
            for gi in range(0, NBLK, GRP):
                grp = use_order[gi:gi + GRP]
                chn = "cv%d_%d" % (l, gi // GRP)
                for blk in grp:
                    dma(wbf_d[l, blk].rearrange("p (k c) -> p k c", k=KC), blk_src(l, blk).rearrange("(k p) c -> p k c", p=128),
                        [], ["wbf%d_%d" % (l, blk)], chn, eng="pool")
                for blk in grp:
                    grp_of[(l, blk)] = grp

        def wload(l, blk):
            si = state["ws"] % NWS
            state["ws"] += 1
            key = "wsl%d" % si
            dma(wsl[si][:], wbf_d[l, blk].rearrange("p (k c) -> p k c", k=KC), ["wbf%d_%d" % (l, b_) for b_ in grp_of[(l, blk)]], [key],
                "wch%d" % si)
            return wsl[si], key

        def proj(out_bank, okey, wslot, wkey, coff, NT, rhs=None, rkey="xnT"):
            rhs = xnT if rhs is None else rhs
            for k in range(KC):
                mm(out_bank[:, 0:NT], wslot[:, k, coff:coff + 128], rhs[:, k, 0:NT], [wkey, rkey], [okey],
                   start=(k == 0), stop=(k == KC - 1), passes=1.0)

        def conv(bk, bkey, ti, wcol0, K, l, NT, dest, dkey, bias_col=None, mul=None, mulkey=None):
            H = K - 1
            i = rot("pre", 4)
            pr, pk = pre[i], "pre%d" % i
            ca, ck = cacc[i % 3], "cacc%d" % (i % 3)
            vcopy(pr[:, 0:H], tails[:, ti, 0:H], ["tails%d" % ti], [pk], eng="pool")
            if mul is None:
                acopy(pr[:, H:H + NT], bk[:, 0:NT], [bkey], [pk])
            else:
                tt(pr[:, H:H + NT], bk[:, 0:NT], mul, ALU.mult, [bkey, mulkey], [pk])
            vcopy(tails[:, ti, 0:H], pr[:, NT:NT + H], [pk], ["tails%d" % ti], eng="pool")
            wl = pf[:, l, wcol0 + K - 1:wcol0 + K]
            if mul is None:
                act(ca[:, 0:NT], bk[:, 0:NT], AF.Identity, [bkey, "pf"], [ck], scale=wl)
            else:
                act(ca[:, 0:NT], pr[:, H:H + NT], AF.Identity, [pk, "pf"], [ck], scale=wl)
            for k in range(0, K - 1):
                last = (k == K - 2 and K == 3)
                o = dest if last else ca[:, 0:NT]
                ok = [dkey] if last else [ck]
                stt(o, pr[:, k:k + NT], pf[:, l, wcol0 + k:wcol0 + k + 1], ca[:, 0:NT], ALU.mult, ALU.add,
                    [pk, "pf", ck], ok)
            if K == 4:
                if bias_col is not None:
                    act(dest, ca[:, 0:NT], AF.Silu, [ck, "pf"], [dkey], bias=pf[:, l, bias_col:bias_col + 1])
                else:
                    act(dest, ca[:, 0:NT], AF.Silu, [ck], [dkey])

        bc = lambda ap, shape, axis: ap.unsqueeze(axis).to_broadcast(list(shape))

        for l in range(L):
            memset(sst[:], 0.0, ["sst"])
            memset(dst[:], 0.0, ["dst"])
            memset(dstB[:], 0.0, ["dstB"])
            memset(tails[:], 0.0, ["tails%d" % i for i in range(44)])
            w1 = w1_d[l]
            for ti, (c0, NT) in enumerate(tiles):
                nch = NT // 64
                hk = hkey(ti)
                cols = slice(c0, c0 + NT)
                p.cur_tag = "S0.%d.%d" % (l, ti)
                bN, bNk = bank(7)
                for k in range(KC):
                    i = rot("sqs", 2)
                    act(sqs[i][:, 0:NT], hT[:, k, cols], AF.Square, [hk], ["sqs%d" % i])
                    mm(bN[:, 0:NT], ones, sqs[i][:, 0:NT], ["cst", "sqs%d" % i], [bNk], start=(k == 0), stop=(k == KC - 1))
                rsqrt_to(rsb[:, 0, 0:NT], bN[:, 0:NT], 1.0 / D, [bNk], ["rsb"])
                for k in range(KC):
                    stt(xnT[:, k, 0:NT], hT[:, k, cols], pf[:, l, PF_NPRE + k:PF_NPRE + k + 1], rsb[:, 0, 0:NT],
                        ALU.mult, ALU.mult, [hk, "pf", "rsb"], ["xnT"])
                if l == 0 and ti == 0 and "xnT" in dbg_d:
                    acopy(merged[:, :, 0:NT], xnT[:, :, 0:NT], ["xnT"], ["merged"])
                    dump("xnT", merged[:, :, 0:NT], ["merged"])
                p.cur_tag = "S1.%d.%d" % (l, ti)
                bS, bSk = bank(6)
                for c in range(nch):
                    for k in range(KC):
                        mm(bS[0:64, c * 32:(c + 1) * 32], xnT[:, k, c * 64:(c + 1) * 64], wsm[:, l, k, :], ["xnT", "wsm%d" % l], [bSk],
                           start=(k == 0), stop=(k == KC - 1), passes=1.0)
                acopy(tsm[:, 0:nch, :], bS[0:64, 0:nch * 32].rearrange("p (c f) -> p c f", f=32), [bSk], ["tsm"])
                tt(tk[:, 0:nch, 0:16], tsm[:, 0:nch, 0:16], bc(pbt[0:64, l, 0:16], [64, nch, 16], 1), ALU.add, ["tsm", "pbt"], ["tk"])
                tt(tk[:, 0:nch, 40:48], tsm[:, 0:nch, 24:32], bc(pbt[0:64, l, 32:40], [64, nch, 8], 1), ALU.add, ["tsm", "pbt"], ["tk"])
                act(tk[:, 0:nch, 0:16], tk[:, 0:nch, 0:16], AF.Exp, ["tk"], ["tk"])
                act(tk[:, 0:nch, 40:48], tk[:, 0:nch, 40:48], AF.Exp, ["tk"], ["tk"])
                act(tk[:, 0:nch, 0:16], tk[:, 0:nch, 0:16], AF.Ln, ["tk"], ["tk"], bias=1.0)
                act(tk[:, 0:nch, 40:48], tk[:, 0:nch, 40:48], AF.Ln, ["tk"], ["tk"], bias=1.0)
                act(tk[:, 0:nch, 32:40], tsm[:, 0:nch, 16:24], AF.Sigmoid, ["tsm"], ["tk"])
                tt(tk[:, 0:nch, 16:32], tk[:, 0:nch, 0:16], bc(negA[0:64, l, 0:16], [64, nch, 16], 1), ALU.mult, ["tk", "negA"], ["tk"])
                tt(tk[:, 0:nch, 40:48], tk[:, 0:nch, 40:48], bc(negA[0:64, l, 16:24], [64, nch, 8], 1), ALU.mult, ["tk", "negA"], ["tk"])
                if l == 0 and ti == 0:
                    dump("tk", tk[:, 0:nch, :], ["tk"])
                p.cur_tag = "S2.%d.%d" % (l, ti)
                for blk in range(2):
                    ws, wk = wload(l, blk)
                    for q in range(4):
                        fc = blk * 4 + q
                        bk, bkey = nb()
                        proj(bk, bkey, ws, wk, q * 128, NT)
                        act(siluz[:, fc, 0:NT], bk[:, 0:NT], AF.Silu, [bkey], ["dnT3"])
                for blk in range(3):
                    ws, wk = wload(l, 2 + blk)
                    for q in range(4):
                        fc = blk * 4 + q
                        bk, bkey = nb()
                        proj(bk, bkey, ws, wk, q * 128, NT)
                        if fc < 8:
                            dest, dkey = xsT[:, fc, 0:NT], "dnT0"
                        else:
                            dest, dkey = bcT[:, fc - 8, 0:NT], "bcT"
                        conv(bk, bkey, fc, PF_SCW + fc * 4, 4, l, NT, dest, dkey, bias_col=PF_SCB + fc)
                if l == 0 and ti == 0:
                    pass
                    dump("bcT", bcT[:, :, 0:NT], ["bcT"])
                p.cur_tag = "S3.%d.%d" % (l, ti)
                for c in range(nch):
                    cc = slice(c * 64, (c + 1) * 64)
                    dt_c = tk[:, c, 0:16]
                    dtA_c = tk[:, c, 16:32]
                    bM, bMk = bank(6)
                    mm(bM[0:64, 0:16], U, dtA_c, ["cst", "tk"], [bMk])
                    mm(bM[0:64, 16:32], G, dtA_c, ["cst", "tk"], [bMk])
                    mm(bM[0:128, 32:48], ones[0:64, :], dtA_c, ["cst", "tk"], [bMk])
                    act(smE[:, 0:32], bM[0:64, 0:32], AF.Exp, [bMk], ["smE"])
                    act(elast[:, 0:16], bM[:, 32:48], AF.Exp, [bMk], ["elast"])
                    for g in range(2):
                        hs = slice(g * 8, (g + 1) * 8)
                        state["tm"] = 0
                        bX, bXk = nb()
                        for q in range(4):
                            trb(bX[0:64, q * 128:(q + 1) * 128], xsT[:, g * 4 + q, cc], ["dnT0"], [bXk])
                        bB, bBk = nb()
                        tr(bB[0:64, 0:128], bcT[:, g, cc], ["bcT"], [bBk])
                        xc, xck = Bv(fpool[0], 512), "fp0"
                        tt(xc.rearrange("p (h d) -> p h d", d=64), bX[0:64, :].rearrange("p (h d) -> p h d", d=64),
                           bc(dt_c[:, hs], [64, 8, 64], 2), ALU.mult, [bXk, "tk"], [xck])
                        xd, xdk = Bv(fpool[1], 512), "fp1"
                        tt(xd.rearrange("p (h d) -> p h d", d=64), xc.rearrange("p (h d) -> p h d", d=64),
                           bc(smE[:, 16 + g * 8:16 + (g + 1) * 8], [64, 8, 64], 2), ALU.mult, [xck, "smE"], [xdk], eng="pool")
                        btok, btk = Bv(fpool[2], 128), "fp2"
                        acopy(btok, bB[0:64, 0:128], [bBk], [btk])
                        gm, gmk = F(3)
                        tt(gm[:].rearrange("p (h d) -> p h d", d=64), bc(G, [64, 8, 64], 1), bc(dtA_c[:, hs], [64, 8, 64], 2),
                           ALU.mult, ["cst", "tk"], [gmk], eng="pool")
                        bE, bEk = nb()
                        for h in range(8):
                            mm(bE[0:64, h * 64:(h + 1) * 64], gm[:, h * 64:(h + 1) * 64], U, [gmk, "cst"], [bEk])
                        E, Ek = F(4)
                        act(E[:], bE[0:64, :], AF.Exp, [bEk], [Ek])
                        bC, bCk = nb()
                        mm(bC[0:64, 0:64], bcT[:, g, cc], bcT[:, 2 + g, cc], ["bcT"], [bCk])
                        cbm, cbk = F(2)[0][:, 128:192], "fp2"
                        tt(cbm[:, 0:64], bC[0:64, 0:64], U, ALU.mult, [bCk, "cst"], [cbk])
                        MT, MTk = Bv(fpool[5], 512), "fp5"
                        tt(MT.rearrange("p (h d) -> p h d", d=64), E[:].rearrange("p (h d) -> p h d", d=64),
                           bc(cbm[:, 0:64], [64, 8, 64], 1), ALU.mult, [Ek, cbk], [MTk])
                        bY, bYk = nb()
                        for h in range(8):
                            mm(bY[0:64, h * 64:(h + 1) * 64], MT[:, h * 64:(h + 1) * 64], xc[:, h * 64:(h + 1) * 64], [MTk, xck], [bYk], passes=1.0)
                        bT, bTk = nb()
                        mm(bT[0:64, :], bcT[:, 2 + g, cc], sst[:, g * 512:(g + 1) * 512], ["bcT", "sst"], [bTk])
                        tmp, tmpk = F(4)
                        tt(tmp[:].rearrange("p (h d) -> p h d", d=64), bT[0:64, :].rearrange("p (h d) -> p h d", d=64),
                           bc(smE[:, g * 8:(g + 1) * 8], [64, 8, 64], 2), ALU.mult, [bTk, "smE"], [tmpk])
                        ytok, ytk = tbb[3], "tbb3"
                        tt(ytok[:], bY[0:64, :], tmp[:], ALU.add, [bYk, tmpk], [ytk])
                        bR, bRk = nb()
                        for q in range(4):
                            trb(bR[:, q * 64:(q + 1) * 64], ytok[:, q * 128:(q + 1) * 128], [ytk], [bRk])
                        acopy(yT[:, g * 4:(g + 1) * 4, cc], bR[:, 0:256].rearrange("p (a b) -> p a b", b=64), [bRk], ["yT"])
                        bU, bUk = nb()
                        mm(bU[:, :], btok, xd, [btk, xdk], [bUk], passes=1.0)
                        sg_ = sst[:, g * 512:(g + 1) * 512]
                        tt(sg_.rearrange("p (h d) -> p h d", d=64), sg_.rearrange("p (h d) -> p h d", d=64),
                           bc(elast[:, hs], [128, 8, 64], 2), ALU.mult, ["sst", "elast"], ["sst"], eng="pool")
                        tt(sg_, sg_, bU[:, :], ALU.add, ["sst", bUk], ["sst"])
                p.cur_tag = "S3b.%d.%d" % (l, ti)
                for fc in range(KC):
                    stt(yT[:, fc, 0:NT], xsT[:, fc, 0:NT], pf[:, l, PF_SD + fc:PF_SD + fc + 1], yT[:, fc, 0:NT], ALU.mult, ALU.add,
                        ["dnT0", "pf", "yT"], ["yT"])
                tt(yT[:, :, 0:NT], yT[:, :, 0:NT], siluz[:, :, 0:NT], ALU.mult, ["yT", "dnT3"], ["yT"])
                for g in range(2):
                    bN, bNk = bank(7)
                    for q in range(4):
                        fc = g * 4 + q
                        i = rot("sqs", 2)
                        act(sqs[i][:, 0:NT], yT[:, fc, 0:NT], AF.Square, ["yT"], ["sqs%d" % i])
                        mm(bN[:, 0:NT], ones, sqs[i][:, 0:NT], ["cst", "sqs%d" % i], [bNk], start=(q == 0), stop=(q == 3))
                    rsqrt_to(rsb[:, g, 0:NT], bN[:, 0:NT], 1.0 / 512, [bNk], ["rsb"])
                for fc in range(KC):
                    stt(yB[:, fc, 0:NT], yT[:, fc, 0:NT], pf[:, l, PF_SNORM + fc:PF_SNORM + fc + 1], rsb[:, fc // 4, 0:NT],
                        ALU.mult, ALU.mult, ["yT", "pf", "rsb"], ["yB"])
                if l == 0 and ti == 0:
                    acopy(yT[:, :, 0:NT], yB[:, :, 0:NT], ["yB"], ["yT"])
                    dump("yssd", yT[:, :, 0:NT], ["yT"])

                def branch(n):
                    p.cur_tag = "BR%d.%d.%d" % (n, l, ti)
                    for half in range(2):
                        gs, gk = wload(l, 21 + n * 2 + half)
                        bs, bk_ = wload(l, 27 + n * 2 + half)
                        for q in range(4):
                            d = half * 4 + q
                            bG, bGk = nb()
                            proj(bG, bGk, gs, gk, q * 128, NT)
                            i = rot("sig", 2)
                            act(sig[i][:, 0:NT], bG[:, 0:NT], AF.Sigmoid, [bGk], ["sig%d" % i])
                            bB2, bB2k = nb()
                            proj(bB2, bB2k, bs, bk_, q * 128, NT, rhs=yB, rkey="yB")
                            if n == 0:
                                tt(merged[:, d, 0:NT], bB2[:, 0:NT], sig[i][:, 0:NT], ALU.mult, [bB2k, "sig%d" % i], ["merged"])
                            else:
                                j = rot("mtmp", 2)
                                tt(mtmp[j][:, 0:NT], bB2[:, 0:NT], sig[i][:, 0:NT], ALU.mult, [bB2k, "sig%d" % i], ["mtmp%d" % j])
                                if n == 2:
                                    tt(mergedB[:, d, 0:NT], merged[:, d, 0:NT], mtmp[j][:, 0:NT], ALU.add, ["merged", "mtmp%d" % j], ["mergedB"], eng="pool")
                                else:
                                    tt(merged[:, d, 0:NT], merged[:, d, 0:NT], mtmp[j][:, 0:NT], ALU.add, ["merged", "mtmp%d" % j], ["merged"], eng="pool")

                branch(0)
                p.cur_tag = "S5.%d.%d" % (l, ti)
                for fc in range(KC):
                    ws, wk = wload(l, 5 + fc)
                    bH, bHk = nb()
                    proj(bH, bHk, ws, wk, 256, NT)
                    i = rot("sig", 2)
                    acopy(sig[i][:, 0:NT], bH[:, 0:NT], [bHk], ["sig%d" % i])
                    bC, bCk = nb()
                    proj(bC, bCk, ws, wk, 128, NT)
                    j = rot("mtmp", 2)
                    conv(bC, bCk, 12 + fc, PF_CCW + fc * 3, 3, l, NT, mtmp[j][:, 0:NT], "mtmp%d" % j, mul=sig[i][:, 0:NT], mulkey="sig%d" % i)
                    bB, bBk = nb()
                    proj(bB, bBk, ws, wk, 0, NT)
                    tt(yT[:, fc, 0:NT], bB[:, 0:NT], mtmp[j][:, 0:NT], ALU.mult, [bBk, "mtmp%d" % j], ["yT"])
                    bG, bGk = nb()
                    proj(bG, bGk, ws, wk, 384, NT)
                    i = rot("sig", 2)
                    act(sig[i][:, 0:NT], bG[:, 0:NT], AF.Silu, [bGk], ["sig%d" % i])
                    tt(yB[:, fc, 0:NT], yT[:, fc, 0:NT], sig[i][:, 0:NT], ALU.mult, ["yT", "sig%d" % i], ["yB"])
                if l == 0 and ti == 0:
                    acopy(yT[:, :, 0:NT], yB[:, :, 0:NT], ["yB"], ["yT"])
                    dump("ysc", yT[:, :, 0:NT], ["yT"])
                branch(1)
                p.cur_tag = "S6.%d.%d" % (l, ti)
                for h in range(8):
                    ws, wk = wload(l, 13 + h)
                    for which in range(3):
                        bk, bkey = nb()
                        proj(bk, bkey, ws, wk, which * 128, NT)
                        fcq = which * 8 + h
                        conv(bk, bkey, 20 + fcq, PF_DCW + fcq * 4, 4, l, NT, dq[:, which, h, 0:NT], "dnT%d" % which)
                    bk, bkey = nb()
                    proj(bk, bkey, ws, wk, 384, NT)
                    act(dz[:, h, 0:NT], bk[:, 0:NT], AF.Silu, [bkey], ["dnT3"])
                if l == 0 and ti == 0:
                    pass
                p.cur_tag = "S7.%d.%d" % (l, ti)
                for c in range(nch):
                    cc = slice(c * 64, (c + 1) * 64)
                    beta_c = tk[:, c, 32:40]
                    g_c = tk[:, c, 40:48]
                    bM, bMk = bank(6)
                    mm(bM[0:64, 0:8], U, g_c, ["cst", "tk"], [bMk])
                    mm(bM[0:64, 8:16], G, g_c, ["cst", "tk"], [bMk])
                    mm(bM[0:128, 16:24], ones[0:64, :], g_c, ["cst", "tk"], [bMk])
                    act(smE[:, 0:16], bM[0:64, 0:16], AF.Exp, [bMk], ["smE"])
                    act(elast[:, 0:8], bM[:, 16:24], AF.Exp, [bMk], ["elast"])
                    for hh in range(2):
                        hs = slice(hh * 4, (hh + 1) * 4)
                        state["hh"] = hh
                        state["bankset"] = (0, 1, 2) if hh == 0 else (3, 4, 5)
                        fac = facs[hh]
                        fack = "fac%d" % hh
                        v3 = lambda ap: ap.rearrange("p (h d) -> p h d", d=128)
                        m3 = lambda ap: ap.rearrange("p (h d) -> p h d", d=64)
                        toks = []
                        for which in range(3):
                            bX, bXk = nb()
                            for q in range(4):
                                trb(bX[0:64, q * 128:(q + 1) * 128], dq[:, which, hh * 4 + q, cc], ["dnT%d" % which], [bXk])
                            if which < 2:
                                tkb, tkk = F(which)
                                acopy(tkb[:], bX[0:64, :], [bXk], [tkk])
                            else:
                                tkb, tkk = A_(0)
                                tt(v3(tkb), v3(bX[0:64, :]), bc(beta_c[:, hs], [64, 4, 128], 2), ALU.mult, [bXk, "tk"], [tkk])
                            toks.append((tkb, tkk))
                        (qtok, qtk), (ktok, ktk), (vb, vbk) = toks
                        for src_, srck_, c0_ in ((qtok, qtk, 0), (ktok, ktk, 4)):
                            sqb, sqk = nb()
                            act(sqb[0:64, :], src_[:], AF.Square, [srck_], [sqk])
                            p.op("dve", dur=0.65, fn=lambda e, sqb=sqb, c0_=c0_, fac=fac: e.tensor_reduce(fac[:, c0_:c0_ + 4], v3(sqb[0:64, :]), AX.X, ALU.add),
                                 reads=[sqk], writes=[fack])
                        rsqrt_to(fac[:, 0:8], fac[:, 0:8], 1.0, [fack], [fack])
                        ts(fac[:, 8:12], fac[:, 0:4], 128.0 ** -0.5, ALU.mult, [fack], [fack])
                        tt(fac[:, 12:16], fac[:, 8:12], smE[:, hh * 4:(hh + 1) * 4], ALU.mult, [fack, "smE"], [fack])
                        tt(fac[:, 16:20], fac[:, 4:8], beta_c[:, hs], ALU.mult, [fack, "tk"], [fack])
                        tt(fac[:, 16:20], fac[:, 16:20], smE[:, hh * 4:(hh + 1) * 4], ALU.mult, [fack, "smE"], [fack])
                        tt(fac[:, 20:24], fac[:, 4:8], smE[:, 8 + hh * 4:8 + (hh + 1) * 4], ALU.mult, [fack, "smE"], [fack])
                        scaled = []
                        for bi, (src, srck, f0) in enumerate(((qtok, qtk, 8), (qtok, qtk, 12), (ktok, ktk, 4), (ktok, ktk, 16), (ktok, ktk, 20))):
                            if bi < 3:
                                o, ok = tbb[bi][:], "tbb%d" % bi
                            else:
                                o, ok = A_(bi - 2)
                            tt(v3(o), v3(src[:]), bc(fac[:, f0:f0 + 4], [64, 4, 128], 2), ALU.mult, [srck, fack], [ok],
                               eng=("pool" if bi >= 3 else "dve"))
                            scaled.append((o, ok))
                        (qn, qnk), (qg, qgk), (kn, knk), (kb, kbk), (kd, kdk) = scaled
                        fTs = []
                        for j, (src, srck) in enumerate(((kn, knk), (qn, qnk), (qg, qgk))):
                            bX, bXk = nb()
                            for q in range(4):
                                trb(bX[:, q * 64:(q + 1) * 64], src[:, q * 128:(q + 1) * 128], [srck], [bXk])
                            fv, fvk = M_(j)
                            acopy(fv, bX[:, 0:256], [bXk], [fvk])
                            fTs.append((fv, fvk))
                        (knT, knTk), (qnT, qnTk), (qgT, qgTk) = fTs
                        Gl, Glk = F(2)
                        Gs, Gsk = F(3)
                        tt(m3(Gl[:, 0:256]), bc(U, [64, 4, 64], 1), bc(g_c[:, hs], [64, 4, 64], 2), ALU.mult, ["cst", "tk"], [Glk], eng="pool")
                        tt(m3(Gs[:, 0:256]), bc(G, [64, 4, 64], 1), bc(g_c[:, hs], [64, 4, 64], 2), ALU.mult, ["cst", "tk"], [Gsk], eng="pool")
                        bS1, bS1k = nb()
                        bS2, bS2k = nb()
                        for q in range(4):
                            qs = slice(q * 64, (q + 1) * 64)
                            mm(bS1[0:64, qs], Gl[:, qs], G, [Glk, "cst"], [bS1k])
                            mm(bS2[0:64, qs], Gs[:, qs], U, [Gsk, "cst"], [bS2k])
                        E1, E1k = F(4)
                        E2, E2k = F(5)
                        act(E1[:, 0:256], bS1[0:64, 0:256], AF.Exp, [bS1k], [E1k])
                        act(E2[:, 0:256], bS2[0:64, 0:256], AF.Exp, [bS2k], [E2k])
                        bK, bKk = nb()
                        bQ, bQk = nb()
                        for q in range(4):
                            qs = slice(q * 64, (q + 1) * 64)
                            mm(bK[0:64, qs], knT[:, qs], knT[:, qs], [knTk], [bKk], passes=1.0)
                            mm(bQ[0:64, qs], knT[:, qs], qnT[:, qs], [knTk, qnTk], [bQk], passes=1.0)
                        tt(m3(Gl[:, 0:256]), bc(G, [64, 4, 64], 1), bc(beta_c[:, hs], [64, 4, 64], 2), ALU.mult, ["cst", "tk"], [Glk])
                        tt(E1[:, 0:256], E1[:, 0:256], Gl[:, 0:256], ALU.mult, [E1k, Glk], [E1k])
                        Lm, Lmk = D_(0)
                        tt(Lm[:, 0:256], bK[0:64, 0:256], E1[:, 0:256], ALU.mult, [bKk, E1k], [Lmk])
                        tt(m3(E2[:, 0:256]), m3(E2[:, 0:256]), bc(U, [64, 4, 64], 1), ALU.mult, [E2k, "cst"], [E2k])
                        Xs, Xsk = D_(1)
                        tt(Xs[:, 0:256], bQ[0:64, 0:256], E2[:, 0:256], ALU.mult, [bQk, E2k], [Xsk])
                        bL, bLk = nb()
                        for q in range(4):
                            qs = slice(q * 64, (q + 1) * 64)
                            trb(bL[0:64, qs], Lm[:, qs], [Lmk], [bLk])
                        LT, LTk = D_(2)
                        acopy(LT[:, 0:256], bL[0:64, 0:256], [bLk], [LTk])
                        P, Pk = D_(3)
                        tt(m3(P[:, 0:256]), bc(I64, [64, 4, 64], 1), m3(LT[:, 0:256]), ALU.subtract, ["cst", LTk], [Pk])
                        cur, curk, curT, curTk = Lm, Lmk, LT, LTk
                        for j in range(1, 6):
                            bA, bAk = nb()
                            for q in range(4):
                                qs = slice(q * 64, (q + 1) * 64)
                                mm(bA[0:64, qs], curT[:, qs], cur[:, qs], [curTk, curk], [bAk], passes=1.0)
                            Xj, Xjk = D_({1: 4, 2: 6, 3: 4, 4: 6, 5: 4}[j])
                            acopy(Xj[:, 0:256], bA[0:64, 0:256], [bAk], [Xjk])
                            if j < 5:
                                bB, bBk = nb()
                                for q in range(4):
                                    qs = slice(q * 64, (q + 1) * 64)
                                    mm(bB[0:64, qs], cur[:, qs], curT[:, qs], [curk, curTk], [bBk], passes=1.0)
                                XjT, XjTk = D_({1: 5, 2: 7, 3: 5, 4: 7}[j])
                                vcopy(XjT[:, 0:256], bB[0:64, 0:256], [bBk], [XjTk])
                            bP, bPk = nb()
                            for q in range(4):
                                qs = slice(q * 64, (q + 1) * 64)
                                mm(bP[0:64, qs], Xj[:, qs], P[:, qs], [Xjk, Pk], [bPk], passes=1.0)
                            tt(P[:, 0:256], P[:, 0:256], bP[0:64, 0:256], ALU.add, [Pk, bPk], [Pk])
                            cur, curk = Xj, Xjk
                            if j < 5:
                                curT, curTk = XjT, XjTk
                        bW, bWk = nb()
                        for q in range(4):
                            mm(bW[:, q * 64:(q + 1) * 64], kb[:, q * 128:(q + 1) * 128], P[:, q * 64:(q + 1) * 64], [kbk, Pk], [bWk], passes=1.0)
                        wTn, wTnk = M_(3)
                        p.op("act", lambda e, wTn=wTn, bW=bW: e.mul(wTn, bW[:, 0:256], -1.0), reads=[bWk], writes=[wTnk], dur=0.47)
                        bV, bVk = nb()
                        for q in range(4):
                            h = hh * 4 + q
                            mm(bV[0:64, q * 128:(q + 1) * 128], P[:, q * 64:(q + 1) * 64], vb[:, q * 128:(q + 1) * 128], [Pk, vbk], [bVk],
                               start=True, stop=False, passes=1.0)
                            mm(bV[0:64, q * 128:(q + 1) * 128], wTn[:, q * 64:(q + 1) * 64], dstB[:, h * 128:(h + 1) * 128], [wTnk, "dstB"], [bVk],
                               start=False, stop=True, passes=1.0)
                        vnew, vnk = A_(3)
                        acopy(vnew, bV[0:64, :], [bVk], [vnk])
                        bO, bOk = nb()
                        for q in range(4):
                            h = hh * 4 + q
                            mm(bO[0:64, q * 128:(q + 1) * 128], qgT[:, q * 64:(q + 1) * 64], dstB[:, h * 128:(h + 1) * 128], [qgTk, "dstB"], [bOk],
                               start=True, stop=False, passes=1.0)
                            mm(bO[0:64, q * 128:(q + 1) * 128], Xs[:, q * 64:(q + 1) * 64], vnew[:, q * 128:(q + 1) * 128], [Xsk, vnk], [bOk],
                               start=False, stop=True, passes=1.0)
                        sqb, sqk = nb()
                        act(sqb[0:64, :], bO[0:64, :], AF.Square, [bOk], [sqk])
                        p.op("dve", dur=0.65, fn=lambda e, sqb=sqb, fac=fac: e.tensor_reduce(fac[:, 24:28], v3(sqb[0:64, :]), AX.X, ALU.add),
                             reads=[sqk], writes=[fack])
                        rsqrt_to(fac[:, 24:28], fac[:, 24:28], 1.0 / 128, [fack], [fack])
                        otb, otbk = tbb[3], "tbb3"
                        tt(v3(otb[:]), v3(bO[0:64, :]), bc(fac[:, 24:28], [64, 4, 128], 2), ALU.mult, [bOk, fack], [otbk])
                        bR, bRk = nb()
                        for q in range(4):
                            trb(bR[:, q * 64:(q + 1) * 64], otb[:, q * 128:(q + 1) * 128], [otbk], [bRk])
                        stt(yB[:, hh * 4:(hh + 1) * 4, cc], bR[:, 0:256].rearrange("p (a b) -> p a b", b=64),
                            pf[:, l, PF_DNORM:PF_DNORM + 1], dz[:, hh * 4:(hh + 1) * 4, cc], ALU.mult, ALU.mult,
                            [bRk, "pf", "dnT3"], ["yB"])
                        bU, bUk = nb()
                        for q in range(4):
                            mm(bU[:, q * 128:(q + 1) * 128], kd[:, q * 128:(q + 1) * 128], vnew[:, q * 128:(q + 1) * 128], [kdk, vnk], [bUk], passes=1.0)
                        dh = dst[:, hh * 512:(hh + 1) * 512]
                        tt(v3(dh), v3(dh), bc(elast[:, hs], [128, 4, 128], 2), ALU.mult, ["dst", "elast"], ["dst"], eng="pool")
                        tt(dh, dh, bU[:, :], ALU.add, ["dst", bUk], ["dst"])
                        acopy(dstB[:, hh * 512:(hh + 1) * 512], dh, ["dst"], ["dstB"])
                state["hh"] = 0
                state["bankset"] = None
                if l == 0 and ti == 0:
                    acopy(yT[:, :, 0:NT], yB[:, :, 0:NT], ["yB"], ["yT"])
                    dump("ydn", yT[:, :, 0:NT], ["yT"])
                branch(2)
                if l == 0 and ti == 0:
                    acopy(merged[:, :, 0:NT], mergedB[:, :, 0:NT], ["mergedB"], ["merged"])
                    dump("merged", merged[:, :, 0:NT], ["merged"])
                p.cur_tag = "S8.%d.%d" % (l, ti)
                bN, bNk = bank(7)
                for half in range(2):
                    ws, wk = wload(l, 33 + half)
                    for q in range(4):
                        d = half * 4 + q
                        bO, bOk = nb()
                        proj(bO, bOk, ws, wk, q * 128, NT, rhs=mergedB, rkey="mergedB")
                        acopy(yT[:, d, 0:NT], bO[:, 0:NT], [bOk], ["yT"])
                        i = rot("sqs", 2)
                        act(sqs[i][:, 0:NT], bO[:, 0:NT], AF.Square, [bOk], ["sqs%d" % i])
                        mm(bN[:, 0:NT], ones, sqs[i][:, 0:NT], ["cst", "sqs%d" % i], [bNk], start=(d == 0), stop=(d == KC - 1))
                rsqrt_to(rsb[:, 0, 0:NT], bN[:, 0:NT], 1.0 / D, [bNk], ["rsb"])
                for d in range(KC):
                    j = rot("mtmp", 2)
                    stt(mtmp[j][:, 0:NT], yT[:, d, 0:NT], pf[:, l, PF_NPOST + d:PF_NPOST + d + 1], rsb[:, 0, 0:NT], ALU.mult, ALU.mult,
                        ["yT", "pf", "rsb"], ["mtmp%d" % j])
                    tt(hT[:, d, cols], hT[:, d, cols], mtmp[j][:, 0:NT], ALU.add, [hk, "mtmp%d" % j], [hk], eng="pool")
                if l == 0 and ti == 0:
                    dump("h1", hT[:, :, cols], [hk])

        for r0 in range(0, S, 128):
            nr = min(128, S - r0)
            col0 = NMETA + r0
            i = rot("xstg", 1)
            skey = "merged"
            stg2 = xstg[i]
            for half in range(2):
                bk, bkey = nb()
                for q in range(4):
                    k = half * 4 + q
                    tr(bk[0:nr, q * 128:(q + 1) * 128], hT[:, k, col0:col0 + nr], hkeys(col0, col0 + nr), [bkey])
                acopy(stg2[0:nr, half * 512:(half + 1) * 512], bk[0:nr, :], [bkey], [skey])
            dma(out_d[r0:r0 + nr, :], stg2[0:nr, :], [skey], [], "och")
        p.finalize(st)
    return nc, p


def make_consts():
    c = np.zeros((128, 384), np.float32)
    c[:, 0:128] = np.eye(128, dtype=np.float32)
    c[:, 128:256] = 1.0
    a = np.arange(64)
    c[0:64, 256:320] = (a[:, None] <= a[None, :]).astype(np.float32)
    c[0:64, 320:384] = (a[:, None] > a[None, :]).astype(np.float32)
    return c


def pack_weights(inp):
    L = inp["w_in"].shape[0]
    w_in = np.asarray(inp["w_in"], np.float32)
    cols = []
    cols += list(range(O_Z, O_Z + 1024))
    cols += list(range(O_XBC, O_XBC + 1536))
    for fc in range(8):
        for base in (O_SCB, O_SCC, O_SCH, O_SCG):
            cols += list(range(base + fc * 128, base + (fc + 1) * 128))
    for h in range(8):
        for base in (O_Q, O_K, O_V, O_DZ):
            cols += list(range(base + h * 128, base + (h + 1) * 128))
    cols += list(range(O_GATE, O_GATE + 3072))
    cols = np.asarray(cols)
    assert cols.size == NW1
    w1 = np.ascontiguousarray(w_in[:, :, cols])
    wsm = np.ascontiguousarray(np.concatenate([w_in[:, :, O_DT:O_DT + 16], w_in[:, :, O_DB:O_DB + 16]], axis=-1))
    pf = np.zeros((128, L, NPF), np.float32)
    fm = lambda v, n: np.asarray(v, np.float32).reshape(n, 128).T
    for l in range(L):
        pf[:, l, PF_NPRE:PF_NPRE + 8] = fm(inp["norm_pre"][l], 8)
        pf[:, l, PF_NPOST:PF_NPOST + 8] = fm(inp["norm_post"][l], 8)
        scw = np.asarray(inp["ssd_conv_w"][l], np.float32)
        pf[:, l, PF_SCW:PF_SCW + 48] = scw.reshape(4, 12, 128).transpose(2, 1, 0).reshape(128, 48)
        pf[:, l, PF_SCB:PF_SCB + 12] = fm(inp["ssd_conv_b"][l], 12)
        pf[:, l, PF_SD:PF_SD + 8] = fm(np.repeat(np.asarray(inp["ssd_d"][l], np.float32), 64), 8)
        pf[:, l, PF_SNORM:PF_SNORM + 8] = fm(inp["ssd_norm"][l], 8)
        ccw = np.asarray(inp["sc_conv_w"][l], np.float32)
        pf[:, l, PF_CCW:PF_CCW + 24] = ccw.reshape(3, 8, 128).transpose(2, 1, 0).reshape(128, 24)
        dcw = np.asarray(inp["dn_conv_w"][l], np.float32)
        pf[:, l, PF_DCW:PF_DCW + 96] = dcw.reshape(4, 24, 128).transpose(2, 1, 0).reshape(128, 96)
        pf[:, l, PF_DNORM] = np.asarray(inp["dn_norm"][l], np.float32)
    pb = np.zeros((128, L, 48), np.float32)
    for l in range(L):
        row = np.concatenate([inp["ssd_dt_bias"][l], inp["ssd_a_log"][l], inp["dn_dt_bias"][l], inp["dn_a_log"][l]]).astype(np.float32)
        pb[:, l, :] = np.broadcast_to(row[None, :], (128, 48))
    return dict(w1=w1, wsm=wsm, wb=np.ascontiguousarray(inp["w_branch"], np.float32), wo=np.ascontiguousarray(inp["w_out"], np.float32),
                pf=pf, pb=pb, cst=make_consts(), meta=np.ascontiguousarray(inp["meta_tokens"], np.float32))


_CACHE = {}


def kernel(**inputs):
    x = np.asarray(inputs["x"], np.float32)
    B, S, _ = x.shape
    L = inputs["w_in"].shape[0]
    shared = pack_weights(inputs)
    key = (S, L)
    if key not in _CACHE:
        _CACHE[key] = build_program(S, L)[0]
    nc = _CACHE[key]
    in_maps = []
    for b in range(B):
        m = dict(shared)
        m["x"] = np.ascontiguousarray(x[b])
        in_maps.append(m)
    res = run_bass_kernel_spmd(nc, in_maps, core_ids=list(range(B)))
    return np.stack([np.asarray(r["out"], np.float32) for r in res.results], axis=0)
```

```python
import contextlib
import numpy as np
import concourse.bass as bass
import concourse.mybir as mybir

F32 = mybir.dt.float32
F32R = mybir.dt.float32r
BF16 = mybir.dt.bfloat16
ALU = mybir.AluOpType
AF = mybir.ActivationFunctionType
AX = mybir.AxisListType


class Op:
    __slots__ = ("eng", "fn", "deps", "chan", "needs_inc", "event", "idx", "dur", "succ", "nd", "ready", "fin", "pos", "lat", "tag", "aset")

    def __init__(self, eng, fn, deps, chan, dur):
        self.eng = eng
        self.fn = fn
        self.deps = deps
        self.chan = chan
        self.needs_inc = chan is not None
        self.event = None
        self.dur = dur
        self.succ = []
        self.ready = 0.0
        self.fin = 0.0


class Prog:
    ENGS = ("pe", "act", "dve", "pool", "sp")

    def __init__(self, nc, same_eng_sync=True, schedule=True, window=4000):
        self.nc = nc
        self.ops = []
        self.last_w = {}
        self.readers = {}
        self.same_eng_sync = same_eng_sync
        self.schedule = schedule
        self.window = window

    def op(self, eng, fn, reads=(), writes=(), chan=None, dur=0.1):
        deps = []
        for k in reads:
            w = self.last_w.get(k)
            if w is not None:
                deps.append(w)
        for k in writes:
            w = self.last_w.get(k)
            if w is not None:
                deps.append(w)
            deps.extend(self.readers.get(k, ()))
        o = Op(eng, fn, deps, chan, dur)
        o.tag = getattr(self, "cur_tag", "")
        o.aset = None
        o.idx = len(self.ops)
        self.ops.append(o)
        for k in reads:
            self.readers.setdefault(k, []).append(o)
        for k in writes:
            self.last_w[k] = o
            self.readers[k] = []
        return o

    def _sem_edge(self, d, o):
        if d.chan is None and o.chan is None and d.eng == o.eng:
            if o.eng == "pe" or not self.same_eng_sync:
                return False
        return True

    def _list_schedule(self):
        import heapq
        ops = self.ops
        for o in ops:
            ds = []
            seen = set()
            for d in o.deps:
                if d is o or id(d) in seen:
                    continue
                seen.add(id(d))
                ds.append(d)
            o.deps = ds
            o.nd = len(ds)
            for d in ds:
                d.succ.append(o)
        SEM_LAT = 0.12
        cp = [0.0] * len(ops)
        for o in reversed(ops):
            m = 0.0
            for s_ in o.succ:
                if cp[s_.idx] > m:
                    m = cp[s_.idx]
            cp[o.idx] = m + (o.dur + (2.0 if o.chan is not None else 0.0)) + 0.1
        self.cp_len = max(cp) if cp else 0.0
        free = {e: 0.0 for e in self.ENGS}
        pend = {e: [] for e in self.ENGS}
        avail = {e: [] for e in self.ENGS}
        order = {e: [] for e in self.ENGS}
        dma_pipe = 0.0
        for o in ops:
            if o.nd == 0:
                heapq.heappush(pend[o.eng], (0.0, o.idx))
        nleft = len(ops)
        lo = 0
        done = [False] * (len(ops) + 1)
        W = self.window
        TBL = 1.3
        cur_set = [None]
        cand_l = {e: [] for e in self.ENGS}
        for e in self.ENGS:
            while pend[e]:
                cand_l[e].append(heapq.heappop(pend[e])[1])
        while nleft:
            while done[lo]:
                lo += 1
            best = None
            for e in self.ENGS:
                t = free[e]
                bc_ = None
                for idx in cand_l[e]:
                    if idx >= lo + W:
                        continue
                    stt_ = max(t, ops[idx].ready)
                    if e == "act":
                        as_ = ops[idx].aset
                        if as_ is not None and as_ != cur_set[0]:
                            stt_ += TBL
                    key_ = (stt_, -cp[idx], idx)
                    if bc_ is None or key_ < bc_:
                        bc_ = key_
                if bc_ is None:
                    continue
                if best is None or bc_ < best[0]:
                    best = (bc_, e)
            st, idx, e = best[0][0], best[0][2], best[1]
            cand_l[e].remove(idx)
            o = ops[idx]
            if e == "act" and o.aset is not None:
                cur_set[0] = o.aset
            if o.chan is not None:
                dma_pipe = max(dma_pipe, st) + o.dur
                o.fin = dma_pipe + 2.0
                free[e] = st + 0.06
            else:
                o.fin = st + o.dur
                free[e] = o.fin
            o.pos = len(order[e])
            order[e].append(o)
            done[idx] = True
            nleft -= 1
            for s in o.succ:
                lat = SEM_LAT if self._sem_edge(o, s) else 0.0
                if o.fin + lat > s.ready:
                    s.ready = o.fin + lat
                s.nd -= 1
                if s.nd == 0:
                    cand_l[s.eng].append(s.idx)
        self.sim_time = max(o.fin for o in ops)
        return order

    def finalize(self, stack):
        nc = self.nc
        if self.schedule:
            order = self._list_schedule()
        else:
            order = {e: [] for e in self.ENGS}
            for o in self.ops:
                seen = set()
                ds = []
                for d in o.deps:
                    if d is o or id(d) in seen:
                        continue
                    seen.add(id(d))
                    ds.append(d)
                o.deps = ds
                o.pos = len(order[o.eng])
                order[o.eng].append(o)
        for o in self.ops:
            o.deps = [d for d in o.deps if self._sem_edge(d, o)]
        sems = {}
        cnt = {}

        def getsem(name):
            if name not in sems:
                sems[name] = stack.enter_context(nc.semaphore(name))
                cnt[name] = 0
            return sems[name]

        chan_pos = {}
        for e in self.ENGS:
            for o in order[e]:
                if o.chan is not None:
                    chan_pos[o.chan] = chan_pos.get(o.chan, 0) + 1
                    o.lat = chan_pos[o.chan]
        per = {}
        nw = 0
        for e in self.ENGS:
            wpos = {}
            lst = []
            for o in order[e]:
                best = {}
                for d in o.deps:
                    if d.chan is not None:
                        st_, ps_ = d.chan, d.lat
                    else:
                        st_, ps_ = "e_" + d.eng, d.pos
                    if wpos.get(st_, -1) >= ps_:
                        continue
                    if st_ not in best or best[st_][0] < ps_:
                        best[st_] = (ps_, d)
                need = []
                for st_, (ps_, d) in best.items():
                    wpos[st_] = ps_
                    d.needs_inc = True
                    need.append(d)
                nw += len(need)
                lst.append((o, need))
            per[e] = lst
        for e in self.ENGS:
            for o in order[e]:
                if o.chan is not None:
                    getsem(o.chan)
                    cnt[o.chan] += 16
                    o.event = (o.chan, cnt[o.chan])
                elif o.needs_inc:
                    nm = "e_" + o.eng
                    getsem(nm)
                    cnt[nm] += 1
                    o.event = (nm, cnt[nm])
        self.nwaits = nw
        self.sem_max = dict(cnt)
        assert max(cnt.values()) < 30000, cnt

        def emit(eng_obj, lst):
            for o, need in lst:
                for d in need:
                    eng_obj.wait_ge(sems[d.event[0]], d.event[1])
                ins = o.fn(eng_obj)
                if o.event is not None:
                    if o.chan is not None:
                        ins.then_inc(sems[o.chan], 16)
                    else:
                        ins.then_inc(sems[o.event[0]], 1)

        with nc.Block() as block:
            @block.tensor
            def _(e):
                emit(e, per["pe"])

            @block.scalar
            def _(e):
                emit(e, per["act"])

            @block.vector
            def _(e):
                emit(e, per["dve"])

            @block.gpsimd
            def _(e):
                emit(e, per["pool"])

            @block.sync
            def _(e):
                emit(e, per["sp"])
                for s, v in cnt.items():
                    if v > 0:
                        e.wait_ge(sems[s], v)


from concourse.bass_utils import run_bass_kernel_spmd

D = 1024
KC = 8
NMETA = 16
EPS = 1e-6
PF_NPRE, PF_NPOST, PF_SCW, PF_SCB, PF_SD, PF_SNORM, PF_CCW, PF_DCW, PF_DNORM, NPF = 0, 8, 16, 64, 76, 84, 92, 116, 212, 213
O_Z, O_XBC, O_DT, O_SCB, O_SCC, O_SCH, O_SCG, O_Q, O_K, O_V, O_DZ, O_DB, O_DA, O_GATE = (
    0, 1024, 2560, 2576, 3600, 4624, 5648, 6672, 7696, 8720, 9744, 10768, 10776, 10784)
NW1 = 13824


def build_program(S, L, NTM=256, dbg=None, same_eng_sync=True, schedule=True, window=100000):
    TOK = NMETA + S
    TP = ((TOK + 63) // 64) * 64
    tiles = []
    c = 0
    while c < TP:
        n = min(NTM, TP - c)
        tiles.append((c, n))
        c += n
    nc = bass.Bass("TRN2", target_bir_lowering=False)
    dt_in = lambda n, s: nc.dram_tensor(n, s, F32, kind="ExternalInput").ap()
    x_d = dt_in("x", [S, D])
    meta_d = dt_in("meta", [NMETA, D])
    w1_d = dt_in("w1", [L, D, NW1])
    wsm_d = dt_in("wsm", [L, D, 32])
    wb_d = dt_in("wb", [L, 3, D, D])
    wo_d = dt_in("wo", [L, D, D])
    pf_d = dt_in("pf", [128, L, NPF])
    pb_d = dt_in("pb", [128, L, 48])
    cst_d = dt_in("cst", [128, 384])
    out_d = nc.dram_tensor("out", [S, D], F32, kind="ExternalOutput").ap()
    NBLK = 35
    wbf_d = nc.dram_tensor("wbf", [L, NBLK, 128, KC * 512], BF16, kind="Internal").ap()
    dbg_d = {}
    if dbg:
        for name, shp in dbg.items():
            dbg_d[name] = nc.dram_tensor("dbg_" + name, list(shp), F32, kind="ExternalOutput").ap()

    with contextlib.ExitStack() as st:
        def sb(n, s):
            return st.enter_context(nc.sbuf_tensor("s_" + n, list(s), F32))
        hT = sb("hT", [128, KC, TP])
        NWS = 3
        sbb = lambda n, shp: st.enter_context(nc.sbuf_tensor("s_" + n, list(shp), BF16))
        wsl = [sbb("wsl%d" % i, [128, KC, 512]) for i in range(NWS)]
        xnT = sbb("xnT", [128, KC, NTM])
        yB = sbb("yB", [128, KC, NTM])
        mergedB = sbb("mergedB", [128, KC, NTM])
        bcT = sb("bcT", [128, 4, NTM])
        yT = sb("yT", [128, KC, NTM])
        merged = sb("merged", [128, KC, NTM])
        xstg = [merged[:].rearrange("p a b -> p (a b)")[:, 0:1024]]
        dq = sbb("dq", [128, 3, 8, NTM])
        dz = sbb("dz", [128, 8, NTM])
        xsT = dq[:, 0]
        siluz = dz[:]
        cstB = sbb("cstB", [128, 384])
        NTMB = 16
        fpool = [sb("fp%d" % i, [64, 512]) for i in range(6)]
        BA = [[sbb("ba%d_%d" % (h_, i), [64, 512]) for i in range(4)] for h_ in range(2)]
        BD = [[sbb("bd%d_%d" % (h_, i), [64, 256]) for i in range(8)] for h_ in range(2)]
        BFm = [[sbb("bm%d_%d" % (h_, i), [128, 256]) for i in range(4)] for h_ in range(2)]
        tbb = [sbb("tbb%d" % i, [64, 512]) for i in range(4)]
        sst = sb("sst", [128, 1024])
        dst = sb("dst", [128, 1024])
        dstB = sbb("dstB", [128, 1024])
        pre = [sb("pre%d" % i, [128, NTM + 3]) for i in range(4)]
        cacc = [sb("cacc%d" % i, [128, NTM]) for i in range(3)]
        sqs = [sb("sqs%d" % i, [128, NTM]) for i in range(2)]
        rsb = sb("rsb", [128, 2, NTM])
        sig = [sb("sig%d" % i, [128, NTM]) for i in range(2)]
        mtmp = [sb("mtmp%d" % i, [128, NTM]) for i in range(2)]
        tails = sb("tails", [128, 44, 3])
        pf = sb("pf", [128, L, NPF])
        pbt = sb("pbt", [128, L, 48])
        negA = sb("negA", [128, L, 24])
        cst = sb("cst", [128, 384])
        wsm = sbb("wsm", [128, L, KC, 32])
        tsm = sb("tsm", [64, NTM // 64, 32])
        tk = sb("tk", [64, NTM // 64, 48])
        smE = sb("smE", [64, 32])
        elast = sb("elast", [128, 16])
        facs = [sb("fac%d" % i, [64, 32]) for i in range(2)]
        ps = st.enter_context(nc.psum_tensor("ps", [128, 4096], F32))

        ident = cst[:, 0:128]
        ones = cst[:, 128:256]
        U = cst[0:64, 256:320]
        G = cst[0:64, 320:384]
        I64 = cst[0:64, 0:64]

        p = Prog(nc, same_eng_sync=same_eng_sync, schedule=schedule, window=window)
        state = {"rr": 0, "tm": 0, "ws": 0, "i2": {}}

        def bank(i):
            return ps[:, i * 512:(i + 1) * 512], "b%d" % i

        def nb():
            bs_ = state.get("bankset")
            if bs_ is None:
                i = state["rr"]
                state["rr"] = (i + 1) % 6
                return bank(i)
            j = state["i2"].get(("bs", bs_), 0)
            state["i2"][("bs", bs_)] = (j + 1) % len(bs_)
            return bank(bs_[j])

        def rot(name, n):
            i = state["i2"].get(name, 0)
            state["i2"][name] = (i + 1) % n
            return i

        def Bv(t, n):
            return t[:].bitcast(BF16)[:, 0:n]

        def F(i):
            return fpool[i], "fp%d" % i

        def A_(i):
            h_ = state["hh"]
            return BA[h_][i][:], "ba%d_%d" % (h_, i)

        def D_(i):
            h_ = state["hh"]
            return BD[h_][i][:], "bd%d_%d" % (h_, i)

        def M_(i):
            h_ = state["hh"]
            return BFm[h_][i][:], "bm%d_%d" % (h_, i)

        def fsz(ap):
            n = 1
            for d in ap.shape[1:]:
                n *= int(d)
            return n

        PASSES = 4.0

        def mm(out, lhsT, rhs, r, w, start=True, stop=True, passes=PASSES):
            p.op("pe", lambda e: e.matmul(out, lhsT, rhs, start=start, stop=stop), reads=r, writes=w,
                 dur=max(fsz(rhs), 64) * passes / 2400.0 + 0.03)

        def tr(out, in_, r, w):
            n = in_.shape[0]
            p.op("pe", lambda e: e.transpose(out, in_, ident[0:n, 0:n]), reads=r + ["cst"], writes=w, dur=0.09)

        def trb(out, in_, r, w):
            n = in_.shape[0]
            p.op("pe", lambda e: e.matmul(out, in_, cstB[0:n, 0:n], start=True, stop=True), reads=r + ["cstB"], writes=w,
                 dur=max(n, 64) / 2400.0 + 0.03)

        def act(out, in_, func, r, w, bias=None, scale=None):
            kw = {}
            if bias is not None:
                kw["bias"] = bias
            if scale is not None:
                kw["scale"] = scale
            o_ = p.op("act", lambda e: e.activation(out, in_, func, **kw), reads=r, writes=w, dur=0.2 + fsz(out) / 960.0)
            o_.aset = ASET.get(func)

        ASET = {AF.Silu: "silu", AF.Sigmoid: "sig", AF.Exp: "el", AF.Ln: "el", AF.Sqrt: "sqrt"}

        def acopy(out, in_, r, w):
            p.op("act", lambda e: e.copy(out, in_), reads=r, writes=w, dur=0.2 + fsz(out) / 960.0)

        def edur(eng, n):
            return (0.12 + n / 960.0) if eng == "dve" else (0.2 + n / 400.0)

        def tt(out, a, b, op, r, w, eng="dve"):
            p.op(eng, lambda e: e.tensor_tensor(out, a, b, op), reads=r, writes=w, dur=edur(eng, fsz(out)))

        def ts(out, a, s1, op0, r, w, s2=None, op1=None, eng="dve"):
            if op1 is None:
                p.op(eng, lambda e: e.tensor_scalar(out, a, s1, None, op0), reads=r, writes=w, dur=edur(eng, fsz(out)))
            else:
                p.op(eng, lambda e: e.tensor_scalar(out, a, s1, s2, op0, op1), reads=r, writes=w, dur=edur(eng, fsz(out)))

        def stt(out, in0, scalar, in1, op0, op1, r, w):
            p.op("dve", lambda e: e.scalar_tensor_tensor(out, in0, scalar, in1, op0, op1), reads=r, writes=w,
                 dur=edur("dve", fsz(out)))

        def recip(out, in_, r, w):
            p.op("dve", lambda e: e.reciprocal(out, in_), reads=r, writes=w, dur=edur("dve", fsz(out)))

        def vcopy(out, in_, r, w, eng="dve"):
            p.op(eng, lambda e: e.tensor_copy(out, in_), reads=r, writes=w, dur=edur(eng, fsz(out)))

        def memset(ap, val, w, eng="pool"):
            p.op(eng, lambda e: e.memset(ap, val), writes=w, dur=edur(eng, fsz(ap)))

        def dma(out, in_, r, w, chan, eng="sp"):
            nbytes = (2 if out.dtype == BF16 else 4) * fsz(out) * int(out.shape[0]) * (3 if eng == "pool" else 1)
            p.op(eng, lambda e: e.dma_start(out=out, in_=in_), reads=r, writes=w, chan=chan, dur=nbytes / 250e3)

        def dump(name, ap, keys):
            if name in dbg_d:
                dma(dbg_d[name], ap, keys, [], "dbg")

        def rsqrt_to(out, in_, scale, r, w):
            act(out, in_, AF.Ln, r, w, bias=EPS, scale=scale)
            act(out, out, AF.Exp, w, w, scale=-0.5)

        dma(cst[:], cst_d, [], ["cst"], "ld_cst")
        dma(pf[:], pf_d, [], ["pf"], "ld_pf")
        dma(pbt[:], pb_d, [], ["pbt"], "ld_pb")
        for l in range(L):
            dma(wsm[:, l], wsm_d[l].rearrange("(k p) c -> p k c", p=128), [], ["wsm%d" % l], "ld_wsm%d" % l, eng="pool")
        acopy(cstB[:], cst[:], ["cst"], ["cstB"])
        for l in range(L):
            act(negA[:, l, 0:16], pbt[:, l, 16:32], AF.Exp, ["pbt"], ["negA"])
            act(negA[:, l, 16:24], pbt[:, l, 40:48], AF.Exp, ["pbt"], ["negA"])
        ts(negA[:], negA[:], -1.0, ALU.mult, ["negA"], ["negA"])

        hkey = lambda t: "hT%d" % t

        def tile_of(col):
            for ti, (c0, n) in enumerate(tiles):
                if c0 <= col < c0 + n:
                    return ti
            raise ValueError

        def hkeys(c_lo, c_hi):
            return sorted({hkey(tile_of(c)) for c in (c_lo, c_hi - 1)} | {hkey(t) for t in range(tile_of(c_lo), tile_of(c_hi - 1) + 1)})

        if TP > TOK:
            memset(hT[:, :, TOK:TP], 0.0, hkeys(TOK, TP))
        row_blocks = [("meta", 0, NMETA, 0)] + [("x", r0, min(128, S - r0), NMETA + r0) for r0 in range(0, S, 128)]
        for (src, r0, nr, col0) in row_blocks:
            si = rot("xstg", 1)
            skey = "merged"
            stg2 = xstg[si]
            srcap = meta_d[0:nr, :] if src == "meta" else x_d[r0:r0 + nr, :]
            dma(stg2[0:nr, :], srcap, [], [skey], "xch%d" % si)
            for half in range(2):
                bk, bkey = nb()
                for q in range(4):
                    k = half * 4 + q
                    tr(bk[:, q * 128:q * 128 + nr], stg2[0:nr, k * 128:(k + 1) * 128], [skey], [bkey])
                acopy(hT[:, half * 4:half * 4 + 4, col0:col0 + nr],
                      bk.rearrange("p (a b) -> p a b", b=128)[:, :, 0:nr], [bkey], hkeys(col0, col0 + nr))

        def blk_src(l, blk):
            if blk < 27:
                return w1_d[l][:, blk * 512:(blk + 1) * 512]
            if blk < 33:
                n, half = divmod(blk - 27, 2)
                return wb_d[l, n][:, half * 512:(half + 1) * 512]
            return wo_d[l][:, (blk - 33) * 512:(blk - 32) * 512]

        use_order = [0, 1, 2, 3, 4, 21, 27, 22, 28] + list(range(5, 13)) + [23, 29, 24, 30] + list(range(13, 21)) + [25, 31, 26, 32, 33, 34]
        assert sorted(use_order) == list(range(NBLK))
        GRP = 9
        grp_of = {}
        for l in range(L):
            for gi in range(0, NBLK, GRP):
                grp = use_order[gi:gi + GRP]
                chn = "cv%d_%d" % (l, gi // GRP)
                for blk in grp:
                    dma(wbf_d[l, blk].rearrange("p (k c) -> p k c", k=KC), blk_src(l, blk).rearrange("(k p) c -> p k c", p=128),
                        [], ["wbf%d_%d" % (l, blk)], chn, eng="pool")
                for blk in grp:
                    grp_of[(l, blk)] = grp

        def wload(l, blk):
            si = state["ws"] % NWS
            state["ws"] += 1
            key = "wsl%d" % si
            dma(wsl[si][:], wbf_d[l, blk].rearrange("p (k c) -> p k c", k=KC), ["wbf%d_%d" % (l, b_) for b_ in grp_of[(l, blk)]], [key],
                "wch%d" % si)
            return wsl[si], key

        def proj(out_bank, okey, wslot, wkey, coff, NT, rhs=None, rkey="xnT"):
            rhs = xnT if rhs is None else rhs
            for k in range(KC):
                mm(out_bank[:, 0:NT], wslot[:, k, coff:coff + 128], rhs[:, k, 0:NT], [wkey, rkey], [okey],
                   start=(k == 0), stop=(k == KC - 1), passes=1.0)

        def conv(bk, bkey, ti, wcol0, K, l, NT, dest, dkey, bias_col=None, mul=None, mulkey=None):
            H = K - 1
            i = rot("pre", 4)
            pr, pk = pre[i], "pre%d" % i
            ca, ck = cacc[i % 3], "cacc%d" % (i % 3)
            vcopy(pr[:, 0:H], tails[:, ti, 0:H], ["tails%d" % ti], [pk], eng="pool")
            if mul is None:
                acopy(pr[:, H:H + NT], bk[:, 0:NT], [bkey], [pk])
            else:
                tt(pr[:, H:H + NT], bk[:, 0:NT], mul, ALU.mult, [bkey, mulkey], [pk])
            vcopy(tails[:, ti, 0:H], pr[:, NT:NT + H], [pk], ["tails%d" % ti], eng="pool")
            wl = pf[:, l, wcol0 + K - 1:wcol0 + K]
            if mul is None:
                act(ca[:, 0:NT], bk[:, 0:NT], AF.Identity, [bkey, "pf"], [ck], scale=wl)
            else:
                act(ca[:, 0:NT], pr[:, H:H + NT], AF.Identity, [pk, "pf"], [ck], scale=wl)
            for k in range(0, K - 1):
                last = (k == K - 2 and K == 3)
                o = dest if last else ca[:, 0:NT]
                ok = [dkey] if last else [ck]
                stt(o, pr[:, k:k + NT], pf[:, l, wcol0 + k:wcol0 + k + 1], ca[:, 0:NT], ALU.mult, ALU.add,
                    [pk, "pf", ck], ok)
            if K == 4:
                if bias_col is not None:
                    act(dest, ca[:, 0:NT], AF.Silu, [ck, "pf"], [dkey], bias=pf[:, l, bias_col:bias_col + 1])
                else:
                    act(dest, ca[:, 0:NT], AF.Silu, [ck], [dkey])

        bc = lambda ap, shape, axis: ap.unsqueeze(axis).to_broadcast(list(shape))

        for l in range(L):
            memset(sst[:], 0.0, ["sst"])
            memset(dst[:], 0.0, ["dst"])
            memset(dstB[:], 0.0, ["dstB"])
            memset(tails[:], 0.0, ["tails%d" % i for i in range(44)])
            w1 = w1_d[l]
            for ti, (c0, NT) in enumerate(tiles):
                nch = NT // 64
                hk = hkey(ti)
                cols = slice(c0, c0 + NT)
                p.cur_tag = "S0.%d.%d" % (l, ti)
                bN, bNk = bank(7)
                for k in range(KC):
                    i = rot("sqs", 2)
                    act(sqs[i][:, 0:NT], hT[:, k, cols], AF.Square, [hk], ["sqs%d" % i])
                    mm(bN[:, 0:NT], ones, sqs[i][:, 0:NT], ["cst", "sqs%d" % i], [bNk], start=(k == 0), stop=(k == KC - 1))
                rsqrt_to(rsb[:, 0, 0:NT], bN[:, 0:NT], 1.0 / D, [bNk], ["rsb"])
                for k in range(KC):
                    stt(xnT[:, k, 0:NT], hT[:, k, cols], pf[:, l, PF_NPRE + k:PF_NPRE + k + 1], rsb[:, 0, 0:NT],
                        ALU.mult, ALU.mult, [hk, "pf", "rsb"], ["xnT"])
                if l == 0 and ti == 0 and "xnT" in dbg_d:
                    acopy(merged[:, :, 0:NT], xnT[:, :, 0:NT], ["xnT"], ["merged"])
                    dump("xnT", merged[:, :, 0:NT], ["merged"])
                p.cur_tag = "S1.%d.%d" % (l, ti)
                bS, bSk = bank(6)
                for c in range(nch):
                    for k in range(KC):
                        mm(bS[0:64, c * 32:(c + 1) * 32], xnT[:, k, c * 64:(c + 1) * 64], wsm[:, l, k, :], ["xnT", "wsm%d" % l], [bSk],
                           start=(k == 0), stop=(k == KC - 1), passes=1.0)
                acopy(tsm[:, 0:nch, :], bS[0:64, 0:nch * 32].rearrange("p (c f) -> p c f", f=32), [bSk], ["tsm"])
                tt(tk[:, 0:nch, 0:16], tsm[:, 0:nch, 0:16], bc(pbt[0:64, l, 0:16], [64, nch, 16], 1), ALU.add, ["tsm", "pbt"], ["tk"])
                tt(tk[:, 0:nch, 40:48], tsm[:, 0:nch, 24:32], bc(pbt[0:64, l, 32:40], [64, nch, 8], 1), ALU.add, ["tsm", "pbt"], ["tk"])
                act(tk[:, 0:nch, 0:16], tk[:, 0:nch, 0:16], AF.Exp, ["tk"], ["tk"])
                act(tk[:, 0:nch, 40:48], tk[:, 0:nch, 40:48], AF.Exp, ["tk"], ["tk"])
                act(tk[:, 0:nch, 0:16], tk[:, 0:nch, 0:16], AF.Ln, ["tk"], ["tk"], bias=1.0)
                act(tk[:, 0:nch, 40:48], tk[:, 0:nch, 40:48], AF.Ln, ["tk"], ["tk"], bias=1.0)
                act(tk[:, 0:nch, 32:40], tsm[:, 0:nch, 16:24], AF.Sigmoid, ["tsm"], ["tk"])
                tt(tk[:, 0:nch, 16:32], tk[:, 0:nch, 0:16], bc(negA[0:64, l, 0:16], [64, nch, 16], 1), ALU.mult, ["tk", "negA"], ["tk"])
                tt(tk[:, 0:nch, 40:48], tk[:, 0:nch, 40:48], bc(negA[0:64, l, 16:24], [64, nch, 8], 1), ALU.mult, ["tk", "negA"], ["tk"])
                if l == 0 and ti == 0:
                    dump("tk", tk[:, 0:nch, :], ["tk"])
                p.cur_tag = "S2.%d.%d" % (l, ti)
                for blk in range(2):
                    ws, wk = wload(l, blk)
                    for q in range(4):
                        fc = blk * 4 + q
                        bk, bkey = nb()
                        proj(bk, bkey, ws, wk, q * 128, NT)
                        act(siluz[:, fc, 0:NT], bk[:, 0:NT], AF.Silu, [bkey], ["dnT3"])
                for blk in range(3):
                    ws, wk = wload(l, 2 + blk)
                    for q in range(4):
                        fc = blk * 4 + q
                        bk, bkey = nb()
                        proj(bk, bkey, ws, wk, q * 128, NT)
                        if fc < 8:
                            dest, dkey = xsT[:, fc, 0:NT], "dnT0"
                        else:
                            dest, dkey = bcT[:, fc - 8, 0:NT], "bcT"
                        conv(bk, bkey, fc, PF_SCW + fc * 4, 4, l, NT, dest, dkey, bias_col=PF_SCB + fc)
                if l == 0 and ti == 0:
                    pass
                    dump("bcT", bcT[:, :, 0:NT], ["bcT"])
                p.cur_tag = "S3.%d.%d" % (l, ti)
                for c in range(nch):
                    cc = slice(c * 64, (c + 1) * 64)
                    dt_c = tk[:, c, 0:16]
                    dtA_c = tk[:, c, 16:32]
                    bM, bMk = bank(6)
                    mm(bM[0:64, 0:16], U, dtA_c, ["cst", "tk"], [bMk])
                    mm(bM[0:64, 16:32], G, dtA_c, ["cst", "tk"], [bMk])
                    mm(bM[0:128, 32:48], ones[0:64, :], dtA_c, ["cst", "tk"], [bMk])
                    act(smE[:, 0:32], bM[0:64, 0:32], AF.Exp, [bMk], ["smE"])
                    act(elast[:, 0:16], bM[:, 32:48], AF.Exp, [bMk], ["elast"])
                    for g in range(2):
                        hs = slice(g * 8, (g + 1) * 8)
                        state["bankset"] = (0, 1, 2) if g == 0 else (3, 4, 5)
                        bX, bXk = nb()
                        for q in range(4):
                            trb(bX[0:64, q * 128:(q + 1) * 128], xsT[:, g * 4 + q, cc], ["dnT0"], [bXk])
                        bB, bBk = nb()
                        tr(bB[0:64, 0:128], bcT[:, g, cc], ["bcT"], [bBk])
                        xc, xck = Bv(fpool[0], 512), "fp0"
                        tt(xc.rearrange("p (h d) -> p h d", d=64), bX[0:64, :].rearrange("p (h d) -> p h d", d=64),
                           bc(dt_c[:, hs], [64, 8, 64], 2), ALU.mult, [bXk, "tk"], [xck])
                        xd, xdk = Bv(fpool[1], 512), "fp1"
                        tt(xd.rearrange("p (h d) -> p h d", d=64), xc.rearrange("p (h d) -> p h d", d=64),
                           bc(smE[:, 16 + g * 8:16 + (g + 1) * 8], [64, 8, 64], 2), ALU.mult, [xck, "smE"], [xdk], eng="pool")
                        btok, btk = Bv(fpool[2], 128), "fp2"
                        acopy(btok, bB[0:64, 0:128], [bBk], [btk])
                        gm, gmk = F(3)
                        tt(gm[:].rearrange("p (h d) -> p h d", d=64), bc(G, [64, 8, 64], 1), bc(dtA_c[:, hs], [64, 8, 64], 2),
                           ALU.mult, ["cst", "tk"], [gmk], eng="pool")
                        bE, bEk = nb()
                        for h in range(8):
                            mm(bE[0:64, h * 64:(h + 1) * 64], gm[:, h * 64:(h + 1) * 64], U, [gmk, "cst"], [bEk])
                        E, Ek = F(4)
                        act(E[:], bE[0:64, :], AF.Exp, [bEk], [Ek])
                        bC, bCk = nb()
                        mm(bC[0:64, 0:64], bcT[:, g, cc], bcT[:, 2 + g, cc], ["bcT"], [bCk])
                        cbm, cbk = F(2)[0][:, 128:192], "fp2"
                        tt(cbm[:, 0:64], bC[0:64, 0:64], U, ALU.mult, [bCk, "cst"], [cbk])
                        MT, MTk = Bv(fpool[5], 512), "fp5"
                        tt(MT.rearrange("p (h d) -> p h d", d=64), E[:].rearrange("p (h d) -> p h d", d=64),
                           bc(cbm[:, 0:64], [64, 8, 64], 1), ALU.mult, [Ek, cbk], [MTk])
                        bY, bYk = nb()
                        for h in range(8):
                            mm(bY[0:64, h * 64:(h + 1) * 64], MT[:, h * 64:(h + 1) * 64], xc[:, h * 64:(h + 1) * 64], [MTk, xck], [bYk], passes=1.0)
                        bT, bTk = nb()
                        mm(bT[0:64, :], bcT[:, 2 + g, cc], sst[:, g * 512:(g + 1) * 512], ["bcT", "sst"], [bTk])
                        tmp, tmpk = F(4)
                        tt(tmp[:].rearrange("p (h d) -> p h d", d=64), bT[0:64, :].rearrange("p (h d) -> p h d", d=64),
                           bc(smE[:, g * 8:(g + 1) * 8], [64, 8, 64], 2), ALU.mult, [bTk, "smE"], [tmpk])
                        ytok, ytk = tbb[3], "tbb3"
                        tt(ytok[:], bY[0:64, :], tmp[:], ALU.add, [bYk, tmpk], [ytk])
                        bR, bRk = nb()
                        for q in range(4):
                            trb(bR[:, q * 64:(q + 1) * 64], ytok[:, q * 128:(q + 1) * 128], [ytk], [bRk])
                        acopy(yT[:, g * 4:(g + 1) * 4, cc], bR[:, 0:256].rearrange("p (a b) -> p a b", b=64), [bRk], ["yT"])
                        bU, bUk = nb()
                        mm(bU[:, :], btok, xd, [btk, xdk], [bUk], passes=1.0)
                        sg_ = sst[:, g * 512:(g + 1) * 512]
                        tt(sg_.rearrange("p (h d) -> p h d", d=64), sg_.rearrange("p (h d) -> p h d", d=64),
                           bc(elast[:, hs], [128, 8, 64], 2), ALU.mult, ["sst", "elast"], ["sst"], eng="pool")
                        tt(sg_, sg_, bU[:, :], ALU.add, ["sst", bUk], ["sst"])
                p.cur_tag = "S3b.%d.%d" % (l, ti)
                state["bankset"] = None
                for fc in range(KC):
                    stt(yT[:, fc, 0:NT], xsT[:, fc, 0:NT], pf[:, l, PF_SD + fc:PF_SD + fc + 1], yT[:, fc, 0:NT], ALU.mult, ALU.add,
                        ["dnT0", "pf", "yT"], ["yT"])
                tt(yT[:, :, 0:NT], yT[:, :, 0:NT], siluz[:, :, 0:NT], ALU.mult, ["yT", "dnT3"], ["yT"])
                for g in range(2):
                    bN, bNk = bank(7)
                    for q in range(4):
                        fc = g * 4 + q
                        i = rot("sqs", 2)
                        act(sqs[i][:, 0:NT], yT[:, fc, 0:NT], AF.Square, ["yT"], ["sqs%d" % i])
                        mm(bN[:, 0:NT], ones, sqs[i][:, 0:NT], ["cst", "sqs%d" % i], [bNk], start=(q == 0), stop=(q == 3))
                    rsqrt_to(rsb[:, g, 0:NT], bN[:, 0:NT], 1.0 / 512, [bNk], ["rsb"])
                for fc in range(KC):
                    stt(yB[:, fc, 0:NT], yT[:, fc, 0:NT], pf[:, l, PF_SNORM + fc:PF_SNORM + fc + 1], rsb[:, fc // 4, 0:NT],
                        ALU.mult, ALU.mult, ["yT", "pf", "rsb"], ["yB"])
                if l == 0 and ti == 0:
                    acopy(yT[:, :, 0:NT], yB[:, :, 0:NT], ["yB"], ["yT"])
                    dump("yssd", yT[:, :, 0:NT], ["yT"])

                def branch(n):
                    p.cur_tag = "BR%d.%d.%d" % (n, l, ti)
                    for half in range(2):
                        gs, gk = wload(l, 21 + n * 2 + half)
                        bs, bk_ = wload(l, 27 + n * 2 + half)
                        for q in range(4):
                            d = half * 4 + q
                            bG, bGk = nb()
                            proj(bG, bGk, gs, gk, q * 128, NT)
                            i = rot("sig", 2)
                            act(sig[i][:, 0:NT], bG[:, 0:NT], AF.Sigmoid, [bGk], ["sig%d" % i])
                            bB2, bB2k = nb()
                            proj(bB2, bB2k, bs, bk_, q * 128, NT, rhs=yB, rkey="yB")
                            if n == 0:
                                tt(merged[:, d, 0:NT], bB2[:, 0:NT], sig[i][:, 0:NT], ALU.mult, [bB2k, "sig%d" % i], ["merged"])
                            else:
                                j = rot("mtmp", 2)
                                tt(mtmp[j][:, 0:NT], bB2[:, 0:NT], sig[i][:, 0:NT], ALU.mult, [bB2k, "sig%d" % i], ["mtmp%d" % j])
                                if n == 2:
                                    tt(mergedB[:, d, 0:NT], merged[:, d, 0:NT], mtmp[j][:, 0:NT], ALU.add, ["merged", "mtmp%d" % j], ["mergedB"], eng="pool")
                                else:
                                    tt(merged[:, d, 0:NT], merged[:, d, 0:NT], mtmp[j][:, 0:NT], ALU.add, ["merged", "mtmp%d" % j], ["merged"], eng="pool")

                branch(0)
                p.cur_tag = "S5.%d.%d" % (l, ti)
                for fc in range(KC):
                    ws, wk = wload(l, 5 + fc)
                    bH, bHk = nb()
                    proj(bH, bHk, ws, wk, 256, NT)
                    i = rot("sig", 2)
                    acopy(sig[i][:, 0:NT], bH[:, 0:NT], [bHk], ["sig%d" % i])
                    bC, bCk = nb()
                    proj(bC, bCk, ws, wk, 128, NT)
                    j = rot("mtmp", 2)
                    conv(bC, bCk, 12 + fc, PF_CCW + fc * 3, 3, l, NT, mtmp[j][:, 0:NT], "mtmp%d" % j, mul=sig[i][:, 0:NT], mulkey="sig%d" % i)
                    bB, bBk = nb()
                    proj(bB, bBk, ws, wk, 0, NT)
                    tt(yT[:, fc, 0:NT], bB[:, 0:NT], mtmp[j][:, 0:NT], ALU.mult, [bBk, "mtmp%d" % j], ["yT"])
                    bG, bGk = nb()
                    proj(bG, bGk, ws, wk, 384, NT)
                    i = rot("sig", 2)
                    act(sig[i][:, 0:NT], bG[:, 0:NT], AF.Silu, [bGk], ["sig%d" % i])
                    tt(yB[:, fc, 0:NT], yT[:, fc, 0:NT], sig[i][:, 0:NT], ALU.mult, ["yT", "sig%d" % i], ["yB"])
                if l == 0 and ti == 0:
                    acopy(yT[:, :, 0:NT], yB[:, :, 0:NT], ["yB"], ["yT"])
                    dump("ysc", yT[:, :, 0:NT], ["yT"])
                branch(1)
                p.cur_tag = "S6.%d.%d" % (l, ti)
                for h in range(8):
                    ws, wk = wload(l, 13 + h)
                    for which in range(3):
                        bk, bkey = nb()
                        proj(bk, bkey, ws, wk, which * 128, NT)
                        fcq = which * 8 + h
                        conv(bk, bkey, 20 + fcq, PF_DCW + fcq * 4, 4, l, NT, dq[:, which, h, 0:NT], "dnT%d" % which)
                    bk, bkey = nb()
                    proj(bk, bkey, ws, wk, 384, NT)
                    act(dz[:, h, 0:NT], bk[:, 0:NT], AF.Silu, [bkey], ["dnT3"])
                if l == 0 and ti == 0:
                    pass
                p.cur_tag = "S7.%d.%d" % (l, ti)
                for c in range(nch):
                    cc = slice(c * 64, (c + 1) * 64)
                    beta_c = tk[:, c, 32:40]
                    g_c = tk[:, c, 40:48]
                    bM, bMk = bank(6)
                    mm(bM[0:64, 0:8], U, g_c, ["cst", "tk"], [bMk])
                    mm(bM[0:64, 8:16], G, g_c, ["cst", "tk"], [bMk])
                    mm(bM[0:128, 16:24], ones[0:64, :], g_c, ["cst", "tk"], [bMk])
                    act(smE[:, 0:16], bM[0:64, 0:16], AF.Exp, [bMk], ["smE"])
                    act(elast[:, 0:8], bM[:, 16:24], AF.Exp, [bMk], ["elast"])
                    for hh in range(2):
                        hs = slice(hh * 4, (hh + 1) * 4)
                        state["hh"] = hh
                        state["bankset"] = (0, 1, 2) if hh == 0 else (3, 4, 5)
                        fac = facs[hh]
                        fack = "fac%d" % hh
                        v3 = lambda ap: ap.rearrange("p (h d) -> p h d", d=128)
                        m3 = lambda ap: ap.rearrange("p (h d) -> p h d", d=64)
                        toks = []
                        for which in range(3):
                            bX, bXk = nb()
                            for q in range(4):
                                trb(bX[0:64, q * 128:(q + 1) * 128], dq[:, which, hh * 4 + q, cc], ["dnT%d" % which], [bXk])
                            if which < 2:
                                tkb, tkk = F(which)
                                acopy(tkb[:], bX[0:64, :], [bXk], [tkk])
                            else:
                                tkb, tkk = A_(0)
                                tt(v3(tkb), v3(bX[0:64, :]), bc(beta_c[:, hs], [64, 4, 128], 2), ALU.mult, [bXk, "tk"], [tkk])
                            toks.append((tkb, tkk))
                        (qtok, qtk), (ktok, ktk), (vb, vbk) = toks
                        for src_, srck_, c0_ in ((qtok, qtk, 0), (ktok, ktk, 4)):
                            sqb, sqk = nb()
                            act(sqb[0:64, :], src_[:], AF.Square, [srck_], [sqk])
                            p.op("dve", dur=0.65, fn=lambda e, sqb=sqb, c0_=c0_, fac=fac: e.tensor_reduce(fac[:, c0_:c0_ + 4], v3(sqb[0:64, :]), AX.X, ALU.add),
                                 reads=[sqk], writes=[fack])
                        rsqrt_to(fac[:, 0:8], fac[:, 0:8], 1.0, [fack], [fack])
                        ts(fac[:, 8:12], fac[:, 0:4], 128.0 ** -0.5, ALU.mult, [fack], [fack])
                        tt(fac[:, 12:16], fac[:, 8:12], smE[:, hh * 4:(hh + 1) * 4], ALU.mult, [fack, "smE"], [fack])
                        tt(fac[:, 16:20], fac[:, 4:8], beta_c[:, hs], ALU.mult, [fack, "tk"], [fack])
                        tt(fac[:, 16:20], fac[:, 16:20], smE[:, hh * 4:(hh + 1) * 4], ALU.mult, [fack, "smE"], [fack])
                        tt(fac[:, 20:24], fac[:, 4:8], smE[:, 8 + hh * 4:8 + (hh + 1) * 4], ALU.mult, [fack, "smE"], [fack])
                        scaled = []
                        for bi, (src, srck, f0) in enumerate(((qtok, qtk, 8), (qtok, qtk, 12), (ktok, ktk, 4), (ktok, ktk, 16), (ktok, ktk, 20))):
                            if bi < 3:
                                o, ok = tbb[bi][:], "tbb%d" % bi
                            else:
                                o, ok = A_(bi - 2)
                            tt(v3(o), v3(src[:]), bc(fac[:, f0:f0 + 4], [64, 4, 128], 2), ALU.mult, [srck, fack], [ok],
                               eng=("pool" if bi >= 3 else "dve"))
                            scaled.append((o, ok))
                        (qn, qnk), (qg, qgk), (kn, knk), (kb, kbk), (kd, kdk) = scaled
                        fTs = []
                        for j, (src, srck) in enumerate(((kn, knk), (qn, qnk), (qg, qgk))):
                            bX, bXk = nb()
                            for q in range(4):
                                trb(bX[:, q * 64:(q + 1) * 64], src[:, q * 128:(q + 1) * 128], [srck], [bXk])
                            fv, fvk = M_(j)
                            acopy(fv, bX[:, 0:256], [bXk], [fvk])
                            fTs.append((fv, fvk))
                        (knT, knTk), (qnT, qnTk), (qgT, qgTk) = fTs
                        Gl, Glk = F(2)
                        Gs, Gsk = F(3)
                        tt(m3(Gl[:, 0:256]), bc(U, [64, 4, 64], 1), bc(g_c[:, hs], [64, 4, 64], 2), ALU.mult, ["cst", "tk"], [Glk], eng="pool")
                        tt(m3(Gs[:, 0:256]), bc(G, [64, 4, 64], 1), bc(g_c[:, hs], [64, 4, 64], 2), ALU.mult, ["cst", "tk"], [Gsk], eng="pool")
                        bS1, bS1k = nb()
                        bS2, bS2k = nb()
                        for q in range(4):
                            qs = slice(q * 64, (q + 1) * 64)
                            mm(bS1[0:64, qs], Gl[:, qs], G, [Glk, "cst"], [bS1k])
                            mm(bS2[0:64, qs], Gs[:, qs], U, [Gsk, "cst"], [bS2k])
                        E1, E1k = F(4)
                        E2, E2k = F(5)
                        act(E1[:, 0:256], bS1[0:64, 0:256], AF.Exp, [bS1k], [E1k])
                        act(E2[:, 0:256], bS2[0:64, 0:256], AF.Exp, [bS2k], [E2k])
                        bK, bKk = nb()
                        bQ, bQk = nb()
                        for q in range(4):
                            qs = slice(q * 64, (q + 1) * 64)
                            mm(bK[0:64, qs], knT[:, qs], knT[:, qs], [knTk], [bKk], passes=1.0)
                            mm(bQ[0:64, qs], knT[:, qs], qnT[:, qs], [knTk, qnTk], [bQk], passes=1.0)
                        tt(m3(Gl[:, 0:256]), bc(G, [64, 4, 64], 1), bc(beta_c[:, hs], [64, 4, 64], 2), ALU.mult, ["cst", "tk"], [Glk])
                        tt(E1[:, 0:256], E1[:, 0:256], Gl[:, 0:256], ALU.mult, [E1k, Glk], [E1k])
                        Lm, Lmk = D_(0)
                        tt(Lm[:, 0:256], bK[0:64, 0:256], E1[:, 0:256], ALU.mult, [bKk, E1k], [Lmk])
                        tt(m3(E2[:, 0:256]), m3(E2[:, 0:256]), bc(U, [64, 4, 64], 1), ALU.mult, [E2k, "cst"], [E2k])
                        Xs, Xsk = D_(1)
                        tt(Xs[:, 0:256], bQ[0:64, 0:256], E2[:, 0:256], ALU.mult, [bQk, E2k], [Xsk])
                        bL, bLk = nb()
                        for q in range(4):
                            qs = slice(q * 64, (q + 1) * 64)
                            trb(bL[0:64, qs], Lm[:, qs], [Lmk], [bLk])
                        LT, LTk = D_(2)
                        acopy(LT[:, 0:256], bL[0:64, 0:256], [bLk], [LTk])
                        P, Pk = D_(3)
                        tt(m3(P[:, 0:256]), bc(I64, [64, 4, 64], 1), m3(LT[:, 0:256]), ALU.subtract, ["cst", LTk], [Pk])
                        cur, curk, curT, curTk = Lm, Lmk, LT, LTk
                        for j in range(1, 6):
                            bA, bAk = nb()
                            for q in range(4):
                                qs = slice(q * 64, (q + 1) * 64)
                                mm(bA[0:64, qs], curT[:, qs], cur[:, qs], [curTk, curk], [bAk], passes=1.0)
                            Xj, Xjk = D_({1: 4, 2: 6, 3: 4, 4: 6, 5: 4}[j])
                            acopy(Xj[:, 0:256], bA[0:64, 0:256], [bAk], [Xjk])
                            if j < 5:
                                bB, bBk = nb()
                                for q in range(4):
                                    qs = slice(q * 64, (q + 1) * 64)
                                    mm(bB[0:64, qs], cur[:, qs], curT[:, qs], [curk, curTk], [bBk], passes=1.0)
                                XjT, XjTk = D_({1: 5, 2: 7, 3: 5, 4: 7}[j])
                                vcopy(XjT[:, 0:256], bB[0:64, 0:256], [bBk], [XjTk])
                            bP, bPk = nb()
                            for q in range(4):
                                qs = slice(q * 64, (q + 1) * 64)
                                mm(bP[0:64, qs], Xj[:, qs], P[:, qs], [Xjk, Pk], [bPk], passes=1.0)
                            tt(P[:, 0:256], P[:, 0:256], bP[0:64, 0:256], ALU.add, [Pk, bPk], [Pk])
                            cur, curk = Xj, Xjk
                            if j < 5:
                                curT, curTk = XjT, XjTk
                        bW, bWk = nb()
                        for q in range(4):
                            mm(bW[:, q * 64:(q + 1) * 64], kb[:, q * 128:(q + 1) * 128], P[:, q * 64:(q + 1) * 64], [kbk, Pk], [bWk], passes=1.0)
                        wTn, wTnk = M_(3)
                        p.op("act", lambda e, wTn=wTn, bW=bW: e.mul(wTn, bW[:, 0:256], -1.0), reads=[bWk], writes=[wTnk], dur=0.47)
                        bV, bVk = nb()
                        for q in range(4):
                            h = hh * 4 + q
                            mm(bV[0:64, q * 128:(q + 1) * 128], P[:, q * 64:(q + 1) * 64], vb[:, q * 128:(q + 1) * 128], [Pk, vbk], [bVk],
                               start=True, stop=False, passes=1.0)
                            mm(bV[0:64, q * 128:(q + 1) * 128], wTn[:, q * 64:(q + 1) * 64], dstB[:, h * 128:(h + 1) * 128], [wTnk, "dstB"], [bVk],
                               start=False, stop=True, passes=1.0)
                        vnew, vnk = A_(3)
                        acopy(vnew, bV[0:64, :], [bVk], [vnk])
                        bO, bOk = nb()
                        for q in range(4):
                            h = hh * 4 + q
                            mm(bO[0:64, q * 128:(q + 1) * 128], qgT[:, q * 64:(q + 1) * 64], dstB[:, h * 128:(h + 1) * 128], [qgTk, "dstB"], [bOk],
                               start=True, stop=False, passes=1.0)
                            mm(bO[0:64, q * 128:(q + 1) * 128], Xs[:, q * 64:(q + 1) * 64], vnew[:, q * 128:(q + 1) * 128], [Xsk, vnk], [bOk],
                               start=False, stop=True, passes=1.0)
                        sqb, sqk = nb()
                        act(sqb[0:64, :], bO[0:64, :], AF.Square, [bOk], [sqk])
                        p.op("dve", dur=0.65, fn=lambda e, sqb=sqb, fac=fac: e.tensor_reduce(fac[:, 24:28], v3(sqb[0:64, :]), AX.X, ALU.add),
                             reads=[sqk], writes=[fack])
                        rsqrt_to(fac[:, 24:28], fac[:, 24:28], 1.0 / 128, [fack], [fack])
                        otb, otbk = tbb[3], "tbb3"
                        tt(v3(otb[:]), v3(bO[0:64, :]), bc(fac[:, 24:28], [64, 4, 128], 2), ALU.mult, [bOk, fack], [otbk])
                        bR, bRk = nb()
                        for q in range(4):
                            trb(bR[:, q * 64:(q + 1) * 64], otb[:, q * 128:(q + 1) * 128], [otbk], [bRk])
                        stt(yB[:, hh * 4:(hh + 1) * 4, cc], bR[:, 0:256].rearrange("p (a b) -> p a b", b=64),
                            pf[:, l, PF_DNORM:PF_DNORM + 1], dz[:, hh * 4:(hh + 1) * 4, cc], ALU.mult, ALU.mult,
                            [bRk, "pf", "dnT3"], ["yB"])
                        bU, bUk = nb()
                        for q in range(4):
                            mm(bU[:, q * 128:(q + 1) * 128], kd[:, q * 128:(q + 1) * 128], vnew[:, q * 128:(q + 1) * 128], [kdk, vnk], [bUk], passes=1.0)
                        dh = dst[:, hh * 512:(hh + 1) * 512]
                        tt(v3(dh), v3(dh), bc(elast[:, hs], [128, 4, 128], 2), ALU.mult, ["dst", "elast"], ["dst"], eng="pool")
                        tt(dh, dh, bU[:, :], ALU.add, ["dst", bUk], ["dst"])
                        acopy(dstB[:, hh * 512:(hh + 1) * 512], dh, ["dst"], ["dstB"])
                state["hh"] = 0
                state["bankset"] = None
                if l == 0 and ti == 0:
                    acopy(yT[:, :, 0:NT], yB[:, :, 0:NT], ["yB"], ["yT"])
                    dump("ydn", yT[:, :, 0:NT], ["yT"])
                branch(2)
                if l == 0 and ti == 0:
                    acopy(merged[:, :, 0:NT], mergedB[:, :, 0:NT], ["mergedB"], ["merged"])
                    dump("merged", merged[:, :, 0:NT], ["merged"])
                p.cur_tag = "S8.%d.%d" % (l, ti)
                bN, bNk = bank(7)
                for half in range(2):
                    ws, wk = wload(l, 33 + half)
                    for q in range(4):
                        d = half * 4 + q
                        bO, bOk = nb()
                        proj(bO, bOk, ws, wk, q * 128, NT, rhs=mergedB, rkey="mergedB")
                        acopy(yT[:, d, 0:NT], bO[:, 0:NT], [bOk], ["yT"])
                        i = rot("sqs", 2)
                        act(sqs[i][:, 0:NT], bO[:, 0:NT], AF.Square, [bOk], ["sqs%d" % i])
                        mm(bN[:, 0:NT], ones, sqs[i][:, 0:NT], ["cst", "sqs%d" % i], [bNk], start=(d == 0), stop=(d == KC - 1))
                rsqrt_to(rsb[:, 0, 0:NT], bN[:, 0:NT], 1.0 / D, [bNk], ["rsb"])
                for d in range(KC):
                    j = rot("mtmp", 2)
                    stt(mtmp[j][:, 0:NT], yT[:, d, 0:NT], pf[:, l, PF_NPOST + d:PF_NPOST + d + 1], rsb[:, 0, 0:NT], ALU.mult, ALU.mult,
                        ["yT", "pf", "rsb"], ["mtmp%d" % j])
                    tt(hT[:, d, cols], hT[:, d, cols], mtmp[j][:, 0:NT], ALU.add, [hk, "mtmp%d" % j], [hk], eng="pool")
                if l == 0 and ti == 0:
                    dump("h1", hT[:, :, cols], [hk])

        for r0 in range(0, S, 128):
            nr = min(128, S - r0)
            col0 = NMETA + r0
            i = rot("xstg", 1)
            skey = "merged"
            stg2 = xstg[i]
            for half in range(2):
                bk, bkey = nb()
                for q in range(4):
                    k = half * 4 + q
                    tr(bk[0:nr, q * 128:(q + 1) * 128], hT[:, k, col0:col0 + nr], hkeys(col0, col0 + nr), [bkey])
                acopy(stg2[0:nr, half * 512:(half + 1) * 512], bk[0:nr, :], [bkey], [skey])
            dma(out_d[r0:r0 + nr, :], stg2[0:nr, :], [skey], [], "och")
        p.finalize(st)
    return nc, p


def make_consts():
    c = np.zeros((128, 384), np.float32)
    c[:, 0:128] = np.eye(128, dtype=np.float32)
    c[:, 128:256] = 1.0
    a = np.arange(64)
    c[0:64, 256:320] = (a[:, None] <= a[None, :]).astype(np.float32)
    c[0:64, 320:384] = (a[:, None] > a[None, :]).astype(np.float32)
    return c


def pack_weights(inp):
    L = inp["w_in"].shape[0]
    w_in = np.asarray(inp["w_in"], np.float32)
    cols = []
    cols += list(range(O_Z, O_Z + 1024))
    cols += list(range(O_XBC, O_XBC + 1536))
    for fc in range(8):
        for base in (O_SCB, O_SCC, O_SCH, O_SCG):
            cols += list(range(base + fc * 128, base + (fc + 1) * 128))
    for h in range(8):
        for base in (O_Q, O_K, O_V, O_DZ):
            cols += list(range(base + h * 128, base + (h + 1) * 128))
    cols += list(range(O_GATE, O_GATE + 3072))
    cols = np.asarray(cols)
    assert cols.size == NW1
    w1 = np.ascontiguousarray(w_in[:, :, cols])
    wsm = np.ascontiguousarray(np.concatenate([w_in[:, :, O_DT:O_DT + 16], w_in[:, :, O_DB:O_DB + 16]], axis=-1))
    pf = np.zeros((128, L, NPF), np.float32)
    fm = lambda v, n: np.asarray(v, np.float32).reshape(n, 128).T
    for l in range(L):
        pf[:, l, PF_NPRE:PF_NPRE + 8] = fm(inp["norm_pre"][l], 8)
        pf[:, l, PF_NPOST:PF_NPOST + 8] = fm(inp["norm_post"][l], 8)
        scw = np.asarray(inp["ssd_conv_w"][l], np.float32)
        pf[:, l, PF_SCW:PF_SCW + 48] = scw.reshape(4, 12, 128).transpose(2, 1, 0).reshape(128, 48)
        pf[:, l, PF_SCB:PF_SCB + 12] = fm(inp["ssd_conv_b"][l], 12)
        pf[:, l, PF_SD:PF_SD + 8] = fm(np.repeat(np.asarray(inp["ssd_d"][l], np.float32), 64), 8)
        pf[:, l, PF_SNORM:PF_SNORM + 8] = fm(inp["ssd_norm"][l], 8)
        ccw = np.asarray(inp["sc_conv_w"][l], np.float32)
        pf[:, l, PF_CCW:PF_CCW + 24] = ccw.reshape(3, 8, 128).transpose(2, 1, 0).reshape(128, 24)
        dcw = np.asarray(inp["dn_conv_w"][l], np.float32)
        pf[:, l, PF_DCW:PF_DCW + 96] = dcw.reshape(4, 24, 128).transpose(2, 1, 0).reshape(128, 96)
        pf[:, l, PF_DNORM] = np.asarray(inp["dn_norm"][l], np.float32)
    pb = np.zeros((128, L, 48), np.float32)
    for l in range(L):
        row = np.concatenate([inp["ssd_dt_bias"][l], inp["ssd_a_log"][l], inp["dn_dt_bias"][l], inp["dn_a_log"][l]]).astype(np.float32)
        pb[:, l, :] = np.broadcast_to(row[None, :], (128, 48))
    return dict(w1=w1, wsm=wsm, wb=np.ascontiguousarray(inp["w_branch"], np.float32), wo=np.ascontiguousarray(inp["w_out"], np.float32),
                pf=pf, pb=pb, cst=make_consts(), meta=np.ascontiguousarray(inp["meta_tokens"], np.float32))


_CACHE = {}


def kernel(**inputs):
    x = np.asarray(inputs["x"], np.float32)
    B, S, _ = x.shape
    L = inputs["w_in"].shape[0]
    shared = pack_weights(inputs)
    key = (S, L)
    if key not in _CACHE:
        _CACHE[key] = build_program(S, L)[0]
    nc = _CACHE[key]
    in_maps = []
    for b in range(B):
        m = dict(shared)
        m["x"] = np.ascontiguousarray(x[b])
        in_maps.append(m)
    res = run_bass_kernel_spmd(nc, in_maps, core_ids=list(range(B)))
    return np.stack([np.asarray(r["out"], np.float32) for r in res.results], axis=0)
```

```python
import contextlib
import numpy as np
import concourse.bass as bass
import concourse.mybir as mybir

F32 = mybir.dt.float32
F32R = mybir.dt.float32r
BF16 = mybir.dt.bfloat16
ALU = mybir.AluOpType
AF = mybir.ActivationFunctionType
AX = mybir.AxisListType


class Op:
    __slots__ = ("eng", "fn", "deps", "chan", "needs_inc", "event", "idx", "dur", "succ", "nd", "ready", "fin", "pos", "lat", "tag", "aset")

    def __init__(self, eng, fn, deps, chan, dur):
        self.eng = eng
        self.fn = fn
        self.deps = deps
        self.chan = chan
        self.needs_inc = chan is not None
        self.event = None
        self.dur = dur
        self.succ = []
        self.ready = 0.0
        self.fin = 0.0


class Prog:
    ENGS = ("pe", "act", "dve", "pool", "sp")

    def __init__(self, nc, same_eng_sync=True, schedule=True, window=4000):
        self.nc = nc
        self.ops = []
        self.last_w = {}
        self.readers = {}
        self.same_eng_sync = same_eng_sync
        self.schedule = schedule
        self.window = window

    def op(self, eng, fn, reads=(), writes=(), chan=None, dur=0.1):
        deps = []
        for k in reads:
            w = self.last_w.get(k)
            if w is not None:
                deps.append(w)
        for k in writes:
            w = self.last_w.get(k)
            if w is not None:
                deps.append(w)
            deps.extend(self.readers.get(k, ()))
        o = Op(eng, fn, deps, chan, dur)
        o.tag = getattr(self, "cur_tag", "")
        o.aset = None
        o.idx = len(self.ops)
        self.ops.append(o)
        for k in reads:
            self.readers.setdefault(k, []).append(o)
        for k in writes:
            self.last_w[k] = o
            self.readers[k] = []
        return o

    def _sem_edge(self, d, o):
        if d.chan is None and o.chan is None and d.eng == o.eng:
            if o.eng == "pe" or not self.same_eng_sync:
                return False
        return True

    def _list_schedule(self):
        import heapq
        ops = self.ops
        for o in ops:
            ds = []
            seen = set()
            for d in o.deps:
                if d is o or id(d) in seen:
                    continue
                seen.add(id(d))
                ds.append(d)
            o.deps = ds
            o.nd = len(ds)
            for d in ds:
                d.succ.append(o)
        SEM_LAT = 0.12
        cp = [0.0] * len(ops)
        for o in reversed(ops):
            m = 0.0
            for s_ in o.succ:
                if cp[s_.idx] > m:
                    m = cp[s_.idx]
            cp[o.idx] = m + (o.dur + (2.0 if o.chan is not None else 0.0)) + 0.1
        self.cp_len = max(cp) if cp else 0.0
        free = {e: 0.0 for e in self.ENGS}
        pend = {e: [] for e in self.ENGS}
        avail = {e: [] for e in self.ENGS}
        order = {e: [] for e in self.ENGS}
        dma_pipe = 0.0
        for o in ops:
            if o.nd == 0:
                heapq.heappush(pend[o.eng], (0.0, o.idx))
        nleft = len(ops)
        lo = 0
        done = [False] * (len(ops) + 1)
        W = self.window
        TBL = 1.3
        cur_set = [None]
        cand_l = {e: [] for e in self.ENGS}
        for e in self.ENGS:
            while pend[e]:
                cand_l[e].append(heapq.heappop(pend[e])[1])
        while nleft:
            while done[lo]:
                lo += 1
            best = None
            for e in self.ENGS:
                t = free[e]
                bc_ = None
                for idx in cand_l[e]:
                    if idx >= lo + W:
                        continue
                    stt_ = max(t, ops[idx].ready)
                    if e == "act":
                        as_ = ops[idx].aset
                        if as_ is not None and as_ != cur_set[0]:
                            stt_ += TBL
                    key_ = (stt_, -cp[idx], idx)
                    if bc_ is None or key_ < bc_:
                        bc_ = key_
                if bc_ is None:
                    continue
                if best is None or bc_ < best[0]:
                    best = (bc_, e)
            st, idx, e = best[0][0], best[0][2], best[1]
            cand_l[e].remove(idx)
            o = ops[idx]
            if e == "act" and o.aset is not None:
                cur_set[0] = o.aset
            if o.chan is not None:
                dma_pipe = max(dma_pipe, st) + o.dur
                o.fin = dma_pipe + 2.0
                free[e] = st + 0.06
            else:
                o.fin = st + o.dur
                free[e] = o.fin
            o.pos = len(order[e])
            order[e].append(o)
            done[idx] = True
            nleft -= 1
            for s in o.succ:
                lat = SEM_LAT if self._sem_edge(o, s) else 0.0
                if o.fin + lat > s.ready:
                    s.ready = o.fin + lat
                s.nd -= 1
                if s.nd == 0:
                    cand_l[s.eng].append(s.idx)
        self.sim_time = max(o.fin for o in ops)
        return order

    def finalize(self, stack):
        nc = self.nc
        if self.schedule:
            order = self._list_schedule()
        else:
            order = {e: [] for e in self.ENGS}
            for o in self.ops:
                seen = set()
                ds = []
                for d in o.deps:
                    if d is o or id(d) in seen:
                        continue
                    seen.add(id(d))
                    ds.append(d)
                o.deps = ds
                o.pos = len(order[o.eng])
                order[o.eng].append(o)
        for o in self.ops:
            o.deps = [d for d in o.deps if self._sem_edge(d, o)]
        sems = {}
        cnt = {}

        def getsem(name):
            if name not in sems:
                sems[name] = stack.enter_context(nc.semaphore(name))
                cnt[name] = 0
            return sems[name]

        chan_pos = {}
        for e in self.ENGS:
            for o in order[e]:
                if o.chan is not None:
                    chan_pos[o.chan] = chan_pos.get(o.chan, 0) + 1
                    o.lat = chan_pos[o.chan]
        per = {}
        nw = 0
        for e in self.ENGS:
            wpos = {}
            lst = []
            for o in order[e]:
                best = {}
                for d in o.deps:
                    if d.chan is not None:
                        st_, ps_ = d.chan, d.lat
                    else:
                        st_, ps_ = "e_" + d.eng, d.pos
                    if wpos.get(st_, -1) >= ps_:
                        continue
                    if st_ not in best or best[st_][0] < ps_:
                        best[st_] = (ps_, d)
                need = []
                for st_, (ps_, d) in best.items():
                    wpos[st_] = ps_
                    d.needs_inc = True
                    need.append(d)
                nw += len(need)
                lst.append((o, need))
            per[e] = lst
        for e in self.ENGS:
            for o in order[e]:
                if o.chan is not None:
                    getsem(o.chan)
                    cnt[o.chan] += 16
                    o.event = (o.chan, cnt[o.chan])
                elif o.needs_inc:
                    nm = "e_" + o.eng
                    getsem(nm)
                    cnt[nm] += 1
                    o.event = (nm, cnt[nm])
        self.nwaits = nw
        self.sem_max = dict(cnt)
        assert max(cnt.values()) < 30000, cnt

        def emit(eng_obj, lst):
            for o, need in lst:
                for d in need:
                    eng_obj.wait_ge(sems[d.event[0]], d.event[1])
                ins = o.fn(eng_obj)
                if o.event is not None:
                    if o.chan is not None:
                        ins.then_inc(sems[o.chan], 16)
                    else:
                        ins.then_inc(sems[o.event[0]], 1)

        with nc.Block() as block:
            @block.tensor
            def _(e):
                emit(e, per["pe"])

            @block.scalar
            def _(e):
                emit(e, per["act"])

            @block.vector
            def _(e):
                emit(e, per["dve"])

            @block.gpsimd
            def _(e):
                emit(e, per["pool"])

            @block.sync
            def _(e):
                emit(e, per["sp"])
                for s, v in cnt.items():
                    if v > 0:
                        e.wait_ge(sems[s], v)


from concourse.bass_utils import run_bass_kernel_spmd

D = 1024
KC = 8
NMETA = 16
EPS = 1e-6
PF_NPRE, PF_NPOST, PF_SCW, PF_SCB, PF_SD, PF_SNORM, PF_CCW, PF_DCW, PF_DNORM, NPF = 0, 8, 16, 64, 76, 84, 92, 116, 212, 213
O_Z, O_XBC, O_DT, O_SCB, O_SCC, O_SCH, O_SCG, O_Q, O_K, O_V, O_DZ, O_DB, O_DA, O_GATE = (
    0, 1024, 2560, 2576, 3600, 4624, 5648, 6672, 7696, 8720, 9744, 10768, 10776, 10784)
NW1 = 13824


def build_program(S, L, NTM=256, dbg=None, same_eng_sync=True, schedule=True, window=100000):
    TOK = NMETA + S
    TP = ((TOK + 63) // 64) * 64
    tiles = []
    c = 0
    while c < TP:
        n = min(NTM, TP - c)
        tiles.append((c, n))
        c += n
    nc = bass.Bass("TRN2", target_bir_lowering=False)
    dt_in = lambda n, s: nc.dram_tensor(n, s, F32, kind="ExternalInput").ap()
    x_d = dt_in("x", [S, D])
    meta_d = dt_in("meta", [NMETA, D])
    w1_d = dt_in("w1", [L, D, NW1])
    wsm_d = dt_in("wsm", [L, D, 32])
    wb_d = dt_in("wb", [L, 3, D, D])
    wo_d = dt_in("wo", [L, D, D])
    pf_d = dt_in("pf", [128, L, NPF])
    pb_d = dt_in("pb", [128, L, 48])
    cst_d = dt_in("cst", [128, 384])
    out_d = nc.dram_tensor("out", [S, D], F32, kind="ExternalOutput").ap()
    NBLK = 35
    wbf_d = nc.dram_tensor("wbf", [L, NBLK, 128, KC * 512], BF16, kind="Internal").ap()
    dbg_d = {}
    if dbg:
        for name, shp in dbg.items():
            dbg_d[name] = nc.dram_tensor("dbg_" + name, list(shp), F32, kind="ExternalOutput").ap()

    with contextlib.ExitStack() as st:
        def sb(n, s):
            return st.enter_context(nc.sbuf_tensor("s_" + n, list(s), F32))
        hT = sb("hT", [128, KC, TP])
        NWS = 3
        sbb = lambda n, shp: st.enter_context(nc.sbuf_tensor("s_" + n, list(shp), BF16))
        wsl = [sbb("wsl%d" % i, [128, KC, 512]) for i in range(NWS)]
        xnT = sbb("xnT", [128, KC, NTM])
        yB = sbb("yB", [128, KC, NTM])
        mergedB = sbb("mergedB", [128, KC, NTM])
        bcT = sb("bcT", [128, 4, NTM])
        yT = sb("yT", [128, KC, NTM])
        merged = sb("merged", [128, KC, NTM])
        xstg = [merged[:].rearrange("p a b -> p (a b)")[:, 0:1024]]
        dq = sbb("dq", [128, 3, 8, NTM])
        dz = sbb("dz", [128, 8, NTM])
        xsT = dq[:, 0]
        siluz = dz[:]
        cstB = sbb("cstB", [128, 384])
        NTMB = 16
        fpool = [sb("fp%d" % i, [64, 512]) for i in range(6)]
        BA = [[sbb("ba%d_%d" % (h_, i), [64, 512]) for i in range(4)] for h_ in range(2)]
        BD = [[sbb("bd%d_%d" % (h_, i), [64, 256]) for i in range(8)] for h_ in range(2)]
        BFm = [[sbb("bm%d_%d" % (h_, i), [128, 256]) for i in range(4)] for h_ in range(2)]
        tbb = [sbb("tbb%d" % i, [64, 512]) for i in range(4)]
        sst = sb("sst", [128, 1024])
        dst = sb("dst", [128, 1024])
        dstB = sbb("dstB", [128, 1024])
        pre = [sb("pre%d" % i, [128, NTM + 3]) for i in range(4)]
        cacc = [sb("cacc%d" % i, [128, NTM]) for i in range(3)]
        sqs = [sb("sqs%d" % i, [128, NTM]) for i in range(2)]
        rsb = sb("rsb", [128, 2, NTM])
        sig = [sb("sig%d" % i, [128, NTM]) for i in range(2)]
        mtmp = [sb("mtmp%d" % i, [128, NTM]) for i in range(2)]
        tails = sb("tails", [128, 44, 3])
        pf = sb("pf", [128, L, NPF])
        pbt = sb("pbt", [128, L, 48])
        negA = sb("negA", [128, L, 24])
        cst = sb("cst", [128, 384])
        wsm = sbb("wsm", [128, L, KC, 32])
        tsm = sb("tsm", [64, NTM // 64, 32])
        tk = sb("tk", [64, NTM // 64, 48])
        smE = sb("smE", [64, 32])
        elast = sb("elast", [128, 16])
        facs = [sb("fac%d" % i, [64, 32]) for i in range(2)]
        ps = st.enter_context(nc.psum_tensor("ps", [128, 4096], F32))

        ident = cst[:, 0:128]
        ones = cst[:, 128:256]
        U = cst[0:64, 256:320]
        G = cst[0:64, 320:384]
        I64 = cst[0:64, 0:64]

        p = Prog(nc, same_eng_sync=same_eng_sync, schedule=schedule, window=window)
        state = {"rr": 0, "tm": 0, "ws": 0, "i2": {}}

        def bank(i):
            return ps[:, i * 512:(i + 1) * 512], "b%d" % i

        def nb():
            bs_ = state.get("bankset")
            if bs_ is None:
                i = state["rr"]
                state["rr"] = (i + 1) % 6
                return bank(i)
            j = state["i2"].get(("bs", bs_), 0)
            state["i2"][("bs", bs_)] = (j + 1) % len(bs_)
            return bank(bs_[j])

        def rot(name, n):
            i = state["i2"].get(name, 0)
            state["i2"][name] = (i + 1) % n
            return i

        def Bv(t, n):
            return t[:].bitcast(BF16)[:, 0:n]

        def F(i):
            return fpool[i], "fp%d" % i

        def A_(i):
            h_ = state["hh"]
            return BA[h_][i][:], "ba%d_%d" % (h_, i)

        def D_(i):
            h_ = state["hh"]
            return BD[h_][i][:], "bd%d_%d" % (h_, i)

        def M_(i):
            h_ = state["hh"]
            return BFm[h_][i][:], "bm%d_%d" % (h_, i)

        def fsz(ap):
            n = 1
            for d in ap.shape[1:]:
                n *= int(d)
            return n

        PASSES = 4.0

        def mm(out, lhsT, rhs, r, w, start=True, stop=True, passes=PASSES):
            p.op("pe", lambda e: e.matmul(out, lhsT, rhs, start=start, stop=stop), reads=r, writes=w,
                 dur=max(fsz(rhs), 64) * passes / 2400.0 + 0.03)

        def tr(out, in_, r, w):
            n = in_.shape[0]
            p.op("pe", lambda e: e.transpose(out, in_, ident[0:n, 0:n]), reads=r + ["cst"], writes=w, dur=0.09)

        def trb(out, in_, r, w):
            n = in_.shape[0]
            p.op("pe", lambda e: e.matmul(out, in_, cstB[0:n, 0:n], start=True, stop=True), reads=r + ["cstB"], writes=w,
                 dur=max(n, 64) / 2400.0 + 0.03)

        def act(out, in_, func, r, w, bias=None, scale=None):
            kw = {}
            if bias is not None:
                kw["bias"] = bias
            if scale is not None:
                kw["scale"] = scale
            o_ = p.op("act", lambda e: e.activation(out, in_, func, **kw), reads=r, writes=w, dur=0.2 + fsz(out) / 960.0)
            o_.aset = ASET.get(func)

        ASET = {AF.Silu: "silu", AF.Sigmoid: "sig", AF.Exp: "el", AF.Ln: "el", AF.Sqrt: "sqrt"}

        def acopy(out, in_, r, w):
            p.op("act", lambda e: e.copy(out, in_), reads=r, writes=w, dur=0.2 + fsz(out) / 960.0)

        def edur(eng, n):
            return (0.12 + n / 960.0) if eng == "dve" else (0.2 + n / 400.0)

        def tt(out, a, b, op, r, w, eng="dve"):
            p.op(eng, lambda e: e.tensor_tensor(out, a, b, op), reads=r, writes=w, dur=edur(eng, fsz(out)))

        def ts(out, a, s1, op0, r, w, s2=None, op1=None, eng="dve"):
            if op1 is None:
                p.op(eng, lambda e: e.tensor_scalar(out, a, s1, None, op0), reads=r, writes=w, dur=edur(eng, fsz(out)))
            else:
                p.op(eng, lambda e: e.tensor_scalar(out, a, s1, s2, op0, op1), reads=r, writes=w, dur=edur(eng, fsz(out)))

        def stt(out, in0, scalar, in1, op0, op1, r, w):
            p.op("dve", lambda e: e.scalar_tensor_tensor(out, in0, scalar, in1, op0, op1), reads=r, writes=w,
                 dur=edur("dve", fsz(out)))

        def recip(out, in_, r, w):
            p.op("dve", lambda e: e.reciprocal(out, in_), reads=r, writes=w, dur=edur("dve", fsz(out)))

        def vcopy(out, in_, r, w, eng="dve"):
            p.op(eng, lambda e: e.tensor_copy(out, in_), reads=r, writes=w, dur=edur(eng, fsz(out)))

        def memset(ap, val, w, eng="pool"):
            p.op(eng, lambda e: e.memset(ap, val), writes=w, dur=edur(eng, fsz(ap)))

        def dma(out, in_, r, w, chan, eng="sp"):
            nbytes = (2 if out.dtype == BF16 else 4) * fsz(out) * int(out.shape[0]) * (3 if eng == "pool" else 1)
            p.op(eng, lambda e: e.dma_start(out=out, in_=in_), reads=r, writes=w, chan=chan, dur=nbytes / 250e3)

        def dump(name, ap, keys):
            if name in dbg_d:
                dma(dbg_d[name], ap, keys, [], "dbg")

        def rsqrt_to(out, in_, scale, r, w):
            act(out, in_, AF.Ln, r, w, bias=EPS, scale=scale)
            act(out, out, AF.Exp, w, w, scale=-0.5)

        dma(cst[:], cst_d, [], ["cst"], "ld_cst")
        dma(pf[:], pf_d, [], ["pf"], "ld_pf")
        dma(pbt[:], pb_d, [], ["pbt"], "ld_pb")
        for l in range(L):
            dma(wsm[:, l], wsm_d[l].rearrange("(k p) c -> p k c", p=128), [], ["wsm%d" % l], "ld_wsm%d" % l, eng="pool")
        acopy(cstB[:], cst[:], ["cst"], ["cstB"])
        for l in range(L):
            act(negA[:, l, 0:16], pbt[:, l, 16:32], AF.Exp, ["pbt"], ["negA"])
            act(negA[:, l, 16:24], pbt[:, l, 40:48], AF.Exp, ["pbt"], ["negA"])
        ts(negA[:], negA[:], -1.0, ALU.mult, ["negA"], ["negA"])

        hkey = lambda t: "hT%d" % t

        def tile_of(col):
            for ti, (c0, n) in enumerate(tiles):
                if c0 <= col < c0 + n:
                    return ti
            raise ValueError

        def hkeys(c_lo, c_hi):
            return sorted({hkey(tile_of(c)) for c in (c_lo, c_hi - 1)} | {hkey(t) for t in range(tile_of(c_lo), tile_of(c_hi - 1) + 1)})

        if TP > TOK:
            memset(hT[:, :, TOK:TP], 0.0, hkeys(TOK, TP))
        row_blocks = [("meta", 0, NMETA, 0)] + [("x", r0, min(128, S - r0), NMETA + r0) for r0 in range(0, S, 128)]
        for (src, r0, nr, col0) in row_blocks:
            si = rot("xstg", 1)
            skey = "merged"
            stg2 = xstg[si]
            srcap = meta_d[0:nr, :] if src == "meta" else x_d[r0:r0 + nr, :]
            dma(stg2[0:nr, :], srcap, [], [skey], "xch%d" % si)
            for half in range(2):
                bk, bkey = nb()
                for q in range(4):
                    k = half * 4 + q
                    tr(bk[:, q * 128:q * 128 + nr], stg2[0:nr, k * 128:(k + 1) * 128], [skey], [bkey])
                acopy(hT[:, half * 4:half * 4 + 4, col0:col0 + nr],
                      bk.rearrange("p (a b) -> p a b", b=128)[:, :, 0:nr], [bkey], hkeys(col0, col0 + nr))

        def blk_src(l, blk):
            if blk < 27:
                return w1_d[l][:, blk * 512:(blk + 1) * 512]
            if blk < 33:
                n, half = divmod(blk - 27, 2)
                return wb_d[l, n][:, half * 512:(half + 1) * 512]
            return wo_d[l][:, (blk - 33) * 512:(blk - 32) * 512]

        use_order = [0, 1, 2, 3, 4, 21, 27, 22, 28] + list(range(5, 13)) + [23, 29, 24, 30] + list(range(13, 21)) + [25, 31, 26, 32, 33, 34]
        assert sorted(use_order) == list(range(NBLK))
        GRP = 9
        grp_of = {}
        for l in range(L):
            for gi in range(0, NBLK, GRP):
                grp = use_order[gi:gi + GRP]
                chn = "cv%d_%d" % (l, gi // GRP)
                for blk in grp:
                    dma(wbf_d[l, blk].rearrange("p (k c) -> p k c", k=KC), blk_src(l, blk).rearrange("(k p) c -> p k c", p=128),
                        [], ["wbf%d_%d" % (l, blk)], chn, eng="pool")
                for blk in grp:
                    grp_of[(l, blk)] = grp

        def wload(l, blk):
            si = state["ws"] % NWS
            state["ws"] += 1
            key = "wsl%d" % si
            dma(wsl[si][:], wbf_d[l, blk].rearrange("p (k c) -> p k c", k=KC), ["wbf%d_%d" % (l, b_) for b_ in grp_of[(l, blk)]], [key],
                "wch%d" % si)
            return wsl[si], key

        def proj(out_bank, okey, wslot, wkey, coff, NT, rhs=None, rkey="xnT"):
            rhs = xnT if rhs is None else rhs
            for k in range(KC):
                mm(out_bank[:, 0:NT], wslot[:, k, coff:coff + 128], rhs[:, k, 0:NT], [wkey, rkey], [okey],
                   start=(k == 0), stop=(k == KC - 1), passes=1.0)

        def conv(bk, bkey, ti, wcol0, K, l, NT, dest, dkey, bias_col=None, mul=None, mulkey=None):
            H = K - 1
            i = rot("pre", 4)
            pr, pk = pre[i], "pre%d" % i
            ca, ck = cacc[i % 3], "cacc%d" % (i % 3)
            vcopy(pr[:, 0:H], tails[:, ti, 0:H], ["tails%d" % ti], [pk], eng="pool")
            if mul is None:
                acopy(pr[:, H:H + NT], bk[:, 0:NT], [bkey], [pk])
            else:
                tt(pr[:, H:H + NT], bk[:, 0:NT], mul, ALU.mult, [bkey, mulkey], [pk])
            vcopy(tails[:, ti, 0:H], pr[:, NT:NT + H], [pk], ["tails%d" % ti], eng="pool")
            wl = pf[:, l, wcol0 + K - 1:wcol0 + K]
            if mul is None:
                act(ca[:, 0:NT], bk[:, 0:NT], AF.Identity, [bkey, "pf"], [ck], scale=wl)
            else:
                act(ca[:, 0:NT], pr[:, H:H + NT], AF.Identity, [pk, "pf"], [ck], scale=wl)
            for k in range(0, K - 1):
                last = (k == K - 2 and K == 3)
                o = dest if last else ca[:, 0:NT]
                ok = [dkey] if last else [ck]
                stt(o, pr[:, k:k + NT], pf[:, l, wcol0 + k:wcol0 + k + 1], ca[:, 0:NT], ALU.mult, ALU.add,
                    [pk, "pf", ck], ok)
            if K == 4:
                if bias_col is not None:
                    act(dest, ca[:, 0:NT], AF.Silu, [ck, "pf"], [dkey], bias=pf[:, l, bias_col:bias_col + 1])
                else:
                    act(dest, ca[:, 0:NT], AF.Silu, [ck], [dkey])

        bc = lambda ap, shape, axis: ap.unsqueeze(axis).to_broadcast(list(shape))

        for l in range(L):
            memset(sst[:], 0.0, ["sst"])
            memset(dst[:], 0.0, ["dst"])
            memset(dstB[:], 0.0, ["dstB"])
            memset(tails[:], 0.0, ["tails%d" % i for i in range(44)])
            w1 = w1_d[l]
            for ti, (c0, NT) in enumerate(tiles):
                nch = NT // 64
                hk = hkey(ti)
                cols = slice(c0, c0 + NT)
                p.cur_tag = "S0.%d.%d" % (l, ti)
                bN, bNk = bank(7)
                for k in range(KC):
                    i = rot("sqs", 2)
                    act(sqs[i][:, 0:NT], hT[:, k, cols], AF.Square, [hk], ["sqs%d" % i])
                    mm(bN[:, 0:NT], ones, sqs[i][:, 0:NT], ["cst", "sqs%d" % i], [bNk], start=(k == 0), stop=(k == KC - 1))
                rsqrt_to(rsb[:, 0, 0:NT], bN[:, 0:NT], 1.0 / D, [bNk], ["rsb"])
                for k in range(KC):
                    stt(xnT[:, k, 0:NT], hT[:, k, cols], pf[:, l, PF_NPRE + k:PF_NPRE + k + 1], rsb[:, 0, 0:NT],
                        ALU.mult, ALU.mult, [hk, "pf", "rsb"], ["xnT"])
                if l == 0 and ti == 0 and "xnT" in dbg_d:
                    acopy(merged[:, :, 0:NT], xnT[:, :, 0:NT], ["xnT"], ["merged"])
                    dump("xnT", merged[:, :, 0:NT], ["merged"])
                p.cur_tag = "S1.%d.%d" % (l, ti)
                bS, bSk = bank(6)
                for c in range(nch):
                    for k in range(KC):
                        mm(bS[0:64, c * 32:(c + 1) * 32], xnT[:, k, c * 64:(c + 1) * 64], wsm[:, l, k, :], ["xnT", "wsm%d" % l], [bSk],
                           start=(k == 0), stop=(k == KC - 1), passes=1.0)
                acopy(tsm[:, 0:nch, :], bS[0:64, 0:nch * 32].rearrange("p (c f) -> p c f", f=32), [bSk], ["tsm"])
                tt(tk[:, 0:nch, 0:16], tsm[:, 0:nch, 0:16], bc(pbt[0:64, l, 0:16], [64, nch, 16], 1), ALU.add, ["tsm", "pbt"], ["tk"])
                tt(tk[:, 0:nch, 40:48], tsm[:, 0:nch, 24:32], bc(pbt[0:64, l, 32:40], [64, nch, 8], 1), ALU.add, ["tsm", "pbt"], ["tk"])
                act(tk[:, 0:nch, 0:16], tk[:, 0:nch, 0:16], AF.Exp, ["tk"], ["tk"])
                act(tk[:, 0:nch, 40:48], tk[:, 0:nch, 40:48], AF.Exp, ["tk"], ["tk"])
                act(tk[:, 0:nch, 0:16], tk[:, 0:nch, 0:16], AF.Ln, ["tk"], ["tk"], bias=1.0)
                act(tk[:, 0:nch, 40:48], tk[:, 0:nch, 40:48], AF.Ln, ["tk"], ["tk"], bias=1.0)
                act(tk[:, 0:nch, 32:40], tsm[:, 0:nch, 16:24], AF.Sigmoid, ["tsm"], ["tk"])
                tt(tk[:, 0:nch, 16:32], tk[:, 0:nch, 0:16], bc(negA[0:64, l, 0:16], [64, nch, 16], 1), ALU.mult, ["tk", "negA"], ["tk"])
                tt(tk[:, 0:nch, 40:48], tk[:, 0:nch, 40:48], bc(negA[0:64, l, 16:24], [64, nch, 8], 1), ALU.mult, ["tk", "negA"], ["tk"])
                if l == 0 and ti == 0:
                    dump("tk", tk[:, 0:nch, :], ["tk"])
                p.cur_tag = "S2.%d.%d" % (l, ti)
                for blk in range(2):
                    ws, wk = wload(l, blk)
                    for q in range(4):
                        fc = blk * 4 + q
                        bk, bkey = nb()
                        proj(bk, bkey, ws, wk, q * 128, NT)
                        act(siluz[:, fc, 0:NT], bk[:, 0:NT], AF.Silu, [bkey], ["dnT3"])
                for blk in range(3):
                    ws, wk = wload(l, 2 + blk)
                    for q in range(4):
                        fc = blk * 4 + q
                        bk, bkey = nb()
                        proj(bk, bkey, ws, wk, q * 128, NT)
                        if fc < 8:
                            dest, dkey = xsT[:, fc, 0:NT], "dnT0"
                        else:
                            dest, dkey = bcT[:, fc - 8, 0:NT], "bcT"
                        conv(bk, bkey, fc, PF_SCW + fc * 4, 4, l, NT, dest, dkey, bias_col=PF_SCB + fc)
                if l == 0 and ti == 0:
                    pass
                    dump("bcT", bcT[:, :, 0:NT], ["bcT"])
                p.cur_tag = "S3.%d.%d" % (l, ti)
                for c in range(nch):
                    cc = slice(c * 64, (c + 1) * 64)
                    dt_c = tk[:, c, 0:16]
                    dtA_c = tk[:, c, 16:32]
                    bM, bMk = bank(6)
                    mm(bM[0:64, 0:16], U, dtA_c, ["cst", "tk"], [bMk])
                    mm(bM[0:64, 16:32], G, dtA_c, ["cst", "tk"], [bMk])
                    mm(bM[0:128, 32:48], ones[0:64, :], dtA_c, ["cst", "tk"], [bMk])
                    act(smE[:, 0:32], bM[0:64, 0:32], AF.Exp, [bMk], ["smE"])
                    act(elast[:, 0:16], bM[:, 32:48], AF.Exp, [bMk], ["elast"])
                    for g in range(2):
                        hs = slice(g * 8, (g + 1) * 8)
                        state["tm"] = 0
                        bX, bXk = nb()
                        for q in range(4):
                            trb(bX[0:64, q * 128:(q + 1) * 128], xsT[:, g * 4 + q, cc], ["dnT0"], [bXk])
                        bB, bBk = nb()
                        tr(bB[0:64, 0:128], bcT[:, g, cc], ["bcT"], [bBk])
                        xc, xck = Bv(fpool[0], 512), "fp0"
                        tt(xc.rearrange("p (h d) -> p h d", d=64), bX[0:64, :].rearrange("p (h d) -> p h d", d=64),
                           bc(dt_c[:, hs], [64, 8, 64], 2), ALU.mult, [bXk, "tk"], [xck])
                        xd, xdk = Bv(fpool[1], 512), "fp1"
                        tt(xd.rearrange("p (h d) -> p h d", d=64), xc.rearrange("p (h d) -> p h d", d=64),
                           bc(smE[:, 16 + g * 8:16 + (g + 1) * 8], [64, 8, 64], 2), ALU.mult, [xck, "smE"], [xdk], eng="pool")
                        btok, btk = Bv(fpool[2], 128), "fp2"
                        acopy(btok, bB[0:64, 0:128], [bBk], [btk])
                        gm, gmk = F(3)
                        tt(gm[:].rearrange("p (h d) -> p h d", d=64), bc(G, [64, 8, 64], 1), bc(dtA_c[:, hs], [64, 8, 64], 2),
                           ALU.mult, ["cst", "tk"], [gmk], eng="pool")
                        bE, bEk = nb()
                        for h in range(8):
                            mm(bE[0:64, h * 64:(h + 1) * 64], gm[:, h * 64:(h + 1) * 64], U, [gmk, "cst"], [bEk])
                        E, Ek = F(4)
                        act(E[:], bE[0:64, :], AF.Exp, [bEk], [Ek])
                        bC, bCk = nb()
                        mm(bC[0:64, 0:64], bcT[:, g, cc], bcT[:, 2 + g, cc], ["bcT"], [bCk])
                        cbm, cbk = F(2)[0][:, 128:192], "fp2"
                        tt(cbm[:, 0:64], bC[0:64, 0:64], U, ALU.mult, [bCk, "cst"], [cbk])
                        MT, MTk = Bv(fpool[5], 512), "fp5"
                        tt(MT.rearrange("p (h d) -> p h d", d=64), E[:].rearrange("p (h d) -> p h d", d=64),
                           bc(cbm[:, 0:64], [64, 8, 64], 1), ALU.mult, [Ek, cbk], [MTk])
                        bY, bYk = nb()
                        for h in range(8):
                            mm(bY[0:64, h * 64:(h + 1) * 64], MT[:, h * 64:(h + 1) * 64], xc[:, h * 64:(h + 1) * 64], [MTk, xck], [bYk], passes=1.0)
                        bT, bTk = nb()
                        mm(bT[0:64, :], bcT[:, 2 + g, cc], sst[:, g * 512:(g + 1) * 512], ["bcT", "sst"], [bTk])
                        tmp, tmpk = F(4)
                        tt(tmp[:].rearrange("p (h d) -> p h d", d=64), bT[0:64, :].rearrange("p (h d) -> p h d", d=64),
                           bc(smE[:, g * 8:(g + 1) * 8], [64, 8, 64], 2), ALU.mult, [bTk, "smE"], [tmpk])
                        ytok, ytk = tbb[3], "tbb3"
                        tt(ytok[:], bY[0:64, :], tmp[:], ALU.add, [bYk, tmpk], [ytk])
                        bR, bRk = nb()
                        for q in range(4):
                            trb(bR[:, q * 64:(q + 1) * 64], ytok[:, q * 128:(q + 1) * 128], [ytk], [bRk])
                        acopy(yT[:, g * 4:(g + 1) * 4, cc], bR[:, 0:256].rearrange("p (a b) -> p a b", b=64), [bRk], ["yT"])
                        bU, bUk = nb()
                        mm(bU[:, :], btok, xd, [btk, xdk], [bUk], passes=1.0)
                        sg_ = sst[:, g * 512:(g + 1) * 512]
                        tt(sg_.rearrange("p (h d) -> p h d", d=64), sg_.rearrange("p (h d) -> p h d", d=64),
                           bc(elast[:, hs], [128, 8, 64], 2), ALU.mult, ["sst", "elast"], ["sst"], eng="pool")
                        tt(sg_, sg_, bU[:, :], ALU.add, ["sst", bUk], ["sst"])
                p.cur_tag = "S3b.%d.%d" % (l, ti)
                for fc in range(KC):
                    stt(yT[:, fc, 0:NT], xsT[:, fc, 0:NT], pf[:, l, PF_SD + fc:PF_SD + fc + 1], yT[:, fc, 0:NT], ALU.mult, ALU.add,
                        ["dnT0", "pf", "yT"], ["yT"])
                tt(yT[:, :, 0:NT], yT[:, :, 0:NT], siluz[:, :, 0:NT], ALU.mult, ["yT", "dnT3"], ["yT"])
                for g in range(2):
                    bN, bNk = bank(7)
                    for q in range(4):
                        fc = g * 4 + q
                        i = rot("sqs", 2)
                        act(sqs[i][:, 0:NT], yT[:, fc, 0:NT], AF.Square, ["yT"], ["sqs%d" % i])
                        mm(bN[:, 0:NT], ones, sqs[i][:, 0:NT], ["cst", "sqs%d" % i], [bNk], start=(q == 0), stop=(q == 3))
                    rsqrt_to(rsb[:, g, 0:NT], bN[:, 0:NT], 1.0 / 512, [bNk], ["rsb"])
                for fc in range(KC):
                    stt(yB[:, fc, 0:NT], yT[:, fc, 0:NT], pf[:, l, PF_SNORM + fc:PF_SNORM + fc + 1], rsb[:, fc // 4, 0:NT],
                        ALU.mult, ALU.mult, ["yT", "pf", "rsb"], ["yB"])
                if l == 0 and ti == 0:
                    acopy(yT[:, :, 0:NT], yB[:, :, 0:NT], ["yB"], ["yT"])
                    dump("yssd", yT[:, :, 0:NT], ["yT"])

                def branch(n):
                    p.cur_tag = "BR%d.%d.%d" % (n, l, ti)
                    for half in range(2):
                        gs, gk = wload(l, 21 + n * 2 + half)
                        bs, bk_ = wload(l, 27 + n * 2 + half)
                        for q in range(4):
                            d = half * 4 + q
                            bG, bGk = nb()
                            proj(bG, bGk, gs, gk, q * 128, NT)
                            i = rot("sig", 2)
                            act(sig[i][:, 0:NT], bG[:, 0:NT], AF.Sigmoid, [bGk], ["sig%d" % i])
                            bB2, bB2k = nb()
                            proj(bB2, bB2k, bs, bk_, q * 128, NT, rhs=yB, rkey="yB")
                            if n == 0:
                                tt(merged[:, d, 0:NT], bB2[:, 0:NT], sig[i][:, 0:NT], ALU.mult, [bB2k, "sig%d" % i], ["merged"])
                            else:
                                j = rot("mtmp", 2)
                                tt(mtmp[j][:, 0:NT], bB2[:, 0:NT], sig[i][:, 0:NT], ALU.mult, [bB2k, "sig%d" % i], ["mtmp%d" % j])
                                if n == 2:
                                    tt(mergedB[:, d, 0:NT], merged[:, d, 0:NT], mtmp[j][:, 0:NT], ALU.add, ["merged", "mtmp%d" % j], ["mergedB"], eng="pool")
                                else:
                                    tt(merged[:, d, 0:NT], merged[:, d, 0:NT], mtmp[j][:, 0:NT], ALU.add, ["merged", "mtmp%d" % j], ["merged"], eng="pool")

                branch(0)
                p.cur_tag = "S6.%d.%d" % (l, ti)
                for h in range(8):
                    ws, wk = wload(l, 13 + h)
                    for which in range(3):
                        bk, bkey = nb()
                        proj(bk, bkey, ws, wk, which * 128, NT)
                        fcq = which * 8 + h
                        conv(bk, bkey, 20 + fcq, PF_DCW + fcq * 4, 4, l, NT, dq[:, which, h, 0:NT], "dnT%d" % which)
                    bk, bkey = nb()
                    proj(bk, bkey, ws, wk, 384, NT)
                    act(dz[:, h, 0:NT], bk[:, 0:NT], AF.Silu, [bkey], ["dnT3"])
                if l == 0 and ti == 0:
                    pass
                p.cur_tag = "S5.%d.%d" % (l, ti)
                for fc in range(KC):
                    ws, wk = wload(l, 5 + fc)
                    bH, bHk = nb()
                    proj(bH, bHk, ws, wk, 256, NT)
                    i = rot("sig", 2)
                    acopy(sig[i][:, 0:NT], bH[:, 0:NT], [bHk], ["sig%d" % i])
                    bC, bCk = nb()
                    proj(bC, bCk, ws, wk, 128, NT)
                    j = rot("mtmp", 2)
                    conv(bC, bCk, 12 + fc, PF_CCW + fc * 3, 3, l, NT, mtmp[j][:, 0:NT], "mtmp%d" % j, mul=sig[i][:, 0:NT], mulkey="sig%d" % i)
                    bB, bBk = nb()
                    proj(bB, bBk, ws, wk, 0, NT)
                    tt(yT[:, fc, 0:NT], bB[:, 0:NT], mtmp[j][:, 0:NT], ALU.mult, [bBk, "mtmp%d" % j], ["yT"])
                    bG, bGk = nb()
                    proj(bG, bGk, ws, wk, 384, NT)
                    i = rot("sig", 2)
                    act(sig[i][:, 0:NT], bG[:, 0:NT], AF.Silu, [bGk], ["sig%d" % i])
                    tt(yB[:, fc, 0:NT], yT[:, fc, 0:NT], sig[i][:, 0:NT], ALU.mult, ["yT", "sig%d" % i], ["yB"])
                if l == 0 and ti == 0:
                    acopy(yT[:, :, 0:NT], yB[:, :, 0:NT], ["yB"], ["yT"])
                    dump("ysc", yT[:, :, 0:NT], ["yT"])
                branch(1)
                p.cur_tag = "S7.%d.%d" % (l, ti)
                for c in range(nch):
                    cc = slice(c * 64, (c + 1) * 64)
                    beta_c = tk[:, c, 32:40]
                    g_c = tk[:, c, 40:48]
                    bM, bMk = bank(6)
                    mm(bM[0:64, 0:8], U, g_c, ["cst", "tk"], [bMk])
                    mm(bM[0:64, 8:16], G, g_c, ["cst", "tk"], [bMk])
                    mm(bM[0:128, 16:24], ones[0:64, :], g_c, ["cst", "tk"], [bMk])
                    act(smE[:, 0:16], bM[0:64, 0:16], AF.Exp, [bMk], ["smE"])
                    act(elast[:, 0:8], bM[:, 16:24], AF.Exp, [bMk], ["elast"])
                    for hh in range(2):
                        hs = slice(hh * 4, (hh + 1) * 4)
                        state["hh"] = hh
                        state["bankset"] = (0, 1, 2) if hh == 0 else (3, 4, 5)
                        fac = facs[hh]
                        fack = "fac%d" % hh
                        v3 = lambda ap: ap.rearrange("p (h d) -> p h d", d=128)
                        m3 = lambda ap: ap.rearrange("p (h d) -> p h d", d=64)
                        toks = []
                        for which in range(3):
                            bX, bXk = nb()
                            for q in range(4):
                                trb(bX[0:64, q * 128:(q + 1) * 128], dq[:, which, hh * 4 + q, cc], ["dnT%d" % which], [bXk])
                            if which < 2:
                                tkb, tkk = F(which)
                                acopy(tkb[:], bX[0:64, :], [bXk], [tkk])
                            else:
                                tkb, tkk = A_(0)
                                tt(v3(tkb), v3(bX[0:64, :]), bc(beta_c[:, hs], [64, 4, 128], 2), ALU.mult, [bXk, "tk"], [tkk])
                            toks.append((tkb, tkk))
                        (qtok, qtk), (ktok, ktk), (vb, vbk) = toks
                        for src_, srck_, c0_ in ((qtok, qtk, 0), (ktok, ktk, 4)):
                            sqb, sqk = nb()
                            act(sqb[0:64, :], src_[:], AF.Square, [srck_], [sqk])
                            p.op("dve", dur=0.65, fn=lambda e, sqb=sqb, c0_=c0_, fac=fac: e.tensor_reduce(fac[:, c0_:c0_ + 4], v3(sqb[0:64, :]), AX.X, ALU.add),
                                 reads=[sqk], writes=[fack])
                        rsqrt_to(fac[:, 0:8], fac[:, 0:8], 1.0, [fack], [fack])
                        ts(fac[:, 8:12], fac[:, 0:4], 128.0 ** -0.5, ALU.mult, [fack], [fack])
                        tt(fac[:, 12:16], fac[:, 8:12], smE[:, hh * 4:(hh + 1) * 4], ALU.mult, [fack, "smE"], [fack])
                        tt(fac[:, 16:20], fac[:, 4:8], beta_c[:, hs], ALU.mult, [fack, "tk"], [fack])
                        tt(fac[:, 16:20], fac[:, 16:20], smE[:, hh * 4:(hh + 1) * 4], ALU.mult, [fack, "smE"], [fack])
                        tt(fac[:, 20:24], fac[:, 4:8], smE[:, 8 + hh * 4:8 + (hh + 1) * 4], ALU.mult, [fack, "smE"], [fack])
                        scaled = []
                        for bi, (src, srck, f0) in enumerate(((qtok, qtk, 8), (qtok, qtk, 12), (ktok, ktk, 4), (ktok, ktk, 16), (ktok, ktk, 20))):
                            if bi < 3:
                                o, ok = tbb[bi][:], "tbb%d" % bi
                            else:
                                o, ok = A_(bi - 2)
                            tt(v3(o), v3(src[:]), bc(fac[:, f0:f0 + 4], [64, 4, 128], 2), ALU.mult, [srck, fack], [ok],
                               eng=("pool" if bi >= 3 else "dve"))
                            scaled.append((o, ok))
                        (qn, qnk), (qg, qgk), (kn, knk), (kb, kbk), (kd, kdk) = scaled
                        fTs = []
                        for j, (src, srck) in enumerate(((kn, knk), (qn, qnk), (qg, qgk))):
                            bX, bXk = nb()
                            for q in range(4):
                                trb(bX[:, q * 64:(q + 1) * 64], src[:, q * 128:(q + 1) * 128], [srck], [bXk])
                            fv, fvk = M_(j)
                            acopy(fv, bX[:, 0:256], [bXk], [fvk])
                            fTs.append((fv, fvk))
                        (knT, knTk), (qnT, qnTk), (qgT, qgTk) = fTs
                        Gl, Glk = F(2)
                        Gs, Gsk = F(3)
                        tt(m3(Gl[:, 0:256]), bc(U, [64, 4, 64], 1), bc(g_c[:, hs], [64, 4, 64], 2), ALU.mult, ["cst", "tk"], [Glk], eng="pool")
                        tt(m3(Gs[:, 0:256]), bc(G, [64, 4, 64], 1), bc(g_c[:, hs], [64, 4, 64], 2), ALU.mult, ["cst", "tk"], [Gsk], eng="pool")
                        bS1, bS1k = nb()
                        bS2, bS2k = nb()
                        for q in range(4):
                            qs = slice(q * 64, (q + 1) * 64)
                            mm(bS1[0:64, qs], Gl[:, qs], G, [Glk, "cst"], [bS1k])
                            mm(bS2[0:64, qs], Gs[:, qs], U, [Gsk, "cst"], [bS2k])
                        E1, E1k = F(4)
                        E2, E2k = F(5)
                        act(E1[:, 0:256], bS1[0:64, 0:256], AF.Exp, [bS1k], [E1k])
                        act(E2[:, 0:256], bS2[0:64, 0:256], AF.Exp, [bS2k], [E2k])
                        bK, bKk = nb()
                        bQ, bQk = nb()
                        for q in range(4):
                            qs = slice(q * 64, (q + 1) * 64)
                            mm(bK[0:64, qs], knT[:, qs], knT[:, qs], [knTk], [bKk], passes=1.0)
                            mm(bQ[0:64, qs], knT[:, qs], qnT[:, qs], [knTk, qnTk], [bQk], passes=1.0)
                        tt(m3(Gl[:, 0:256]), bc(G, [64, 4, 64], 1), bc(beta_c[:, hs], [64, 4, 64], 2), ALU.mult, ["cst", "tk"], [Glk])
                        tt(E1[:, 0:256], E1[:, 0:256], Gl[:, 0:256], ALU.mult, [E1k, Glk], [E1k])
                        Lm, Lmk = D_(0)
                        tt(Lm[:, 0:256], bK[0:64, 0:256], E1[:, 0:256], ALU.mult, [bKk, E1k], [Lmk])
                        tt(m3(E2[:, 0:256]), m3(E2[:, 0:256]), bc(U, [64, 4, 64], 1), ALU.mult, [E2k, "cst"], [E2k])
                        Xs, Xsk = D_(1)
                        tt(Xs[:, 0:256], bQ[0:64, 0:256], E2[:, 0:256], ALU.mult, [bQk, E2k], [Xsk])
                        bL, bLk = nb()
                        for q in range(4):
                            qs = slice(q * 64, (q + 1) * 64)
                            trb(bL[0:64, qs], Lm[:, qs], [Lmk], [bLk])
                        LT, LTk = D_(2)
                        acopy(LT[:, 0:256], bL[0:64, 0:256], [bLk], [LTk])
                        P, Pk = D_(3)
                        tt(m3(P[:, 0:256]), bc(I64, [64, 4, 64], 1), m3(LT[:, 0:256]), ALU.subtract, ["cst", LTk], [Pk])
                        cur, curk, curT, curTk = Lm, Lmk, LT, LTk
                        for j in range(1, 6):
                            bA, bAk = nb()
                            for q in range(4):
                                qs = slice(q * 64, (q + 1) * 64)
                                mm(bA[0:64, qs], curT[:, qs], cur[:, qs], [curTk, curk], [bAk], passes=1.0)
                            Xj, Xjk = D_({1: 4, 2: 6, 3: 4, 4: 6, 5: 4}[j])
                            acopy(Xj[:, 0:256], bA[0:64, 0:256], [bAk], [Xjk])
                            if j < 5:
                                bB, bBk = nb()
                                for q in range(4):
                                    qs = slice(q * 64, (q + 1) * 64)
                                    mm(bB[0:64, qs], cur[:, qs], curT[:, qs], [curk, curTk], [bBk], passes=1.0)
                                XjT, XjTk = D_({1: 5, 2: 7, 3: 5, 4: 7}[j])
                                vcopy(XjT[:, 0:256], bB[0:64, 0:256], [bBk], [XjTk])
                            bP, bPk = nb()
                            for q in range(4):
                                qs = slice(q * 64, (q + 1) * 64)
                                mm(bP[0:64, qs], Xj[:, qs], P[:, qs], [Xjk, Pk], [bPk], passes=1.0)
                            tt(P[:, 0:256], P[:, 0:256], bP[0:64, 0:256], ALU.add, [Pk, bPk], [Pk])
                            cur, curk = Xj, Xjk
                            if j < 5:
                                curT, curTk = XjT, XjTk
                        bW, bWk = nb()
                        for q in range(4):
                            mm(bW[:, q * 64:(q + 1) * 64], kb[:, q * 128:(q + 1) * 128], P[:, q * 64:(q + 1) * 64], [kbk, Pk], [bWk], passes=1.0)
                        wTn, wTnk = M_(3)
                        p.op("act", lambda e, wTn=wTn, bW=bW: e.mul(wTn, bW[:, 0:256], -1.0), reads=[bWk], writes=[wTnk], dur=0.47)
                        bV, bVk = nb()
                        for q in range(4):
                            h = hh * 4 + q
                            mm(bV[0:64, q * 128:(q + 1) * 128], P[:, q * 64:(q + 1) * 64], vb[:, q * 128:(q + 1) * 128], [Pk, vbk], [bVk],
                               start=True, stop=False, passes=1.0)
                            mm(bV[0:64, q * 128:(q + 1) * 128], wTn[:, q * 64:(q + 1) * 64], dstB[:, h * 128:(h + 1) * 128], [wTnk, "dstB"], [bVk],
                               start=False, stop=True, passes=1.0)
                        vnew, vnk = A_(3)
                        acopy(vnew, bV[0:64, :], [bVk], [vnk])
                        bO, bOk = nb()
                        for q in range(4):
                            h = hh * 4 + q
                            mm(bO[0:64, q * 128:(q + 1) * 128], qgT[:, q * 64:(q + 1) * 64], dstB[:, h * 128:(h + 1) * 128], [qgTk, "dstB"], [bOk],
                               start=True, stop=False, passes=1.0)
                            mm(bO[0:64, q * 128:(q + 1) * 128], Xs[:, q * 64:(q + 1) * 64], vnew[:, q * 128:(q + 1) * 128], [Xsk, vnk], [bOk],
                               start=False, stop=True, passes=1.0)
                        sqb, sqk = nb()
                        act(sqb[0:64, :], bO[0:64, :], AF.Square, [bOk], [sqk])
                        p.op("dve", dur=0.65, fn=lambda e, sqb=sqb, fac=fac: e.tensor_reduce(fac[:, 24:28], v3(sqb[0:64, :]), AX.X, ALU.add),
                             reads=[sqk], writes=[fack])
                        rsqrt_to(fac[:, 24:28], fac[:, 24:28], 1.0 / 128, [fack], [fack])
                        otb, otbk = tbb[3], "tbb3"
                        tt(v3(otb[:]), v3(bO[0:64, :]), bc(fac[:, 24:28], [64, 4, 128], 2), ALU.mult, [bOk, fack], [otbk])
                        bR, bRk = nb()
                        for q in range(4):
                            trb(bR[:, q * 64:(q + 1) * 64], otb[:, q * 128:(q + 1) * 128], [otbk], [bRk])
                        stt(yB[:, hh * 4:(hh + 1) * 4, cc], bR[:, 0:256].rearrange("p (a b) -> p a b", b=64),
                            pf[:, l, PF_DNORM:PF_DNORM + 1], dz[:, hh * 4:(hh + 1) * 4, cc], ALU.mult, ALU.mult,
                            [bRk, "pf", "dnT3"], ["yB"])
                        bU, bUk = nb()
                        for q in range(4):
                            mm(bU[:, q * 128:(q + 1) * 128], kd[:, q * 128:(q + 1) * 128], vnew[:, q * 128:(q + 1) * 128], [kdk, vnk], [bUk], passes=1.0)
                        dh = dst[:, hh * 512:(hh + 1) * 512]
                        tt(v3(dh), v3(dh), bc(elast[:, hs], [128, 4, 128], 2), ALU.mult, ["dst", "elast"], ["dst"], eng="pool")
                        tt(dh, dh, bU[:, :], ALU.add, ["dst", bUk], ["dst"])
                        acopy(dstB[:, hh * 512:(hh + 1) * 512], dh, ["dst"], ["dstB"])
                state["hh"] = 0
                state["bankset"] = None
                if l == 0 and ti == 0:
                    acopy(yT[:, :, 0:NT], yB[:, :, 0:NT], ["yB"], ["yT"])
                    dump("ydn", yT[:, :, 0:NT], ["yT"])
                branch(2)
                if l == 0 and ti == 0:
                    acopy(merged[:, :, 0:NT], mergedB[:, :, 0:NT], ["mergedB"], ["merged"])
                    dump("merged", merged[:, :, 0:NT], ["merged"])
                p.cur_tag = "S8.%d.%d" % (l, ti)
                bN, bNk = bank(7)
                for half in range(2):
                    ws, wk = wload(l, 33 + half)
                    for q in range(4):
                        d = half * 4 + q
                        bO, bOk = nb()
                        proj(bO, bOk, ws, wk, q * 128, NT, rhs=mergedB, rkey="mergedB")
                        acopy(yT[:, d, 0:NT], bO[:, 0:NT], [bOk], ["yT"])
                        i = rot("sqs", 2)
                        act(sqs[i][:, 0:NT], bO[:, 0:NT], AF.Square, [bOk], ["sqs%d" % i])
                        mm(bN[:, 0:NT], ones, sqs[i][:, 0:NT], ["cst", "sqs%d" % i], [bNk], start=(d == 0), stop=(d == KC - 1))
                rsqrt_to(rsb[:, 0, 0:NT], bN[:, 0:NT], 1.0 / D, [bNk], ["rsb"])
                for d in range(KC):
                    j = rot("mtmp", 2)
                    stt(mtmp[j][:, 0:NT], yT[:, d, 0:NT], pf[:, l, PF_NPOST + d:PF_NPOST + d + 1], rsb[:, 0, 0:NT], ALU.mult, ALU.mult,
                        ["yT", "pf", "rsb"], ["mtmp%d" % j])
                    tt(hT[:, d, cols], hT[:, d, cols], mtmp[j][:, 0:NT], ALU.add, [hk, "mtmp%d" % j], [hk], eng="pool")
                if l == 0 and ti == 0:
                    dump("h1", hT[:, :, cols], [hk])

        for r0 in range(0, S, 128):
            nr = min(128, S - r0)
            col0 = NMETA + r0
            i = rot("xstg", 1)
            skey = "merged"
            stg2 = xstg[i]
            for half in range(2):
                bk, bkey = nb()
                for q in range(4):
                    k = half * 4 + q
                    tr(bk[0:nr, q * 128:(q + 1) * 128], hT[:, k, col0:col0 + nr], hkeys(col0, col0 + nr), [bkey])
                acopy(stg2[0:nr, half * 512:(half + 1) * 512], bk[0:nr, :], [bkey], [skey])
            dma(out_d[r0:r0 + nr, :], stg2[0:nr, :], [skey], [], "och")
        p.finalize(st)
    return nc, p


def make_consts():
    c = np.zeros((128, 384), np.float32)
    c[:, 0:128] = np.eye(128, dtype=np.float32)
    c[:, 128:256] = 1.0
    a = np.arange(64)
    c[0:64, 256:320] = (a[:, None] <= a[None, :]).astype(np.float32)
    c[0:64, 320:384] = (a[:, None] > a[None, :]).astype(np.float32)
    return c


def pack_weights(inp):
    L = inp["w_in"].shape[0]
    w_in = np.asarray(inp["w_in"], np.float32)
    cols = []
    cols += list(range(O_Z, O_Z + 1024))
    cols += list(range(O_XBC, O_XBC + 1536))
    for fc in range(8):
        for base in (O_SCB, O_SCC, O_SCH, O_SCG):
            cols += list(range(base + fc * 128, base + (fc + 1) * 128))
    for h in range(8):
        for base in (O_Q, O_K, O_V, O_DZ):
            cols += list(range(base + h * 128, base + (h + 1) * 128))
    cols += list(range(O_GATE, O_GATE + 3072))
    cols = np.asarray(cols)
    assert cols.size == NW1
    w1 = np.ascontiguousarray(w_in[:, :, cols])
    wsm = np.ascontiguousarray(np.concatenate([w_in[:, :, O_DT:O_DT + 16], w_in[:, :, O_DB:O_DB + 16]], axis=-1))
    pf = np.zeros((128, L, NPF), np.float32)
    fm = lambda v, n: np.asarray(v, np.float32).reshape(n, 128).T
    for l in range(L):
        pf[:, l, PF_NPRE:PF_NPRE + 8] = fm(inp["norm_pre"][l], 8)
        pf[:, l, PF_NPOST:PF_NPOST + 8] = fm(inp["norm_post"][l], 8)
        scw = np.asarray(inp["ssd_conv_w"][l], np.float32)
        pf[:, l, PF_SCW:PF_SCW + 48] = scw.reshape(4, 12, 128).transpose(2, 1, 0).reshape(128, 48)
        pf[:, l, PF_SCB:PF_SCB + 12] = fm(inp["ssd_conv_b"][l], 12)
        pf[:, l, PF_SD:PF_SD + 8] = fm(np.repeat(np.asarray(inp["ssd_d"][l], np.float32), 64), 8)
        pf[:, l, PF_SNORM:PF_SNORM + 8] = fm(inp["ssd_norm"][l], 8)
        ccw = np.asarray(inp["sc_conv_w"][l], np.float32)
        pf[:, l, PF_CCW:PF_CCW + 24] = ccw.reshape(3, 8, 128).transpose(2, 1, 0).reshape(128, 24)
        dcw = np.asarray(inp["dn_conv_w"][l], np.float32)
        pf[:, l, PF_DCW:PF_DCW + 96] = dcw.reshape(4, 24, 128).transpose(2, 1, 0).reshape(128, 96)
        pf[:, l, PF_DNORM] = np.asarray(inp["dn_norm"][l], np.float32)
    pb = np.zeros((128, L, 48), np.float32)
    for l in range(L):
        row = np.concatenate([inp["ssd_dt_bias"][l], inp["ssd_a_log"][l], inp["dn_dt_bias"][l], inp["dn_a_log"][l]]).astype(np.float32)
        pb[:, l, :] = np.broadcast_to(row[None, :], (128, 48))
    return dict(w1=w1, wsm=wsm, wb=np.ascontiguousarray(inp["w_branch"], np.float32), wo=np.ascontiguousarray(inp["w_out"], np.float32),
                pf=pf, pb=pb, cst=make_consts(), meta=np.ascontiguousarray(inp["meta_tokens"], np.float32))


_CACHE = {}


def kernel(**inputs):
    x = np.asarray(inputs["x"], np.float32)
    B, S, _ = x.shape
    L = inputs["w_in"].shape[0]
    shared = pack_weights(inputs)
    key = (S, L)
    if key not in _CACHE:
        _CACHE[key] = build_program(S, L)[0]
    nc = _CACHE[key]
    in_maps = []
    for b in range(B):
        m = dict(shared)
        m["x"] = np.ascontiguousarray(x[b])
        in_maps.append(m)
    res = run_bass_kernel_spmd(nc, in_maps, core_ids=list(range(B)))
    return np.stack([np.asarray(r["out"], np.float32) for r in res.results], axis=0)
```

```python
import contextlib
import numpy as np
import concourse.bass as bass
import concourse.mybir as mybir

F32 = mybir.dt.float32
F32R = mybir.dt.float32r
BF16 = mybir.dt.bfloat16
ALU = mybir.AluOpType
AF = mybir.ActivationFunctionType
AX = mybir.AxisListType


class Op:
    __slots__ = ("eng", "fn", "deps", "chan", "needs_inc", "event", "idx", "dur", "succ", "nd", "ready", "fin", "pos", "lat", "tag", "aset")

    def __init__(self, eng, fn, deps, chan, dur):
        self.eng = eng
        self.fn = fn
        self.deps = deps
        self.chan = chan
        self.needs_inc = chan is not None
        self.event = None
        self.dur = dur
        self.succ = []
        self.ready = 0.0
        self.fin = 0.0


class Prog:
    ENGS = ("pe", "act", "dve", "pool", "sp")

    def __init__(self, nc, same_eng_sync=True, schedule=True, window=4000):
        self.nc = nc
        self.ops = []
        self.last_w = {}
        self.readers = {}
        self.same_eng_sync = same_eng_sync
        self.schedule = schedule
        self.window = window

    def op(self, eng, fn, reads=(), writes=(), chan=None, dur=0.1):
        deps = []
        for k in reads:
            w = self.last_w.get(k)
            if w is not None:
                deps.append(w)
        for k in writes:
            w = self.last_w.get(k)
            if w is not None:
                deps.append(w)
            deps.extend(self.readers.get(k, ()))
        o = Op(eng, fn, deps, chan, dur)
        o.tag = getattr(self, "cur_tag", "")
        o.aset = None
        o.idx = len(self.ops)
        self.ops.append(o)
        for k in reads:
            self.readers.setdefault(k, []).append(o)
        for k in writes:
            self.last_w[k] = o
            self.readers[k] = []
        return o

    def _sem_edge(self, d, o):
        if d.chan is None and o.chan is None and d.eng == o.eng:
            if o.eng == "pe" or not self.same_eng_sync:
                return False
        return True

    def _list_schedule(self):
        import heapq
        ops = self.ops
        for o in ops:
            ds = []
            seen = set()
            for d in o.deps:
                if d is o or id(d) in seen:
                    continue
                seen.add(id(d))
                ds.append(d)
            o.deps = ds
            o.nd = len(ds)
            for d in ds:
                d.succ.append(o)
        SEM_LAT = 0.12
        cp = [0.0] * len(ops)
        for o in reversed(ops):
            m = 0.0
            for s_ in o.succ:
                if cp[s_.idx] > m:
                    m = cp[s_.idx]
            cp[o.idx] = m + (o.dur + (2.0 if o.chan is not None else 0.0)) + 0.1
        self.cp_len = max(cp) if cp else 0.0
        free = {e: 0.0 for e in self.ENGS}
        pend = {e: [] for e in self.ENGS}
        avail = {e: [] for e in self.ENGS}
        order = {e: [] for e in self.ENGS}
        dma_pipe = 0.0
        for o in ops:
            if o.nd == 0:
                heapq.heappush(pend[o.eng], (0.0, o.idx))
        nleft = len(ops)
        lo = 0
        done = [False] * (len(ops) + 1)
        W = self.window
        TBL = 1.3
        cur_set = [None]
        cand_l = {e: [] for e in self.ENGS}
        for e in self.ENGS:
            while pend[e]:
                cand_l[e].append(heapq.heappop(pend[e])[1])
        while nleft:
            while done[lo]:
                lo += 1
            best = None
            for e in self.ENGS:
                t = free[e]
                bc_ = None
                for idx in cand_l[e]:
                    if idx >= lo + W:
                        continue
                    stt_ = max(t, ops[idx].ready)
                    if e == "act":
                        as_ = ops[idx].aset
                        if as_ is not None and as_ != cur_set[0]:
                            stt_ += TBL
                    key_ = (stt_, -cp[idx], idx)
                    if bc_ is None or key_ < bc_:
                        bc_ = key_
                if bc_ is None:
                    continue
                if best is None or bc_ < best[0]:
                    best = (bc_, e)
            st, idx, e = best[0][0], best[0][2], best[1]
            cand_l[e].remove(idx)
            o = ops[idx]
            if e == "act" and o.aset is not None:
                cur_set[0] = o.aset
            if o.chan is not None:
                dma_pipe = max(dma_pipe, st) + o.dur
                o.fin = dma_pipe + 2.0
                free[e] = st + 0.06
            else:
                o.fin = st + o.dur
                free[e] = o.fin
            o.pos = len(order[e])
            order[e].append(o)
            done[idx] = True
            nleft -= 1
            for s in o.succ:
                lat = SEM_LAT if self._sem_edge(o, s) else 0.0
                if o.fin + lat > s.ready:
                    s.ready = o.fin + lat
                s.nd -= 1
                if s.nd == 0:
                    cand_l[s.eng].append(s.idx)
        self.sim_time = max(o.fin for o in ops)
        return order

    def finalize(self, stack):
        nc = self.nc
        if self.schedule:
            order = self._list_schedule()
        else:
            order = {e: [] for e in self.ENGS}
            for o in self.ops:
                seen = set()
                ds = []
                for d in o.deps:
                    if d is o or id(d) in seen:
                        continue
                    seen.add(id(d))
                    ds.append(d)
                o.deps = ds
                o.pos = len(order[o.eng])
                order[o.eng].append(o)
        for o in self.ops:
            o.deps = [d for d in o.deps if self._sem_edge(d, o)]
        sems = {}
        cnt = {}

        def getsem(name):
            if name not in sems:
                sems[name] = stack.enter_context(nc.semaphore(name))
                cnt[name] = 0
            return sems[name]

        chan_pos = {}
        for e in self.ENGS:
            for o in order[e]:
                if o.chan is not None:
                    chan_pos[o.chan] = chan_pos.get(o.chan, 0) + 1
                    o.lat = chan_pos[o.chan]
        per = {}
        nw = 0
        for e in self.ENGS:
            wpos = {}
            lst = []
            for o in order[e]:
                best = {}
                for d in o.deps:
                    if d.chan is not None:
                        st_, ps_ = d.chan, d.lat
                    else:
                        st_, ps_ = "e_" + d.eng, d.pos
                    if wpos.get(st_, -1) >= ps_:
                        continue
                    if st_ not in best or best[st_][0] < ps_:
                        best[st_] = (ps_, d)
                need = []
                for st_, (ps_, d) in best.items():
                    wpos[st_] = ps_
                    d.needs_inc = True
                    need.append(d)
                nw += len(need)
                lst.append((o, need))
            per[e] = lst
        for e in self.ENGS:
            for o in order[e]:
                if o.chan is not None:
                    getsem(o.chan)
                    cnt[o.chan] += 16
                    o.event = (o.chan, cnt[o.chan])
                elif o.needs_inc:
                    nm = "e_" + o.eng
                    getsem(nm)
                    cnt[nm] += 1
                    o.event = (nm, cnt[nm])
        self.nwaits = nw
        self.sem_max = dict(cnt)
        assert max(cnt.values()) < 30000, cnt

        def emit(eng_obj, lst):
            for o, need in lst:
                for d in need:
                    eng_obj.wait_ge(sems[d.event[0]], d.event[1])
                ins = o.fn(eng_obj)
                if o.event is not None:
                    if o.chan is not None:
                        ins.then_inc(sems[o.chan], 16)
                    else:
                        ins.then_inc(sems[o.event[0]], 1)

        with nc.Block() as block:
            @block.tensor
            def _(e):
                emit(e, per["pe"])

            @block.scalar
            def _(e):
                emit(e, per["act"])

            @block.vector
            def _(e):
                emit(e, per["dve"])

            @block.gpsimd
            def _(e):
                emit(e, per["pool"])

            @block.sync
            def _(e):
                emit(e, per["sp"])
                for s, v in cnt.items():
                    if v > 0:
                        e.wait_ge(sems[s], v)


from concourse.bass_utils import run_bass_kernel_spmd

D = 1024
KC = 8
NMETA = 16
EPS = 1e-6
PF_NPRE, PF_NPOST, PF_SCW, PF_SCB, PF_SD, PF_SNORM, PF_CCW, PF_DCW, PF_DNORM, NPF = 0, 8, 16, 64, 76, 84, 92, 116, 212, 213
O_Z, O_XBC, O_DT, O_SCB, O_SCC, O_SCH, O_SCG, O_Q, O_K, O_V, O_DZ, O_DB, O_DA, O_GATE = (
    0, 1024, 2560, 2576, 3600, 4624, 5648, 6672, 7696, 8720, 9744, 10768, 10776, 10784)
NW1 = 13824


def build_program(S, L, NTM=256, dbg=None, same_eng_sync=True, schedule=True, window=100000):
    TOK = NMETA + S
    TP = ((TOK + 63) // 64) * 64
    tiles = []
    c = 0
    while c < TP:
        n = min(NTM, TP - c)
        tiles.append((c, n))
        c += n
    nc = bass.Bass("TRN2", target_bir_lowering=False)
    dt_in = lambda n, s: nc.dram_tensor(n, s, F32, kind="ExternalInput").ap()
    x_d = dt_in("x", [S, D])
    meta_d = dt_in("meta", [NMETA, D])
    w1_d = dt_in("w1", [L, D, NW1])
    wsm_d = dt_in("wsm", [L, D, 32])
    wb_d = dt_in("wb", [L, 3, D, D])
    wo_d = dt_in("wo", [L, D, D])
    pf_d = dt_in("pf", [128, L, NPF])
    pb_d = dt_in("pb", [128, L, 48])
    cst_d = dt_in("cst", [128, 384])
    out_d = nc.dram_tensor("out", [S, D], F32, kind="ExternalOutput").ap()
    NBLK = 35
    wbf_d = nc.dram_tensor("wbf", [L, NBLK, 128, KC * 512], BF16, kind="Internal").ap()
    dbg_d = {}
    if dbg:
        for name, shp in dbg.items():
            dbg_d[name] = nc.dram_tensor("dbg_" + name, list(shp), F32, kind="ExternalOutput").ap()

    with contextlib.ExitStack() as st:
        def sb(n, s):
            return st.enter_context(nc.sbuf_tensor("s_" + n, list(s), F32))
        hT = sb("hT", [128, KC, TP])
        NWS = 3
        sbb = lambda n, shp: st.enter_context(nc.sbuf_tensor("s_" + n, list(shp), BF16))
        wsl = [sbb("wsl%d" % i, [128, KC, 512]) for i in range(NWS)]
        xnT = sbb("xnT", [128, KC, NTM])
        yB = sbb("yB", [128, KC, NTM])
        mergedB = sbb("mergedB", [128, KC, NTM])
        bcT = sb("bcT", [128, 4, NTM])
        yT = sb("yT", [128, KC, NTM])
        merged = sb("merged", [128, KC, NTM])
        xstg = [merged[:].rearrange("p a b -> p (a b)")[:, 0:1024]]
        dq = sbb("dq", [128, 3, 8, NTM])
        dz = sbb("dz", [128, 8, NTM])
        xsT = dq[:, 0]
        siluz = dz[:]
        cstB = sbb("cstB", [128, 384])
        NTMB = 16
        fpool = [sb("fp%d" % i, [64, 512]) for i in range(6)]
        BA = [[sbb("ba%d_%d" % (h_, i), [64, 512]) for i in range(4)] for h_ in range(2)]
        BD = [[sbb("bd%d_%d" % (h_, i), [64, 256]) for i in range(8)] for h_ in range(2)]
        BFm = [[sbb("bm%d_%d" % (h_, i), [128, 256]) for i in range(4)] for h_ in range(2)]
        tbb = [sbb("tbb%d" % i, [64, 512]) for i in range(4)]
        sst = sb("sst", [128, 1024])
        dst = sb("dst", [128, 1024])
        dstB = sbb("dstB", [128, 1024])
        pre = [sb("pre%d" % i, [128, NTM + 3]) for i in range(4)]
        cacc = [sb("cacc%d" % i, [128, NTM]) for i in range(3)]
        sqs = [sb("sqs%d" % i, [128, NTM]) for i in range(2)]
        rsb = sb("rsb", [128, 2, NTM])
        sig = [sb("sig%d" % i, [128, NTM]) for i in range(2)]
        mtmp = [sb("mtmp%d" % i, [128, NTM]) for i in range(2)]
        tails = sb("tails", [128, 44, 3])
        pf = sb("pf", [128, L, NPF])
        pbt = sb("pbt", [128, L, 48])
        negA = sb("negA", [128, L, 24])
        cst = sb("cst", [128, 384])
        wsm = sbb("wsm", [128, L, KC, 32])
        tsm = sb("tsm", [64, NTM // 64, 32])
        tk = sb("tk", [64, NTM // 64, 48])
        smE = sb("smE", [64, 32])
        elast = sb("elast", [128, 16])
        facs = [sb("fac%d" % i, [64, 32]) for i in range(2)]
        ps = st.enter_context(nc.psum_tensor("ps", [128, 4096], F32))

        ident = cst[:, 0:128]
        ones = cst[:, 128:256]
        U = cst[0:64, 256:320]
        G = cst[0:64, 320:384]
        I64 = cst[0:64, 0:64]

        p = Prog(nc, same_eng_sync=same_eng_sync, schedule=schedule, window=window)
        state = {"rr": 0, "tm": 0, "ws": 0, "i2": {}}

        def bank(i):
            return ps[:, i * 512:(i + 1) * 512], "b%d" % i

        def nb():
            bs_ = state.get("bankset")
            if bs_ is None:
                i = state["rr"]
                state["rr"] = (i + 1) % 6
                return bank(i)
            j = state["i2"].get(("bs", bs_), 0)
            state["i2"][("bs", bs_)] = (j + 1) % len(bs_)
            return bank(bs_[j])

        def rot(name, n):
            i = state["i2"].get(name, 0)
            state["i2"][name] = (i + 1) % n
            return i

        def Bv(t, n):
            return t[:].bitcast(BF16)[:, 0:n]

        def F(i):
            return fpool[i], "fp%d" % i

        def A_(i):
            h_ = state["hh"]
            return BA[h_][i][:], "ba%d_%d" % (h_, i)

        def D_(i):
            h_ = state["hh"]
            return BD[h_][i][:], "bd%d_%d" % (h_, i)

        def M_(i):
            h_ = state["hh"]
            return BFm[h_][i][:], "bm%d_%d" % (h_, i)

        def fsz(ap):
            n = 1
            for d in ap.shape[1:]:
                n *= int(d)
            return n

        PASSES = 4.0

        def mm(out, lhsT, rhs, r, w, start=True, stop=True, passes=PASSES):
            p.op("pe", lambda e: e.matmul(out, lhsT, rhs, start=start, stop=stop), reads=r, writes=w,
                 dur=max(fsz(rhs), 64) * passes / 2400.0 + 0.03)

        def tr(out, in_, r, w):
            n = in_.shape[0]
            p.op("pe", lambda e: e.transpose(out, in_, ident[0:n, 0:n]), reads=r + ["cst"], writes=w, dur=0.09)

        def trb(out, in_, r, w):
            n = in_.shape[0]
            p.op("pe", lambda e: e.matmul(out, in_, cstB[0:n, 0:n], start=True, stop=True), reads=r + ["cstB"], writes=w,
                 dur=max(n, 64) / 2400.0 + 0.03)

        def act(out, in_, func, r, w, bias=None, scale=None):
            kw = {}
            if bias is not None:
                kw["bias"] = bias
            if scale is not None:
                kw["scale"] = scale
            o_ = p.op("act", lambda e: e.activation(out, in_, func, **kw), reads=r, writes=w, dur=0.2 + fsz(out) / 960.0)
            o_.aset = ASET.get(func)

        ASET = {AF.Silu: "silu", AF.Sigmoid: "sig", AF.Exp: "el", AF.Ln: "el", AF.Sqrt: "sqrt"}

        def acopy(out, in_, r, w):
            p.op("act", lambda e: e.copy(out, in_), reads=r, writes=w, dur=0.2 + fsz(out) / 960.0)

        def edur(eng, n):
            return (0.12 + n / 960.0) if eng == "dve" else (0.2 + n / 400.0)

        def tt(out, a, b, op, r, w, eng="dve"):
            p.op(eng, lambda e: e.tensor_tensor(out, a, b, op), reads=r, writes=w, dur=edur(eng, fsz(out)))

        def ts(out, a, s1, op0, r, w, s2=None, op1=None, eng="dve"):
            if op1 is None:
                p.op(eng, lambda e: e.tensor_scalar(out, a, s1, None, op0), reads=r, writes=w, dur=edur(eng, fsz(out)))
            else:
                p.op(eng, lambda e: e.tensor_scalar(out, a, s1, s2, op0, op1), reads=r, writes=w, dur=edur(eng, fsz(out)))

        def stt(out, in0, scalar, in1, op0, op1, r, w):
            p.op("dve", lambda e: e.scalar_tensor_tensor(out, in0, scalar, in1, op0, op1), reads=r, writes=w,
                 dur=edur("dve", fsz(out)))

        def recip(out, in_, r, w):
            p.op("dve", lambda e: e.reciprocal(out, in_), reads=r, writes=w, dur=edur("dve", fsz(out)))

        def vcopy(out, in_, r, w, eng="dve"):
            p.op(eng, lambda e: e.tensor_copy(out, in_), reads=r, writes=w, dur=edur(eng, fsz(out)))

        def memset(ap, val, w, eng="pool"):
            p.op(eng, lambda e: e.memset(ap, val), writes=w, dur=edur(eng, fsz(ap)))

        def dma(out, in_, r, w, chan, eng="sp"):
            nbytes = (2 if out.dtype == BF16 else 4) * fsz(out) * int(out.shape[0]) * (3 if eng == "pool" else 1)
            p.op(eng, lambda e: e.dma_start(out=out, in_=in_), reads=r, writes=w, chan=chan, dur=nbytes / 250e3)

        def dump(name, ap, keys):
            if name in dbg_d:
                dma(dbg_d[name], ap, keys, [], "dbg")

        def rsqrt_to(out, in_, scale, r, w):
            act(out, in_, AF.Ln, r, w, bias=EPS, scale=scale)
            act(out, out, AF.Exp, w, w, scale=-0.5)

        dma(cst[:], cst_d, [], ["cst"], "ld_cst")
        dma(pf[:], pf_d, [], ["pf"], "ld_pf")
        dma(pbt[:], pb_d, [], ["pbt"], "ld_pb")
        for l in range(L):
            dma(wsm[:, l], wsm_d[l].rearrange("(k p) c -> p k c", p=128), [], ["wsm%d" % l], "ld_wsm%d" % l, eng="pool")
        acopy(cstB[:], cst[:], ["cst"], ["cstB"])
        for l in range(L):
            act(negA[:, l, 0:16], pbt[:, l, 16:32], AF.Exp, ["pbt"], ["negA"])
            act(negA[:, l, 16:24], pbt[:, l, 40:48], AF.Exp, ["pbt"], ["negA"])
        ts(negA[:], negA[:], -1.0, ALU.mult, ["negA"], ["negA"])

        hkey = lambda t: "hT%d" % t

        def tile_of(col):
            for ti, (c0, n) in enumerate(tiles):
                if c0 <= col < c0 + n:
                    return ti
            raise ValueError

        def hkeys(c_lo, c_hi):
            return sorted({hkey(tile_of(c)) for c in (c_lo, c_hi - 1)} | {hkey(t) for t in range(tile_of(c_lo), tile_of(c_hi - 1) + 1)})

        if TP > TOK:
            memset(hT[:, :, TOK:TP], 0.0, hkeys(TOK, TP))
        row_blocks = [("meta", 0, NMETA, 0)] + [("x", r0, min(128, S - r0), NMETA + r0) for r0 in range(0, S, 128)]
        for (src, r0, nr, col0) in row_blocks:
            si = rot("xstg", 1)
            skey = "merged"
            stg2 = xstg[si]
            srcap = meta_d[0:nr, :] if src == "meta" else x_d[r0:r0 + nr, :]
            dma(stg2[0:nr, :], srcap, [], [skey], "xch%d" % si)
            for half in range(2):
                bk, bkey = nb()
                for q in range(4):
                    k = half * 4 + q
                    tr(bk[:, q * 128:q * 128 + nr], stg2[0:nr, k * 128:(k + 1) * 128], [skey], [bkey])
                acopy(hT[:, half * 4:half * 4 + 4, col0:col0 + nr],
                      bk.rearrange("p (a b) -> p a b", b=128)[:, :, 0:nr], [bkey], hkeys(col0, col0 + nr))

        def blk_src(l, blk):
            if blk < 27:
                return w1_d[l][:, blk * 512:(blk + 1) * 512]
            if blk < 33:
                n, half = divmod(blk - 27, 2)
                return wb_d[l, n][:, half * 512:(half + 1) * 512]
            return wo_d[l][:, (blk - 33) * 512:(blk - 32) * 512]

        use_order = [0, 1, 2, 3, 4, 21, 27, 22, 28] + list(range(5, 13)) + [23, 29, 24, 30] + list(range(13, 21)) + [25, 31, 26, 32, 33, 34]
        assert sorted(use_order) == list(range(NBLK))
        GRP = 9
        grp_of = {}
        for l in range(L):
            for gi in range(0, NBLK, GRP):
                grp = use_order[gi:gi + GRP]
                chn = "cv%d_%d" % (l, gi // GRP)
                for blk in grp:
                    dma(wbf_d[l, blk].rearrange("p (k c) -> p k c", k=KC), blk_src(l, blk).rearrange("(k p) c -> p k c", p=128),
                        [], ["wbf%d_%d" % (l, blk)], chn, eng="pool")
                for blk in grp:
                    grp_of[(l, blk)] = grp

        def wload(l, blk):
            si = state["ws"] % NWS
            state["ws"] += 1
            key = "wsl%d" % si
            dma(wsl[si][:], wbf_d[l, blk].rearrange("p (k c) -> p k c", k=KC), ["wbf%d_%d" % (l, b_) for b_ in grp_of[(l, blk)]], [key],
                "wch%d" % si)
            return wsl[si], key

        def proj(out_bank, okey, wslot, wkey, coff, NT, rhs=None, rkey="xnT"):
            rhs = xnT if rhs is None else rhs
            for k in range(KC):
                mm(out_bank[:, 0:NT], wslot[:, k, coff:coff + 128], rhs[:, k, 0:NT], [wkey, rkey], [okey],
                   start=(k == 0), stop=(k == KC - 1), passes=1.0)

        def conv(bk, bkey, ti, wcol0, K, l, NT, dest, dkey, bias_col=None, mul=None, mulkey=None):
            H = K - 1
            i = rot("pre", 4)
            pr, pk = pre[i], "pre%d" % i
            ca, ck = cacc[i % 3], "cacc%d" % (i % 3)
            vcopy(pr[:, 0:H], tails[:, ti, 0:H], ["tails%d" % ti], [pk], eng="pool")
            if mul is None:
                acopy(pr[:, H:H + NT], bk[:, 0:NT], [bkey], [pk])
            else:
                tt(pr[:, H:H + NT], bk[:, 0:NT], mul, ALU.mult, [bkey, mulkey], [pk])
            vcopy(tails[:, ti, 0:H], pr[:, NT:NT + H], [pk], ["tails%d" % ti], eng="pool")
            wl = pf[:, l, wcol0 + K - 1:wcol0 + K]
            if mul is None:
                act(ca[:, 0:NT], bk[:, 0:NT], AF.Identity, [bkey, "pf"], [ck], scale=wl)
            else:
                act(ca[:, 0:NT], pr[:, H:H + NT], AF.Identity, [pk, "pf"], [ck], scale=wl)
            for k in range(0, K - 1):
                last = (k == K - 2 and K == 3)
                o = dest if last else ca[:, 0:NT]
                ok = [dkey] if last else [ck]
                stt(o, pr[:, k:k + NT], pf[:, l, wcol0 + k:wcol0 + k + 1], ca[:, 0:NT], ALU.mult, ALU.add,
                    [pk, "pf", ck], ok)
            if K == 4:
                if bias_col is not None:
                    act(dest, ca[:, 0:NT], AF.Silu, [ck, "pf"], [dkey], bias=pf[:, l, bias_col:bias_col + 1])
                else:
                    act(dest, ca[:, 0:NT], AF.Silu, [ck], [dkey])

        bc = lambda ap, shape, axis: ap.unsqueeze(axis).to_broadcast(list(shape))

        for l in range(L):
            memset(sst[:], 0.0, ["sst"])
            memset(dst[:], 0.0, ["dst"])
            memset(dstB[:], 0.0, ["dstB"])
            memset(tails[:], 0.0, ["tails%d" % i for i in range(44)])
            w1 = w1_d[l]
            for ti, (c0, NT) in enumerate(tiles):
                nch = NT // 64
                hk = hkey(ti)
                cols = slice(c0, c0 + NT)
                p.cur_tag = "S0.%d.%d" % (l, ti)
                bN, bNk = bank(7)
                for k in range(KC):
                    i = rot("sqs", 2)
                    act(sqs[i][:, 0:NT], hT[:, k, cols], AF.Square, [hk], ["sqs%d" % i])
                    mm(bN[:, 0:NT], ones, sqs[i][:, 0:NT], ["cst", "sqs%d" % i], [bNk], start=(k == 0), stop=(k == KC - 1))
                rsqrt_to(rsb[:, 0, 0:NT], bN[:, 0:NT], 1.0 / D, [bNk], ["rsb"])
                for k in range(KC):
                    stt(xnT[:, k, 0:NT], hT[:, k, cols], pf[:, l, PF_NPRE + k:PF_NPRE + k + 1], rsb[:, 0, 0:NT],
                        ALU.mult, ALU.mult, [hk, "pf", "rsb"], ["xnT"])
                if l == 0 and ti == 0 and "xnT" in dbg_d:
                    acopy(merged[:, :, 0:NT], xnT[:, :, 0:NT], ["xnT"], ["merged"])
                    dump("xnT", merged[:, :, 0:NT], ["merged"])
                p.cur_tag = "S1.%d.%d" % (l, ti)
                bS, bSk = bank(6)
                for c in range(nch):
                    for k in range(KC):
                        mm(bS[0:64, c * 32:(c + 1) * 32], xnT[:, k, c * 64:(c + 1) * 64], wsm[:, l, k, :], ["xnT", "wsm%d" % l], [bSk],
                           start=(k == 0), stop=(k == KC - 1), passes=1.0)
                acopy(tsm[:, 0:nch, :], bS[0:64, 0:nch * 32].rearrange("p (c f) -> p c f", f=32), [bSk], ["tsm"])
                tt(tk[:, 0:nch, 0:16], tsm[:, 0:nch, 0:16], bc(pbt[0:64, l, 0:16], [64, nch, 16], 1), ALU.add, ["tsm", "pbt"], ["tk"])
                tt(tk[:, 0:nch, 40:48], tsm[:, 0:nch, 24:32], bc(pbt[0:64, l, 32:40], [64, nch, 8], 1), ALU.add, ["tsm", "pbt"], ["tk"])
                act(tk[:, 0:nch, 0:16], tk[:, 0:nch, 0:16], AF.Exp, ["tk"], ["tk"])
                act(tk[:, 0:nch, 40:48], tk[:, 0:nch, 40:48], AF.Exp, ["tk"], ["tk"])
                act(tk[:, 0:nch, 0:16], tk[:, 0:nch, 0:16], AF.Ln, ["tk"], ["tk"], bias=1.0)
                act(tk[:, 0:nch, 40:48], tk[:, 0:nch, 40:48], AF.Ln, ["tk"], ["tk"], bias=1.0)
                act(tk[:, 0:nch, 32:40], tsm[:, 0:nch, 16:24], AF.Sigmoid, ["tsm"], ["tk"])
                tt(tk[:, 0:nch, 16:32], tk[:, 0:nch, 0:16], bc(negA[0:64, l, 0:16], [64, nch, 16], 1), ALU.mult, ["tk", "negA"], ["tk"])
                tt(tk[:, 0:nch, 40:48], tk[:, 0:nch, 40:48], bc(negA[0:64, l, 16:24], [64, nch, 8], 1), ALU.mult, ["tk", "negA"], ["tk"])
                if l == 0 and ti == 0:
                    dump("tk", tk[:, 0:nch, :], ["tk"])
                p.cur_tag = "S2.%d.%d" % (l, ti)
                for blk in range(2):
                    ws, wk = wload(l, blk)
                    for q in range(4):
                        fc = blk * 4 + q
                        bk, bkey = nb()
                        proj(bk, bkey, ws, wk, q * 128, NT)
                        act(siluz[:, fc, 0:NT], bk[:, 0:NT], AF.Silu, [bkey], ["dnT3"])
                for blk in range(3):
                    ws, wk = wload(l, 2 + blk)
                    for q in range(4):
                        fc = blk * 4 + q
                        bk, bkey = nb()
                        proj(bk, bkey, ws, wk, q * 128, NT)
                        if fc < 8:
                            dest, dkey = xsT[:, fc, 0:NT], "dnT0"
                        else:
                            dest, dkey = bcT[:, fc - 8, 0:NT], "bcT"
                        conv(bk, bkey, fc, PF_SCW + fc * 4, 4, l, NT, dest, dkey, bias_col=PF_SCB + fc)
                if l == 0 and ti == 0:
                    pass
                    dump("bcT", bcT[:, :, 0:NT], ["bcT"])
                p.cur_tag = "S3.%d.%d" % (l, ti)
                for c in range(nch):
                    cc = slice(c * 64, (c + 1) * 64)
                    dt_c = tk[:, c, 0:16]
                    dtA_c = tk[:, c, 16:32]
                    bM, bMk = bank(6)
                    mm(bM[0:64, 0:16], U, dtA_c, ["cst", "tk"], [bMk])
                    mm(bM[0:64, 16:32], G, dtA_c, ["cst", "tk"], [bMk])
                    mm(bM[0:128, 32:48], ones[0:64, :], dtA_c, ["cst", "tk"], [bMk])
                    act(smE[:, 0:32], bM[0:64, 0:32], AF.Exp, [bMk], ["smE"])
                    act(elast[:, 0:16], bM[:, 32:48], AF.Exp, [bMk], ["elast"])
                    for g in range(2):
                        hs = slice(g * 8, (g + 1) * 8)
                        state["bankset"] = (0, 1, 2) if g == 0 else (3, 4, 5)
                        bX, bXk = nb()
                        for q in range(4):
                            trb(bX[0:64, q * 128:(q + 1) * 128], xsT[:, g * 4 + q, cc], ["dnT0"], [bXk])
                        bB, bBk = nb()
                        tr(bB[0:64, 0:128], bcT[:, g, cc], ["bcT"], [bBk])
                        xc, xck = Bv(fpool[0], 512), "fp0"
                        tt(xc.rearrange("p (h d) -> p h d", d=64), bX[0:64, :].rearrange("p (h d) -> p h d", d=64),
                           bc(dt_c[:, hs], [64, 8, 64], 2), ALU.mult, [bXk, "tk"], [xck])
                        xd, xdk = Bv(fpool[1], 512), "fp1"
                        tt(xd.rearrange("p (h d) -> p h d", d=64), xc.rearrange("p (h d) -> p h d", d=64),
                           bc(smE[:, 16 + g * 8:16 + (g + 1) * 8], [64, 8, 64], 2), ALU.mult, [xck, "smE"], [xdk], eng="pool")
                        btok, btk = Bv(fpool[2], 128), "fp2"
                        acopy(btok, bB[0:64, 0:128], [bBk], [btk])
                        gm, gmk = F(3)
                        tt(gm[:].rearrange("p (h d) -> p h d", d=64), bc(G, [64, 8, 64], 1), bc(dtA_c[:, hs], [64, 8, 64], 2),
                           ALU.mult, ["cst", "tk"], [gmk], eng="pool")
                        bE, bEk = nb()
                        for h in range(8):
                            mm(bE[0:64, h * 64:(h + 1) * 64], gm[:, h * 64:(h + 1) * 64], U, [gmk, "cst"], [bEk])
                        E, Ek = F(4)
                        act(E[:], bE[0:64, :], AF.Exp, [bEk], [Ek])
                        bC, bCk = nb()
                        mm(bC[0:64, 0:64], bcT[:, g, cc], bcT[:, 2 + g, cc], ["bcT"], [bCk])
                        cbm, cbk = F(2)[0][:, 128:192], "fp2"
                        tt(cbm[:, 0:64], bC[0:64, 0:64], U, ALU.mult, [bCk, "cst"], [cbk])
                        MT, MTk = Bv(fpool[5], 512), "fp5"
                        tt(MT.rearrange("p (h d) -> p h d", d=64), E[:].rearrange("p (h d) -> p h d", d=64),
                           bc(cbm[:, 0:64], [64, 8, 64], 1), ALU.mult, [Ek, cbk], [MTk])
                        bY, bYk = nb()
                        for h in range(8):
                            mm(bY[0:64, h * 64:(h + 1) * 64], MT[:, h * 64:(h + 1) * 64], xc[:, h * 64:(h + 1) * 64], [MTk, xck], [bYk], passes=1.0)
                        bT, bTk = nb()
                        mm(bT[0:64, :], bcT[:, 2 + g, cc], sst[:, g * 512:(g + 1) * 512], ["bcT", "sst"], [bTk])
                        tmp, tmpk = F(4)
                        tt(tmp[:].rearrange("p (h d) -> p h d", d=64), bT[0:64, :].rearrange("p (h d) -> p h d", d=64),
                           bc(smE[:, g * 8:(g + 1) * 8], [64, 8, 64], 2), ALU.mult, [bTk, "smE"], [tmpk])
                        ytok, ytk = tbb[3], "tbb3"
                        tt(ytok[:], bY[0:64, :], tmp[:], ALU.add, [bYk, tmpk], [ytk])
                        bR, bRk = nb()
                        for q in range(4):
                            trb(bR[:, q * 64:(q + 1) * 64], ytok[:, q * 128:(q + 1) * 128], [ytk], [bRk])
                        acopy(yT[:, g * 4:(g + 1) * 4, cc], bR[:, 0:256].rearrange("p (a b) -> p a b", b=64), [bRk], ["yT"])
                        bU, bUk = nb()
                        mm(bU[:, :], btok, xd, [btk, xdk], [bUk], passes=1.0)
                        sg_ = sst[:, g * 512:(g + 1) * 512]
                        tt(sg_.rearrange("p (h d) -> p h d", d=64), sg_.rearrange("p (h d) -> p h d", d=64),
                           bc(elast[:, hs], [128, 8, 64], 2), ALU.mult, ["sst", "elast"], ["sst"], eng="pool")
                        tt(sg_, sg_, bU[:, :], ALU.add, ["sst", bUk], ["sst"])
                p.cur_tag = "S3b.%d.%d" % (l, ti)
                state["bankset"] = None
                for fc in range(KC):
                    stt(yT[:, fc, 0:NT], xsT[:, fc, 0:NT], pf[:, l, PF_SD + fc:PF_SD + fc + 1], yT[:, fc, 0:NT], ALU.mult, ALU.add,
                        ["dnT0", "pf", "yT"], ["yT"])
                tt(yT[:, :, 0:NT], yT[:, :, 0:NT], siluz[:, :, 0:NT], ALU.mult, ["yT", "dnT3"], ["yT"])
                for g in range(2):
                    bN, bNk = bank(7)
                    for q in range(4):
                        fc = g * 4 + q
                        i = rot("sqs", 2)
                        act(sqs[i][:, 0:NT], yT[:, fc, 0:NT], AF.Square, ["yT"], ["sqs%d" % i])
                        mm(bN[:, 0:NT], ones, sqs[i][:, 0:NT], ["cst", "sqs%d" % i], [bNk], start=(q == 0), stop=(q == 3))
                    rsqrt_to(rsb[:, g, 0:NT], bN[:, 0:NT], 1.0 / 512, [bNk], ["rsb"])
                for fc in range(KC):
                    stt(yB[:, fc, 0:NT], yT[:, fc, 0:NT], pf[:, l, PF_SNORM + fc:PF_SNORM + fc + 1], rsb[:, fc // 4, 0:NT],
                        ALU.mult, ALU.mult, ["yT", "pf", "rsb"], ["yB"])
                if l == 0 and ti == 0:
                    acopy(yT[:, :, 0:NT], yB[:, :, 0:NT], ["yB"], ["yT"])
                    dump("yssd", yT[:, :, 0:NT], ["yT"])

                def branch(n):
                    p.cur_tag = "BR%d.%d.%d" % (n, l, ti)
                    for half in range(2):
                        gs, gk = wload(l, 21 + n * 2 + half)
                        bs, bk_ = wload(l, 27 + n * 2 + half)
                        for q in range(4):
                            d = half * 4 + q
                            bG, bGk = nb()
                            proj(bG, bGk, gs, gk, q * 128, NT)
                            i = rot("sig", 2)
                            act(sig[i][:, 0:NT], bG[:, 0:NT], AF.Sigmoid, [bGk], ["sig%d" % i])
                            bB2, bB2k = nb()
                            proj(bB2, bB2k, bs, bk_, q * 128, NT, rhs=yB, rkey="yB")
                            if n == 0:
                                tt(merged[:, d, 0:NT], bB2[:, 0:NT], sig[i][:, 0:NT], ALU.mult, [bB2k, "sig%d" % i], ["merged"])
                            else:
                                j = rot("mtmp", 2)
                                tt(mtmp[j][:, 0:NT], bB2[:, 0:NT], sig[i][:, 0:NT], ALU.mult, [bB2k, "sig%d" % i], ["mtmp%d" % j])
                                if n == 2:
                                    tt(mergedB[:, d, 0:NT], merged[:, d, 0:NT], mtmp[j][:, 0:NT], ALU.add, ["merged", "mtmp%d" % j], ["mergedB"], eng="pool")
                                else:
                                    tt(merged[:, d, 0:NT], merged[:, d, 0:NT], mtmp[j][:, 0:NT], ALU.add, ["merged", "mtmp%d" % j], ["merged"], eng="pool")

                branch(0)
                p.cur_tag = "S6.%d.%d" % (l, ti)
                for h in range(8):
                    ws, wk = wload(l, 13 + h)
                    for which in range(3):
                        bk, bkey = nb()
                        proj(bk, bkey, ws, wk, which * 128, NT)
                        fcq = which * 8 + h
                        conv(bk, bkey, 20 + fcq, PF_DCW + fcq * 4, 4, l, NT, dq[:, which, h, 0:NT], "dnT%d" % which)
                    bk, bkey = nb()
                    proj(bk, bkey, ws, wk, 384, NT)
                    act(dz[:, h, 0:NT], bk[:, 0:NT], AF.Silu, [bkey], ["dnT3"])
                if l == 0 and ti == 0:
                    pass
                p.cur_tag = "S5.%d.%d" % (l, ti)
                for fc in range(KC):
                    ws, wk = wload(l, 5 + fc)
                    bH, bHk = nb()
                    proj(bH, bHk, ws, wk, 256, NT)
                    i = rot("sig", 2)
                    acopy(sig[i][:, 0:NT], bH[:, 0:NT], [bHk], ["sig%d" % i])
                    bC, bCk = nb()
                    proj(bC, bCk, ws, wk, 128, NT)
                    j = rot("mtmp", 2)
                    conv(bC, bCk, 12 + fc, PF_CCW + fc * 3, 3, l, NT, mtmp[j][:, 0:NT], "mtmp%d" % j, mul=sig[i][:, 0:NT], mulkey="sig%d" % i)
                    bB, bBk = nb()
                    proj(bB, bBk, ws, wk, 0, NT)
                    tt(yT[:, fc, 0:NT], bB[:, 0:NT], mtmp[j][:, 0:NT], ALU.mult, [bBk, "mtmp%d" % j], ["yT"])
                    bG, bGk = nb()
                    proj(bG, bGk, ws, wk, 384, NT)
                    i = rot("sig", 2)
                    act(sig[i][:, 0:NT], bG[:, 0:NT], AF.Silu, [bGk], ["sig%d" % i])
                    tt(yB[:, fc, 0:NT], yT[:, fc, 0:NT], sig[i][:, 0:NT], ALU.mult, ["yT", "sig%d" % i], ["yB"])
                if l == 0 and ti == 0:
                    acopy(yT[:, :, 0:NT], yB[:, :, 0:NT], ["yB"], ["yT"])
                    dump("ysc", yT[:, :, 0:NT], ["yT"])
                branch(1)
                p.cur_tag = "S7.%d.%d" % (l, ti)
                for c in range(nch):
                    cc = slice(c * 64, (c + 1) * 64)
                    beta_c = tk[:, c, 32:40]
                    g_c = tk[:, c, 40:48]
                    bM, bMk = bank(6)
                    mm(bM[0:64, 0:8], U, g_c, ["cst", "tk"], [bMk])
                    mm(bM[0:64, 8:16], G, g_c, ["cst", "tk"], [bMk])
                    mm(bM[0:128, 16:24], ones[0:64, :], g_c, ["cst", "tk"], [bMk])
                    act(smE[:, 0:16], bM[0:64, 0:16], AF.Exp, [bMk], ["smE"])
                    act(elast[:, 0:8], bM[:, 16:24], AF.Exp, [bMk], ["elast"])
                    for hh in range(2):
                        hs = slice(hh * 4, (hh + 1) * 4)
                        state["hh"] = hh
                        state["bankset"] = (0, 1, 2) if hh == 0 else (3, 4, 5)
                        fac = facs[hh]
                        fack = "fac%d" % hh
                        v3 = lambda ap: ap.rearrange("p (h d) -> p h d", d=128)
                        m3 = lambda ap: ap.rearrange("p (h d) -> p h d", d=64)
                        toks = []
                        for which in range(3):
                            bX, bXk = nb()
                            for q in range(4):
                                trb(bX[0:64, q * 128:(q + 1) * 128], dq[:, which, hh * 4 + q, cc], ["dnT%d" % which], [bXk])
                            if which < 2:
                                tkb, tkk = F(which)
                                acopy(tkb[:], bX[0:64, :], [bXk], [tkk])
                            else:
                                tkb, tkk = A_(0)
                                tt(v3(tkb), v3(bX[0:64, :]), bc(beta_c[:, hs], [64, 4, 128], 2), ALU.mult, [bXk, "tk"], [tkk])
                            toks.append((tkb, tkk))
                        (qtok, qtk), (ktok, ktk), (vb, vbk) = toks
                        for src_, srck_, c0_ in ((qtok, qtk, 0), (ktok, ktk, 4)):
                            sqb, sqk = nb()
                            act(sqb[0:64, :], src_[:], AF.Square, [srck_], [sqk])
                            p.op("dve", dur=0.65, fn=lambda e, sqb=sqb, c0_=c0_, fac=fac: e.tensor_reduce(fac[:, c0_:c0_ + 4], v3(sqb[0:64, :]), AX.X, ALU.add),
                                 reads=[sqk], writes=[fack])
                        rsqrt_to(fac[:, 0:8], fac[:, 0:8], 1.0, [fack], [fack])
                        ts(fac[:, 8:12], fac[:, 0:4], 128.0 ** -0.5, ALU.mult, [fack], [fack])
                        tt(fac[:, 12:16], fac[:, 8:12], smE[:, hh * 4:(hh + 1) * 4], ALU.mult, [fack, "smE"], [fack])
                        tt(fac[:, 16:20], fac[:, 4:8], beta_c[:, hs], ALU.mult, [fack, "tk"], [fack])
                        tt(fac[:, 16:20], fac[:, 16:20], smE[:, hh * 4:(hh + 1) * 4], ALU.mult, [fack, "smE"], [fack])
                        tt(fac[:, 20:24], fac[:, 4:8], smE[:, 8 + hh * 4:8 + (hh + 1) * 4], ALU.mult, [fack, "smE"], [fack])
                        scaled = []
                        for bi, (src, srck, f0) in enumerate(((qtok, qtk, 8), (qtok, qtk, 12), (ktok, ktk, 4), (ktok, ktk, 16), (ktok, ktk, 20))):
                            if bi < 3:
                                o, ok = tbb[bi][:], "tbb%d" % bi
                            else:
                                o, ok = A_(bi - 2)
                            tt(v3(o), v3(src[:]), bc(fac[:, f0:f0 + 4], [64, 4, 128], 2), ALU.mult, [srck, fack], [ok],
                               eng=("pool" if bi >= 3 else "dve"))
                            scaled.append((o, ok))
                        (qn, qnk), (qg, qgk), (kn, knk), (kb, kbk), (kd, kdk) = scaled
                        fTs = []
                        for j, (src, srck) in enumerate(((kn, knk), (qn, qnk), (qg, qgk))):
                            bX, bXk = nb()
                            for q in range(4):
                                trb(bX[:, q * 64:(q + 1) * 64], src[:, q * 128:(q + 1) * 128], [srck], [bXk])
                            fv, fvk = M_(j)
                            acopy(fv, bX[:, 0:256], [bXk], [fvk])
                            fTs.append((fv, fvk))
                        (knT, knTk), (qnT, qnTk), (qgT, qgTk) = fTs
                        Gl, Glk = F(2)
                        Gs, Gsk = F(3)
                        tt(m3(Gl[:, 0:256]), bc(U, [64, 4, 64], 1), bc(g_c[:, hs], [64, 4, 64], 2), ALU.mult, ["cst", "tk"], [Glk], eng="pool")
                        tt(m3(Gs[:, 0:256]), bc(G, [64, 4, 64], 1), bc(g_c[:, hs], [64, 4, 64], 2), ALU.mult, ["cst", "tk"], [Gsk], eng="pool")
                        bS1, bS1k = nb()
                        bS2, bS2k = nb()
                        for q in range(4):
                            qs = slice(q * 64, (q + 1) * 64)
                            mm(bS1[0:64, qs], Gl[:, qs], G, [Glk, "cst"], [bS1k])
                            mm(bS2[0:64, qs], Gs[:, qs], U, [Gsk, "cst"], [bS2k])
                        E1, E1k = F(4)
                        E2, E2k = F(5)
                        act(E1[:, 0:256], bS1[0:64, 0:256], AF.Exp, [bS1k], [E1k])
                        act(E2[:, 0:256], bS2[0:64, 0:256], AF.Exp, [bS2k], [E2k])
                        bK, bKk = nb()
                        bQ, bQk = nb()
                        for q in range(4):
                            qs = slice(q * 64, (q + 1) * 64)
                            mm(bK[0:64, qs], knT[:, qs], knT[:, qs], [knTk], [bKk], passes=1.0)
                            mm(bQ[0:64, qs], knT[:, qs], qnT[:, qs], [knTk, qnTk], [bQk], passes=1.0)
                        tt(m3(Gl[:, 0:256]), bc(G, [64, 4, 64], 1), bc(beta_c[:, hs], [64, 4, 64], 2), ALU.mult, ["cst", "tk"], [Glk])
                        tt(E1[:, 0:256], E1[:, 0:256], Gl[:, 0:256], ALU.mult, [E1k, Glk], [E1k])
                        Lm, Lmk = D_(0)
                        tt(Lm[:, 0:256], bK[0:64, 0:256], E1[:, 0:256], ALU.mult, [bKk, E1k], [Lmk])
                        tt(m3(E2[:, 0:256]), m3(E2[:, 0:256]), bc(U, [64, 4, 64], 1), ALU.mult, [E2k, "cst"], [E2k])
                        Xs, Xsk = D_(1)
                        tt(Xs[:, 0:256], bQ[0:64, 0:256], E2[:, 0:256], ALU.mult, [bQk, E2k], [Xsk])
                        bL, bLk = nb()
                        for q in range(4):
                            qs = slice(q * 64, (q + 1) * 64)
                            trb(bL[0:64, qs], Lm[:, qs], [Lmk], [bLk])
                        LT, LTk = D_(2)
                        acopy(LT[:, 0:256], bL[0:64, 0:256], [bLk], [LTk])
                        P, Pk = D_(3)
                        tt(m3(P[:, 0:256]), bc(I64, [64, 4, 64], 1), m3(LT[:, 0:256]), ALU.subtract, ["cst", LTk], [Pk])
                        cur, curk, curT, curTk = Lm, Lmk, LT, LTk
                        for j in range(1, 6):
                            bA, bAk = nb()
                            for q in range(4):
                                qs = slice(q * 64, (q + 1) * 64)
                                mm(bA[0:64, qs], curT[:, qs], cur[:, qs], [curTk, curk], [bAk], passes=1.0)
                            Xj, Xjk = D_({1: 4, 2: 6, 3: 4, 4: 6, 5: 4}[j])
                            acopy(Xj[:, 0:256], bA[0:64, 0:256], [bAk], [Xjk])
                            if j < 5:
                                bB, bBk = nb()
                                for q in range(4):
                                    qs = slice(q * 64, (q + 1) * 64)
                                    mm(bB[0:64, qs], cur[:, qs], curT[:, qs], [curk, curTk], [bBk], passes=1.0)
                                XjT, XjTk = D_({1: 5, 2: 7, 3: 5, 4: 7}[j])
                                vcopy(XjT[:, 0:256], bB[0:64, 0:256], [bBk], [XjTk])
                            bP, bPk = nb()
                            for q in range(4):
                                qs = slice(q * 64, (q + 1) * 64)
                                mm(bP[0:64, qs], Xj[:, qs], P[:, qs], [Xjk, Pk], [bPk], passes=1.0)
                            tt(P[:, 0:256], P[:, 0:256], bP[0:64, 0:256], ALU.add, [Pk, bPk], [Pk])
                            cur, curk = Xj, Xjk
                            if j < 5:
                                curT, curTk = XjT, XjTk
                        bW, bWk = nb()
                        for q in range(4):
                            mm(bW[:, q * 64:(q + 1) * 64], kb[:, q * 128:(q + 1) * 128], P[:, q * 64:(q + 1) * 64], [kbk, Pk], [bWk], passes=1.0)
                        wTn, wTnk = M_(3)
                        p.op("act", lambda e, wTn=wTn, bW=bW: e.mul(wTn, bW[:, 0:256], -1.0), reads=[bWk], writes=[wTnk], dur=0.47)
                        bV, bVk = nb()
                        for q in range(4):
                            h = hh * 4 + q
                            mm(bV[0:64, q * 128:(q + 1) * 128], P[:, q * 64:(q + 1) * 64], vb[:, q * 128:(q + 1) * 128], [Pk, vbk], [bVk],
                               start=True, stop=False, passes=1.0)
                            mm(bV[0:64, q * 128:(q + 1) * 128], wTn[:, q * 64:(q + 1) * 64], dstB[:, h * 128:(h + 1) * 128], [wTnk, "dstB"], [bVk],
                               start=False, stop=True, passes=1.0)
                        vnew, vnk = A_(3)
                        acopy(vnew, bV[0:64, :], [bVk], [vnk])
                        bO, bOk = nb()
                        for q in range(4):
                            h = hh * 4 + q
                            mm(bO[0:64, q * 128:(q + 1) * 128], qgT[:, q * 64:(q + 1) * 64], dstB[:, h * 128:(h + 1) * 128], [qgTk, "dstB"], [bOk],
                               start=True, stop=False, passes=1.0)
                            mm(bO[0:64, q * 128:(q + 1) * 128], Xs[:, q * 64:(q + 1) * 64], vnew[:, q * 128:(q + 1) * 128], [Xsk, vnk], [bOk],
                               start=False, stop=True, passes=1.0)
                        sqb, sqk = nb()
                        act(sqb[0:64, :], bO[0:64, :], AF.Square, [bOk], [sqk])
                        p.op("dve", dur=0.65, fn=lambda e, sqb=sqb, fac=fac: e.tensor_reduce(fac[:, 24:28], v3(sqb[0:64, :]), AX.X, ALU.add),
                             reads=[sqk], writes=[fack])
                        rsqrt_to(fac[:, 24:28], fac[:, 24:28], 1.0 / 128, [fack], [fack])
                        otb, otbk = tbb[3], "tbb3"
                        tt(v3(otb[:]), v3(bO[0:64, :]), bc(fac[:, 24:28], [64, 4, 128], 2), ALU.mult, [bOk, fack], [otbk])
                        bR, bRk = nb()
                        for q in range(4):
                            trb(bR[:, q * 64:(q + 1) * 64], otb[:, q * 128:(q + 1) * 128], [otbk], [bRk])
                        stt(yB[:, hh * 4:(hh + 1) * 4, cc], bR[:, 0:256].rearrange("p (a b) -> p a b", b=64),
                            pf[:, l, PF_DNORM:PF_DNORM + 1], dz[:, hh * 4:(hh + 1) * 4, cc], ALU.mult, ALU.mult,
                            [bRk, "pf", "dnT3"], ["yB"])
                        bU, bUk = nb()
                        for q in range(4):
                            mm(bU[:, q * 128:(q + 1) * 128], kd[:, q * 128:(q + 1) * 128], vnew[:, q * 128:(q + 1) * 128], [kdk, vnk], [bUk], passes=1.0)
                        dh = dst[:, hh * 512:(hh + 1) * 512]
                        tt(v3(dh), v3(dh), bc(elast[:, hs], [128, 4, 128], 2), ALU.mult, ["dst", "elast"], ["dst"], eng="pool")
                        tt(dh, dh, bU[:, :], ALU.add, ["dst", bUk], ["dst"])
                        acopy(dstB[:, hh * 512:(hh + 1) * 512], dh, ["dst"], ["dstB"])
                state["hh"] = 0
                state["bankset"] = None
                if l == 0 and ti == 0:
                    acopy(yT[:, :, 0:NT], yB[:, :, 0:NT], ["yB"], ["yT"])
                    dump("ydn", yT[:, :, 0:NT], ["yT"])
                branch(2)
                if l == 0 and ti == 0:
                    acopy(merged[:, :, 0:NT], mergedB[:, :, 0:NT], ["mergedB"], ["merged"])
                    dump("merged", merged[:, :, 0:NT], ["merged"])
                p.cur_tag = "S8.%d.%d" % (l, ti)
                bN, bNk = bank(7)
                for half in range(2):
                    ws, wk = wload(l, 33 + half)
                    for q in range(4):
                        d = half * 4 + q
                        bO, bOk = nb()
                        proj(bO, bOk, ws, wk, q * 128, NT, rhs=mergedB, rkey="mergedB")
                        acopy(yT[:, d, 0:NT], bO[:, 0:NT], [bOk], ["yT"])
                        i = rot("sqs", 2)
                        act(sqs[i][:, 0:NT], bO[:, 0:NT], AF.Square, [bOk], ["sqs%d" % i])
                        mm(bN[:, 0:NT], ones, sqs[i][:, 0:NT], ["cst", "sqs%d" % i], [bNk], start=(d == 0), stop=(d == KC - 1))
                rsqrt_to(rsb[:, 0, 0:NT], bN[:, 0:NT], 1.0 / D, [bNk], ["rsb"])
                for d in range(KC):
                    j = rot("mtmp", 2)
                    stt(mtmp[j][:, 0:NT], yT[:, d, 0:NT], pf[:, l, PF_NPOST + d:PF_NPOST + d + 1], rsb[:, 0, 0:NT], ALU.mult, ALU.mult,
                        ["yT", "pf", "rsb"], ["mtmp%d" % j])
                    tt(hT[:, d, cols], hT[:, d, cols], mtmp[j][:, 0:NT], ALU.add, [hk, "mtmp%d" % j], [hk], eng="pool")
                if l == 0 and ti == 0:
                    dump("h1", hT[:, :, cols], [hk])

        for r0 in range(0, S, 128):
            nr = min(128, S - r0)
            col0 = NMETA + r0
            i = rot("xstg", 1)
            skey = "merged"
            stg2 = xstg[i]
            for half in range(2):
                bk, bkey = nb()
                for q in range(4):
                    k = half * 4 + q
                    tr(bk[0:nr, q * 128:(q + 1) * 128], hT[:, k, col0:col0 + nr], hkeys(col0, col0 + nr), [bkey])
                acopy(stg2[0:nr, half * 512:(half + 1) * 512], bk[0:nr, :], [bkey], [skey])
            dma(out_d[r0:r0 + nr, :], stg2[0:nr, :], [skey], [], "och")
        p.finalize(st)
    return nc, p


def make_consts():
    c = np.zeros((128, 384), np.float32)
    c[:, 0:128] = np.eye(128, dtype=np.float32)
    c[:, 128:256] = 1.0
    a = np.arange(64)
    c[0:64, 256:320] = (a[:, None] <= a[None, :]).astype(np.float32)
    c[0:64, 320:384] = (a[:, None] > a[None, :]).astype(np.float32)
    return c


def pack_weights(inp):
    L = inp["w_in"].shape[0]
    w_in = np.asarray(inp["w_in"], np.float32)
    cols = []
    cols += list(range(O_Z, O_Z + 1024))
    cols += list(range(O_XBC, O_XBC + 1536))
    for fc in range(8):
        for base in (O_SCB, O_SCC, O_SCH, O_SCG):
            cols += list(range(base + fc * 128, base + (fc + 1) * 128))
    for h in range(8):
        for base in (O_Q, O_K, O_V, O_DZ):
            cols += list(range(base + h * 128, base + (h + 1) * 128))
    cols += list(range(O_GATE, O_GATE + 3072))
    cols = np.asarray(cols)
    assert cols.size == NW1
    w1 = np.ascontiguousarray(w_in[:, :, cols])
    wsm = np.ascontiguousarray(np.concatenate([w_in[:, :, O_DT:O_DT + 16], w_in[:, :, O_DB:O_DB + 16]], axis=-1))
    pf = np.zeros((128, L, NPF), np.float32)
    fm = lambda v, n: np.asarray(v, np.float32).reshape(n, 128).T
    for l in range(L):
        pf[:, l, PF_NPRE:PF_NPRE + 8] = fm(inp["norm_pre"][l], 8)
        pf[:, l, PF_NPOST:PF_NPOST + 8] = fm(inp["norm_post"][l], 8)
        scw = np.asarray(inp["ssd_conv_w"][l], np.float32)
        pf[:, l, PF_SCW:PF_SCW + 48] = scw.reshape(4, 12, 128).transpose(2, 1, 0).reshape(128, 48)
        pf[:, l, PF_SCB:PF_SCB + 12] = fm(inp["ssd_conv_b"][l], 12)
        pf[:, l, PF_SD:PF_SD + 8] = fm(np.repeat(np.asarray(inp["ssd_d"][l], np.float32), 64), 8)
        pf[:, l, PF_SNORM:PF_SNORM + 8] = fm(inp["ssd_norm"][l], 8)
        ccw = np.asarray(inp["sc_conv_w"][l], np.float32)
        pf[:, l, PF_CCW:PF_CCW + 24] = ccw.reshape(3, 8, 128).transpose(2, 1, 0).reshape(128, 24)
        dcw = np.asarray(inp["dn_conv_w"][l], np.float32)
        pf[:, l, PF_DCW:PF_DCW + 96] = dcw.reshape(4, 24, 128).transpose(2, 1, 0).reshape(128, 96)
        pf[:, l, PF_DNORM] = np.asarray(inp["dn_norm"][l], np.float32)
    pb = np.zeros((128, L, 48), np.float32)
    for l in range(L):
        row = np.concatenate([inp["ssd_dt_bias"][l], inp["ssd_a_log"][l], inp["dn_dt_bias"][l], inp["dn_a_log"][l]]).astype(np.float32)
        pb[:, l, :] = np.broadcast_to(row[None, :], (128, 48))
    return dict(w1=w1, wsm=wsm, wb=np.ascontiguousarray(inp["w_branch"], np.float32), wo=np.ascontiguousarray(inp["w_out"], np.float32),
                pf=pf, pb=pb, cst=make_consts(), meta=np.ascontiguousarray(inp["meta_tokens"], np.float32))


_CACHE = {}


def kernel(**inputs):
    x = np.asarray(inputs["x"], np.float32)
    B, S, _ = x.shape
    L = inputs["w_in"].shape[0]
    shared = pack_weights(inputs)
    key = (S, L)
    if key not in _CACHE:
        _CACHE[key] = build_program(S, L)[0]
    nc = _CACHE[key]
    in_maps = []
    for b in range(B):
        m = dict(shared)
        m["x"] = np.ascontiguousarray(x[b])
        in_maps.append(m)
    res = run_bass_kernel_spmd(nc, in_maps, core_ids=list(range(B)))
    return np.stack([np.asarray(r["out"], np.float32) for r in res.results], axis=0)
```
